# Optimizing a Trainium2 kernel written in Bass

```python
import jax, jax.numpy as jnp
from jax import lax
import numpy as np

D_MODEL = 1024
BATCH = 8
SEQ = 2048
DEPTH = 1
DEC_BATCH = 8
DEC_SEQ = 64
PAST_LEN = 4096

CHUNK = 64
Q_BLOCK = 128
FOX_HEADS = 8
FOX_DIM = 64
ML_HEADS = 4
ML_DIM = 128
ML_CONV = 4
MEM_HEADS = 4
MEM_DIM = 128
MEM_LEN = 256
D_FF = 2816
FFN_CONV = 3
N_BRANCH = 3
FORGET_BIAS = 3.0
EPS = 1e-6

FOX_W = FOX_HEADS * FOX_DIM
ML_W = ML_HEADS * ML_DIM
MEM_W = MEM_HEADS * MEM_DIM
IN_SIZES = (FOX_W, FOX_W, FOX_W, FOX_HEADS, ML_W, ML_W, ML_W, ML_HEADS, ML_HEADS, ML_W, MEM_W, N_BRANCH * D_MODEL)
IN_OFFSETS = tuple(sum(IN_SIZES[:i + 1]) for i in range(len(IN_SIZES)))
IN_W = IN_OFFSETS[-1]
SPLIT_IDX = list(IN_OFFSETS[:-1])
FOXF_LO, FOXF_HI = IN_OFFSETS[2], IN_OFFSETS[3]
MLF_LO, MLF_HI = IN_OFFSETS[7], IN_OFFSETS[8]

kernel_name = 'hybrid_fox_mlstm_streaming_encoder'


def rms_norm(x, g):
    xf = x.astype(jnp.float32)
    y = xf * lax.rsqrt(jnp.mean(xf * xf, axis=-1, keepdims=True) + EPS)
    return (y * g.astype(jnp.float32)).astype(x.dtype)


def causal_dwconv(x, prev, w, b):
    width = w.shape[0]
    t = x.shape[1]
    xp = jnp.concatenate([prev.astype(x.dtype), x], axis=1)
    y = sum(xp[:, j:j + t] * w[j] for j in range(width)) + b
    return y, xp[:, -(width - 1):]


def fox_attend(q, k, v, cq, ck, q_pos, k_pos):
    s = jnp.einsum('bqhd,bkhd->bhqk', q, k).astype(jnp.float32) * (FOX_DIM ** -0.5)
    s = s + jnp.transpose(cq, (0, 2, 1))[..., :, None] - jnp.transpose(ck, (0, 2, 1))[..., None, :]
    s = jnp.where(k_pos[None, :] <= q_pos[:, None], s, -jnp.inf)
    p = jax.nn.softmax(s, axis=-1)
    return jnp.einsum('bhqk,bkhd->bqhd', p.astype(v.dtype), v)


def mem_attend(q, k, v):
    s = jnp.einsum('bqhd,bmhd->bhqm', q, k).astype(jnp.float32) * (MEM_DIM ** -0.5)
    p = jax.nn.softmax(s, axis=-1)
    return jnp.einsum('bhqm,bmhd->bqhd', p.astype(v.dtype), v)


def memory_kv(mem, norm_mem, w_mem_kv):
    b, m, _ = mem.shape
    kv = rms_norm(mem, norm_mem) @ w_mem_kv
    k, v = jnp.split(kv, 2, axis=-1)
    return k.reshape(b, m, MEM_HEADS, MEM_DIM), v.reshape(b, m, MEM_HEADS, MEM_DIM)


def mlstm_chunkwise(q, k, v, i_pre, logf, c0, n0, m0):
    b_, t, h, d = q.shape
    L = min(CHUNK, t)
    n_blk = t // L

    def blocks(a):
        return a.reshape((b_, n_blk, L) + a.shape[2:])

    q, k, v = blocks(q), blocks(k), blocks(v)
    it = jnp.transpose(blocks(i_pre), (0, 1, 3, 2))
    bt = jnp.transpose(jnp.cumsum(blocks(logf), axis=2), (0, 1, 3, 2))
    b_end = bt[..., -1]
    g = b_end[..., None] - bt + it
    g_max = jnp.max(g, axis=-1)
    wg = jnp.exp(g - g_max[..., None])
    kv_blk = jnp.einsum('bnhs,bnshv,bnshk->bnhvk', wg, v, k)
    k_blk = jnp.einsum('bnhs,bnshk->bnhk', wg, k)

    def step(carry, xs):
        c, n, m = carry
        be, gm, kvb, kb = xs
        m_new = jnp.maximum(be + m, gm)
        decay = jnp.exp(be + m - m_new)
        scale = jnp.exp(gm - m_new)
        c_new = decay[..., None, None] * c + scale[..., None, None] * kvb
        n_new = decay[..., None] * n + scale[..., None] * kb
        return (c_new, n_new, m_new), (c, n, m)

    to_n = lambda a: jnp.moveaxis(a, 1, 0)
    (c_T, n_T, m_T), (c_p, n_p, m_p) = lax.scan(
        step, (c0, n0, m0), (to_n(b_end), to_n(g_max), to_n(kv_blk), to_n(k_blk)))
    c_p, n_p, m_p = jnp.moveaxis(c_p, 0, 1), jnp.moveaxis(n_p, 0, 1), jnp.moveaxis(m_p, 0, 1)

    log_w = bt[..., :, None] - bt[..., None, :] + it[..., None, :]
    causal = jnp.tril(jnp.ones((L, L), dtype=bool))
    log_w = jnp.where(causal, log_w, -jnp.inf)
    log_inter = bt + m_p[..., None]
    m_t = jnp.maximum(log_inter, jnp.max(log_w, axis=-1))
    s = jnp.einsum('bnthd,bnshd->bnhts', q, k) * jnp.exp(log_w - m_t[..., None])
    w_inter = jnp.exp(log_inter - m_t)
    num = (w_inter[..., None] * jnp.einsum('bnthk,bnhvk->bnhtv', q, c_p)
           + jnp.einsum('bnhts,bnshv->bnhtv', s, v))
    den = w_inter * jnp.einsum('bnthk,bnhk->bnht', q, n_p) + jnp.sum(s, axis=-1)
    hout = num / jnp.maximum(jnp.abs(den), jnp.exp(-m_t))[..., None]
    hout = jnp.transpose(hout, (0, 1, 3, 2, 4)).reshape(b_, t, h, d)
    return hout, c_T, n_T, m_T


def encoder_layer(x, fox_cache, ml_c0, ml_n0, ml_m0, ml_conv_prev, mem_k, mem_v, ffn_conv_prev, lw):
    (norm_mix_pre, w_in, b_in, fox_q_norm, fox_k_norm, mlstm_conv_w, mlstm_conv_b, mlstm_head_norm,
     w_br_a, w_br_b, w_br_m, w_out, norm_mix_post, norm_ffn_pre, w_up, ffn_conv_w, ffn_conv_b,
     w_down, norm_ffn_post) = lw
    f32 = jnp.float32
    B, T, _ = x.shape
    xn = rms_norm(x, norm_mix_pre)
    z = xn @ w_in + b_in
    fq, fk, fv, ff, mq, mk, mv, mi, mf, mo, cq, gate_logits = jnp.split(z, SPLIT_IDX, axis=-1)

    fq = rms_norm(fq.reshape(B, T, FOX_HEADS, FOX_DIM), fox_q_norm)
    fk = rms_norm(fk.reshape(B, T, FOX_HEADS, FOX_DIM), fox_k_norm)
    fv = fv.reshape(B, T, FOX_HEADS, FOX_DIM)
    f_log = jax.nn.log_sigmoid(ff.astype(f32))
    if fox_cache is None:
        c = jnp.cumsum(f_log, axis=1)
        pos = jnp.arange(T)
        outs = []
        for lo in range(0, T, Q_BLOCK):
            hi = min(lo + Q_BLOCK, T)
            outs.append(fox_attend(fq[:, lo:hi], fk[:, :hi], fv[:, :hi],
                                   c[:, lo:hi], c[:, :hi], pos[lo:hi], pos[:hi]))
        a_out = jnp.concatenate(outs, axis=1)
    else:
        k_cache, v_cache, logf_cache = fox_cache
        P = k_cache.shape[1]
        c_cache = jnp.cumsum(logf_cache.astype(f32), axis=1)
        c_cache = c_cache - c_cache[:, -1:]
        c_new = jnp.cumsum(f_log, axis=1)
        k_all = jnp.concatenate([k_cache.astype(fk.dtype), fk], axis=1)
        v_all = jnp.concatenate([v_cache.astype(fv.dtype), fv], axis=1)
        c_all = jnp.concatenate([c_cache, c_new], axis=1)
        a_out = fox_attend(fq, k_all, v_all, c_new, c_all, P + jnp.arange(T), jnp.arange(P + T))
    a_out = a_out.reshape(B, T, FOX_W)

    qk, ml_conv_new = causal_dwconv(jnp.concatenate([mq, mk], axis=-1), ml_conv_prev,
                                    mlstm_conv_w, mlstm_conv_b)
    qk = jax.nn.silu(qk)
    mq2, mk2 = jnp.split(qk, 2, axis=-1)
    hq = mq2.reshape(B, T, ML_HEADS, ML_DIM).astype(f32) * (ML_DIM ** -0.5)
    hk = mk2.reshape(B, T, ML_HEADS, ML_DIM).astype(f32)
    hv = mv.reshape(B, T, ML_HEADS, ML_DIM).astype(f32)
    h, c_T, n_T, m_T = mlstm_chunkwise(hq, hk, hv, mi.astype(f32), jax.nn.log_sigmoid(mf.astype(f32)),
                                       ml_c0.astype(f32), ml_n0.astype(f32), ml_m0.astype(f32))
    h = h * jax.nn.sigmoid(mo.astype(f32)).reshape(B, T, ML_HEADS, ML_DIM)
    b_out = rms_norm(h, mlstm_head_norm).astype(x.dtype).reshape(B, T, ML_W)

    m_out = mem_attend(cq.reshape(B, T, MEM_HEADS, MEM_DIM), mem_k.astype(x.dtype),
                       mem_v.astype(x.dtype)).reshape(B, T, MEM_W)

    gates = jax.nn.sigmoid(gate_logits.astype(f32)).astype(x.dtype).reshape(B, T, N_BRANCH, D_MODEL)
    merged = (gates[:, :, 0] * (a_out @ w_br_a) + gates[:, :, 1] * (b_out @ w_br_b)
              + gates[:, :, 2] * (m_out @ w_br_m))
    x = x + rms_norm(merged @ w_out, norm_mix_post)

    up = rms_norm(x, norm_ffn_pre) @ w_up
    up, ffn_conv_new = causal_dwconv(up, ffn_conv_prev, ffn_conv_w, ffn_conv_b)
    ua, ub = jnp.split(up, 2, axis=-1)
    hid = jax.nn.gelu(ua, approximate=True) * ub
    x = x + rms_norm(hid @ w_down, norm_ffn_post)
    return x, (fk, fv, f_log, c_T, n_T, m_T, ml_conv_new, ffn_conv_new)


def setup_inputs(seed: int = 0) -> dict:
    key = jax.random.key(seed)
    ks = list(jax.random.split(key, 48))

    def nrm(shape, scale=1.0):
        return scale * jax.random.normal(ks.pop(), shape, jnp.float32)

    def gain(shape):
        return 1.0 + 0.05 * nrm(shape)

    L = DEPTH
    b_in = nrm((L, IN_W), 0.02)
    b_in = b_in.at[:, FOXF_LO:FOXF_HI].add(FORGET_BIAS).at[:, MLF_LO:MLF_HI].add(FORGET_BIAS)
    return {
        'x_prompt': nrm((BATCH, SEQ, D_MODEL)),
        'x_sample': nrm((DEC_BATCH, DEC_SEQ, D_MODEL)),
        'cache_fox_k': nrm((L, DEC_BATCH, PAST_LEN, FOX_HEADS, FOX_DIM)),
        'cache_fox_v': nrm((L, DEC_BATCH, PAST_LEN, FOX_HEADS, FOX_DIM)),
        'cache_fox_logf': jax.nn.log_sigmoid(nrm((L, DEC_BATCH, PAST_LEN, FOX_HEADS)) + FORGET_BIAS),
        'state_mlstm_c': nrm((L, DEC_BATCH, ML_HEADS, ML_DIM, ML_DIM), 0.1),
        'state_mlstm_n': nrm((L, DEC_BATCH, ML_HEADS, ML_DIM), 0.1),
        'state_mlstm_m': nrm((L, DEC_BATCH, ML_HEADS), 0.5),
        'state_mlstm_conv': nrm((L, DEC_BATCH, ML_CONV - 1, 2 * ML_W)),
        'cache_mem_k': nrm((L, DEC_BATCH, MEM_LEN, MEM_HEADS, MEM_DIM)),
        'cache_mem_v': nrm((L, DEC_BATCH, MEM_LEN, MEM_HEADS, MEM_DIM)),
        'state_ffn_conv': nrm((L, DEC_BATCH, FFN_CONV - 1, 2 * D_FF)),
        'mem_prompt': nrm((BATCH, MEM_LEN, D_MODEL)),
        'norm_mix_pre': gain((L, D_MODEL)),
        'w_in': nrm((L, D_MODEL, IN_W), D_MODEL ** -0.5),
        'b_in': b_in,
        'fox_q_norm': gain((L, FOX_DIM)),
        'fox_k_norm': gain((L, FOX_DIM)),
        'mlstm_conv_w': nrm((L, ML_CONV, 2 * ML_W), ML_CONV ** -0.5),
        'mlstm_conv_b': nrm((L, 2 * ML_W), 0.02),
        'mlstm_head_norm': gain((L, ML_HEADS, ML_DIM)),
        'norm_mem': gain((L, D_MODEL)),
        'w_mem_kv': nrm((L, D_MODEL, 2 * MEM_W), D_MODEL ** -0.5),
        'w_br_a': nrm((L, FOX_W, D_MODEL), FOX_W ** -0.5),
        'w_br_b': nrm((L, ML_W, D_MODEL), ML_W ** -0.5),
        'w_br_m': nrm((L, MEM_W, D_MODEL), MEM_W ** -0.5),
        'w_out': nrm((L, D_MODEL, D_MODEL), D_MODEL ** -0.5),
        'norm_mix_post': gain((L, D_MODEL)),
        'norm_ffn_pre': gain((L, D_MODEL)),
        'w_up': nrm((L, D_MODEL, 2 * D_FF), D_MODEL ** -0.5),
        'ffn_conv_w': nrm((L, FFN_CONV, 2 * D_FF), FFN_CONV ** -0.5),
        'ffn_conv_b': nrm((L, 2 * D_FF), 0.02),
        'w_down': nrm((L, D_FF, D_MODEL), D_FF ** -0.5),
        'norm_ffn_post': gain((L, D_MODEL)),
    }


def reference(x_prompt, x_sample, cache_fox_k, cache_fox_v, cache_fox_logf, state_mlstm_c,
              state_mlstm_n, state_mlstm_m, state_mlstm_conv, cache_mem_k, cache_mem_v,
              state_ffn_conv, mem_prompt, norm_mix_pre, w_in, b_in, fox_q_norm, fox_k_norm,
              mlstm_conv_w, mlstm_conv_b, mlstm_head_norm, norm_mem, w_mem_kv, w_br_a, w_br_b,
              w_br_m, w_out, norm_mix_post, norm_ffn_pre, w_up, ffn_conv_w, ffn_conv_b, w_down,
              norm_ffn_post):
    f32 = jnp.float32
    B = x_prompt.shape[0]
    hp, hs = x_prompt, x_sample
    new_p = [[] for _ in range(10)]
    new_s = [[] for _ in range(8)]
    for l in range(DEPTH):
        lw = (norm_mix_pre[l], w_in[l], b_in[l], fox_q_norm[l], fox_k_norm[l], mlstm_conv_w[l],
              mlstm_conv_b[l], mlstm_head_norm[l], w_br_a[l], w_br_b[l], w_br_m[l], w_out[l],
              norm_mix_post[l], norm_ffn_pre[l], w_up[l], ffn_conv_w[l], ffn_conv_b[l], w_down[l],
              norm_ffn_post[l])
        mem_k_p, mem_v_p = memory_kv(mem_prompt, norm_mem[l], w_mem_kv[l])
        hp, st_p = encoder_layer(
            hp, None,
            jnp.zeros((B, ML_HEADS, ML_DIM, ML_DIM), f32), jnp.zeros((B, ML_HEADS, ML_DIM), f32),
            jnp.zeros((B, ML_HEADS), f32), jnp.zeros((B, ML_CONV - 1, 2 * ML_W), hp.dtype),
            mem_k_p, mem_v_p, jnp.zeros((B, FFN_CONV - 1, 2 * D_FF), hp.dtype), lw)
        hs, st_s = encoder_layer(
            hs, (cache_fox_k[l], cache_fox_v[l], cache_fox_logf[l]),
            state_mlstm_c[l], state_mlstm_n[l], state_mlstm_m[l], state_mlstm_conv[l],
            cache_mem_k[l], cache_mem_v[l], state_ffn_conv[l], lw)
        for acc, a in zip(new_p, st_p + (mem_k_p, mem_v_p)):
            acc.append(a)
        for acc, a in zip(new_s, st_s):
            acc.append(a)
    sp = [jnp.stack(a, axis=0) for a in new_p]
    ss = [jnp.stack(a, axis=0) for a in new_s]
    return (hp, hs, sp[0], sp[1], sp[2], sp[3], sp[4], sp[5], sp[6], sp[7], sp[8], sp[9],
            ss[0], ss[1], ss[2], ss[3], ss[4], ss[5], ss[6], ss[7])
```

```python
import numpy as np
from contextlib import ExitStack
import concourse.bass as bass
import concourse.mybir as mybir

F32 = mybir.dt.float32
BF16 = mybir.dt.bfloat16
AF = mybir.ActivationFunctionType
ALU = mybir.AluOpType
AX = mybir.AxisListType


def _esize(dt):
    n = str(dt)
    if "float32" in n or "int32" in n:
        return 4
    if "bfloat16" in n or "float16" in n or "int16" in n:
        return 2
    if "int8" in n or "float8" in n:
        return 1
    if "64" in n:
        return 8
    raise ValueError(n)


def _boxes(ap):
    t = ap.tensor
    name = t.name
    es = _esize(ap.dtype)
    dims = [(int(s), int(n)) for s, n in ap.ap]
    off = int(ap.offset)
    if "DRAM" in str(ap.space).upper() or "HBM" in str(ap.space).upper():
        ext = sum((n - 1) * abs(s) for s, n in dims)
        return name, [(0, 1, off * es, (off + ext + 1) * es)]
    tes = _esize(t.dtype)
    rowbytes = int(np.prod([int(x) for x in t.shape[1:]])) * tes
    row = rowbytes // es
    p0 = off // row
    f0 = off % row
    pext = 0
    fd = []
    for s, n in dims:
        if n == 1:
            continue
        if s != 0 and s % row == 0:
            pext += (n - 1) * (s // row)
        elif s != 0:
            fd.append((abs(s), n))
    fd.sort(reverse=True)
    p1 = p0 + pext + 1
    if len(fd) >= 2:
        inner = sum((n - 1) * s for s, n in fd[1:]) + 1
        s0, n0 = fd[0]
        if s0 >= inner and n0 <= 64:
            return name, [(p0, p1, (f0 + i * s0) * es, (f0 + i * s0 + inner) * es) for i in range(n0)]
    ext = sum((n - 1) * s for s, n in fd) + 1
    return name, [(p0, p1, f0 * es, (f0 + ext) * es)]


def _ov(a, b):
    for x in a:
        for y in b:
            if x[0] < y[1] and y[0] < x[1] and x[2] < y[3] and y[2] < x[3]:
                return True
    return False


def _contained(a, b):
    for x in a:
        ok = False
        for y in b:
            if y[0] <= x[0] and x[1] <= y[1] and y[2] <= x[2] and x[3] <= y[3]:
                ok = True
                break
        if not ok:
            return False
    return True


class _Stop(Exception):
    pass


class Prog:
    ENGS = ("pe", "act", "dve", "pool", "sp")

    def __init__(self, nc):
        self.nc = nc
        self.ops = []
        self.hist = {}

    def add(self, eng, fn, reads=(), writes=(), dma=False):
        idx = len(self.ops)
        deps = set()
        rb = [_boxes(a) for a in reads]
        wb = [_boxes(a) for a in writes]
        for name, bx in rb:
            for e in self.hist.get(name, ()):
                if e[2] and _ov(e[0], bx):
                    deps.add(e[1])
        for name, bx in wb:
            for e in self.hist.get(name, ()):
                if _ov(e[0], bx):
                    deps.add(e[1])
        for name, bx in wb:
            lst = [e for e in self.hist.get(name, ()) if not _contained(e[0], bx)]
            lst.append((bx, idx, True, eng, dma))
            self.hist[name] = lst
        for name, bx in rb:
            lst = self.hist.setdefault(name, [])
            rep = False
            if not dma:
                for i, e in enumerate(lst):
                    if (not e[2]) and e[3] == eng and (not e[4]) and e[0] == bx:
                        lst[i] = (bx, idx, False, eng, dma)
                        rep = True
                        break
            if not rep:
                lst.append((bx, idx, False, eng, dma))
        deps.discard(idx)
        self.ops.append((eng, fn, deps, dma))
        return idx

    def mark(self, name):
        if not hasattr(self, 'marks'):
            self.marks = []
        self.marks.append((name, sum(1 for o in self.ops if o[0] == 'pe')))
        if getattr(self, 'stop_at', None) == name:
            self.emit()
            raise _Stop()

    def dma(self, q, out, in_, **kw):
        kw.setdefault('allow_slow_non_contiguous', True)
        return self.add(q, lambda e: e.dma_start(out=out, in_=in_, **kw), [in_], [out], dma=True)

    def mm(self, out, lhsT, rhs, start=True, stop=True, acc=False):
        rd = [lhsT, rhs] + ([out] if not start else [])
        return self.add("pe", lambda e: e.matmul(out, lhsT, rhs, start=start, stop=stop), rd, [out])

    def tr(self, out, in_, ident):
        return self.add("pe", lambda e: e.transpose(out, in_, ident), [in_, ident], [out])

    def act(self, out, in_, func, bias=None, scale=None, accum_out=None, eng="act"):
        rd = [in_]
        kw = {}
        if bias is not None:
            kw["bias"] = bias
            if not isinstance(bias, (int, float)):
                rd.append(bias)
        if scale is not None:
            kw["scale"] = scale
            if not isinstance(scale, (int, float)):
                rd.append(scale)
        wr = [out]
        if accum_out is not None:
            kw["accum_out"] = accum_out
            wr.append(accum_out)
        return self.add("act", lambda e: e.activation(out, in_, func, **kw), rd, wr)

    def tt(self, out, in0, in1, op, eng="dve"):
        return self.add(eng, lambda e: e.tensor_tensor(out, in0, in1, op), [in0, in1], [out])

    def ts(self, out, in0, s1, op0, s2=None, op1=None, eng="dve"):
        rd = [in0] + [s for s in (s1, s2) if s is not None and not isinstance(s, (int, float))]
        if op1 is None:
            return self.add(eng, lambda e: e.tensor_scalar(out, in0, s1, None, op0), rd, [out])
        return self.add(eng, lambda e: e.tensor_scalar(out, in0, s1, s2, op0, op1), rd, [out])

    def stt(self, out, in0, scalar, in1, op0, op1, eng="dve"):
        rd = [in0, in1] + ([] if isinstance(scalar, (int, float)) else [scalar])
        return self.add(eng, lambda e: e.scalar_tensor_tensor(out, in0, scalar, in1, op0, op1), rd, [out])

    def copy(self, out, in_, eng="dve"):
        return self.add(eng, lambda e: e.tensor_copy(out, in_), [in_], [out])

    def memset(self, out, val, eng="dve"):
        return self.add(eng, lambda e: e.memset(out, val), [], [out])

    def recip(self, out, in_):
        return self.add("dve", lambda e: e.reciprocal(out, in_), [in_], [out])

    def scan(self, out, d0, d1, init, op0, op1):
        rd = [d0, d1] + ([] if isinstance(init, (int, float)) else [init])
        return self.add("dve", lambda e: e.tensor_tensor_scan(out, d0, d1, init, op0, op1), rd, [out])

    def emit(self, R=20000, K=8, Kq=None):
        Kq = dict(Kq or {})
        KK = {e: Kq.get(e, K) for e in self.ENGS}
        nc = self.nc
        ops = self.ops
        needed = set()
        for eng, fn, deps, dma in ops:
            for d in deps:
                de, _, _, ddma = ops[d]
                if ddma:
                    continue
                if de == "pe" and eng == "pe" and not dma:
                    continue
                needed.add(d)
        sigidx = {}
        cnt = {e: 0 for e in self.ENGS}
        for i, (eng, fn, deps, dma) in enumerate(ops):
            if (not dma) and i in needed:
                sigidx[i] = cnt[eng]
                cnt[eng] += 1
        dmaidx = {}
        dcnt = {e: 0 for e in self.ENGS}
        for i, (eng, fn, deps, dma) in enumerate(ops):
            if dma:
                dmaidx[i] = dcnt[eng]
                dcnt[eng] += 1
        with ExitStack() as st:
            csem = {e: [st.enter_context(nc.semaphore(f"c_{e}_{j}")) for j in range(max(1, (cnt[e] + R - 1) // R))]
                    for e in self.ENGS}
            dsem = {e: [st.enter_context(nc.semaphore(f"d_{e}_{j}")) for j in range(min(KK[e], dcnt[e]))]
                    for e in self.ENGS}
            block = st.enter_context(nc.Block())

            def run(me, e):
                waited_c = {x: -1 for x in self.ENGS}
                waited_d = {}

                def wait_dma(d):
                    q = ops[d][0]
                    n = dmaidx[d]
                    K = KK[q]
                    sem = dsem[q][n % K]
                    val = 16 * (n // K + 1)
                    key = (q, n % K)
                    if waited_d.get(key, 0) >= val:
                        return
                    e.wait_ge(sem, val)
                    waited_d[key] = val

                for i, (eng, fn, deps, dma) in enumerate(ops):
                    if eng != me:
                        continue
                    for d in sorted(deps):
                        de, _, _, ddma = ops[d]
                        if ddma:
                            wait_dma(d)
                        else:
                            if de == "pe" and me == "pe" and not dma:
                                continue
                            g = sigidx[d]
                            if waited_c[de] >= g:
                                continue
                            e.wait_ge(csem[de][g // R], g % R + 1)
                            waited_c[de] = g
                    if dma:
                        n = dmaidx[i]
                        K = KK[me]
                        sem = dsem[me][n % K]
                        if n >= K:
                            key = (me, n % K)
                            val = 16 * (n // K)
                            if waited_d.get(key, 0) < val:
                                e.wait_ge(sem, val)
                                waited_d[key] = val
                        ins = fn(e)
                        ins.then_inc(sem, 16)
                    else:
                        ins = fn(e)
                        if i in sigidx:
                            g = sigidx[i]
                            ins.then_inc(csem[me][g // R], 1)
                K = KK[me]
                for j in range(min(K, dcnt[me])):
                    uses = (dcnt[me] - 1 - j) // K + 1
                    val = 16 * uses
                    if waited_d.get((me, j), 0) < val:
                        e.wait_ge(dsem[me][j], val)

            @block.sync
            def _(e):
                run("sp", e)

            @block.scalar
            def _(e):
                run("act", e)

            @block.vector
            def _(e):
                run("dve", e)

            @block.gpsimd
            def _(e):
                run("pool", e)

            @block.tensor
            def _(e):
                run("pe", e)

from concourse.bass_utils import run_bass_kernel_spmd

NT = 2112
TP = 2048
TS = 64
PAST = 4096
EPS = 1e-6
GROUPS = [(0, 512), (512, 512), (1024, 512), (1536, 512), (2048, 64)]
TILES = [(i * 128, 128) for i in range(16)] + [(2048, 64)]
QSC = 128.0 ** -0.5

FM_STARTS = ([0, 128, 256, 384] + [512, 640, 768, 896] + [1544 + 128 * i for i in range(8)]
             + [3088 + 128 * i for i in range(4)]
             + [3600 + 128 * i for i in range(4)] + [4112 + 128 * i for i in range(24)])
FM_COL = {s: i for i, s in enumerate(FM_STARTS)}

IN_SHAPES = dict(
    xT=[1024, NT], w_in=[1024, 7184], b_fm=[128, 48], bg_fox=[8, 1], bg_i=[4, 1], bg_f=[4, 1],
    b_row=[1, 7184], g_pre=[128, 8], gq2=[128, 1], gk2=[128, 1], mconv_w=[128, 8, 4], mconv_b=[128, 8],
    mhn=[128, 4], g_mem=[128, 8], w_mem_kv=[1024, 1024], w_br_a=[512, 1024], w_br_b=[512, 1024],
    w_br_m=[512, 1024], w_out=[1024, 1024], g_post=[128, 8], g_fpre=[128, 8], w_up=[1024, 5632],
    fconv_w=[128, 44, 3], fconv_b=[128, 44], w_down=[2816, 1024], g_fpost=[128, 8],
    ckT=[512, PAST], cv=[8, 128, 32, 64], clogfT=[8, PAST], sC=[128, 4, 128], sn=[128, 4], sm=[4, 1],
    sconv=[128, 8, 3], cmkT=[512, 256], cmv=[256, 512], sfconv=[128, 44, 2], memT=[1024, 256],
    ident=[128, 128], trimask=[128, 128], negmask=[128, 128], bdones=[128, 128], sel4=[4, 4, 128],
)
OUT_SHAPES = dict(
    yT=[1024, NT], fox_kT=[512, NT], fox_v=[NT, 512], fox_logfT=[8, NT],
    ml_cT=[2, 128, 4, 128], ml_n=[2, 128, 4], ml_m=[2, 4, 1], ml_convT=[2, 128, 8, 3],
    ffn_convT=[2, 128, 44, 2], mem_kT=[512, 256], mem_v=[256, 512],
)


class Arena:
    def __init__(self, A, cap):
        self.A = A
        self.cap = cap
        self.top = 0
        self.hw = 0

    def _alloc(self, words):
        off = self.top
        self.top += words
        assert self.top <= self.cap, f"arena overflow {self.top} > {self.cap}"
        self.hw = max(self.hw, self.top)
        return off

    def f32(self, *shape):
        n = int(np.prod(shape[1:]))
        off = self._alloc(n)
        return self._view(self.A[:, off:off + n], shape)

    def bf(self, *shape):
        n = int(np.prod(shape[1:]))
        words = (n + 1) // 2
        off = self._alloc(words)
        return self._view(self.A[:, off:off + words].bitcast(BF16)[:, 0:n], shape)

    @staticmethod
    def _view(v, shape):
        p = shape[0]
        if len(shape) == 3:
            v = v.rearrange("p (a b) -> p a b", b=shape[2])
        elif len(shape) == 4:
            v = v.rearrange("p (a b c) -> p a b c", b=shape[2], c=shape[3])
        if p < 128:
            v = v[0:p]
        return v

    def f32_at(self, off, *shape):
        n = int(np.prod(shape[1:]))
        return self._view(self.A[:, off:off + n], shape)

    def bf_at(self, off, *shape):
        n = int(np.prod(shape[1:]))
        words = (n + 1) // 2
        return self._view(self.A[:, off:off + words].bitcast(BF16)[:, 0:n], shape)

    def mark(self):
        return self.top

    def release(self, m):
        self.top = m


def build(debug=(), stop_at=None):
    nc = bass.Bass("TRN2", target_bir_lowering=False)
    I = {k: nc.dram_tensor(k, list(v), F32, kind="ExternalInput").ap() for k, v in IN_SHAPES.items()}
    O = {k: nc.dram_tensor(k, list(v), F32, kind="ExternalOutput").ap() for k, v in OUT_SHAPES.items()}
    x1scr = nc.dram_tensor("x1scr", [1024, NT], F32, kind="Internal").ap()
    DBG = {}
    P = Prog(nc)
    P.stop_at = stop_at
    CAP = 53000
    try:
      with nc.sbuf_tensor("A", [128, CAP], F32) as A_, nc.psum_tensor("PS", [128, 8, 512], F32) as PS:
        ar = Arena(A_, CAP)
        hw_box = [ar]
        bank_ctr = [0]

        def nb():
            b = bank_ctr[0] % 8
            bank_ctr[0] += 1
            return PS[:, b, :]

        def dbg(name, ap):
            if name in debug:
                shp = [int(s) for s in ap.shape]
                d = nc.dram_tensor("dbg_" + name, shp, F32, kind="ExternalOutput").ap()
                DBG[name] = shp
                if ap.dtype == F32 and "PSUM" not in str(ap.space).upper():
                    P.dma("sp", d, ap)
                else:
                    m = ar.mark()
                    t = ar.f32(*([128] + shp[1:]))[0:shp[0]]
                    P.copy(t, ap)
                    P.dma("sp", d, t)
                    ar.release(m)

        wv = lambda name: I[name].rearrange("(c p) n -> p c n", p=128)
        r3 = lambda ap, b: ap.rearrange("p (a b) -> p a b", b=b)

        ident_f = ar.f32(128, 128)
        ident_b = ar.bf(128, 128)
        trimask = ar.bf(128, 128)
        negmask = ar.bf(128, 128)
        bdones = ar.bf(128, 128)
        ones_b = ar.bf(128, 128)
        ones_f = ar.f32(128, 128)
        sel4 = ar.f32(4, 4, 128)
        b_fm = ar.f32(128, 48)
        g_pre = ar.f32(128, 8)
        g_post = ar.f32(128, 8)
        g_fpre = ar.f32(128, 8)
        g_fpost = ar.f32(128, 8)
        g_mem = ar.f32(128, 8)
        gq2 = ar.f32(128, 1)
        gk2 = ar.f32(128, 1)
        mhn = ar.f32(128, 4)
        mconv_w = ar.f32(128, 8, 4)
        mconv_b = ar.f32(128, 8)
        fconv_w = ar.f32(128, 44, 3)
        fconv_b = ar.f32(128, 44)
        def load_consts():
            P.dma("sp", ident_f, I["ident"])
            P.dma("pool", ident_b, I["ident"])
            P.dma("pool", trimask, I["trimask"])
            P.dma("pool", negmask, I["negmask"])
            P.dma("pool", bdones, I["bdones"])
            P.dma("sp", sel4, I["sel4"])
            for t, n in ((b_fm, "b_fm"), (g_pre, "g_pre"), (g_post, "g_post"), (g_fpre, "g_fpre"), (g_fpost, "g_fpost"),
                         (g_mem, "g_mem"), (gq2, "gq2"), (gk2, "gk2"), (mhn, "mhn"), (mconv_w, "mconv_w"),
                         (mconv_b, "mconv_b"), (fconv_w, "fconv_w"), (fconv_b, "fconv_b")):
                P.dma("sp", t, I[n])
            P.ts(gq8, gq2, 0.125, ALU.mult)
        P.memset(ones_b, 1.0)
        P.memset(ones_f, 1.0)
        eps_col = ar.f32(128, 1)
        P.memset(eps_col, EPS)
        one_col = ar.f32(128, 1)
        P.memset(one_col, 1.0)
        gq8 = ar.f32(128, 1)
        bcol = lambda start: b_fm[:, FM_COL[start]:FM_COL[start] + 1]

        def rms_bcast(src, C, n, D, rb, sq=None):
            m = ar.mark()
            if sq is None:
                sq = ar.bf(128, C, n)
            P.act(sq, src, AF.Square)
            ps = nb()
            for c in range(C):
                P.mm(ps[:, 0:n], ones_b, sq[:, c, :], start=(c == 0), stop=(c == C - 1))
            P.act(rb, ps[:, 0:n], AF.Ln, bias=eps_col, scale=1.0 / D)
            P.act(rb, rb, AF.Exp, scale=-0.5)
            ar.release(m)

        xn_off = ar.mark()
        xnT = ar.bf(128, 8, NT)
        m_after_xn = ar.mark()
        aT_off = ar.mark()
        aT = ar.bf(128, 4, NT)
        chi = ar.bf(8, NT)
        clo = ar.bf(8, NT)
        rdl = ar.f32(4, NT)
        wg_tok = ar.f32(128, 17, 4)
        dec_b = ar.f32(128, 4, 17)

        xT3 = I["xT"].rearrange("(c p) n -> p c n", p=128)
        m = ar.mark()
        xs2 = [ar.f32(128, 8, 512), ar.f32(128, 8, 512)]
        rb2 = [ar.f32(128, 512), ar.f32(128, 512)]
        sq2 = [ar.bf(128, 8, 512), ar.bf(128, 8, 512)]
        for gi, (t0, n) in enumerate(GROUPS):
            xs = xs2[gi % 2][:, :, 0:n]
            P.dma("sp", xs, xT3[:, :, t0:t0 + n])
            if gi == 0:
                load_consts()
            rb = rb2[gi % 2][:, 0:n]
            rms_bcast(xs, 8, n, 1024.0, rb, sq=sq2[gi % 2][:, :, 0:n])
            for c in range(8):
                P.stt(xnT[:, c, t0:t0 + n], xs[:, c, :], g_pre[:, c:c + 1], rb, ALU.mult, ALU.mult)
        ar.release(m)
        dbg("xnT", xnT[:, 0, :])

        P.mark("s0_norm")
        m1a = ar.mark()
        wgf = ar.bf(128, 8, 8)
        wgi = ar.bf(128, 8, 4)
        wgm = ar.bf(128, 8, 4)
        P.dma("pool", wgf, wv("w_in")[:, :, 1536:1544])
        P.dma("pool", wgi, wv("w_in")[:, :, 3080:3084])
        P.dma("pool", wgm, wv("w_in")[:, :, 3084:3088])
        bgf = ar.f32(8, 1)
        bgi = ar.f32(4, 1)
        bgm = ar.f32(4, 1)
        P.dma("sp", bgf, I["bg_fox"])
        P.dma("sp", bgi, I["bg_i"])
        P.dma("sp", bgm, I["bg_f"])
        zrow = ar.f32(8, NT)
        P.memset(zrow, 0.0)
        flog = ar.f32(8, NT)
        cfox = ar.f32(8, NT)
        gi_r = ar.f32(4, NT)
        mlf = ar.f32(4, NT)
        A_r = ar.f32(4, NT)
        G_r = ar.f32(4, NT)
        wgT = ar.f32(4, NT)
        for (wt, bc, dst, rows, ls) in ((wgf, bgf, flog, 8, True), (wgi, bgi, gi_r, 4, False), (wgm, bgm, mlf, 4, True)):
            nbias = ar.f32(rows, 1)
            P.ts(nbias, bc, -1.0, ALU.mult)
            for (t0, n) in GROUPS:
                ps = nb()
                for c in range(8):
                    P.mm(ps[0:rows, 0:n], wt[:, c, :], xnT[:, c, t0:t0 + n], start=(c == 0), stop=(c == 7))
                if ls:
                    m = ar.mark()
                    e = ar.f32(rows, n)
                    P.act(e, ps[0:rows, 0:n], AF.Exp, bias=nbias, scale=-1.0)
                    P.act(e, e, AF.Ln, bias=one_col[0:rows], scale=1.0)
                    P.ts(dst[:, t0:t0 + n], e, -1.0, ALU.mult)
                    ar.release(m)
                else:
                    P.act(dst[:, t0:t0 + n], ps[0:rows, 0:n], AF.Identity, bias=bc)
        P.dma("sp", O["fox_logfT"], flog)
        P.scan(cfox[:, 0:TP], flog[:, 0:TP], zrow[:, 0:TP], 0.0, ALU.add, ALU.add)
        P.scan(cfox[:, TP:NT], flog[:, TP:NT], zrow[:, 0:TS], 0.0, ALU.add, ALU.add)
        P.copy(chi, cfox)
        P.tt(clo, cfox, chi, ALU.subtract)
        sm0 = ar.f32(4, 1)
        P.dma("sp", sm0, I["sm"])
        zr4 = zrow[0:4]
        Bc = ar.f32(4, NT)
        P.scan(Bc[:, 0:TP], mlf[:, 0:TP], zr4[:, 0:TP], 0.0, ALU.add, ALU.add)
        P.scan(Bc[:, TP:NT], mlf[:, TP:NT], zr4[:, 0:TS], 0.0, ALU.add, ALU.add)
        P.tt(A_r, gi_r, Bc, ALU.subtract)
        P.scan(G_r[:, 0:TP], A_r[:, 0:TP], A_r[:, 0:TP], 0.0, ALU.max, ALU.max)
        P.scan(G_r[:, TP:NT], A_r[:, TP:NT], A_r[:, TP:NT], sm0, ALU.max, ALU.max)
        gend = ar.f32(4, 17)
        mprev = ar.f32(4, 17)
        P.copy(gend[:, 0:16], G_r[:, 127:TP:128])
        P.copy(gend[:, 16:17], G_r[:, NT - 1:NT])
        P.memset(mprev[:, 0:1], 0.0)
        P.copy(mprev[:, 1:16], gend[:, 0:15])
        P.copy(mprev[:, 16:17], sm0)
        dec = ar.f32(4, 17)
        P.tt(dec, mprev, gend, ALU.subtract)
        P.act(dec, dec, AF.Exp)
        gend_b = gend[:, 0:16].unsqueeze(2).broadcast_to([4, 16, 128])
        P.tt(r3(wgT[:, 0:TP], 128), r3(A_r[:, 0:TP], 128), gend_b, ALU.subtract)
        P.ts(wgT[:, TP:NT], A_r[:, TP:NT], gend[:, 16:17], ALU.subtract)
        P.act(wgT, wgT, AF.Exp)
        P.tt(r3(rdl[:, 0:TP], 128), r3(Bc[:, 0:TP], 128), gend_b, ALU.add)
        P.ts(rdl[:, TP:NT], Bc[:, TP:NT], gend[:, 16:17], ALU.add)
        P.ts(rdl, rdl, -1.0, ALU.mult)
        mTo = ar.f32(4, 2)
        P.tt(mTo[:, 0:1], G_r[:, TP - 1:TP], Bc[:, TP - 1:TP], ALU.add)
        P.tt(mTo[:, 1:2], G_r[:, NT - 1:NT], Bc[:, NT - 1:NT], ALU.add)
        P.dma("sp", O["ml_m"][0], mTo[:, 0:1])
        P.dma("sp", O["ml_m"][1], mTo[:, 1:2])
        for ti, (t0, L) in enumerate(TILES):
            ps = nb()
            P.tr(ps[0:L, 0:4], wgT[:, t0:t0 + L], ident_f[0:4, 0:4])
            P.copy(wg_tok[0:L, ti, :], ps[0:L, 0:4])
        for h in range(4):
            ps = nb()
            P.mm(ps[:, 0:17], sel4[:, h, :], dec)
            P.copy(dec_b[:, h, :], ps[:, 0:17])
        dbg("cfox", cfox)
        dbg("G_r", G_r)
        dbg("wgT", wgT)
        dbg("rdl", rdl)
        ar.release(m1a)
        s1 = ar.mark()

        P.mark("s1a_gates")
        cchi = ar.bf(8, PAST)
        cclo = ar.bf(8, PAST)
        m_s = ar.mark()
        ccache = ar.f32(8, PAST)
        lcache = ar.f32(8, PAST)
        zc = ar.f32(8, PAST)
        P.memset(zc, 0.0)
        P.dma("sp", lcache, I["clogfT"])
        P.scan(ccache, lcache, zc, 0.0, ALU.add, ALU.add)
        ctot = ar.f32(8, 1)
        P.copy(ctot, ccache[:, PAST - 1:PAST])
        P.ts(ccache, ccache, ctot, ALU.subtract)
        P.copy(cchi, ccache)
        P.tt(cclo, ccache, cchi, ALU.subtract)
        ar.release(m_s)
        qT = ar.bf(128, 4, NT)
        kT = ar.bf(128, 4, NT)
        V = ar.bf(128, 17, 512)
        m_w = ar.mark()
        wf = ar.bf(128, 8, 1536)
        P.dma("pool", wf, wv("w_in")[:, :, 0:1536])
        bvrow = ar.f32(128, 512)
        P.dma("sp", bvrow, I["b_row"][:, 1024:1536].partition_broadcast(128))
        ptmp = [(ar.f32(128, 512), ar.bf(128, 512), ar.f32(128, 512), ar.f32(128, 512)) for _ in range(3)]
        its = [(t0, n, which, ch) for (t0, n) in GROUPS for which in range(2) for ch in range(4)]

        def fp_A(it):
            t0, n, which, ch = it
            col0 = which * 512 + ch * 128
            ps = nb()
            for c in range(8):
                P.mm(ps[:, 0:n], wf[:, c, col0:col0 + 128], xnT[:, c, t0:t0 + n], start=(c == 0), stop=(c == 7))
            return ps

        def fp_B(i, it, ps):
            t0, n, which, ch = it
            col0 = which * 512 + ch * 128
            z_, sq_, r_, kf_ = ptmp[i % 3]
            z = z_[:, 0:n]
            sq = sq_[:, 0:n]
            P.act(z, ps[:, 0:n], AF.Identity, bias=bcol(col0))
            P.act(sq, ps[:, 0:n], AF.Square, bias=bcol(col0))
            ps2 = nb()
            P.mm(ps2[:, 0:n], bdones, sq)
            r = r_[:, 0:n]
            P.act(r, ps2[:, 0:n], AF.Ln, bias=eps_col, scale=1.0 / 64)
            P.act(r, r, AF.Exp, scale=-0.5)
            if which == 0:
                P.stt(qT[:, ch, t0:t0 + n], z, gq8, r, ALU.mult, ALU.mult)
            else:
                kf = kf_[:, 0:n]
                P.stt(kf, z, gk2, r, ALU.mult, ALU.mult)
                P.copy(kT[:, ch, t0:t0 + n], kf)
                P.dma("sp", O["fox_kT"][ch * 128:(ch + 1) * 128, t0:t0 + n], kf)

        vtmp = [ar.f32(128, 512), ar.f32(128, 512)]

        def fp_V(ti):
            t0, L = TILES[ti]
            ps = nb()
            for c in range(8):
                P.mm(ps[0:L, :], xnT[:, c, t0:t0 + L], wf[:, c, 1024:1536], start=(c == 0), stop=(c == 7))
            vf = vtmp[ti % 2]
            P.tt(vf[0:L], ps[0:L, :], bvrow[0:L], ALU.add)
            P.copy(V[0:L, ti, :], vf[0:L])
            P.dma("sp", O["fox_v"][t0:t0 + L, :], vf[0:L])

        psq = [fp_A(its[0])]
        vnext = 0
        for i, it in enumerate(its):
            if i + 1 < len(its):
                psq.append(fp_A(its[i + 1]))
            if i % 2 == 1 and vnext < 17:
                fp_V(vnext)
                vnext += 1
            fp_B(i, it, psq[i])
        while vnext < 17:
            fp_V(vnext)
            vnext += 1
        ar.release(m_w)
        dbg("qT", qT[:, 0, :])
        dbg("kT", kT[:, 0, :])

        P.mark("s1b_foxproj")
        pbufs = [ar.bf(128, 2, 512) for _ in range(4)]
        pctr = [0]
        fctr = [0, 0]
        rbuf = ar.f32(128, 512)
        KMAX = PAST + TS
        qas = [ar.bf(128, NT), ar.bf(128, NT)]
        kas = [ar.bf(128, KMAX), ar.bf(128, KMAX)]
        vexts = [ar.bf(128, 33, 128), ar.bf(128, 33, 128)]
        for i in range(2):
            P.memset(qas[i][64:68, :], -1.0)
            P.memset(kas[i][64:68, :], 1.0)
        P.memset(vexts[0][:, :, 64:128], 1.0)
        P.memset(vexts[1][:, :, 0:64], 1.0)

        def fox_attend(h, qa, qc0, qn, keys, ka, vext):
            ch, half = h // 2, h % 2
            pb = half * 64
            po = (1 - half) * 64
            acc = PS[:, 6 + (fctr[0] % 2), :]
            fctr[0] += 1
            full = [k for k in keys if k[3] is None]
            diag = [k for k in keys if k[3] is not None]
            per = 2 if qn > 64 else 8
            units = [full[i:i + per] for i in range(0, len(full), per)] + [[k] for k in diag]
            nk = len(keys)
            done = [0]

            def emit_front(unit):
                base = 2 * (fctr[1] % 3)
                fctr[1] += 1
                pt = pbufs[pctr[0] % len(pbufs)]
                pctr[0] += 1
                pvs = []
                if len(unit) == 1:
                    kc0, vti, L, doff = unit[0]
                    q_lo = 0 if doff is None else doff
                    sps = PS[:, base, :]
                    P.mm(sps[0:L, q_lo:qn], ka[0:68, kc0:kc0 + L], qa[0:68, qc0 + q_lo:qc0 + qn], start=True, stop=(doff is None))
                    if doff is not None:
                        dq = min(L, qn - q_lo)
                        P.mm(sps[0:L, q_lo:q_lo + dq], ident_b[0:L, 0:L], negmask[0:L, 0:dq], start=False, stop=True)
                    P.act(pt[0:L, 0, q_lo:qn], sps[0:L, q_lo:qn], AF.Exp)
                    pvs.append((acc[:, q_lo:qn], vext[0:L, vti, :], pt[0:L, 0, q_lo:qn]))
                elif qn > 64:
                    for j, (kc0, vti, L, doff) in enumerate(unit):
                        P.mm(PS[:, base + j, 0:qn], ka[0:68, kc0:kc0 + L], qa[0:68, qc0:qc0 + qn])
                        pvs.append((acc[:, 0:qn], vext[0:L, vti, :], pt[:, j, 0:qn]))
                    P.act(pt[:, 0:2, 0:qn], PS[:, base:base + 2, 0:qn], AF.Exp)
                else:
                    m = len(unit)
                    for j, (kc0, vti, L, doff) in enumerate(unit):
                        P.mm(PS[:, base, j * qn:(j + 1) * qn], ka[0:68, kc0:kc0 + L], qa[0:68, qc0:qc0 + qn])
                        pvs.append((acc[:, 0:qn], vext[0:L, vti, :], pt[:, 0, j * qn:(j + 1) * qn]))
                    P.act(pt[:, 0, 0:m * qn], PS[:, base, 0:m * qn], AF.Exp)
                return pvs

            def emit_pv(pvs):
                for (o_, l_, r_) in pvs:
                    P.mm(o_, l_, r_, start=(done[0] == 0), stop=(done[0] == nk - 1))
                    done[0] += 1

            LA = 2
            q_ = [emit_front(units[u]) for u in range(min(LA, len(units)))]
            for u in range(len(units)):
                if u + LA < len(units):
                    q_.append(emit_front(units[u + LA]))
                emit_pv(q_[u])
            P.act(rbuf[pb:pb + 64, 0:qn], acc[po:po + 64, 0:qn], AF.Ln)
            P.act(rbuf[pb:pb + 64, 0:qn], rbuf[pb:pb + 64, 0:qn], AF.Exp, scale=-1.0)
            P.tt(aT[pb:pb + 64, ch, qc0:qc0 + qn], acc[pb:pb + 64, 0:qn], rbuf[pb:pb + 64, 0:qn], ALU.mult)

        for h in range(8):
            ch, half = h // 2, h % 2
            pb = half * 64
            qa, ka, vext = qas[h % 2], kas[h % 2], vexts[h % 2]
            vo = 0 if half == 0 else 64
            P.dma("sp", qa[0:64, :], qT[pb:pb + 64, ch, :])
            P.dma("sp", qa[66:67, :], chi[h:h + 1, :])
            P.dma("sp", qa[67:68, :], clo[h:h + 1, :])
            P.dma("sp", ka[0:64, 0:TP], kT[pb:pb + 64, ch, 0:TP])
            P.dma("sp", ka[64:65, 0:TP], chi[h:h + 1, 0:TP])
            P.dma("sp", ka[65:66, 0:TP], clo[h:h + 1, 0:TP])
            P.copy(vext[:, 0:16, vo:vo + 64], V[:, 0:16, h * 64:h * 64 + 64], eng="pool")
            for gi in range(4):
                keys = []
                for ti in range(gi * 4 + 4):
                    doff = None if ti < gi * 4 else (ti - gi * 4) * 128
                    keys.append((ti * 128, ti, 128, doff))
                fox_attend(h, qa, gi * 512, 512, keys, ka, vext)
            if h == 0:
                dbg("aT0", aT[0:64, 0, 0:TP])
        dbg("aT", aT[:, 0, :])

        P.mark("fox_prompt")
        ckT_d = I["ckT"]
        for h in range(8):
            ch, half = h // 2, h % 2
            pb = half * 64
            qa, ka, vext = qas[h % 2], kas[h % 2], vexts[h % 2]
            vo = 0 if half == 0 else 64
            P.dma("sp", qa[0:64, TP:NT], qT[pb:pb + 64, ch, TP:NT])
            P.dma("sp", qa[66:67, TP:NT], chi[h:h + 1, TP:NT])
            P.dma("sp", qa[67:68, TP:NT], clo[h:h + 1, TP:NT])
            P.dma("pool", ka[0:64, 0:PAST], ckT_d[h * 64:(h + 1) * 64, :])
            P.dma("sp", ka[64:65, 0:PAST], cchi[h:h + 1, :])
            P.dma("sp", ka[65:66, 0:PAST], cclo[h:h + 1, :])
            P.dma("sp", ka[0:64, PAST:KMAX], kT[pb:pb + 64, ch, TP:NT])
            P.dma("sp", ka[64:65, PAST:KMAX], chi[h:h + 1, TP:NT])
            P.dma("sp", ka[65:66, PAST:KMAX], clo[h:h + 1, TP:NT])
            P.dma("pool", vext[:, 0:32, vo:vo + 64], I["cv"][h])
            P.copy(vext[0:64, 32, vo:vo + 64], V[0:64, 16, h * 64:h * 64 + 64], eng="pool")
            keys = [(ti * 128, ti, 128, None) for ti in range(32)] + [(PAST, 32, 64, 0)]
            fox_attend(h, qa, TP, TS, keys, ka, vext)
        dbg("aTs", aT[:, 0, TP:NT])
        ar.release(s1)
        bT_off = ar.mark()
        bT = ar.bf(128, 4, NT)
        s1 = ar.mark()

        P.mark("fox_sample")
        wm = ar.bf(128, 8, 2048)
        P.dma("pool", wm[:, :, 0:1536], wv("w_in")[:, :, 1544:3080])
        P.dma("pool", wm[:, :, 1536:2048], wv("w_in")[:, :, 3088:3600])
        bmv = ar.f32(128, 512)
        P.dma("sp", bmv, I["b_row"][:, 2568:3080].partition_broadcast(128))
        CT = ar.f32(128, 4, 129)
        pc = ar.f32(128, 8, 515)
        qks = [ar.bf(128, 8, 512), ar.bf(128, 8, 512)]
        qkc = [qks[0]]
        sigo = ar.f32(128, 4, 512)
        rdb = ar.f32(128, 4, 512)
        caccs = [ar.f32(128, 512), ar.f32(128, 512)]
        ones3 = ones_f.unsqueeze(1).broadcast_to([128, 4, 128])
        msets = []
        for _ in range(2):
            msets.append(dict(vfull=ar.f32(128, 512), vw=ar.bf(128, 4, 129), wgb=ar.bf(128, 4, 128),
                              ktok=ar.bf(128, 4, 128), ET=ar.bf(128, 4, 128), Cq=ar.bf(128, 4, 128),
                              nbm=ar.bf(128, 4, 128), tden=ar.f32(128, 4, 128), hs=ar.f32(128, 4, 128),
                              hsq=ar.bf(128, 4, 128)))
            msets[-1]["rr"] = msets[-1]["tden"]
        B_v = PS[:, 0, :]
        B_tr = PS[:, 1, :].bitcast(BF16)
        B_S = PS[:, 2, :]
        B_k = [PS[:, 3, :], PS[:, 4, :]]
        B_Y = PS[:, 5, :]
        B_D = PS[:, 6, :]
        B_q = PS[:, 7, :]
        P.memset(CT, 0.0)
        P.memset(pc[:, :, 0:3], 0.0)

        def ml_step(prev, cur):
            if prev is not None:
                tiP, ttP, LP, loP, SP = prev
                Dv = r3(B_D[:, 0:4 * LP], LP)
                Yv = r3(B_Y[:, 0:4 * LP], LP)
                td = SP["tden"][:, :, 0:LP]
                hs_ = SP["hs"][:, :, 0:LP]
                hq_ = SP["hsq"][:, :, 0:LP]
                rr_ = SP["rr"][:, :, 0:LP]
            if cur is not None:
                ti, tt0, L, lo, S = cur
                wg3 = wg_tok[0:L, ti, :].unsqueeze(2)
            if prev is not None:
                for h in range(4):
                    q_h = qkc[0][:, h, loP:loP + LP]
                    P.mm(B_Y[:, h * LP:(h + 1) * LP], SP["Cq"][:, h, :], q_h, start=True, stop=False)
                    P.mm(B_Y[:, h * LP:(h + 1) * LP], SP["vw"][0:LP, h, 0:128], SP["ET"][0:LP, h, 0:LP], start=False, stop=True)
                for h in range(4):
                    q_h = qkc[0][:, h, loP:loP + LP]
                    P.mm(B_D[:, h * LP:(h + 1) * LP], SP["nbm"][:, h, :], q_h, start=True, stop=False)
                    P.mm(B_D[:, h * LP:(h + 1) * LP], SP["wgb"][0:LP, h, :], SP["ET"][0:LP, h, 0:LP], start=False, stop=True)
            if cur is not None:
                for c in range(8):
                    P.mm(B_v[0:L, :], xnT[:, c, tt0:tt0 + L], wm[:, c, 1024:1536], start=(c == 0), stop=(c == 7))
            if prev is not None:
                P.ts(td, Dv, -1.0, ALU.mult)
                P.tt(td, td, Dv, ALU.max)
                P.tt(td, td, rdb[:, :, loP:loP + LP], ALU.max)
                P.act(td, td, AF.Ln)
                P.act(td, td, AF.Exp, scale=-1.0)
            if cur is not None:
                P.tt(S["vfull"][0:L], B_v[0:L, :], bmv[0:L], ALU.add)
                P.tt(S["vw"][0:L, :, 0:128], r3(S["vfull"][0:L], 128), wg3.broadcast_to([L, 4, 128]), ALU.mult)
                P.copy(S["vw"][0:L, :, 128:129], wg3)
                P.tt(S["wgb"][0:L], ones_f[0:L].unsqueeze(1).broadcast_to([L, 4, 128]), wg3.broadcast_to([L, 4, 128]), ALU.mult)
                for h in range(4):
                    P.tr(B_tr[0:L, h * 128:(h + 1) * 128], qkc[0][:, 4 + h, lo:lo + L], ident_b)
                P.act(S["ktok"][0:L], r3(B_tr[0:L, 0:512], 128), AF.Copy)
                for h in range(4):
                    P.mm(B_S[0:L, h * L:(h + 1) * L], qkc[0][:, 4 + h, lo:lo + L], qkc[0][:, h, lo:lo + L])
            if prev is not None:
                P.tt(hs_, Yv, td, ALU.mult)
                P.tt(hs_, hs_, sigo[:, :, loP:loP + LP], ALU.mult)
                P.tt(hq_, hs_, hs_, ALU.mult)
                for h in range(4):
                    P.mm(B_q[:, h * LP:(h + 1) * LP], ones_b, SP["hsq"][:, h, 0:LP])
                P.act(rr_, r3(B_q[:, 0:4 * LP], LP), AF.Ln, bias=eps_col, scale=1.0 / 128)
                P.act(rr_, rr_, AF.Exp, scale=-0.5)
            if cur is not None:
                P.stt(S["ET"][0:L, :, 0:L], r3(B_S[0:L, 0:4 * L], L), QSC,
                      trimask[0:L, 0:L].unsqueeze(1).broadcast_to([L, 4, L]), ALU.mult, ALU.mult)
                for h in range(4):
                    P.mm(B_k[h // 2][:, (h % 2) * 129:(h % 2) * 129 + 129], S["ktok"][0:L, h, :], S["vw"][0:L, h, :])
                P.tt(CT, CT, dec_b[:, :, ti:ti + 1].broadcast_to([128, 4, 129]), ALU.mult)
                P.act(S["Cq"], CT[:, :, 0:128], AF.Copy, scale=QSC)
                P.stt(S["nbm"], ones3, QSC, CT[:, :, 128:129].broadcast_to([128, 4, 128]), ALU.mult, ALU.mult)
            if prev is not None:
                P.tt(hs_, hs_, rr_, ALU.mult)
                P.tt(bT[:, :, ttP:ttP + LP], hs_, mhn.unsqueeze(2).broadcast_to([128, 4, LP]), ALU.mult)
            if cur is not None:
                P.tt(CT[:, 0:2, :], CT[:, 0:2, :], r3(B_k[0][:, 0:258], 129), ALU.add)
                P.tt(CT[:, 2:4, :], CT[:, 2:4, :], r3(B_k[1][:, 0:258], 129), ALU.add)

        def ml_proj_parts(gi):
            t0, n = GROUPS[gi]
            qk = qks[gi % 2]

            def part(pj):
                if pj == 0 and gi == 4:
                    P.dma("sp", pc[:, :, 0:3], I["sconv"])
                for j in (2 * pj, 2 * pj + 1):
                    ps = nb()
                    for c in range(8):
                        P.mm(ps[:, 0:n], wm[:, c, j * 128:(j + 1) * 128], xnT[:, c, t0:t0 + n], start=(c == 0), stop=(c == 7))
                    P.act(pc[:, j, 3:3 + n], ps[:, 0:n], AF.Identity, bias=bcol(1544 + 128 * j))
                for j in (2 * pj, 2 * pj + 1):
                    cacc = caccs[j % 2]
                    P.ts(cacc[:, 0:n], pc[:, j, 0:n], mconv_w[:, j, 0:1], ALU.mult)
                    for tap in (1, 2, 3):
                        P.stt(cacc[:, 0:n], pc[:, j, tap:tap + n], mconv_w[:, j, tap:tap + 1], cacc[:, 0:n], ALU.mult, ALU.add)
                    P.act(qk[:, j, 0:n], cacc[:, 0:n], AF.Silu, bias=mconv_b[:, j:j + 1])
                if pj == 3:
                    if gi == 3:
                        P.dma("sp", O["ml_convT"][0], pc[:, :, n:n + 3])
                    if gi == 4:
                        P.dma("sp", O["ml_convT"][1], pc[:, :, n:n + 3])
                    if gi < 3:
                        P.copy(pc[:, :, 0:3], pc[:, :, n:n + 3])
            return [lambda pj=pj: part(pj) for pj in range(4)]

        def ml_gates(gi):
            t0, n = GROUPS[gi]
            for h in range(4):
                ps = nb()
                for c in range(8):
                    P.mm(ps[:, 0:n], wm[:, c, 1536 + h * 128:1536 + (h + 1) * 128], xnT[:, c, t0:t0 + n], start=(c == 0), stop=(c == 7))
                P.act(sigo[:, h, 0:n], ps[:, 0:n], AF.Sigmoid, bias=bcol(3088 + 128 * h))
            for h in range(4):
                ps = nb()
                P.mm(ps[:, 0:n], sel4[:, h, :], rdl[:, t0:t0 + n])
                P.act(rdb[:, h, 0:n], ps[:, 0:n], AF.Exp)

        for p_ in ml_proj_parts(0):
            p_()
        ti_global = 0
        for gi, (t0, n) in enumerate(GROUPS):
            if gi == 4:
                P.dma("sp", O["ml_cT"][0], CT[:, :, 0:128])
                P.dma("sp", O["ml_n"][0], CT[:, :, 128])
                P.dma("sp", CT[:, :, 0:128], I["sC"])
                P.dma("sp", CT[:, :, 128], I["sn"])
            ml_gates(gi)
            qkc[0] = qks[gi % 2]
            nparts = ml_proj_parts(gi + 1) if gi + 1 < len(GROUPS) else []
            tiles = [(tt0, L) for (tt0, L) in TILES if t0 <= tt0 < t0 + n]
            prev = None
            for k_, (tt0, L) in enumerate(tiles):
                ti = ti_global
                ti_global += 1
                cur = (ti, tt0, L, tt0 - t0, msets[ti % 2])
                ml_step(prev, cur)
                prev = cur
                if k_ < len(nparts):
                    nparts[k_]()
            ml_step(prev, None)
        P.dma("sp", O["ml_cT"][1], CT[:, :, 0:128])
        P.dma("sp", O["ml_n"][1], CT[:, :, 128])
        ar.release(s1)
        dbg("bT", bT[:, 0, :])
        mT_off = ar.mark()
        mT = ar.bf(128, 4, NT)
        s1 = ar.mark()

        P.mark("mlstm")
        wq = ar.bf(128, 8, 512)
        P.dma("pool", wq, wv("w_in")[:, :, 3600:4112])
        wkv = ar.bf(128, 8, 1024)
        P.dma("pool", wkv, wv("w_mem_kv"))
        memx = ar.f32(128, 8, 256)
        P.dma("sp", memx, I["memT"].rearrange("(c p) n -> p c n", p=128))
        rbm = ar.f32(128, 256)
        rms_bcast(memx, 8, 256, 1024.0, rbm)
        memn = ar.bf(128, 8, 256)
        for c in range(8):
            P.stt(memn[:, c, :], memx[:, c, :], g_mem[:, c:c + 1], rbm, ALU.mult, ALU.mult)
        mkT = ar.bf(128, 2, 4, 256)
        mv = ar.bf(128, 2, 2, 512)
        tmpf = ar.f32(128, 512)
        for chh in range(4):
            ps = nb()
            for c in range(8):
                P.mm(ps[:, 0:256], wkv[:, c, chh * 128:(chh + 1) * 128], memn[:, c, :], start=(c == 0), stop=(c == 7))
            P.copy(tmpf[:, 0:256], ps[:, 0:256])
            P.copy(mkT[:, 0, chh, :], tmpf[:, 0:256])
            P.dma("sp", O["mem_kT"][chh * 128:(chh + 1) * 128, :], tmpf[:, 0:256])
        for mt in range(2):
            ps = nb()
            for c in range(8):
                P.mm(ps[:, :], memn[:, c, mt * 128:(mt + 1) * 128], wkv[:, c, 512:1024], start=(c == 0), stop=(c == 7))
            P.copy(tmpf, ps)
            P.copy(mv[:, 0, mt, :], tmpf)
            P.dma("sp", O["mem_v"][mt * 128:(mt + 1) * 128, :], tmpf)
        P.dma("pool", mkT[:, 1], I["cmkT"].rearrange("(c p) n -> p c n", p=128))
        P.dma("pool", mv[:, 1], I["cmv"].rearrange("(t p) f -> p t f", p=128))
        qhs = [ar.bf(128, 512), ar.bf(128, 512)]
        ptms = [ar.bf(128, 2, 512), ar.bf(128, 2, 512)]
        rlms = [ar.f32(128, 512), ar.f32(128, 512)]
        mits = [(gi, h) for gi in range(len(GROUPS)) for h in range(4)]

        def mm_A(i):
            gi, h = mits[i]
            t0, n = GROUPS[gi]
            ps = nb()
            for c in range(8):
                P.mm(ps[:, 0:n], wq[:, c, h * 128:(h + 1) * 128], xnT[:, c, t0:t0 + n], start=(c == 0), stop=(c == 7))
            P.act(qhs[i % 2][:, 0:n], ps[:, 0:n], AF.Identity, bias=bcol(3600 + 128 * h))

        def mm_B(i):
            gi, h = mits[i]
            t0, n = GROUPS[gi]
            seq = 0 if gi < 4 else 1
            qh, ptm, rlm = qhs[i % 2], ptms[i % 2], rlms[i % 2]
            for mt in range(2):
                sps = nb()
                P.mm(sps[:, 0:n], mkT[:, seq, h, mt * 128:(mt + 1) * 128], qh[:, 0:n])
                P.act(ptm[:, mt, 0:n], sps[:, 0:n], AF.Exp, scale=QSC)
            ops_ = nb()
            lps = nb()
            for mt in range(2):
                P.mm(ops_[:, 0:n], mv[:, seq, mt, h * 128:(h + 1) * 128], ptm[:, mt, 0:n], start=(mt == 0), stop=(mt == 1))
            for mt in range(2):
                P.mm(lps[:, 0:n], ones_b, ptm[:, mt, 0:n], start=(mt == 0), stop=(mt == 1))
            P.act(rlm[:, 0:n], lps[:, 0:n], AF.Ln)
            P.act(rlm[:, 0:n], rlm[:, 0:n], AF.Exp, scale=-1.0)
            P.tt(mT[:, h, t0:t0 + n], ops_[:, 0:n], rlm[:, 0:n], ALU.mult)

        mm_A(0)
        for i in range(len(mits)):
            if i + 1 < len(mits):
                mm_A(i + 1)
            mm_B(i)
        dbg("mT", mT[:, 0, :])
        ar.release(s1)

        P.mark("mem")
        mergedT = ar.bf(128, 8, NT)
        s2 = ar.mark()
        wgs = [[ar.bf(128, 8, 128) for b in range(3)] for _ in range(2)]
        wbs = [[ar.bf(128, 4, 128) for b in range(3)] for _ in range(2)]
        sg = [ar.f32(128, 512) for b in range(3)]
        macc = ar.f32(128, 512)
        mtmp = ar.f32(128, 512)
        brs = ("w_br_a", "w_br_b", "w_br_m")
        srcs = (aT, bT, mT)
        for oc in range(8):
            wg_ = wgs[oc % 2]
            wb_ = wbs[oc % 2]
            for b in range(3):
                g0 = 4112 + b * 1024 + oc * 128
                P.dma("pool", wg_[b], wv("w_in")[:, :, g0:g0 + 128])
                P.dma("pool", wb_[b], wv(brs[b])[:, :, oc * 128:(oc + 1) * 128])
            for (t0, n) in GROUPS:
                pp = []
                for b in range(3):
                    g0 = 4112 + b * 1024 + oc * 128
                    ps = nb()
                    for c in range(8):
                        P.mm(ps[:, 0:n], wg_[b][:, c, :], xnT[:, c, t0:t0 + n], start=(c == 0), stop=(c == 7))
                    P.act(sg[b][:, 0:n], ps[:, 0:n], AF.Sigmoid, bias=bcol(g0))
                for b in range(3):
                    ps = nb()
                    for c in range(4):
                        P.mm(ps[:, 0:n], wb_[b][:, c, :], srcs[b][:, c, t0:t0 + n], start=(c == 0), stop=(c == 3))
                    pp.append(ps)
                P.tt(macc[:, 0:n], sg[0][:, 0:n], pp[0][:, 0:n], ALU.mult)
                P.tt(mtmp[:, 0:n], sg[1][:, 0:n], pp[1][:, 0:n], ALU.mult)
                P.tt(macc[:, 0:n], macc[:, 0:n], mtmp[:, 0:n], ALU.add)
                P.tt(mtmp[:, 0:n], sg[2][:, 0:n], pp[2][:, 0:n], ALU.mult)
                P.tt(mergedT[:, oc, t0:t0 + n], macc[:, 0:n], mtmp[:, 0:n], ALU.add)
        dbg("mergedT", mergedT[:, 0, :])
        ar.release(s2)

        P.mark("s2_merge")
        wo = ar.bf(128, 8, 1024)
        P.dma("pool", wo, wv("w_out"))
        oTs = [ar.f32(128, 8, 512), ar.f32_at(aT_off, 128, 8, 512)]
        xss = [ar.f32(128, 8, 512), ar.f32_at(bT_off, 128, 8, 512)]
        rbs = [ar.f32(128, 512), ar.f32(128, 512)]
        sqs = [ar.bf(128, 8, 512), ar.bf_at(mT_off, 128, 8, 512)]
        x1s3w = x1scr.rearrange("(c p) n -> p c n", p=128)

        def ob_A(gi):
            t0, n = GROUPS[gi]
            oT, xs = oTs[gi % 2], xss[gi % 2]
            P.dma("sp", xs[:, :, 0:n], xT3[:, :, t0:t0 + n])
            for oc in range(8):
                ps = nb()
                for c in range(8):
                    P.mm(ps[:, 0:n], wo[:, c, oc * 128:(oc + 1) * 128], mergedT[:, c, t0:t0 + n], start=(c == 0), stop=(c == 7))
                P.act(oT[:, oc, 0:n], ps[:, 0:n], AF.Copy)

        def ob_B(gi):
            t0, n = GROUPS[gi]
            oT, xs, rb, sq = oTs[gi % 2], xss[gi % 2], rbs[gi % 2], sqs[gi % 2]
            rms_bcast(oT[:, :, 0:n], 8, n, 1024.0, rb[:, 0:n], sq=sq[:, :, 0:n])
            for oc in range(8):
                P.stt(oT[:, oc, 0:n], oT[:, oc, 0:n], g_post[:, oc:oc + 1], rb[:, 0:n], ALU.mult, ALU.mult)
                P.tt(xs[:, oc, 0:n], xs[:, oc, 0:n], oT[:, oc, 0:n], ALU.add)
            P.dma("sp", x1s3w[:, :, t0:t0 + n], xs[:, :, 0:n])
            rms_bcast(xs[:, :, 0:n], 8, n, 1024.0, rb[:, 0:n], sq=sq[:, :, 0:n])
            for c in range(8):
                P.stt(xnT[:, c, t0:t0 + n], xs[:, c, 0:n], g_fpre[:, c:c + 1], rb[:, 0:n], ALU.mult, ALU.mult)

        ob_A(0)
        for gi in range(len(GROUPS)):
            if gi + 1 < len(GROUPS):
                ob_A(gi + 1)
            ob_B(gi)
        dbg("x1nT", xnT[:, 0, :])
        ar.release(m_after_xn)

        P.mark("s2b_out")
        hidT = ar.bf(128, 22, NT)
        m_h = ar.mark()
        wus = [ar.bf(128, 8, 256), ar.bf(128, 8, 256)]
        apres = [ar.f32(128, 514), ar.f32(128, 514)]
        bpres = [ar.f32(128, 514), ar.f32(128, 514)]
        accas = [ar.f32(128, 512) for _ in range(3)]
        accbs = [ar.f32(128, 512) for _ in range(3)]
        gas = [ar.f32(128, 512) for _ in range(3)]
        fit = [0]
        ftail = [None]
        w_up3 = wv("w_up")
        for c in range(22):
            wu = wus[c % 2]
            P.dma("pool", wu[:, :, 0:128], w_up3[:, :, c * 128:(c + 1) * 128])
            P.dma("pool", wu[:, :, 128:256], w_up3[:, :, 2816 + c * 128:2816 + (c + 1) * 128])
            ja, jb = c, 22 + c
            for gi, (t0, n) in enumerate(GROUPS):
                apre, bpre = apres[fit[0] % 2], bpres[fit[0] % 2]
                apre_n, bpre_n = apres[(fit[0] + 1) % 2], bpres[(fit[0] + 1) % 2]
                if gi == 0:
                    P.memset(apre[:, 0:2], 0.0)
                    P.memset(bpre[:, 0:2], 0.0)
                if gi == 4:
                    P.dma("sp", apre[:, 0:2], I["sfconv"][:, ja, :])
                    P.dma("sp", bpre[:, 0:2], I["sfconv"][:, jb, :])
                acca, accb, ga = accas[fit[0] % 3], accbs[fit[0] % 3], gas[fit[0] % 3]
                fit[0] += 1
                for (pre, off) in ((apre, 0), (bpre, 128)):
                    ps = nb()
                    for k in range(8):
                        P.mm(ps[:, 0:n], wu[:, k, off:off + 128], xnT[:, k, t0:t0 + n], start=(k == 0), stop=(k == 7))
                    P.act(pre[:, 2:2 + n], ps[:, 0:n], AF.Copy)
                    if off == 128:
                        P.act(accb[:, 0:n], ps[:, 0:n], AF.Identity, scale=fconv_w[:, jb, 2:3])
                    else:
                        P.act(acca[:, 0:n], ps[:, 0:n], AF.Identity, scale=fconv_w[:, ja, 2:3])
                P.stt(acca[:, 0:n], apre[:, 0:n], fconv_w[:, ja, 0:1], acca[:, 0:n], ALU.mult, ALU.add)
                P.stt(acca[:, 0:n], apre[:, 1:1 + n], fconv_w[:, ja, 1:2], acca[:, 0:n], ALU.mult, ALU.add)
                P.stt(accb[:, 0:n], bpre[:, 0:n], fconv_w[:, jb, 0:1], accb[:, 0:n], ALU.mult, ALU.add)
                P.stt(accb[:, 0:n], bpre[:, 1:1 + n], fconv_w[:, jb, 1:2], accb[:, 0:n], ALU.mult, ALU.add)
                if ftail[0] is not None:
                    ftail[0]()

                def _tail(ga=ga, acca=acca, accb=accb, n=n, ja=ja, jb=jb, c=c, t0=t0):
                    P.act(ga[:, 0:n], acca[:, 0:n], AF.Gelu_apprx_tanh, bias=fconv_b[:, ja:ja + 1])
                    P.stt(hidT[:, c, t0:t0 + n], accb[:, 0:n], fconv_b[:, jb:jb + 1], ga[:, 0:n], ALU.add, ALU.mult)
                ftail[0] = _tail
                if gi in (3, 4):
                    so = 0 if gi == 3 else 1
                    P.dma("sp", O["ffn_convT"][so][:, ja, :], apre[:, n:n + 2])
                    P.dma("sp", O["ffn_convT"][so][:, jb, :], bpre[:, n:n + 2])
                if gi < 3:
                    P.copy(apre_n[:, 0:2], apre[:, n:n + 2])
                    P.copy(bpre_n[:, 0:2], bpre[:, n:n + 2])
        ftail[0]()
        dbg("hidT", hidT[:, 0, :])
        ar.release(m_h)
        P.mark("ffn_up")
        wd3 = wv("w_down")
        wd0 = ar.bf_at(xn_off, 128, 22, 512)
        wd1 = ar.bf(128, 22, 512)
        P.dma("pool", wd0, wd3[:, :, 0:512])
        P.dma("pool", wd1, wd3[:, :, 512:1024])
        oT = ar.f32(128, 8, 512)
        xs = ar.f32(128, 8, 512)
        rb = ar.f32(128, 512)
        x1s3 = x1scr.rearrange("(c p) n -> p c n", p=128)
        yT3 = O["yT"].rearrange("(c p) n -> p c n", p=128)
        for gi, (t0, n) in enumerate(GROUPS):
            P.dma("sp", xs[:, :, 0:n], x1s3[:, :, t0:t0 + n])
            for oc in range(8):
                wd = wd0 if oc < 4 else wd1
                o4 = oc % 4
                ps = nb()
                for c in range(22):
                    P.mm(ps[:, 0:n], wd[:, c, o4 * 128:(o4 + 1) * 128], hidT[:, c, t0:t0 + n], start=(c == 0), stop=(c == 21))
                P.act(oT[:, oc, 0:n], ps[:, 0:n], AF.Copy)
            rms_bcast(oT[:, :, 0:n], 8, n, 1024.0, rb[:, 0:n])
            for oc in range(8):
                P.stt(oT[:, oc, 0:n], oT[:, oc, 0:n], g_fpost[:, oc:oc + 1], rb[:, 0:n], ALU.mult, ALU.mult)
                P.tt(xs[:, oc, 0:n], xs[:, oc, 0:n], oT[:, oc, 0:n], ALU.add)
            P.dma("sp", yT3[:, :, t0:t0 + n], xs[:, :, 0:n])
        P.mark("ffn_down")
        P.emit(Kq={'pool': 3})
    except _Stop:
        pass
    return nc, DBG, P, 0


_CACHE = {}


def _get_nc(debug=()):
    key = tuple(debug)
    if key not in _CACHE:
        _CACHE[key] = build(debug)
    return _CACHE[key]


def _consts():
    ident = np.eye(128, dtype=np.float32)
    s = np.arange(128)
    trimask = (s[None, :] >= s[:, None]).astype(np.float32)
    negmask = np.where(s[:, None] <= s[None, :], 0.0, -30000.0).astype(np.float32)
    bd = np.zeros((128, 128), np.float32)
    bd[:64, :64] = 1.0
    bd[64:, 64:] = 1.0
    sel4 = np.zeros((4, 4, 128), np.float32)
    for h in range(4):
        sel4[h, h, :] = 1.0
    return dict(ident=ident, trimask=trimask, negmask=negmask, bdones=bd, sel4=sel4)


def _fm(v):
    return np.ascontiguousarray(v.reshape(-1, 128).T)


def _prep_shared(inp):
    f = lambda a: np.ascontiguousarray(np.asarray(a, dtype=np.float32))
    b_in = f(inp["b_in"])[0]
    d = dict(
        w_in=f(inp["w_in"])[0],
        b_fm=np.ascontiguousarray(np.stack([b_in[s:s + 128] for s in FM_STARTS], axis=1)),
        bg_fox=np.ascontiguousarray(b_in[1536:1544][:, None]),
        bg_i=np.ascontiguousarray(b_in[3080:3084][:, None]),
        bg_f=np.ascontiguousarray(b_in[3084:3088][:, None]),
        b_row=np.ascontiguousarray(b_in[None, :]),
        g_pre=_fm(f(inp["norm_mix_pre"])[0]),
        gq2=np.ascontiguousarray(np.concatenate([f(inp["fox_q_norm"])[0]] * 2)[:, None]),
        gk2=np.ascontiguousarray(np.concatenate([f(inp["fox_k_norm"])[0]] * 2)[:, None]),
        mconv_w=np.ascontiguousarray(f(inp["mlstm_conv_w"])[0].reshape(4, 8, 128).transpose(2, 1, 0)),
        mconv_b=_fm(f(inp["mlstm_conv_b"])[0]),
        mhn=np.ascontiguousarray(f(inp["mlstm_head_norm"])[0].T),
        g_mem=_fm(f(inp["norm_mem"])[0]),
        w_mem_kv=f(inp["w_mem_kv"])[0],
        w_br_a=f(inp["w_br_a"])[0], w_br_b=f(inp["w_br_b"])[0], w_br_m=f(inp["w_br_m"])[0],
        w_out=f(inp["w_out"])[0],
        g_post=_fm(f(inp["norm_mix_post"])[0]),
        g_fpre=_fm(f(inp["norm_ffn_pre"])[0]),
        w_up=f(inp["w_up"])[0],
        fconv_w=np.ascontiguousarray(f(inp["ffn_conv_w"])[0].reshape(3, 44, 128).transpose(2, 1, 0)),
        fconv_b=_fm(f(inp["ffn_conv_b"])[0]),
        w_down=f(inp["w_down"])[0],
        g_fpost=_fm(f(inp["norm_ffn_post"])[0]),
    )
    d.update(_consts())
    return d


def _prep_core(inp, b):
    f = lambda a: np.asarray(a, dtype=np.float32)
    c = np.ascontiguousarray
    return dict(
        xT=c(np.concatenate([f(inp["x_prompt"])[b].T, f(inp["x_sample"])[b].T], axis=1)),
        ckT=c(f(inp["cache_fox_k"])[0, b].reshape(PAST, 512).T),
        cv=c(f(inp["cache_fox_v"])[0, b].reshape(32, 128, 8, 64).transpose(2, 1, 0, 3)),
        clogfT=c(f(inp["cache_fox_logf"])[0, b].T),
        sC=c(f(inp["state_mlstm_c"])[0, b].transpose(2, 0, 1)),
        sn=c(f(inp["state_mlstm_n"])[0, b].T),
        sm=c(f(inp["state_mlstm_m"])[0, b][:, None]),
        sconv=c(f(inp["state_mlstm_conv"])[0, b].reshape(3, 8, 128).transpose(2, 1, 0)),
        cmkT=c(f(inp["cache_mem_k"])[0, b].reshape(256, 512).T),
        cmv=c(f(inp["cache_mem_v"])[0, b].reshape(256, 512)),
        sfconv=c(f(inp["state_ffn_conv"])[0, b].reshape(2, 44, 128).transpose(2, 1, 0)),
        memT=c(f(inp["mem_prompt"])[b].T),
    )


def _assemble(results):
    B = len(results)
    z = lambda *s: np.zeros(s, np.float32)
    y_p, y_s = z(B, TP, 1024), z(B, TS, 1024)
    fk_p, fv_p, fl_p = z(1, B, TP, 8, 64), z(1, B, TP, 8, 64), z(1, B, TP, 8)
    fk_s, fv_s, fl_s = z(1, B, TS, 8, 64), z(1, B, TS, 8, 64), z(1, B, TS, 8)
    c_p, n_p, m_p = z(1, B, 4, 128, 128), z(1, B, 4, 128), z(1, B, 4)
    c_s, n_s, m_s = z(1, B, 4, 128, 128), z(1, B, 4, 128), z(1, B, 4)
    cv_p, cv_s = z(1, B, 3, 1024), z(1, B, 3, 1024)
    fc_p, fc_s = z(1, B, 2, 5632), z(1, B, 2, 5632)
    mk_p, mv_p = z(1, B, 256, 4, 128), z(1, B, 256, 4, 128)
    for b, r in enumerate(results):
        yT = r["yT"]
        y_p[b] = yT[:, :TP].T
        y_s[b] = yT[:, TP:].T
        kT = r["fox_kT"]
        fk_p[0, b] = kT[:, :TP].T.reshape(TP, 8, 64)
        fk_s[0, b] = kT[:, TP:].T.reshape(TS, 8, 64)
        fv = r["fox_v"]
        fv_p[0, b] = fv[:TP].reshape(TP, 8, 64)
        fv_s[0, b] = fv[TP:].reshape(TS, 8, 64)
        lf = r["fox_logfT"]
        fl_p[0, b] = lf[:, :TP].T
        fl_s[0, b] = lf[:, TP:].T
        for (si, cc, nn, mm, cvv, fcc) in ((0, c_p, n_p, m_p, cv_p, fc_p), (1, c_s, n_s, m_s, cv_s, fc_s)):
            cc[0, b] = r["ml_cT"][si].transpose(1, 2, 0)
            nn[0, b] = r["ml_n"][si].T
            mm[0, b] = r["ml_m"][si][:, 0]
            cvv[0, b] = r["ml_convT"][si].transpose(2, 1, 0).reshape(3, 1024)
            fcc[0, b] = r["ffn_convT"][si].transpose(2, 1, 0).reshape(2, 5632)
        mk_p[0, b] = r["mem_kT"].T.reshape(256, 4, 128)
        mv_p[0, b] = r["mem_v"].reshape(256, 4, 128)
    return (y_p, y_s, fk_p, fv_p, fl_p, c_p, n_p, m_p, cv_p, fc_p, mk_p, mv_p,
            fk_s, fv_s, fl_s, c_s, n_s, m_s, cv_s, fc_s)


def kernel(**inputs):
    nc = _get_nc()[0]
    shared = _prep_shared(inputs)
    in_maps = []
    for b in range(8):
        d = dict(shared)
        d.update(_prep_core(inputs, b))
        in_maps.append(d)
    res = run_bass_kernel_spmd(nc, in_maps, core_ids=list(range(8)))
    return _assemble(res.results)
```

```python
import numpy as np
from contextlib import ExitStack
import concourse.bass as bass
import concourse.mybir as mybir

F32 = mybir.dt.float32
BF16 = mybir.dt.bfloat16
AF = mybir.ActivationFunctionType
ALU = mybir.AluOpType
AX = mybir.AxisListType


def _esize(dt):
    n = str(dt)
    if "float32" in n or "int32" in n:
        return 4
    if "bfloat16" in n or "float16" in n or "int16" in n:
        return 2
    if "int8" in n or "float8" in n:
        return 1
    if "64" in n:
        return 8
    raise ValueError(n)


def _boxes(ap):
    t = ap.tensor
    name = t.name
    es = _esize(ap.dtype)
    dims = [(int(s), int(n)) for s, n in ap.ap]
    off = int(ap.offset)
    if "DRAM" in str(ap.space).upper() or "HBM" in str(ap.space).upper():
        ext = sum((n - 1) * abs(s) for s, n in dims)
        return name, [(0, 1, off * es, (off + ext + 1) * es)]
    tes = _esize(t.dtype)
    rowbytes = int(np.prod([int(x) for x in t.shape[1:]])) * tes
    row = rowbytes // es
    p0 = off // row
    f0 = off % row
    pext = 0
    fd = []
    for s, n in dims:
        if n == 1:
            continue
        if s != 0 and s % row == 0:
            pext += (n - 1) * (s // row)
        elif s != 0:
            fd.append((abs(s), n))
    fd.sort(reverse=True)
    p1 = p0 + pext + 1
    if len(fd) >= 2:
        inner = sum((n - 1) * s for s, n in fd[1:]) + 1
        s0, n0 = fd[0]
        if s0 >= inner and n0 <= 64:
            return name, [(p0, p1, (f0 + i * s0) * es, (f0 + i * s0 + inner) * es) for i in range(n0)]
    ext = sum((n - 1) * s for s, n in fd) + 1
    return name, [(p0, p1, f0 * es, (f0 + ext) * es)]


def _ov(a, b):
    for x in a:
        for y in b:
            if x[0] < y[1] and y[0] < x[1] and x[2] < y[3] and y[2] < x[3]:
                return True
    return False


def _contained(a, b):
    for x in a:
        ok = False
        for y in b:
            if y[0] <= x[0] and x[1] <= y[1] and y[2] <= x[2] and x[3] <= y[3]:
                ok = True
                break
        if not ok:
            return False
    return True


class _Stop(Exception):
    pass


class Prog:
    ENGS = ("pe", "act", "dve", "pool", "sp")

    def __init__(self, nc):
        self.nc = nc
        self.ops = []
        self.hist = {}

    def add(self, eng, fn, reads=(), writes=(), dma=False):
        idx = len(self.ops)
        deps = set()
        rb = [_boxes(a) for a in reads]
        wb = [_boxes(a) for a in writes]
        for name, bx in rb:
            for e in self.hist.get(name, ()):
                if e[2] and _ov(e[0], bx):
                    deps.add(e[1])
        for name, bx in wb:
            for e in self.hist.get(name, ()):
                if _ov(e[0], bx):
                    deps.add(e[1])
        for name, bx in wb:
            lst = [e for e in self.hist.get(name, ()) if not _contained(e[0], bx)]
            lst.append((bx, idx, True, eng, dma))
            self.hist[name] = lst
        for name, bx in rb:
            lst = self.hist.setdefault(name, [])
            rep = False
            if not dma:
                for i, e in enumerate(lst):
                    if (not e[2]) and e[3] == eng and (not e[4]) and e[0] == bx:
                        lst[i] = (bx, idx, False, eng, dma)
                        rep = True
                        break
            if not rep:
                lst.append((bx, idx, False, eng, dma))
        deps.discard(idx)
        self.ops.append((eng, fn, deps, dma))
        return idx

    def mark(self, name):
        if not hasattr(self, 'marks'):
            self.marks = []
        self.marks.append((name, sum(1 for o in self.ops if o[0] == 'pe')))
        if getattr(self, 'stop_at', None) == name:
            self.emit()
            raise _Stop()

    def dma(self, q, out, in_, **kw):
        kw.setdefault('allow_slow_non_contiguous', True)
        return self.add(q, lambda e: e.dma_start(out=out, in_=in_, **kw), [in_], [out], dma=True)

    def mm(self, out, lhsT, rhs, start=True, stop=True, acc=False):
        rd = [lhsT, rhs] + ([out] if not start else [])
        return self.add("pe", lambda e: e.matmul(out, lhsT, rhs, start=start, stop=stop), rd, [out])

    def tr(self, out, in_, ident):
        return self.add("pe", lambda e: e.transpose(out, in_, ident), [in_, ident], [out])

    def act(self, out, in_, func, bias=None, scale=None, accum_out=None, eng="act"):
        rd = [in_]
        kw = {}
        if bias is not None:
            kw["bias"] = bias
            if not isinstance(bias, (int, float)):
                rd.append(bias)
        if scale is not None:
            kw["scale"] = scale
            if not isinstance(scale, (int, float)):
                rd.append(scale)
        wr = [out]
        if accum_out is not None:
            kw["accum_out"] = accum_out
            wr.append(accum_out)
        return self.add("act", lambda e: e.activation(out, in_, func, **kw), rd, wr)

    def tt(self, out, in0, in1, op, eng="dve"):
        return self.add(eng, lambda e: e.tensor_tensor(out, in0, in1, op), [in0, in1], [out])

    def ts(self, out, in0, s1, op0, s2=None, op1=None, eng="dve"):
        rd = [in0] + [s for s in (s1, s2) if s is not None and not isinstance(s, (int, float))]
        if op1 is None:
            return self.add(eng, lambda e: e.tensor_scalar(out, in0, s1, None, op0), rd, [out])
        return self.add(eng, lambda e: e.tensor_scalar(out, in0, s1, s2, op0, op1), rd, [out])

    def stt(self, out, in0, scalar, in1, op0, op1, eng="dve"):
        rd = [in0, in1] + ([] if isinstance(scalar, (int, float)) else [scalar])
        return self.add(eng, lambda e: e.scalar_tensor_tensor(out, in0, scalar, in1, op0, op1), rd, [out])

    def copy(self, out, in_, eng="dve"):
        return self.add(eng, lambda e: e.tensor_copy(out, in_), [in_], [out])

    def memset(self, out, val, eng="dve"):
        return self.add(eng, lambda e: e.memset(out, val), [], [out])

    def recip(self, out, in_):
        return self.add("dve", lambda e: e.reciprocal(out, in_), [in_], [out])

    def scan(self, out, d0, d1, init, op0, op1):
        rd = [d0, d1] + ([] if isinstance(init, (int, float)) else [init])
        return self.add("dve", lambda e: e.tensor_tensor_scan(out, d0, d1, init, op0, op1), rd, [out])

    def emit(self, R=20000, K=8, Kq=None):
        Kq = dict(Kq or {})
        KK = {e: Kq.get(e, K) for e in self.ENGS}
        nc = self.nc
        ops = self.ops
        needed = set()
        for eng, fn, deps, dma in ops:
            for d in deps:
                de, _, _, ddma = ops[d]
                if ddma:
                    continue
                if de == "pe" and eng == "pe" and not dma:
                    continue
                needed.add(d)
        sigidx = {}
        cnt = {e: 0 for e in self.ENGS}
        for i, (eng, fn, deps, dma) in enumerate(ops):
            if (not dma) and i in needed:
                sigidx[i] = cnt[eng]
                cnt[eng] += 1
        dmaidx = {}
        dcnt = {e: 0 for e in self.ENGS}
        for i, (eng, fn, deps, dma) in enumerate(ops):
            if dma:
                dmaidx[i] = dcnt[eng]
                dcnt[eng] += 1
        with ExitStack() as st:
            csem = {e: [st.enter_context(nc.semaphore(f"c_{e}_{j}")) for j in range(max(1, (cnt[e] + R - 1) // R))]
                    for e in self.ENGS}
            dsem = {e: [st.enter_context(nc.semaphore(f"d_{e}_{j}")) for j in range(min(KK[e], dcnt[e]))]
                    for e in self.ENGS}
            block = st.enter_context(nc.Block())

            def run(me, e):
                waited_c = {x: -1 for x in self.ENGS}
                waited_d = {}

                def wait_dma(d):
                    q = ops[d][0]
                    n = dmaidx[d]
                    K = KK[q]
                    sem = dsem[q][n % K]
                    val = 16 * (n // K + 1)
                    key = (q, n % K)
                    if waited_d.get(key, 0) >= val:
                        return
                    e.wait_ge(sem, val)
                    waited_d[key] = val

                for i, (eng, fn, deps, dma) in enumerate(ops):
                    if eng != me:
                        continue
                    for d in sorted(deps):
                        de, _, _, ddma = ops[d]
                        if ddma:
                            wait_dma(d)
                        else:
                            if de == "pe" and me == "pe" and not dma:
                                continue
                            g = sigidx[d]
                            if waited_c[de] >= g:
                                continue
                            e.wait_ge(csem[de][g // R], g % R + 1)
                            waited_c[de] = g
                    if dma:
                        n = dmaidx[i]
                        K = KK[me]
                        sem = dsem[me][n % K]
                        if n >= K:
                            key = (me, n % K)
                            val = 16 * (n // K)
                            if waited_d.get(key, 0) < val:
                                e.wait_ge(sem, val)
                                waited_d[key] = val
                        ins = fn(e)
                        ins.then_inc(sem, 16)
                    else:
                        ins = fn(e)
                        if i in sigidx:
                            g = sigidx[i]
                            ins.then_inc(csem[me][g // R], 1)
                K = KK[me]
                for j in range(min(K, dcnt[me])):
                    uses = (dcnt[me] - 1 - j) // K + 1
                    val = 16 * uses
                    if waited_d.get((me, j), 0) < val:
                        e.wait_ge(dsem[me][j], val)

            @block.sync
            def _(e):
                run("sp", e)

            @block.scalar
            def _(e):
                run("act", e)

            @block.vector
            def _(e):
                run("dve", e)

            @block.gpsimd
            def _(e):
                run("pool", e)

            @block.tensor
            def _(e):
                run("pe", e)

from concourse.bass_utils import run_bass_kernel_spmd

NT = 2112
TP = 2048
TS = 64
PAST = 4096
EPS = 1e-6
GROUPS = [(0, 512), (512, 512), (1024, 512), (1536, 512), (2048, 64)]
TILES = [(i * 128, 128) for i in range(16)] + [(2048, 64)]
QSC = 128.0 ** -0.5

FM_STARTS = ([0, 128, 256, 384] + [512, 640, 768, 896] + [1544 + 128 * i for i in range(8)]
             + [3088 + 128 * i for i in range(4)]
             + [3600 + 128 * i for i in range(4)] + [4112 + 128 * i for i in range(24)])
FM_COL = {s: i for i, s in enumerate(FM_STARTS)}

IN_SHAPES = dict(
    xT=[1024, NT], w_in=[1024, 7184], b_fm=[128, 48], bg_fox=[8, 1], bg_i=[4, 1], bg_f=[4, 1],
    b_row=[1, 7184], g_pre=[128, 8], gq2=[128, 1], gk2=[128, 1], mconv_w=[128, 8, 4], mconv_b=[128, 8],
    mhn=[128, 4], g_mem=[128, 8], w_mem_kv=[1024, 1024], w_br_a=[512, 1024], w_br_b=[512, 1024],
    w_br_m=[512, 1024], w_out=[1024, 1024], g_post=[128, 8], g_fpre=[128, 8], w_up=[1024, 5632],
    fconv_w=[128, 44, 3], fconv_b=[128, 44], w_down=[2816, 1024], g_fpost=[128, 8],
    ckT=[512, PAST], cv=[8, 128, 32, 64], clogfT=[8, PAST], sC=[128, 4, 128], sn=[128, 4], sm=[4, 1],
    sconv=[128, 8, 3], cmkT=[512, 256], cmv=[256, 512], sfconv=[128, 44, 2], memT=[1024, 256],
    ident=[128, 128], trimask=[128, 128], negmask=[128, 128], bdones=[128, 128], sel4=[4, 4, 128],
)
OUT_SHAPES = dict(
    yT=[1024, NT], fox_kT=[512, NT], fox_v=[NT, 512], fox_logfT=[8, NT],
    ml_cT=[2, 128, 4, 128], ml_n=[2, 128, 4], ml_m=[2, 4, 1], ml_convT=[2, 128, 8, 3],
    ffn_convT=[2, 128, 44, 2], mem_kT=[512, 256], mem_v=[256, 512],
)


class Arena:
    def __init__(self, A, cap):
        self.A = A
        self.cap = cap
        self.top = 0
        self.hw = 0

    def _alloc(self, words):
        off = self.top
        self.top += words
        assert self.top <= self.cap, f"arena overflow {self.top} > {self.cap}"
        self.hw = max(self.hw, self.top)
        return off

    def f32(self, *shape):
        n = int(np.prod(shape[1:]))
        off = self._alloc(n)
        return self._view(self.A[:, off:off + n], shape)

    def bf(self, *shape):
        n = int(np.prod(shape[1:]))
        words = (n + 1) // 2
        off = self._alloc(words)
        return self._view(self.A[:, off:off + words].bitcast(BF16)[:, 0:n], shape)

    @staticmethod
    def _view(v, shape):
        p = shape[0]
        if len(shape) == 3:
            v = v.rearrange("p (a b) -> p a b", b=shape[2])
        elif len(shape) == 4:
            v = v.rearrange("p (a b c) -> p a b c", b=shape[2], c=shape[3])
        if p < 128:
            v = v[0:p]
        return v

    def f32_at(self, off, *shape):
        n = int(np.prod(shape[1:]))
        return self._view(self.A[:, off:off + n], shape)

    def bf_at(self, off, *shape):
        n = int(np.prod(shape[1:]))
        words = (n + 1) // 2
        return self._view(self.A[:, off:off + words].bitcast(BF16)[:, 0:n], shape)

    def mark(self):
        return self.top

    def release(self, m):
        self.top = m


def build(debug=(), stop_at=None):
    nc = bass.Bass("TRN2", target_bir_lowering=False)
    I = {k: nc.dram_tensor(k, list(v), F32, kind="ExternalInput").ap() for k, v in IN_SHAPES.items()}
    O = {k: nc.dram_tensor(k, list(v), F32, kind="ExternalOutput").ap() for k, v in OUT_SHAPES.items()}
    x1scr = nc.dram_tensor("x1scr", [1024, NT], F32, kind="Internal").ap()
    DBG = {}
    P = Prog(nc)
    P.stop_at = stop_at
    CAP = 53000
    try:
      with nc.sbuf_tensor("A", [128, CAP], F32) as A_, nc.psum_tensor("PS", [128, 8, 512], F32) as PS:
        ar = Arena(A_, CAP)
        hw_box = [ar]
        bank_ctr = [0]

        def nb():
            b = bank_ctr[0] % 8
            bank_ctr[0] += 1
            return PS[:, b, :]

        def dbg(name, ap):
            if name in debug:
                shp = [int(s) for s in ap.shape]
                d = nc.dram_tensor("dbg_" + name, shp, F32, kind="ExternalOutput").ap()
                DBG[name] = shp
                if ap.dtype == F32 and "PSUM" not in str(ap.space).upper():
                    P.dma("sp", d, ap)
                else:
                    m = ar.mark()
                    t = ar.f32(*([128] + shp[1:]))[0:shp[0]]
                    P.copy(t, ap)
                    P.dma("sp", d, t)
                    ar.release(m)

        wv = lambda name: I[name].rearrange("(c p) n -> p c n", p=128)
        r3 = lambda ap, b: ap.rearrange("p (a b) -> p a b", b=b)

        ident_f = ar.f32(128, 128)
        ident_b = ar.bf(128, 128)
        trimask = ar.bf(128, 128)
        negmask = ar.bf(128, 128)
        bdones = ar.bf(128, 128)
        ones_b = ar.bf(128, 128)
        ones_f = ar.f32(128, 128)
        sel4 = ar.f32(4, 4, 128)
        b_fm = ar.f32(128, 48)
        g_pre = ar.f32(128, 8)
        g_post = ar.f32(128, 8)
        g_fpre = ar.f32(128, 8)
        g_fpost = ar.f32(128, 8)
        g_mem = ar.f32(128, 8)
        gq2 = ar.f32(128, 1)
        gk2 = ar.f32(128, 1)
        mhn = ar.f32(128, 4)
        mconv_w = ar.f32(128, 8, 4)
        mconv_b = ar.f32(128, 8)
        fconv_w = ar.f32(128, 44, 3)
        fconv_b = ar.f32(128, 44)
        def load_consts():
            P.dma("sp", ident_f, I["ident"])
            P.dma("pool", ident_b, I["ident"])
            P.dma("pool", trimask, I["trimask"])
            P.dma("pool", negmask, I["negmask"])
            P.dma("pool", bdones, I["bdones"])
            P.dma("sp", sel4, I["sel4"])
            for t, n in ((b_fm, "b_fm"), (g_pre, "g_pre"), (g_post, "g_post"), (g_fpre, "g_fpre"), (g_fpost, "g_fpost"),
                         (g_mem, "g_mem"), (gq2, "gq2"), (gk2, "gk2"), (mhn, "mhn"), (mconv_w, "mconv_w"),
                         (mconv_b, "mconv_b"), (fconv_w, "fconv_w"), (fconv_b, "fconv_b")):
                P.dma("sp", t, I[n])
            P.ts(gq8, gq2, 0.125, ALU.mult)
        P.memset(ones_b, 1.0)
        P.memset(ones_f, 1.0)
        eps_col = ar.f32(128, 1)
        P.memset(eps_col, EPS)
        one_col = ar.f32(128, 1)
        P.memset(one_col, 1.0)
        gq8 = ar.f32(128, 1)
        bcol = lambda start: b_fm[:, FM_COL[start]:FM_COL[start] + 1]

        def rms_bcast(src, C, n, D, rb, sq=None):
            m = ar.mark()
            if sq is None:
                sq = ar.bf(128, C, n)
            P.act(sq, src, AF.Square)
            ps = nb()
            for c in range(C):
                P.mm(ps[:, 0:n], ones_b, sq[:, c, :], start=(c == 0), stop=(c == C - 1))
            P.act(rb, ps[:, 0:n], AF.Ln, bias=eps_col, scale=1.0 / D)
            P.act(rb, rb, AF.Exp, scale=-0.5)
            ar.release(m)

        xn_off = ar.mark()
        xnT = ar.bf(128, 8, NT)
        m_after_xn = ar.mark()
        aT_off = ar.mark()
        aT = ar.bf(128, 4, NT)
        chi = ar.bf(8, NT)
        clo = ar.bf(8, NT)
        rdl = ar.f32(4, NT)
        wg_tok = ar.f32(128, 17, 4)
        dec_b = ar.f32(128, 4, 17)

        xT3 = I["xT"].rearrange("(c p) n -> p c n", p=128)
        m = ar.mark()
        xs2 = [ar.f32(128, 8, 512), ar.f32(128, 8, 512)]
        rb2 = [ar.f32(128, 512), ar.f32(128, 512)]
        sq2 = [ar.bf(128, 8, 512), ar.bf(128, 8, 512)]
        for gi, (t0, n) in enumerate(GROUPS):
            xs = xs2[gi % 2][:, :, 0:n]
            P.dma("sp", xs, xT3[:, :, t0:t0 + n])
            if gi == 0:
                load_consts()
            rb = rb2[gi % 2][:, 0:n]
            rms_bcast(xs, 8, n, 1024.0, rb, sq=sq2[gi % 2][:, :, 0:n])
            for c in range(8):
                P.stt(xnT[:, c, t0:t0 + n], xs[:, c, :], g_pre[:, c:c + 1], rb, ALU.mult, ALU.mult)
        ar.release(m)
        dbg("xnT", xnT[:, 0, :])

        P.mark("s0_norm")
        m1a = ar.mark()
        wgf = ar.bf(128, 8, 8)
        wgi = ar.bf(128, 8, 4)
        wgm = ar.bf(128, 8, 4)
        P.dma("pool", wgf, wv("w_in")[:, :, 1536:1544])
        P.dma("pool", wgi, wv("w_in")[:, :, 3080:3084])
        P.dma("pool", wgm, wv("w_in")[:, :, 3084:3088])
        bgf = ar.f32(8, 1)
        bgi = ar.f32(4, 1)
        bgm = ar.f32(4, 1)
        P.dma("sp", bgf, I["bg_fox"])
        P.dma("sp", bgi, I["bg_i"])
        P.dma("sp", bgm, I["bg_f"])
        zrow = ar.f32(8, NT)
        P.memset(zrow, 0.0)
        flog = ar.f32(8, NT)
        cfox = ar.f32(8, NT)
        gi_r = ar.f32(4, NT)
        mlf = ar.f32(4, NT)
        A_r = ar.f32(4, NT)
        G_r = ar.f32(4, NT)
        wgT = ar.f32(4, NT)
        for (wt, bc, dst, rows, ls) in ((wgf, bgf, flog, 8, True), (wgi, bgi, gi_r, 4, False), (wgm, bgm, mlf, 4, True)):
            nbias = ar.f32(rows, 1)
            P.ts(nbias, bc, -1.0, ALU.mult)
            for (t0, n) in GROUPS:
                ps = nb()
                for c in range(8):
                    P.mm(ps[0:rows, 0:n], wt[:, c, :], xnT[:, c, t0:t0 + n], start=(c == 0), stop=(c == 7))
                if ls:
                    m = ar.mark()
                    e = ar.f32(rows, n)
                    P.act(e, ps[0:rows, 0:n], AF.Exp, bias=nbias, scale=-1.0)
                    P.act(e, e, AF.Ln, bias=one_col[0:rows], scale=1.0)
                    P.ts(dst[:, t0:t0 + n], e, -1.0, ALU.mult)
                    ar.release(m)
                else:
                    P.act(dst[:, t0:t0 + n], ps[0:rows, 0:n], AF.Identity, bias=bc)
        P.dma("sp", O["fox_logfT"], flog)
        P.scan(cfox[:, 0:TP], flog[:, 0:TP], zrow[:, 0:TP], 0.0, ALU.add, ALU.add)
        P.scan(cfox[:, TP:NT], flog[:, TP:NT], zrow[:, 0:TS], 0.0, ALU.add, ALU.add)
        P.copy(chi, cfox)
        P.tt(clo, cfox, chi, ALU.subtract)
        sm0 = ar.f32(4, 1)
        P.dma("sp", sm0, I["sm"])
        zr4 = zrow[0:4]
        Bc = ar.f32(4, NT)
        P.scan(Bc[:, 0:TP], mlf[:, 0:TP], zr4[:, 0:TP], 0.0, ALU.add, ALU.add)
        P.scan(Bc[:, TP:NT], mlf[:, TP:NT], zr4[:, 0:TS], 0.0, ALU.add, ALU.add)
        P.tt(A_r, gi_r, Bc, ALU.subtract)
        P.scan(G_r[:, 0:TP], A_r[:, 0:TP], A_r[:, 0:TP], 0.0, ALU.max, ALU.max)
        P.scan(G_r[:, TP:NT], A_r[:, TP:NT], A_r[:, TP:NT], sm0, ALU.max, ALU.max)
        gend = ar.f32(4, 17)
        mprev = ar.f32(4, 17)
        P.copy(gend[:, 0:16], G_r[:, 127:TP:128])
        P.copy(gend[:, 16:17], G_r[:, NT - 1:NT])
        P.memset(mprev[:, 0:1], 0.0)
        P.copy(mprev[:, 1:16], gend[:, 0:15])
        P.copy(mprev[:, 16:17], sm0)
        dec = ar.f32(4, 17)
        P.tt(dec, mprev, gend, ALU.subtract)
        P.act(dec, dec, AF.Exp)
        gend_b = gend[:, 0:16].unsqueeze(2).broadcast_to([4, 16, 128])
        P.tt(r3(wgT[:, 0:TP], 128), r3(A_r[:, 0:TP], 128), gend_b, ALU.subtract)
        P.ts(wgT[:, TP:NT], A_r[:, TP:NT], gend[:, 16:17], ALU.subtract)
        P.act(wgT, wgT, AF.Exp)
        P.tt(r3(rdl[:, 0:TP], 128), r3(Bc[:, 0:TP], 128), gend_b, ALU.add)
        P.ts(rdl[:, TP:NT], Bc[:, TP:NT], gend[:, 16:17], ALU.add)
        P.ts(rdl, rdl, -1.0, ALU.mult)
        mTo = ar.f32(4, 2)
        P.tt(mTo[:, 0:1], G_r[:, TP - 1:TP], Bc[:, TP - 1:TP], ALU.add)
        P.tt(mTo[:, 1:2], G_r[:, NT - 1:NT], Bc[:, NT - 1:NT], ALU.add)
        P.dma("sp", O["ml_m"][0], mTo[:, 0:1])
        P.dma("sp", O["ml_m"][1], mTo[:, 1:2])
        for ti, (t0, L) in enumerate(TILES):
            ps = nb()
            P.tr(ps[0:L, 0:4], wgT[:, t0:t0 + L], ident_f[0:4, 0:4])
            P.copy(wg_tok[0:L, ti, :], ps[0:L, 0:4])
        for h in range(4):
            ps = nb()
            P.mm(ps[:, 0:17], sel4[:, h, :], dec)
            P.copy(dec_b[:, h, :], ps[:, 0:17])
        dbg("cfox", cfox)
        dbg("G_r", G_r)
        dbg("wgT", wgT)
        dbg("rdl", rdl)
        ar.release(m1a)
        s1 = ar.mark()

        P.mark("s1a_gates")
        cchi = ar.bf(8, PAST)
        cclo = ar.bf(8, PAST)
        m_s = ar.mark()
        ccache = ar.f32(8, PAST)
        lcache = ar.f32(8, PAST)
        zc = ar.f32(8, PAST)
        P.memset(zc, 0.0)
        P.dma("sp", lcache, I["clogfT"])
        P.scan(ccache, lcache, zc, 0.0, ALU.add, ALU.add)
        ctot = ar.f32(8, 1)
        P.copy(ctot, ccache[:, PAST - 1:PAST])
        P.ts(ccache, ccache, ctot, ALU.subtract)
        P.copy(cchi, ccache)
        P.tt(cclo, ccache, cchi, ALU.subtract)
        ar.release(m_s)
        qT = ar.bf(128, 4, NT)
        kT = ar.bf(128, 4, NT)
        V = ar.bf(128, 17, 512)
        m_w = ar.mark()
        wf = ar.bf(128, 8, 1536)
        P.dma("pool", wf, wv("w_in")[:, :, 0:1536])
        bvrow = ar.f32(128, 512)
        P.dma("sp", bvrow, I["b_row"][:, 1024:1536].partition_broadcast(128))
        ptmp = [(ar.f32(128, 512), ar.bf(128, 512), ar.f32(128, 512), ar.f32(128, 512)) for _ in range(3)]
        its = [(t0, n, which, ch) for (t0, n) in GROUPS for which in range(2) for ch in range(4)]

        def fp_A(it):
            t0, n, which, ch = it
            col0 = which * 512 + ch * 128
            ps = nb()
            for c in range(8):
                P.mm(ps[:, 0:n], wf[:, c, col0:col0 + 128], xnT[:, c, t0:t0 + n], start=(c == 0), stop=(c == 7))
            return ps

        def fp_B(i, it, ps):
            t0, n, which, ch = it
            col0 = which * 512 + ch * 128
            z_, sq_, r_, kf_ = ptmp[i % 3]
            z = z_[:, 0:n]
            sq = sq_[:, 0:n]
            P.act(z, ps[:, 0:n], AF.Identity, bias=bcol(col0))
            P.act(sq, ps[:, 0:n], AF.Square, bias=bcol(col0))
            ps2 = nb()
            P.mm(ps2[:, 0:n], bdones, sq)
            r = r_[:, 0:n]
            P.act(r, ps2[:, 0:n], AF.Ln, bias=eps_col, scale=1.0 / 64)
            P.act(r, r, AF.Exp, scale=-0.5)
            if which == 0:
                P.stt(qT[:, ch, t0:t0 + n], z, gq8, r, ALU.mult, ALU.mult)
            else:
                kf = kf_[:, 0:n]
                P.stt(kf, z, gk2, r, ALU.mult, ALU.mult)
                P.copy(kT[:, ch, t0:t0 + n], kf)
                P.dma("sp", O["fox_kT"][ch * 128:(ch + 1) * 128, t0:t0 + n], kf)

        vtmp = [ar.f32(128, 512), ar.f32(128, 512)]

        def fp_V(ti):
            t0, L = TILES[ti]
            ps = nb()
            for c in range(8):
                P.mm(ps[0:L, :], xnT[:, c, t0:t0 + L], wf[:, c, 1024:1536], start=(c == 0), stop=(c == 7))
            vf = vtmp[ti % 2]
            P.tt(vf[0:L], ps[0:L, :], bvrow[0:L], ALU.add)
            P.copy(V[0:L, ti, :], vf[0:L])
            P.dma("sp", O["fox_v"][t0:t0 + L, :], vf[0:L])

        psq = [fp_A(its[0])]
        vnext = 0
        for i, it in enumerate(its):
            if i + 1 < len(its):
                psq.append(fp_A(its[i + 1]))
            if i % 2 == 1 and vnext < 17:
                fp_V(vnext)
                vnext += 1
            fp_B(i, it, psq[i])
        while vnext < 17:
            fp_V(vnext)
            vnext += 1
        ar.release(m_w)
        dbg("qT", qT[:, 0, :])
        dbg("kT", kT[:, 0, :])

        P.mark("s1b_foxproj")
        pbufs = [ar.bf(128, 2, 512) for _ in range(4)]
        pctr = [0]
        fctr = [0, 0]
        rbuf = ar.f32(128, 512)
        KMAX = PAST + TS
        qas = [ar.bf(128, NT), ar.bf(128, NT)]
        kas = [ar.bf(128, KMAX), ar.bf(128, KMAX)]
        vexts = [ar.bf(128, 33, 128), ar.bf(128, 33, 128)]
        for i in range(2):
            P.memset(qas[i][64:68, :], -1.0)
            P.memset(kas[i][64:68, :], 1.0)
        P.memset(vexts[0][:, :, 64:128], 1.0)
        P.memset(vexts[1][:, :, 0:64], 1.0)

        def fox_attend(h, qa, qc0, qn, keys, ka, vext):
            ch, half = h // 2, h % 2
            pb = half * 64
            po = (1 - half) * 64
            acc = PS[:, 6 + (fctr[0] % 2), :]
            fctr[0] += 1
            full = [k for k in keys if k[3] is None]
            diag = [k for k in keys if k[3] is not None]
            per = 2 if qn > 64 else 8
            units = [full[i:i + per] for i in range(0, len(full), per)] + [[k] for k in diag]
            nk = len(keys)
            done = [0]

            def emit_front(unit):
                base = 2 * (fctr[1] % 3)
                fctr[1] += 1
                pt = pbufs[pctr[0] % len(pbufs)]
                pctr[0] += 1
                pvs = []
                if len(unit) == 1:
                    kc0, vti, L, doff = unit[0]
                    q_lo = 0 if doff is None else doff
                    sps = PS[:, base, :]
                    P.mm(sps[0:L, q_lo:qn], ka[0:68, kc0:kc0 + L], qa[0:68, qc0 + q_lo:qc0 + qn], start=True, stop=(doff is None))
                    if doff is not None:
                        dq = min(L, qn - q_lo)
                        P.mm(sps[0:L, q_lo:q_lo + dq], ident_b[0:L, 0:L], negmask[0:L, 0:dq], start=False, stop=True)
                    P.act(pt[0:L, 0, q_lo:qn], sps[0:L, q_lo:qn], AF.Exp)
                    pvs.append((acc[:, q_lo:qn], vext[0:L, vti, :], pt[0:L, 0, q_lo:qn]))
                elif qn > 64:
                    for j, (kc0, vti, L, doff) in enumerate(unit):
                        P.mm(PS[:, base + j, 0:qn], ka[0:68, kc0:kc0 + L], qa[0:68, qc0:qc0 + qn])
                        pvs.append((acc[:, 0:qn], vext[0:L, vti, :], pt[:, j, 0:qn]))
                    P.act(pt[:, 0:2, 0:qn], PS[:, base:base + 2, 0:qn], AF.Exp)
                else:
                    m = len(unit)
                    for j, (kc0, vti, L, doff) in enumerate(unit):
                        P.mm(PS[:, base, j * qn:(j + 1) * qn], ka[0:68, kc0:kc0 + L], qa[0:68, qc0:qc0 + qn])
                        pvs.append((acc[:, 0:qn], vext[0:L, vti, :], pt[:, 0, j * qn:(j + 1) * qn]))
                    P.act(pt[:, 0, 0:m * qn], PS[:, base, 0:m * qn], AF.Exp)
                return pvs

            def emit_pv(pvs):
                for (o_, l_, r_) in pvs:
                    P.mm(o_, l_, r_, start=(done[0] == 0), stop=(done[0] == nk - 1))
                    done[0] += 1

            LA = 2
            q_ = [emit_front(units[u]) for u in range(min(LA, len(units)))]
            for u in range(len(units)):
                if u + LA < len(units):
                    q_.append(emit_front(units[u + LA]))
                emit_pv(q_[u])
            P.act(rbuf[pb:pb + 64, 0:qn], acc[po:po + 64, 0:qn], AF.Ln)
            P.act(rbuf[pb:pb + 64, 0:qn], rbuf[pb:pb + 64, 0:qn], AF.Exp, scale=-1.0)
            P.tt(aT[pb:pb + 64, ch, qc0:qc0 + qn], acc[pb:pb + 64, 0:qn], rbuf[pb:pb + 64, 0:qn], ALU.mult)

        for h in range(8):
            ch, half = h // 2, h % 2
            pb = half * 64
            qa, ka, vext = qas[h % 2], kas[h % 2], vexts[h % 2]
            vo = 0 if half == 0 else 64
            P.dma("sp", qa[0:64, :], qT[pb:pb + 64, ch, :])
            P.dma("sp", qa[66:67, :], chi[h:h + 1, :])
            P.dma("sp", qa[67:68, :], clo[h:h + 1, :])
            P.dma("sp", ka[0:64, 0:TP], kT[pb:pb + 64, ch, 0:TP])
            P.dma("sp", ka[64:65, 0:TP], chi[h:h + 1, 0:TP])
            P.dma("sp", ka[65:66, 0:TP], clo[h:h + 1, 0:TP])
            P.copy(vext[:, 0:16, vo:vo + 64], V[:, 0:16, h * 64:h * 64 + 64], eng="pool")
            for gi in range(4):
                keys = []
                for ti in range(gi * 4 + 4):
                    doff = None if ti < gi * 4 else (ti - gi * 4) * 128
                    keys.append((ti * 128, ti, 128, doff))
                fox_attend(h, qa, gi * 512, 512, keys, ka, vext)
            if h == 0:
                dbg("aT0", aT[0:64, 0, 0:TP])
        dbg("aT", aT[:, 0, :])

        P.mark("fox_prompt")
        ckT_d = I["ckT"]
        for h in range(8):
            ch, half = h // 2, h % 2
            pb = half * 64
            qa, ka, vext = qas[h % 2], kas[h % 2], vexts[h % 2]
            vo = 0 if half == 0 else 64
            P.dma("sp", qa[0:64, TP:NT], qT[pb:pb + 64, ch, TP:NT])
            P.dma("sp", qa[66:67, TP:NT], chi[h:h + 1, TP:NT])
            P.dma("sp", qa[67:68, TP:NT], clo[h:h + 1, TP:NT])
            P.dma("pool", ka[0:64, 0:PAST], ckT_d[h * 64:(h + 1) * 64, :])
            P.dma("sp", ka[64:65, 0:PAST], cchi[h:h + 1, :])
            P.dma("sp", ka[65:66, 0:PAST], cclo[h:h + 1, :])
            P.dma("sp", ka[0:64, PAST:KMAX], kT[pb:pb + 64, ch, TP:NT])
            P.dma("sp", ka[64:65, PAST:KMAX], chi[h:h + 1, TP:NT])
            P.dma("sp", ka[65:66, PAST:KMAX], clo[h:h + 1, TP:NT])
            P.dma("pool", vext[:, 0:32, vo:vo + 64], I["cv"][h])
            P.copy(vext[0:64, 32, vo:vo + 64], V[0:64, 16, h * 64:h * 64 + 64], eng="pool")
            keys = [(ti * 128, ti, 128, None) for ti in range(32)] + [(PAST, 32, 64, 0)]
            fox_attend(h, qa, TP, TS, keys, ka, vext)
        dbg("aTs", aT[:, 0, TP:NT])
        ar.release(s1)
        bT_off = ar.mark()
        bT = ar.bf(128, 4, NT)
        s1 = ar.mark()

        P.mark("fox_sample")
        wm = ar.bf(128, 8, 2048)
        P.dma("pool", wm[:, :, 0:1536], wv("w_in")[:, :, 1544:3080])
        P.dma("pool", wm[:, :, 1536:2048], wv("w_in")[:, :, 3088:3600])
        bmv = ar.f32(128, 512)
        P.dma("sp", bmv, I["b_row"][:, 2568:3080].partition_broadcast(128))
        CT = ar.f32(128, 4, 129)
        pc = ar.f32(128, 8, 515)
        qks = [ar.bf(128, 8, 512), ar.bf(128, 8, 512)]
        qkc = [qks[0]]
        sigos = [ar.f32(128, 4, 512), ar.f32(128, 4, 512)]
        sigc = [sigos[0]]
        caccs = [ar.f32(128, 512), ar.f32(128, 512)]
        ones3 = ones_f.unsqueeze(1).broadcast_to([128, 4, 128])
        msets = []
        for _ in range(2):
            msets.append(dict(vfull=ar.f32(128, 512), vw=ar.bf(128, 4, 129), wgb=ar.bf(128, 4, 128),
                              ktok=ar.bf(128, 4, 128), ET=ar.bf(128, 4, 128), Cq=ar.bf(128, 4, 128),
                              nbm=ar.bf(128, 4, 128), tden=ar.f32(128, 4, 128), hs=ar.f32(128, 4, 128),
                              hsq=ar.bf(128, 4, 128), rdbt=ar.f32(128, 4, 128)))
            msets[-1]["rr"] = msets[-1]["tden"]
        B_v = PS[:, 0, :]
        B_tr = PS[:, 1, :].bitcast(BF16)
        B_S = PS[:, 2, :]
        B_k = [PS[:, 3, :], PS[:, 4, :]]
        B_Y = PS[:, 5, :]
        B_D = PS[:, 6, :]
        B_q = PS[:, 7, :]
        P.memset(CT, 0.0)
        P.memset(pc[:, :, 0:3], 0.0)
        wb3 = ar.f32(128, 8)
        for j in range(8):
            P.tt(wb3[:, j:j + 1], mconv_w[:, j, 3:4], bcol(1544 + 128 * j), ALU.mult)

        def ml_step(prev, cur):
            if prev is not None:
                tiP, ttP, LP, loP, SP = prev
                Dv = r3(B_D[:, 0:4 * LP], LP)
                Yv = r3(B_Y[:, 0:4 * LP], LP)
                td = SP["tden"][:, :, 0:LP]
                hs_ = SP["hs"][:, :, 0:LP]
                hq_ = SP["hsq"][:, :, 0:LP]
                rr_ = SP["rr"][:, :, 0:LP]
            if cur is not None:
                ti, tt0, L, lo, S = cur
                wg3 = wg_tok[0:L, ti, :].unsqueeze(2)
            if prev is not None:
                for h in range(4):
                    q_h = qkc[0][:, h, loP:loP + LP]
                    P.mm(B_Y[:, h * LP:(h + 1) * LP], SP["Cq"][:, h, :], q_h, start=True, stop=False)
                    P.mm(B_Y[:, h * LP:(h + 1) * LP], SP["vw"][0:LP, h, 0:128], SP["ET"][0:LP, h, 0:LP], start=False, stop=True)
                for h in range(4):
                    q_h = qkc[0][:, h, loP:loP + LP]
                    P.mm(B_D[:, h * LP:(h + 1) * LP], SP["nbm"][:, h, :], q_h, start=True, stop=False)
                    P.mm(B_D[:, h * LP:(h + 1) * LP], SP["wgb"][0:LP, h, :], SP["ET"][0:LP, h, 0:LP], start=False, stop=True)
            if cur is not None:
                for c in range(8):
                    P.mm(B_v[0:L, :], xnT[:, c, tt0:tt0 + L], wm[:, c, 1024:1536], start=(c == 0), stop=(c == 7))
            if prev is not None:
                P.act(td, Dv, AF.Abs)
                P.tt(td, td, SP["rdbt"][:, :, 0:LP], ALU.max)
                P.act(td, td, AF.Ln)
                P.act(td, td, AF.Exp, scale=-1.0)
            if cur is not None:
                P.tt(S["vfull"][0:L], B_v[0:L, :], bmv[0:L], ALU.add)
                P.tt(S["vw"][0:L, :, 0:128], r3(S["vfull"][0:L], 128), wg3.broadcast_to([L, 4, 128]), ALU.mult)
                P.copy(S["vw"][0:L, :, 128:129], wg3)
                P.tt(S["wgb"][0:L], ones_f[0:L].unsqueeze(1).broadcast_to([L, 4, 128]), wg3.broadcast_to([L, 4, 128]), ALU.mult)
                for h in range(4):
                    P.tr(B_tr[0:L, h * 128:(h + 1) * 128], qkc[0][:, 4 + h, lo:lo + L], ident_b)
                P.act(S["ktok"][0:L], r3(B_tr[0:L, 0:512], 128), AF.Copy)
                for h in range(4):
                    P.mm(B_S[0:L, h * L:(h + 1) * L], qkc[0][:, 4 + h, lo:lo + L], qkc[0][:, h, lo:lo + L])
            if prev is not None:
                P.tt(hs_, Yv, td, ALU.mult)
                P.tt(hs_, hs_, sigc[0][:, :, loP:loP + LP], ALU.mult)
                P.act(hq_, hs_, AF.Square)
                for h in range(4):
                    P.mm(B_q[:, h * LP:(h + 1) * LP], ones_b, SP["hsq"][:, h, 0:LP])
                P.act(rr_, r3(B_q[:, 0:4 * LP], LP), AF.Ln, bias=eps_col, scale=1.0 / 128)
                P.act(rr_, rr_, AF.Exp, scale=-0.5)
            if cur is not None:
                for h in range(4):
                    P.mm(B_q[:, h * L:(h + 1) * L], sel4[:, h, :], rdl[:, tt0:tt0 + L])
                P.act(S["rdbt"][:, :, 0:L], r3(B_q[:, 0:4 * L], L), AF.Exp)
            if cur is not None:
                P.stt(S["ET"][0:L, :, 0:L], r3(B_S[0:L, 0:4 * L], L), QSC,
                      trimask[0:L, 0:L].unsqueeze(1).broadcast_to([L, 4, L]), ALU.mult, ALU.mult)
                for h in range(4):
                    P.mm(B_k[h // 2][:, (h % 2) * 129:(h % 2) * 129 + 129], S["ktok"][0:L, h, :], S["vw"][0:L, h, :])
                P.tt(CT, CT, dec_b[:, :, ti:ti + 1].broadcast_to([128, 4, 129]), ALU.mult)
                P.act(S["Cq"], CT[:, :, 0:128], AF.Copy, scale=QSC)
                P.stt(S["nbm"], ones3, QSC, CT[:, :, 128:129].broadcast_to([128, 4, 128]), ALU.mult, ALU.mult)
            if prev is not None:
                P.tt(hs_, hs_, rr_, ALU.mult)
                P.tt(bT[:, :, ttP:ttP + LP], hs_, mhn.unsqueeze(2).broadcast_to([128, 4, LP]), ALU.mult)
            if cur is not None:
                P.tt(CT[:, 0:2, :], CT[:, 0:2, :], r3(B_k[0][:, 0:258], 129), ALU.add)
                P.tt(CT[:, 2:4, :], CT[:, 2:4, :], r3(B_k[1][:, 0:258], 129), ALU.add)

        def ml_proj_parts(gi):
            t0, n = GROUPS[gi]
            qk = qks[gi % 2]

            def part(pj):
                if pj == 0 and gi == 4:
                    P.dma("sp", pc[:, :, 0:3], I["sconv"])
                for j in (2 * pj, 2 * pj + 1):
                    ps = nb()
                    for c in range(8):
                        P.mm(ps[:, 0:n], wm[:, c, j * 128:(j + 1) * 128], xnT[:, c, t0:t0 + n], start=(c == 0), stop=(c == 7))
                    P.act(pc[:, j, 3:3 + n], ps[:, 0:n], AF.Identity, bias=bcol(1544 + 128 * j))
                    P.act(caccs[j % 2][:, 0:n], ps[:, 0:n], AF.Identity, scale=mconv_w[:, j, 3:4], bias=wb3[:, j:j + 1])
                for j in (2 * pj, 2 * pj + 1):
                    cacc = caccs[j % 2]
                    for tap in (0, 1, 2):
                        P.stt(cacc[:, 0:n], pc[:, j, tap:tap + n], mconv_w[:, j, tap:tap + 1], cacc[:, 0:n], ALU.mult, ALU.add)
                    P.act(qk[:, j, 0:n], cacc[:, 0:n], AF.Silu, bias=mconv_b[:, j:j + 1])
                if pj == 3:
                    if gi == 3:
                        P.dma("sp", O["ml_convT"][0], pc[:, :, n:n + 3])
                    if gi == 4:
                        P.dma("sp", O["ml_convT"][1], pc[:, :, n:n + 3])
                    if gi < 3:
                        P.copy(pc[:, :, 0:3], pc[:, :, n:n + 3])
            return [lambda pj=pj: part(pj) for pj in range(4)]

        def ml_gates(gi):
            t0, n = GROUPS[gi]
            for h in range(4):
                ps = nb()
                for c in range(8):
                    P.mm(ps[:, 0:n], wm[:, c, 1536 + h * 128:1536 + (h + 1) * 128], xnT[:, c, t0:t0 + n], start=(c == 0), stop=(c == 7))
                P.act(sigos[gi % 2][:, h, 0:n], ps[:, 0:n], AF.Sigmoid, bias=bcol(3088 + 128 * h))

        for p_ in ml_proj_parts(0):
            p_()
        ti_global = 0
        for gi, (t0, n) in enumerate(GROUPS):
            if gi == 4:
                P.dma("sp", O["ml_cT"][0], CT[:, :, 0:128])
                P.dma("sp", O["ml_n"][0], CT[:, :, 128])
                P.dma("sp", CT[:, :, 0:128], I["sC"])
                P.dma("sp", CT[:, :, 128], I["sn"])
            if gi == 0:
                ml_gates(0)
            qkc[0] = qks[gi % 2]
            sigc[0] = sigos[gi % 2]
            nparts = ml_proj_parts(gi + 1) if gi + 1 < len(GROUPS) else []
            tiles = [(tt0, L) for (tt0, L) in TILES if t0 <= tt0 < t0 + n]
            prev = None
            for k_, (tt0, L) in enumerate(tiles):
                ti = ti_global
                ti_global += 1
                cur = (ti, tt0, L, tt0 - t0, msets[ti % 2])
                ml_step(prev, cur)
                prev = cur
                if k_ < len(nparts):
                    nparts[k_]()
                if k_ == min(1, len(tiles) - 1) and gi + 1 < len(GROUPS):
                    ml_gates(gi + 1)
            ml_step(prev, None)
        P.dma("sp", O["ml_cT"][1], CT[:, :, 0:128])
        P.dma("sp", O["ml_n"][1], CT[:, :, 128])
        ar.release(s1)
        dbg("bT", bT[:, 0, :])
        mT_off = ar.mark()
        mT = ar.bf(128, 4, NT)
        s1 = ar.mark()

        P.mark("mlstm")
        wq = ar.bf(128, 8, 512)
        P.dma("pool", wq, wv("w_in")[:, :, 3600:4112])
        wkv = ar.bf(128, 8, 1024)
        P.dma("pool", wkv, wv("w_mem_kv"))
        memx = ar.f32(128, 8, 256)
        P.dma("sp", memx, I["memT"].rearrange("(c p) n -> p c n", p=128))
        rbm = ar.f32(128, 256)
        rms_bcast(memx, 8, 256, 1024.0, rbm)
        memn = ar.bf(128, 8, 256)
        for c in range(8):
            P.stt(memn[:, c, :], memx[:, c, :], g_mem[:, c:c + 1], rbm, ALU.mult, ALU.mult)
        mkT = ar.bf(128, 2, 4, 256)
        mv = ar.bf(128, 2, 2, 512)
        tmpf = ar.f32(128, 512)
        for chh in range(4):
            ps = nb()
            for c in range(8):
                P.mm(ps[:, 0:256], wkv[:, c, chh * 128:(chh + 1) * 128], memn[:, c, :], start=(c == 0), stop=(c == 7))
            P.copy(tmpf[:, 0:256], ps[:, 0:256])
            P.copy(mkT[:, 0, chh, :], tmpf[:, 0:256])
            P.dma("sp", O["mem_kT"][chh * 128:(chh + 1) * 128, :], tmpf[:, 0:256])
        for mt in range(2):
            ps = nb()
            for c in range(8):
                P.mm(ps[:, :], memn[:, c, mt * 128:(mt + 1) * 128], wkv[:, c, 512:1024], start=(c == 0), stop=(c == 7))
            P.copy(tmpf, ps)
            P.copy(mv[:, 0, mt, :], tmpf)
            P.dma("sp", O["mem_v"][mt * 128:(mt + 1) * 128, :], tmpf)
        P.dma("pool", mkT[:, 1], I["cmkT"].rearrange("(c p) n -> p c n", p=128))
        P.dma("pool", mv[:, 1], I["cmv"].rearrange("(t p) f -> p t f", p=128))
        qhs = [ar.bf(128, 512), ar.bf(128, 512)]
        ptms = [ar.bf(128, 2, 512), ar.bf(128, 2, 512)]
        rlms = [ar.f32(128, 512), ar.f32(128, 512)]
        mits = [(gi, h) for gi in range(len(GROUPS)) for h in range(4)]

        def mm_A(i):
            gi, h = mits[i]
            t0, n = GROUPS[gi]
            ps = nb()
            for c in range(8):
                P.mm(ps[:, 0:n], wq[:, c, h * 128:(h + 1) * 128], xnT[:, c, t0:t0 + n], start=(c == 0), stop=(c == 7))
            P.act(qhs[i % 2][:, 0:n], ps[:, 0:n], AF.Identity, bias=bcol(3600 + 128 * h))

        def mm_B(i):
            gi, h = mits[i]
            t0, n = GROUPS[gi]
            seq = 0 if gi < 4 else 1
            qh, ptm, rlm = qhs[i % 2], ptms[i % 2], rlms[i % 2]
            for mt in range(2):
                sps = nb()
                P.mm(sps[:, 0:n], mkT[:, seq, h, mt * 128:(mt + 1) * 128], qh[:, 0:n])
                P.act(ptm[:, mt, 0:n], sps[:, 0:n], AF.Exp, scale=QSC)
            ops_ = nb()
            lps = nb()
            for mt in range(2):
                P.mm(ops_[:, 0:n], mv[:, seq, mt, h * 128:(h + 1) * 128], ptm[:, mt, 0:n], start=(mt == 0), stop=(mt == 1))
            for mt in range(2):
                P.mm(lps[:, 0:n], ones_b, ptm[:, mt, 0:n], start=(mt == 0), stop=(mt == 1))
            P.act(rlm[:, 0:n], lps[:, 0:n], AF.Ln)
            P.act(rlm[:, 0:n], rlm[:, 0:n], AF.Exp, scale=-1.0)
            P.tt(mT[:, h, t0:t0 + n], ops_[:, 0:n], rlm[:, 0:n], ALU.mult)

        mm_A(0)
        for i in range(len(mits)):
            if i + 1 < len(mits):
                mm_A(i + 1)
            mm_B(i)
        dbg("mT", mT[:, 0, :])
        ar.release(s1)

        P.mark("mem")
        mergedT = ar.bf(128, 8, NT)
        s2 = ar.mark()
        wgs = [[ar.bf(128, 8, 128) for b in range(3)] for _ in range(2)]
        wbs = [[ar.bf(128, 4, 128) for b in range(3)] for _ in range(2)]
        sg = [ar.f32(128, 512) for b in range(3)]
        macc = ar.f32(128, 512)
        mtmp = ar.f32(128, 512)
        brs = ("w_br_a", "w_br_b", "w_br_m")
        srcs = (aT, bT, mT)
        for oc in range(8):
            wg_ = wgs[oc % 2]
            wb_ = wbs[oc % 2]
            for b in range(3):
                g0 = 4112 + b * 1024 + oc * 128
                P.dma("pool", wg_[b], wv("w_in")[:, :, g0:g0 + 128])
                P.dma("pool", wb_[b], wv(brs[b])[:, :, oc * 128:(oc + 1) * 128])
            for (t0, n) in GROUPS:
                pp = []
                for b in range(3):
                    g0 = 4112 + b * 1024 + oc * 128
                    ps = nb()
                    for c in range(8):
                        P.mm(ps[:, 0:n], wg_[b][:, c, :], xnT[:, c, t0:t0 + n], start=(c == 0), stop=(c == 7))
                    P.act(sg[b][:, 0:n], ps[:, 0:n], AF.Sigmoid, bias=bcol(g0))
                for b in range(3):
                    ps = nb()
                    for c in range(4):
                        P.mm(ps[:, 0:n], wb_[b][:, c, :], srcs[b][:, c, t0:t0 + n], start=(c == 0), stop=(c == 3))
                    pp.append(ps)
                P.tt(macc[:, 0:n], sg[0][:, 0:n], pp[0][:, 0:n], ALU.mult)
                P.tt(mtmp[:, 0:n], sg[1][:, 0:n], pp[1][:, 0:n], ALU.mult)
                P.tt(macc[:, 0:n], macc[:, 0:n], mtmp[:, 0:n], ALU.add)
                P.tt(mtmp[:, 0:n], sg[2][:, 0:n], pp[2][:, 0:n], ALU.mult)
                P.tt(mergedT[:, oc, t0:t0 + n], macc[:, 0:n], mtmp[:, 0:n], ALU.add)
        dbg("mergedT", mergedT[:, 0, :])
        ar.release(s2)

        P.mark("s2_merge")
        wo = ar.bf(128, 8, 1024)
        P.dma("pool", wo, wv("w_out"))
        oTs = [ar.f32(128, 8, 512), ar.f32_at(aT_off, 128, 8, 512)]
        xss = [ar.f32(128, 8, 512), ar.f32_at(bT_off, 128, 8, 512)]
        rbs = [ar.f32(128, 512), ar.f32(128, 512)]
        sqs = [ar.bf(128, 8, 512), ar.bf_at(mT_off, 128, 8, 512)]
        x1s3w = x1scr.rearrange("(c p) n -> p c n", p=128)

        def ob_A(gi):
            t0, n = GROUPS[gi]
            oT, xs = oTs[gi % 2], xss[gi % 2]
            P.dma("sp", xs[:, :, 0:n], xT3[:, :, t0:t0 + n])
            for oc in range(8):
                ps = nb()
                for c in range(8):
                    P.mm(ps[:, 0:n], wo[:, c, oc * 128:(oc + 1) * 128], mergedT[:, c, t0:t0 + n], start=(c == 0), stop=(c == 7))
                P.act(oT[:, oc, 0:n], ps[:, 0:n], AF.Copy)
                P.act(sqs[gi % 2][:, oc, 0:n], ps[:, 0:n], AF.Square)

        def ob_B(gi):
            t0, n = GROUPS[gi]
            oT, xs, rb, sq = oTs[gi % 2], xss[gi % 2], rbs[gi % 2], sqs[gi % 2]
            ps = nb()
            for c in range(8):
                P.mm(ps[:, 0:n], ones_b, sq[:, c, 0:n], start=(c == 0), stop=(c == 7))
            P.act(rb[:, 0:n], ps[:, 0:n], AF.Ln, bias=eps_col, scale=1.0 / 1024.0)
            P.act(rb[:, 0:n], rb[:, 0:n], AF.Exp, scale=-0.5)
            ps2 = nb()
            for oc in range(8):
                P.stt(oT[:, oc, 0:n], oT[:, oc, 0:n], g_post[:, oc:oc + 1], rb[:, 0:n], ALU.mult, ALU.mult)
                P.tt(xs[:, oc, 0:n], xs[:, oc, 0:n], oT[:, oc, 0:n], ALU.add)
                P.act(sq[:, oc, 0:n], xs[:, oc, 0:n], AF.Square)
                P.mm(ps2[:, 0:n], ones_b, sq[:, oc, 0:n], start=(oc == 0), stop=(oc == 7))
            P.dma("sp", x1s3w[:, :, t0:t0 + n], xs[:, :, 0:n])
            P.act(rb[:, 0:n], ps2[:, 0:n], AF.Ln, bias=eps_col, scale=1.0 / 1024.0)
            P.act(rb[:, 0:n], rb[:, 0:n], AF.Exp, scale=-0.5)
            for c in range(8):
                P.stt(xnT[:, c, t0:t0 + n], xs[:, c, 0:n], g_fpre[:, c:c + 1], rb[:, 0:n], ALU.mult, ALU.mult)

        ob_A(0)
        for gi in range(len(GROUPS)):
            if gi + 1 < len(GROUPS):
                ob_A(gi + 1)
            ob_B(gi)
        dbg("x1nT", xnT[:, 0, :])
        ar.release(m_after_xn)

        P.mark("s2b_out")
        hidT = ar.bf(128, 22, NT)
        m_h = ar.mark()
        wus = [ar.bf(128, 8, 256), ar.bf(128, 8, 256)]
        apre_p = ar.f32(128, 2 + TP)
        bpre_p = ar.f32(128, 2 + TP)
        apre_s = ar.f32(128, 2 + TS)
        bpre_s = ar.f32(128, 2 + TS)
        accas = [ar.f32(128, 512) for _ in range(3)]
        accbs = [ar.f32(128, 512) for _ in range(3)]
        gas = [ar.f32(128, 512) for _ in range(3)]
        fit = [0]
        ftail = [None]
        w_up3 = wv("w_up")
        for c in range(22):
            wu = wus[c % 2]
            P.dma("pool", wu[:, :, 0:128], w_up3[:, :, c * 128:(c + 1) * 128])
            P.dma("pool", wu[:, :, 128:256], w_up3[:, :, 2816 + c * 128:2816 + (c + 1) * 128])
            ja, jb = c, 22 + c
            for gi, (t0, n) in enumerate(GROUPS):
                if gi < 4:
                    apre, bpre = apre_p[:, t0:t0 + n + 2], bpre_p[:, t0:t0 + n + 2]
                else:
                    apre, bpre = apre_s, bpre_s
                if gi == 0:
                    P.memset(apre_p[:, 0:2], 0.0)
                    P.memset(bpre_p[:, 0:2], 0.0)
                if gi == 4:
                    P.dma("sp", apre[:, 0:2], I["sfconv"][:, ja, :])
                    P.dma("sp", bpre[:, 0:2], I["sfconv"][:, jb, :])
                acca, accb, ga = accas[fit[0] % 3], accbs[fit[0] % 3], gas[fit[0] % 3]
                fit[0] += 1
                for (pre, off) in ((apre, 0), (bpre, 128)):
                    ps = nb()
                    for k in range(8):
                        P.mm(ps[:, 0:n], wu[:, k, off:off + 128], xnT[:, k, t0:t0 + n], start=(k == 0), stop=(k == 7))
                    P.act(pre[:, 2:2 + n], ps[:, 0:n], AF.Copy)
                    if off == 128:
                        P.act(accb[:, 0:n], ps[:, 0:n], AF.Identity, scale=fconv_w[:, jb, 2:3])
                    else:
                        P.act(acca[:, 0:n], ps[:, 0:n], AF.Identity, scale=fconv_w[:, ja, 2:3])
                P.stt(acca[:, 0:n], apre[:, 0:n], fconv_w[:, ja, 0:1], acca[:, 0:n], ALU.mult, ALU.add)
                P.stt(acca[:, 0:n], apre[:, 1:1 + n], fconv_w[:, ja, 1:2], acca[:, 0:n], ALU.mult, ALU.add)
                P.stt(accb[:, 0:n], bpre[:, 0:n], fconv_w[:, jb, 0:1], accb[:, 0:n], ALU.mult, ALU.add)
                P.stt(accb[:, 0:n], bpre[:, 1:1 + n], fconv_w[:, jb, 1:2], accb[:, 0:n], ALU.mult, ALU.add)
                if ftail[0] is not None:
                    ftail[0]()

                def _tail(ga=ga, acca=acca, accb=accb, n=n, ja=ja, jb=jb, c=c, t0=t0):
                    P.act(ga[:, 0:n], acca[:, 0:n], AF.Gelu_apprx_tanh, bias=fconv_b[:, ja:ja + 1])
                    P.stt(hidT[:, c, t0:t0 + n], accb[:, 0:n], fconv_b[:, jb:jb + 1], ga[:, 0:n], ALU.add, ALU.mult)
                ftail[0] = _tail
                if gi in (3, 4):
                    so = 0 if gi == 3 else 1
                    P.dma("sp", O["ffn_convT"][so][:, ja, :], apre[:, n:n + 2])
                    P.dma("sp", O["ffn_convT"][so][:, jb, :], bpre[:, n:n + 2])
        ftail[0]()
        dbg("hidT", hidT[:, 0, :])
        ar.release(m_h)
        P.mark("ffn_up")
        wd3 = wv("w_down")
        wd0 = ar.bf_at(xn_off, 128, 22, 512)
        wd1 = ar.bf(128, 22, 512)
        P.dma("pool", wd0, wd3[:, :, 0:512])
        P.dma("pool", wd1, wd3[:, :, 512:1024])
        oT = ar.f32(128, 8, 512)
        xs = ar.f32(128, 8, 512)
        rb = ar.f32(128, 512)
        x1s3 = x1scr.rearrange("(c p) n -> p c n", p=128)
        yT3 = O["yT"].rearrange("(c p) n -> p c n", p=128)
        for gi, (t0, n) in enumerate(GROUPS):
            P.dma("sp", xs[:, :, 0:n], x1s3[:, :, t0:t0 + n])
            for oc in range(8):
                wd = wd0 if oc < 4 else wd1
                o4 = oc % 4
                ps = nb()
                for c in range(22):
                    P.mm(ps[:, 0:n], wd[:, c, o4 * 128:(o4 + 1) * 128], hidT[:, c, t0:t0 + n], start=(c == 0), stop=(c == 21))
                P.act(oT[:, oc, 0:n], ps[:, 0:n], AF.Copy)
            rms_bcast(oT[:, :, 0:n], 8, n, 1024.0, rb[:, 0:n])
            for oc in range(8):
                P.stt(oT[:, oc, 0:n], oT[:, oc, 0:n], g_fpost[:, oc:oc + 1], rb[:, 0:n], ALU.mult, ALU.mult)
                P.tt(xs[:, oc, 0:n], xs[:, oc, 0:n], oT[:, oc, 0:n], ALU.add)
            P.dma("sp", yT3[:, :, t0:t0 + n], xs[:, :, 0:n])
        P.mark("ffn_down")
        P.emit(Kq={'pool': 3})
    except _Stop:
        pass
    return nc, DBG, P, 0


_CACHE = {}


def _get_nc(debug=()):
    key = tuple(debug)
    if key not in _CACHE:
        _CACHE[key] = build(debug)
    return _CACHE[key]


def _consts():
    ident = np.eye(128, dtype=np.float32)
    s = np.arange(128)
    trimask = (s[None, :] >= s[:, None]).astype(np.float32)
    negmask = np.where(s[:, None] <= s[None, :], 0.0, -30000.0).astype(np.float32)
    bd = np.zeros((128, 128), np.float32)
    bd[:64, :64] = 1.0
    bd[64:, 64:] = 1.0
    sel4 = np.zeros((4, 4, 128), np.float32)
    for h in range(4):
        sel4[h, h, :] = 1.0
    return dict(ident=ident, trimask=trimask, negmask=negmask, bdones=bd, sel4=sel4)


def _fm(v):
    return np.ascontiguousarray(v.reshape(-1, 128).T)


def _prep_shared(inp):
    f = lambda a: np.ascontiguousarray(np.asarray(a, dtype=np.float32))
    b_in = f(inp["b_in"])[0]
    d = dict(
        w_in=f(inp["w_in"])[0],
        b_fm=np.ascontiguousarray(np.stack([b_in[s:s + 128] for s in FM_STARTS], axis=1)),
        bg_fox=np.ascontiguousarray(b_in[1536:1544][:, None]),
        bg_i=np.ascontiguousarray(b_in[3080:3084][:, None]),
        bg_f=np.ascontiguousarray(b_in[3084:3088][:, None]),
        b_row=np.ascontiguousarray(b_in[None, :]),
        g_pre=_fm(f(inp["norm_mix_pre"])[0]),
        gq2=np.ascontiguousarray(np.concatenate([f(inp["fox_q_norm"])[0]] * 2)[:, None]),
        gk2=np.ascontiguousarray(np.concatenate([f(inp["fox_k_norm"])[0]] * 2)[:, None]),
        mconv_w=np.ascontiguousarray(f(inp["mlstm_conv_w"])[0].reshape(4, 8, 128).transpose(2, 1, 0)),
        mconv_b=_fm(f(inp["mlstm_conv_b"])[0]),
        mhn=np.ascontiguousarray(f(inp["mlstm_head_norm"])[0].T),
        g_mem=_fm(f(inp["norm_mem"])[0]),
        w_mem_kv=f(inp["w_mem_kv"])[0],
        w_br_a=f(inp["w_br_a"])[0], w_br_b=f(inp["w_br_b"])[0], w_br_m=f(inp["w_br_m"])[0],
        w_out=f(inp["w_out"])[0],
        g_post=_fm(f(inp["norm_mix_post"])[0]),
        g_fpre=_fm(f(inp["norm_ffn_pre"])[0]),
        w_up=f(inp["w_up"])[0],
        fconv_w=np.ascontiguousarray(f(inp["ffn_conv_w"])[0].reshape(3, 44, 128).transpose(2, 1, 0)),
        fconv_b=_fm(f(inp["ffn_conv_b"])[0]),
        w_down=f(inp["w_down"])[0],
        g_fpost=_fm(f(inp["norm_ffn_post"])[0]),
    )
    d.update(_consts())
    return d


def _prep_core(inp, b):
    f = lambda a: np.asarray(a, dtype=np.float32)
    c = np.ascontiguousarray
    return dict(
        xT=c(np.concatenate([f(inp["x_prompt"])[b].T, f(inp["x_sample"])[b].T], axis=1)),
        ckT=c(f(inp["cache_fox_k"])[0, b].reshape(PAST, 512).T),
        cv=c(f(inp["cache_fox_v"])[0, b].reshape(32, 128, 8, 64).transpose(2, 1, 0, 3)),
        clogfT=c(f(inp["cache_fox_logf"])[0, b].T),
        sC=c(f(inp["state_mlstm_c"])[0, b].transpose(2, 0, 1)),
        sn=c(f(inp["state_mlstm_n"])[0, b].T),
        sm=c(f(inp["state_mlstm_m"])[0, b][:, None]),
        sconv=c(f(inp["state_mlstm_conv"])[0, b].reshape(3, 8, 128).transpose(2, 1, 0)),
        cmkT=c(f(inp["cache_mem_k"])[0, b].reshape(256, 512).T),
        cmv=c(f(inp["cache_mem_v"])[0, b].reshape(256, 512)),
        sfconv=c(f(inp["state_ffn_conv"])[0, b].reshape(2, 44, 128).transpose(2, 1, 0)),
        memT=c(f(inp["mem_prompt"])[b].T),
    )


def _assemble(results):
    B = len(results)
    z = lambda *s: np.zeros(s, np.float32)
    y_p, y_s = z(B, TP, 1024), z(B, TS, 1024)
    fk_p, fv_p, fl_p = z(1, B, TP, 8, 64), z(1, B, TP, 8, 64), z(1, B, TP, 8)
    fk_s, fv_s, fl_s = z(1, B, TS, 8, 64), z(1, B, TS, 8, 64), z(1, B, TS, 8)
    c_p, n_p, m_p = z(1, B, 4, 128, 128), z(1, B, 4, 128), z(1, B, 4)
    c_s, n_s, m_s = z(1, B, 4, 128, 128), z(1, B, 4, 128), z(1, B, 4)
    cv_p, cv_s = z(1, B, 3, 1024), z(1, B, 3, 1024)
    fc_p, fc_s = z(1, B, 2, 5632), z(1, B, 2, 5632)
    mk_p, mv_p = z(1, B, 256, 4, 128), z(1, B, 256, 4, 128)
    for b, r in enumerate(results):
        yT = r["yT"]
        y_p[b] = yT[:, :TP].T
        y_s[b] = yT[:, TP:].T
        kT = r["fox_kT"]
        fk_p[0, b] = kT[:, :TP].T.reshape(TP, 8, 64)
        fk_s[0, b] = kT[:, TP:].T.reshape(TS, 8, 64)
        fv = r["fox_v"]
        fv_p[0, b] = fv[:TP].reshape(TP, 8, 64)
        fv_s[0, b] = fv[TP:].reshape(TS, 8, 64)
        lf = r["fox_logfT"]
        fl_p[0, b] = lf[:, :TP].T
        fl_s[0, b] = lf[:, TP:].T
        for (si, cc, nn, mm, cvv, fcc) in ((0, c_p, n_p, m_p, cv_p, fc_p), (1, c_s, n_s, m_s, cv_s, fc_s)):
            cc[0, b] = r["ml_cT"][si].transpose(1, 2, 0)
            nn[0, b] = r["ml_n"][si].T
            mm[0, b] = r["ml_m"][si][:, 0]
            cvv[0, b] = r["ml_convT"][si].transpose(2, 1, 0).reshape(3, 1024)
            fcc[0, b] = r["ffn_convT"][si].transpose(2, 1, 0).reshape(2, 5632)
        mk_p[0, b] = r["mem_kT"].T.reshape(256, 4, 128)
        mv_p[0, b] = r["mem_v"].reshape(256, 4, 128)
    return (y_p, y_s, fk_p, fv_p, fl_p, c_p, n_p, m_p, cv_p, fc_p, mk_p, mv_p,
            fk_s, fv_s, fl_s, c_s, n_s, m_s, cv_s, fc_s)


def kernel(**inputs):
    nc = _get_nc()[0]
    shared = _prep_shared(inputs)
    in_maps = []
    for b in range(8):
        d = dict(shared)
        d.update(_prep_core(inputs, b))
        in_maps.append(d)
    res = run_bass_kernel_spmd(nc, in_maps, core_ids=list(range(8)))
    return _assemble(res.results)
```

```python
import numpy as np
from contextlib import ExitStack
import concourse.bass as bass
import concourse.mybir as mybir

F32 = mybir.dt.float32
BF16 = mybir.dt.bfloat16
AF = mybir.ActivationFunctionType
ALU = mybir.AluOpType
AX = mybir.AxisListType


def _esize(dt):
    n = str(dt)
    if "float32" in n or "int32" in n:
        return 4
    if "bfloat16" in n or "float16" in n or "int16" in n:
        return 2
    if "int8" in n or "float8" in n:
        return 1
    if "64" in n:
        return 8
    raise ValueError(n)


def _boxes(ap):
    t = ap.tensor
    name = t.name
    es = _esize(ap.dtype)
    dims = [(int(s), int(n)) for s, n in ap.ap]
    off = int(ap.offset)
    if "DRAM" in str(ap.space).upper() or "HBM" in str(ap.space).upper():
        ext = sum((n - 1) * abs(s) for s, n in dims)
        return name, [(0, 1, off * es, (off + ext + 1) * es)]
    tes = _esize(t.dtype)
    rowbytes = int(np.prod([int(x) for x in t.shape[1:]])) * tes
    row = rowbytes // es
    p0 = off // row
    f0 = off % row
    pext = 0
    fd = []
    for s, n in dims:
        if n == 1:
            continue
        if s != 0 and s % row == 0:
            pext += (n - 1) * (s // row)
        elif s != 0:
            fd.append((abs(s), n))
    fd.sort(reverse=True)
    p1 = p0 + pext + 1
    if len(fd) >= 2:
        inner = sum((n - 1) * s for s, n in fd[1:]) + 1
        s0, n0 = fd[0]
        if s0 >= inner and n0 <= 64:
            return name, [(p0, p1, (f0 + i * s0) * es, (f0 + i * s0 + inner) * es) for i in range(n0)]
    ext = sum((n - 1) * s for s, n in fd) + 1
    return name, [(p0, p1, f0 * es, (f0 + ext) * es)]


def _ov(a, b):
    for x in a:
        for y in b:
            if x[0] < y[1] and y[0] < x[1] and x[2] < y[3] and y[2] < x[3]:
                return True
    return False


def _contained(a, b):
    for x in a:
        ok = False
        for y in b:
            if y[0] <= x[0] and x[1] <= y[1] and y[2] <= x[2] and x[3] <= y[3]:
                ok = True
                break
        if not ok:
            return False
    return True


class _Stop(Exception):
    pass


class Prog:
    ENGS = ("pe", "act", "dve", "pool", "sp")

    def __init__(self, nc):
        self.nc = nc
        self.ops = []
        self.hist = {}

    def add(self, eng, fn, reads=(), writes=(), dma=False):
        idx = len(self.ops)
        deps = set()
        rb = [_boxes(a) for a in reads]
        wb = [_boxes(a) for a in writes]
        for name, bx in rb:
            for e in self.hist.get(name, ()):
                if e[2] and _ov(e[0], bx):
                    deps.add(e[1])
        for name, bx in wb:
            for e in self.hist.get(name, ()):
                if _ov(e[0], bx):
                    deps.add(e[1])
        for name, bx in wb:
            lst = [e for e in self.hist.get(name, ()) if not _contained(e[0], bx)]
            lst.append((bx, idx, True, eng, dma))
            self.hist[name] = lst
        for name, bx in rb:
            lst = self.hist.setdefault(name, [])
            rep = False
            if not dma:
                for i, e in enumerate(lst):
                    if (not e[2]) and e[3] == eng and (not e[4]) and e[0] == bx:
                        lst[i] = (bx, idx, False, eng, dma)
                        rep = True
                        break
            if not rep:
                lst.append((bx, idx, False, eng, dma))
        deps.discard(idx)
        self.ops.append((eng, fn, deps, dma))
        return idx

    def mark(self, name):
        if not hasattr(self, 'marks'):
            self.marks = []
        self.marks.append((name, sum(1 for o in self.ops if o[0] == 'pe')))
        if getattr(self, 'stop_at', None) == name:
            self.emit()
            raise _Stop()

    def dma(self, q, out, in_, **kw):
        kw.setdefault('allow_slow_non_contiguous', True)
        return self.add(q, lambda e: e.dma_start(out=out, in_=in_, **kw), [in_], [out], dma=True)

    def mm(self, out, lhsT, rhs, start=True, stop=True, acc=False):
        rd = [lhsT, rhs] + ([out] if not start else [])
        return self.add("pe", lambda e: e.matmul(out, lhsT, rhs, start=start, stop=stop), rd, [out])

    def tr(self, out, in_, ident):
        return self.add("pe", lambda e: e.transpose(out, in_, ident), [in_, ident], [out])

    def act(self, out, in_, func, bias=None, scale=None, accum_out=None, eng="act"):
        rd = [in_]
        kw = {}
        if bias is not None:
            kw["bias"] = bias
            if not isinstance(bias, (int, float)):
                rd.append(bias)
        if scale is not None:
            kw["scale"] = scale
            if not isinstance(scale, (int, float)):
                rd.append(scale)
        wr = [out]
        if accum_out is not None:
            kw["accum_out"] = accum_out
            wr.append(accum_out)
        return self.add("act", lambda e: e.activation(out, in_, func, **kw), rd, wr)

    def tt(self, out, in0, in1, op, eng="dve"):
        return self.add(eng, lambda e: e.tensor_tensor(out, in0, in1, op), [in0, in1], [out])

    def ts(self, out, in0, s1, op0, s2=None, op1=None, eng="dve"):
        rd = [in0] + [s for s in (s1, s2) if s is not None and not isinstance(s, (int, float))]
        if op1 is None:
            return self.add(eng, lambda e: e.tensor_scalar(out, in0, s1, None, op0), rd, [out])
        return self.add(eng, lambda e: e.tensor_scalar(out, in0, s1, s2, op0, op1), rd, [out])

    def stt(self, out, in0, scalar, in1, op0, op1, eng="dve"):
        rd = [in0, in1] + ([] if isinstance(scalar, (int, float)) else [scalar])
        return self.add(eng, lambda e: e.scalar_tensor_tensor(out, in0, scalar, in1, op0, op1), rd, [out])

    def copy(self, out, in_, eng="dve"):
        return self.add(eng, lambda e: e.tensor_copy(out, in_), [in_], [out])

    def memset(self, out, val, eng="dve"):
        return self.add(eng, lambda e: e.memset(out, val), [], [out])

    def recip(self, out, in_):
        return self.add("dve", lambda e: e.reciprocal(out, in_), [in_], [out])

    def scan(self, out, d0, d1, init, op0, op1):
        rd = [d0, d1] + ([] if isinstance(init, (int, float)) else [init])
        return self.add("dve", lambda e: e.tensor_tensor_scan(out, d0, d1, init, op0, op1), rd, [out])

    def emit(self, R=20000, K=8, Kq=None):
        Kq = dict(Kq or {})
        KK = {e: Kq.get(e, K) for e in self.ENGS}
        nc = self.nc
        ops = self.ops
        needed = set()
        for eng, fn, deps, dma in ops:
            for d in deps:
                de, _, _, ddma = ops[d]
                if ddma:
                    continue
                if de == "pe" and eng == "pe" and not dma:
                    continue
                needed.add(d)
        sigidx = {}
        cnt = {e: 0 for e in self.ENGS}
        for i, (eng, fn, deps, dma) in enumerate(ops):
            if (not dma) and i in needed:
                sigidx[i] = cnt[eng]
                cnt[eng] += 1
        dmaidx = {}
        dcnt = {e: 0 for e in self.ENGS}
        for i, (eng, fn, deps, dma) in enumerate(ops):
            if dma:
                dmaidx[i] = dcnt[eng]
                dcnt[eng] += 1
        with ExitStack() as st:
            csem = {e: [st.enter_context(nc.semaphore(f"c_{e}_{j}")) for j in range(max(1, (cnt[e] + R - 1) // R))]
                    for e in self.ENGS}
            dsem = {e: [st.enter_context(nc.semaphore(f"d_{e}_{j}")) for j in range(min(KK[e], dcnt[e]))]
                    for e in self.ENGS}
            block = st.enter_context(nc.Block())

            def run(me, e):
                waited_c = {x: -1 for x in self.ENGS}
                waited_d = {}

                def wait_dma(d):
                    q = ops[d][0]
                    n = dmaidx[d]
                    K = KK[q]
                    sem = dsem[q][n % K]
                    val = 16 * (n // K + 1)
                    key = (q, n % K)
                    if waited_d.get(key, 0) >= val:
                        return
                    e.wait_ge(sem, val)
                    waited_d[key] = val

                for i, (eng, fn, deps, dma) in enumerate(ops):
                    if eng != me:
                        continue
                    for d in sorted(deps):
                        de, _, _, ddma = ops[d]
                        if ddma:
                            wait_dma(d)
                        else:
                            if de == "pe" and me == "pe" and not dma:
                                continue
                            g = sigidx[d]
                            if waited_c[de] >= g:
                                continue
                            e.wait_ge(csem[de][g // R], g % R + 1)
                            waited_c[de] = g
                    if dma:
                        n = dmaidx[i]
                        K = KK[me]
                        sem = dsem[me][n % K]
                        if n >= K:
                            key = (me, n % K)
                            val = 16 * (n // K)
                            if waited_d.get(key, 0) < val:
                                e.wait_ge(sem, val)
                                waited_d[key] = val
                        ins = fn(e)
                        ins.then_inc(sem, 16)
                    else:
                        ins = fn(e)
                        if i in sigidx:
                            g = sigidx[i]
                            ins.then_inc(csem[me][g // R], 1)
                K = KK[me]
                for j in range(min(K, dcnt[me])):
                    uses = (dcnt[me] - 1 - j) // K + 1
                    val = 16 * uses
                    if waited_d.get((me, j), 0) < val:
                        e.wait_ge(dsem[me][j], val)

            @block.sync
            def _(e):
                run("sp", e)

            @block.scalar
            def _(e):
                run("act", e)

            @block.vector
            def _(e):
                run("dve", e)

            @block.gpsimd
            def _(e):
                run("pool", e)

            @block.tensor
            def _(e):
                run("pe", e)

from concourse.bass_utils import run_bass_kernel_spmd

NT = 2112
TP = 2048
TS = 64
PAST = 4096
EPS = 1e-6
GROUPS = [(0, 512), (512, 512), (1024, 512), (1536, 512), (2048, 64)]
TILES = [(i * 128, 128) for i in range(16)] + [(2048, 64)]
QSC = 128.0 ** -0.5

FM_STARTS = ([0, 128, 256, 384] + [512, 640, 768, 896] + [1544 + 128 * i for i in range(8)]
             + [3088 + 128 * i for i in range(4)]
             + [3600 + 128 * i for i in range(4)] + [4112 + 128 * i for i in range(24)])
FM_COL = {s: i for i, s in enumerate(FM_STARTS)}

IN_SHAPES = dict(
    xT=[1024, NT], w_in=[1024, 7184], b_fm=[128, 48], bg_fox=[8, 1], bg_i=[4, 1], bg_f=[4, 1],
    b_row=[1, 7184], g_pre=[128, 8], gq2=[128, 1], gk2=[128, 1], mconv_w=[128, 8, 4], mconv_b=[128, 8],
    mhn=[128, 4], g_mem=[128, 8], w_mem_kv=[1024, 1024], w_br_a=[512, 1024], w_br_b=[512, 1024],
    w_br_m=[512, 1024], w_out=[1024, 1024], g_post=[128, 8], g_fpre=[128, 8], w_up=[1024, 5632],
    fconv_w=[128, 44, 3], fconv_b=[128, 44], w_down=[2816, 1024], g_fpost=[128, 8],
    ckT=[512, PAST], cv=[8, 128, 32, 64], clogfT=[8, PAST], sC=[128, 4, 128], sn=[128, 4], sm=[4, 1],
    sconv=[128, 8, 3], cmkT=[512, 256], cmv=[256, 512], sfconv=[128, 44, 2], memT=[1024, 256],
    ident=[128, 128], trimask=[128, 128], negmask=[128, 128], bdones=[128, 128], sel4=[4, 4, 128],
)
OUT_SHAPES = dict(
    yT=[1024, NT], fox_kT=[512, NT], fox_v=[NT, 512], fox_logfT=[8, NT],
    ml_cT=[2, 128, 4, 128], ml_n=[2, 128, 4], ml_m=[2, 4, 1], ml_convT=[2, 128, 8, 3],
    ffn_convT=[2, 128, 44, 2], mem_kT=[512, 256], mem_v=[256, 512],
)


class Arena:
    def __init__(self, A, cap):
        self.A = A
        self.cap = cap
        self.top = 0
        self.hw = 0

    def _alloc(self, words):
        off = self.top
        self.top += words
        assert self.top <= self.cap, f"arena overflow {self.top} > {self.cap}"
        self.hw = max(self.hw, self.top)
        return off

    def f32(self, *shape):
        n = int(np.prod(shape[1:]))
        off = self._alloc(n)
        return self._view(self.A[:, off:off + n], shape)

    def bf(self, *shape):
        n = int(np.prod(shape[1:]))
        words = (n + 1) // 2
        off = self._alloc(words)
        return self._view(self.A[:, off:off + words].bitcast(BF16)[:, 0:n], shape)

    @staticmethod
    def _view(v, shape):
        p = shape[0]
        if len(shape) == 3:
            v = v.rearrange("p (a b) -> p a b", b=shape[2])
        elif len(shape) == 4:
            v = v.rearrange("p (a b c) -> p a b c", b=shape[2], c=shape[3])
        if p < 128:
            v = v[0:p]
        return v

    def f32_at(self, off, *shape):
        n = int(np.prod(shape[1:]))
        return self._view(self.A[:, off:off + n], shape)

    def bf_at(self, off, *shape):
        n = int(np.prod(shape[1:]))
        words = (n + 1) // 2
        return self._view(self.A[:, off:off + words].bitcast(BF16)[:, 0:n], shape)

    def mark(self):
        return self.top

    def release(self, m):
        self.top = m


def build(debug=(), stop_at=None, salt=None):
    nc = bass.Bass("TRN2", target_bir_lowering=False)
    I = {k: nc.dram_tensor(k, list(v), F32, kind="ExternalInput").ap() for k, v in IN_SHAPES.items()}
    O = {k: nc.dram_tensor(k, list(v), F32, kind="ExternalOutput").ap() for k, v in OUT_SHAPES.items()}
    x1scr = nc.dram_tensor("x1scr", [1024, NT], F32, kind="Internal").ap()
    DBG = {}
    P = Prog(nc)
    P.stop_at = stop_at
    CAP = 53000
    try:
      with nc.sbuf_tensor("A", [128, CAP], F32) as A_, nc.psum_tensor("PS", [128, 8, 512], F32) as PS:
        ar = Arena(A_, CAP)
        hw_box = [ar]
        bank_ctr = [0]

        def nb():
            b = bank_ctr[0] % 8
            bank_ctr[0] += 1
            return PS[:, b, :]

        def dbg(name, ap):
            if name in debug:
                shp = [int(s) for s in ap.shape]
                d = nc.dram_tensor("dbg_" + name, shp, F32, kind="ExternalOutput").ap()
                DBG[name] = shp
                if ap.dtype == F32 and "PSUM" not in str(ap.space).upper():
                    P.dma("sp", d, ap)
                else:
                    m = ar.mark()
                    t = ar.f32(*([128] + shp[1:]))[0:shp[0]]
                    P.copy(t, ap)
                    P.dma("sp", d, t)
                    ar.release(m)

        wv = lambda name: I[name].rearrange("(c p) n -> p c n", p=128)
        r3 = lambda ap, b: ap.rearrange("p (a b) -> p a b", b=b)

        ident_f = ar.f32(128, 128)
        ident_b = ar.bf(128, 128)
        trimask = ar.bf(128, 128)
        negmask = ar.bf(128, 128)
        bdones = ar.bf(128, 128)
        ones_b = ar.bf(128, 128)
        ones_f = ar.f32(128, 128)
        sel4 = ar.f32(4, 4, 128)
        b_fm = ar.f32(128, 48)
        g_pre = ar.f32(128, 8)
        g_post = ar.f32(128, 8)
        g_fpre = ar.f32(128, 8)
        g_fpost = ar.f32(128, 8)
        g_mem = ar.f32(128, 8)
        gq2 = ar.f32(128, 1)
        gk2 = ar.f32(128, 1)
        mhn = ar.f32(128, 4)
        mconv_w = ar.f32(128, 8, 4)
        mconv_b = ar.f32(128, 8)
        fconv_w = ar.f32(128, 44, 3)
        fconv_b = ar.f32(128, 44)
        def load_consts():
            P.dma("sp", ident_f, I["ident"])
            P.dma("pool", ident_b, I["ident"])
            P.dma("pool", trimask, I["trimask"])
            P.dma("pool", negmask, I["negmask"])
            P.dma("pool", bdones, I["bdones"])
            P.dma("sp", sel4, I["sel4"])
            for t, n in ((b_fm, "b_fm"), (g_pre, "g_pre"), (g_post, "g_post"), (g_fpre, "g_fpre"), (g_fpost, "g_fpost"),
                         (g_mem, "g_mem"), (gq2, "gq2"), (gk2, "gk2"), (mhn, "mhn"), (mconv_w, "mconv_w"),
                         (mconv_b, "mconv_b"), (fconv_w, "fconv_w"), (fconv_b, "fconv_b")):
                P.dma("sp", t, I[n])
            P.ts(gq8, gq2, 0.125, ALU.mult)
        P.memset(ones_b, 1.0)
        P.memset(ones_f, 1.0)
        eps_col = ar.f32(128, 1)
        P.memset(eps_col, EPS)
        one_col = ar.f32(128, 1)
        P.memset(one_col, 1.0)
        gq8 = ar.f32(128, 1)
        bcol = lambda start: b_fm[:, FM_COL[start]:FM_COL[start] + 1]

        def rms_bcast(src, C, n, D, rb, sq=None):
            m = ar.mark()
            if sq is None:
                sq = ar.bf(128, C, n)
            P.act(sq, src, AF.Square)
            ps = nb()
            for c in range(C):
                P.mm(ps[:, 0:n], ones_b, sq[:, c, :], start=(c == 0), stop=(c == C - 1))
            P.act(rb, ps[:, 0:n], AF.Ln, bias=eps_col, scale=1.0 / D)
            P.act(rb, rb, AF.Exp, scale=-0.5)
            ar.release(m)

        xn_off = ar.mark()
        xnT = ar.bf(128, 8, NT)
        m_after_xn = ar.mark()
        aT_off = ar.mark()
        aT = ar.bf(128, 4, NT)
        chi = ar.bf(8, NT)
        clo = ar.bf(8, NT)
        rdl = ar.f32(4, NT)
        wg_tok = ar.f32(128, 17, 4)
        dec_b = ar.f32(128, 4, 17)

        xT3 = I["xT"].rearrange("(c p) n -> p c n", p=128)
        m = ar.mark()
        xs2 = [ar.f32(128, 8, 512), ar.f32(128, 8, 512)]
        rb2 = [ar.f32(128, 512), ar.f32(128, 512)]
        sq2 = [ar.bf(128, 8, 512), ar.bf(128, 8, 512)]
        for gi, (t0, n) in enumerate(GROUPS):
            xs = xs2[gi % 2][:, :, 0:n]
            P.dma("sp", xs, xT3[:, :, t0:t0 + n])
            if gi == 0:
                load_consts()
            rb = rb2[gi % 2][:, 0:n]
            rms_bcast(xs, 8, n, 1024.0, rb, sq=sq2[gi % 2][:, :, 0:n])
            for c in range(8):
                P.stt(xnT[:, c, t0:t0 + n], xs[:, c, :], g_pre[:, c:c + 1], rb, ALU.mult, ALU.mult)
        ar.release(m)
        dbg("xnT", xnT[:, 0, :])

        P.mark("s0_norm")
        m1a = ar.mark()
        wgf = ar.bf(128, 8, 8)
        wgi = ar.bf(128, 8, 4)
        wgm = ar.bf(128, 8, 4)
        P.dma("pool", wgf, wv("w_in")[:, :, 1536:1544])
        P.dma("pool", wgi, wv("w_in")[:, :, 3080:3084])
        P.dma("pool", wgm, wv("w_in")[:, :, 3084:3088])
        bgf = ar.f32(8, 1)
        bgi = ar.f32(4, 1)
        bgm = ar.f32(4, 1)
        P.dma("sp", bgf, I["bg_fox"])
        P.dma("sp", bgi, I["bg_i"])
        P.dma("sp", bgm, I["bg_f"])
        zrow = ar.f32(8, NT)
        P.memset(zrow, 0.0)
        flog = ar.f32(8, NT)
        cfox = ar.f32(8, NT)
        gi_r = ar.f32(4, NT)
        mlf = ar.f32(4, NT)
        A_r = ar.f32(4, NT)
        G_r = ar.f32(4, NT)
        wgT = ar.f32(4, NT)
        for (wt, bc, dst, rows, ls) in ((wgf, bgf, flog, 8, True), (wgi, bgi, gi_r, 4, False), (wgm, bgm, mlf, 4, True)):
            nbias = ar.f32(rows, 1)
            P.ts(nbias, bc, -1.0, ALU.mult)
            for (t0, n) in GROUPS:
                ps = nb()
                for c in range(8):
                    P.mm(ps[0:rows, 0:n], wt[:, c, :], xnT[:, c, t0:t0 + n], start=(c == 0), stop=(c == 7))
                if ls:
                    m = ar.mark()
                    e = ar.f32(rows, n)
                    P.act(e, ps[0:rows, 0:n], AF.Exp, bias=nbias, scale=-1.0)
                    P.act(e, e, AF.Ln, bias=one_col[0:rows], scale=1.0)
                    P.ts(dst[:, t0:t0 + n], e, -1.0, ALU.mult)
                    ar.release(m)
                else:
                    P.act(dst[:, t0:t0 + n], ps[0:rows, 0:n], AF.Identity, bias=bc)
        P.dma("sp", O["fox_logfT"], flog)
        P.scan(cfox[:, 0:TP], flog[:, 0:TP], zrow[:, 0:TP], 0.0, ALU.add, ALU.add)
        P.scan(cfox[:, TP:NT], flog[:, TP:NT], zrow[:, 0:TS], 0.0, ALU.add, ALU.add)
        P.copy(chi, cfox)
        P.tt(clo, cfox, chi, ALU.subtract)
        sm0 = ar.f32(4, 1)
        P.dma("sp", sm0, I["sm"])
        zr4 = zrow[0:4]
        Bc = ar.f32(4, NT)
        P.scan(Bc[:, 0:TP], mlf[:, 0:TP], zr4[:, 0:TP], 0.0, ALU.add, ALU.add)
        P.scan(Bc[:, TP:NT], mlf[:, TP:NT], zr4[:, 0:TS], 0.0, ALU.add, ALU.add)
        P.tt(A_r, gi_r, Bc, ALU.subtract)
        P.scan(G_r[:, 0:TP], A_r[:, 0:TP], A_r[:, 0:TP], 0.0, ALU.max, ALU.max)
        P.scan(G_r[:, TP:NT], A_r[:, TP:NT], A_r[:, TP:NT], sm0, ALU.max, ALU.max)
        gend = ar.f32(4, 17)
        mprev = ar.f32(4, 17)
        P.copy(gend[:, 0:16], G_r[:, 127:TP:128])
        P.copy(gend[:, 16:17], G_r[:, NT - 1:NT])
        P.memset(mprev[:, 0:1], 0.0)
        P.copy(mprev[:, 1:16], gend[:, 0:15])
        P.copy(mprev[:, 16:17], sm0)
        dec = ar.f32(4, 17)
        P.tt(dec, mprev, gend, ALU.subtract)
        P.act(dec, dec, AF.Exp)
        gend_b = gend[:, 0:16].unsqueeze(2).broadcast_to([4, 16, 128])
        P.tt(r3(wgT[:, 0:TP], 128), r3(A_r[:, 0:TP], 128), gend_b, ALU.subtract)
        P.ts(wgT[:, TP:NT], A_r[:, TP:NT], gend[:, 16:17], ALU.subtract)
        P.act(wgT, wgT, AF.Exp)
        P.tt(r3(rdl[:, 0:TP], 128), r3(Bc[:, 0:TP], 128), gend_b, ALU.add)
        P.ts(rdl[:, TP:NT], Bc[:, TP:NT], gend[:, 16:17], ALU.add)
        P.ts(rdl, rdl, -1.0, ALU.mult)
        mTo = ar.f32(4, 2)
        P.tt(mTo[:, 0:1], G_r[:, TP - 1:TP], Bc[:, TP - 1:TP], ALU.add)
        P.tt(mTo[:, 1:2], G_r[:, NT - 1:NT], Bc[:, NT - 1:NT], ALU.add)
        P.dma("sp", O["ml_m"][0], mTo[:, 0:1])
        P.dma("sp", O["ml_m"][1], mTo[:, 1:2])
        for ti, (t0, L) in enumerate(TILES):
            ps = nb()
            P.tr(ps[0:L, 0:4], wgT[:, t0:t0 + L], ident_f[0:4, 0:4])
            P.copy(wg_tok[0:L, ti, :], ps[0:L, 0:4])
        for h in range(4):
            ps = nb()
            P.mm(ps[:, 0:17], sel4[:, h, :], dec)
            P.copy(dec_b[:, h, :], ps[:, 0:17])
        dbg("cfox", cfox)
        dbg("G_r", G_r)
        dbg("wgT", wgT)
        dbg("rdl", rdl)
        ar.release(m1a)
        s1 = ar.mark()

        P.mark("s1a_gates")
        cchi = ar.bf(8, PAST)
        cclo = ar.bf(8, PAST)
        m_s = ar.mark()
        ccache = ar.f32(8, PAST)
        lcache = ar.f32(8, PAST)
        zc = ar.f32(8, PAST)
        P.memset(zc, 0.0)
        P.dma("sp", lcache, I["clogfT"])
        P.scan(ccache, lcache, zc, 0.0, ALU.add, ALU.add)
        ctot = ar.f32(8, 1)
        P.copy(ctot, ccache[:, PAST - 1:PAST])
        P.ts(ccache, ccache, ctot, ALU.subtract)
        P.copy(cchi, ccache)
        P.tt(cclo, ccache, cchi, ALU.subtract)
        ar.release(m_s)
        qT = ar.bf(128, 4, NT)
        kT = ar.bf(128, 4, NT)
        V = ar.bf(128, 17, 512)
        m_w = ar.mark()
        wf = ar.bf(128, 8, 1536)
        P.dma("pool", wf, wv("w_in")[:, :, 0:1536])
        bvrow = ar.f32(128, 512)
        P.dma("sp", bvrow, I["b_row"][:, 1024:1536].partition_broadcast(128))
        ptmp = [(ar.f32(128, 512), ar.bf(128, 512), ar.f32(128, 512), ar.f32(128, 512)) for _ in range(3)]
        its = [(t0, n, which, ch) for (t0, n) in GROUPS for which in range(2) for ch in range(4)]

        def fp_A(it):
            t0, n, which, ch = it
            col0 = which * 512 + ch * 128
            ps = nb()
            for c in range(8):
                P.mm(ps[:, 0:n], wf[:, c, col0:col0 + 128], xnT[:, c, t0:t0 + n], start=(c == 0), stop=(c == 7))
            return ps

        def fp_B(i, it, ps):
            t0, n, which, ch = it
            col0 = which * 512 + ch * 128
            z_, sq_, r_, kf_ = ptmp[i % 3]
            z = z_[:, 0:n]
            sq = sq_[:, 0:n]
            P.act(z, ps[:, 0:n], AF.Identity, bias=bcol(col0))
            P.act(sq, ps[:, 0:n], AF.Square, bias=bcol(col0))
            ps2 = nb()
            P.mm(ps2[:, 0:n], bdones, sq)
            r = r_[:, 0:n]
            P.act(r, ps2[:, 0:n], AF.Ln, bias=eps_col, scale=1.0 / 64)
            P.act(r, r, AF.Exp, scale=-0.5)
            if which == 0:
                P.stt(qT[:, ch, t0:t0 + n], z, gq8, r, ALU.mult, ALU.mult)
            else:
                kf = kf_[:, 0:n]
                P.stt(kf, z, gk2, r, ALU.mult, ALU.mult)
                P.copy(kT[:, ch, t0:t0 + n], kf)
                P.dma("sp", O["fox_kT"][ch * 128:(ch + 1) * 128, t0:t0 + n], kf)

        vtmp = [ar.f32(128, 512), ar.f32(128, 512)]

        def fp_V(ti):
            t0, L = TILES[ti]
            ps = nb()
            for c in range(8):
                P.mm(ps[0:L, :], xnT[:, c, t0:t0 + L], wf[:, c, 1024:1536], start=(c == 0), stop=(c == 7))
            vf = vtmp[ti % 2]
            P.tt(vf[0:L], ps[0:L, :], bvrow[0:L], ALU.add)
            P.copy(V[0:L, ti, :], vf[0:L])
            P.dma("sp", O["fox_v"][t0:t0 + L, :], vf[0:L])

        psq = [fp_A(its[0])]
        vnext = 0
        for i, it in enumerate(its):
            if i + 1 < len(its):
                psq.append(fp_A(its[i + 1]))
            if i % 2 == 1 and vnext < 17:
                fp_V(vnext)
                vnext += 1
            fp_B(i, it, psq[i])
        while vnext < 17:
            fp_V(vnext)
            vnext += 1
        ar.release(m_w)
        dbg("qT", qT[:, 0, :])
        dbg("kT", kT[:, 0, :])

        P.mark("s1b_foxproj")
        pbufs = [ar.bf(128, 2, 512) for _ in range(4)]
        pctr = [0]
        fctr = [0, 0]
        rbuf = ar.f32(128, 512)
        KMAX = PAST + TS
        qas = [ar.bf(128, NT), ar.bf(128, NT)]
        kas_p = [ar.bf(128, TP), ar.bf(128, TP)]
        kas_s = [ar.bf(128, KMAX), ar.bf(128, KMAX)]
        vexts_p = [ar.bf(128, 16, 128), ar.bf(128, 16, 128)]
        vexts_s = [ar.bf(128, 33, 128), ar.bf(128, 33, 128)]
        for i in range(2):
            P.memset(qas[i][64:68, :], -1.0)
            P.memset(kas_p[i][64:68, :], 1.0)
            P.memset(kas_s[i][64:68, :], 1.0)
        for ve in (vexts_p, vexts_s):
            P.memset(ve[0][:, :, 64:128], 1.0)
            P.memset(ve[1][:, :, 0:64], 1.0)
        ckT_d = I["ckT"]

        def fox_attend(h, qa, qc0, qn, keys, ka, vext):
            ch, half = h // 2, h % 2
            pb = half * 64
            po = (1 - half) * 64
            acc = PS[:, 6 + (fctr[0] % 2), :]
            fctr[0] += 1
            full = [k for k in keys if k[3] is None]
            diag = [k for k in keys if k[3] is not None]
            per = 2 if qn > 64 else 8
            units = [full[i:i + per] for i in range(0, len(full), per)] + [[k] for k in diag]
            nk = len(keys)
            done = [0]

            def emit_front(unit):
                base = 2 * (fctr[1] % 3)
                fctr[1] += 1
                pt = pbufs[pctr[0] % len(pbufs)]
                pctr[0] += 1
                pvs = []
                if len(unit) == 1:
                    kc0, vti, L, doff = unit[0]
                    q_lo = 0 if doff is None else doff
                    sps = PS[:, base, :]
                    P.mm(sps[0:L, q_lo:qn], ka[0:68, kc0:kc0 + L], qa[0:68, qc0 + q_lo:qc0 + qn], start=True, stop=(doff is None))
                    if doff is not None:
                        dq = min(L, qn - q_lo)
                        P.mm(sps[0:L, q_lo:q_lo + dq], ident_b[0:L, 0:L], negmask[0:L, 0:dq], start=False, stop=True)
                    P.act(pt[0:L, 0, q_lo:qn], sps[0:L, q_lo:qn], AF.Exp)
                    pvs.append((acc[:, q_lo:qn], vext[0:L, vti, :], pt[0:L, 0, q_lo:qn]))
                elif qn > 64:
                    for j, (kc0, vti, L, doff) in enumerate(unit):
                        P.mm(PS[:, base + j, 0:qn], ka[0:68, kc0:kc0 + L], qa[0:68, qc0:qc0 + qn])
                        pvs.append((acc[:, 0:qn], vext[0:L, vti, :], pt[:, j, 0:qn]))
                    P.act(pt[:, 0:2, 0:qn], PS[:, base:base + 2, 0:qn], AF.Exp)
                else:
                    m = len(unit)
                    for j, (kc0, vti, L, doff) in enumerate(unit):
                        P.mm(PS[:, base, j * qn:(j + 1) * qn], ka[0:68, kc0:kc0 + L], qa[0:68, qc0:qc0 + qn])
                        pvs.append((acc[:, 0:qn], vext[0:L, vti, :], pt[:, 0, j * qn:(j + 1) * qn]))
                    P.act(pt[:, 0, 0:m * qn], PS[:, base, 0:m * qn], AF.Exp)
                return pvs

            def emit_pv(pvs):
                for (o_, l_, r_) in pvs:
                    P.mm(o_, l_, r_, start=(done[0] == 0), stop=(done[0] == nk - 1))
                    done[0] += 1

            LA = 2
            q_ = [emit_front(units[u]) for u in range(min(LA, len(units)))]
            for u in range(len(units)):
                if u + LA < len(units):
                    q_.append(emit_front(units[u + LA]))
                emit_pv(q_[u])
            P.act(rbuf[pb:pb + 64, 0:qn], acc[po:po + 64, 0:qn], AF.Ln)
            P.act(rbuf[pb:pb + 64, 0:qn], rbuf[pb:pb + 64, 0:qn], AF.Exp, scale=-1.0)
            P.tt(aT[pb:pb + 64, ch, qc0:qc0 + qn], acc[pb:pb + 64, 0:qn], rbuf[pb:pb + 64, 0:qn], ALU.mult)

        def sample_loads(h):
            ch, half = h // 2, h % 2
            pb = half * 64
            ka, vext = kas_s[h % 2], vexts_s[h % 2]
            vo = 0 if half == 0 else 64
            P.dma("pool", ka[0:64, 0:PAST], ckT_d[h * 64:(h + 1) * 64, :])
            P.dma("sp", ka[64:65, 0:PAST], cchi[h:h + 1, :])
            P.dma("sp", ka[65:66, 0:PAST], cclo[h:h + 1, :])
            P.dma("sp", ka[0:64, PAST:KMAX], kT[pb:pb + 64, ch, TP:NT])
            P.dma("sp", ka[64:65, PAST:KMAX], chi[h:h + 1, TP:NT])
            P.dma("sp", ka[65:66, PAST:KMAX], clo[h:h + 1, TP:NT])
            P.dma("pool", vext[:, 0:32, vo:vo + 64], I["cv"][h])
            P.copy(vext[0:64, 32, vo:vo + 64], V[0:64, 16, h * 64:h * 64 + 64], eng="pool")

        sample_loads(0)
        keys_s = [(ti * 128, ti, 128, None) for ti in range(32)] + [(PAST, 32, 64, 0)]
        for h in range(8):
            ch, half = h // 2, h % 2
            pb = half * 64
            qa, ka, vext = qas[h % 2], kas_p[h % 2], vexts_p[h % 2]
            vo = 0 if half == 0 else 64
            P.dma("sp", qa[0:64, :], qT[pb:pb + 64, ch, :])
            P.dma("sp", qa[66:67, :], chi[h:h + 1, :])
            P.dma("sp", qa[67:68, :], clo[h:h + 1, :])
            P.dma("sp", ka[0:64, 0:TP], kT[pb:pb + 64, ch, 0:TP])
            P.dma("sp", ka[64:65, 0:TP], chi[h:h + 1, 0:TP])
            P.dma("sp", ka[65:66, 0:TP], clo[h:h + 1, 0:TP])
            P.copy(vext[:, 0:16, vo:vo + 64], V[:, 0:16, h * 64:h * 64 + 64], eng="pool")
            if h + 1 < 8:
                sample_loads(h + 1)
            for gi in range(4):
                keys = []
                for ti in range(gi * 4 + 4):
                    doff = None if ti < gi * 4 else (ti - gi * 4) * 128
                    keys.append((ti * 128, ti, 128, doff))
                fox_attend(h, qa, gi * 512, 512, keys, ka, vext)
            fox_attend(h, qa, TP, TS, keys_s, kas_s[h % 2], vexts_s[h % 2])
            if h == 0:
                dbg("aT0", aT[0:64, 0, 0:TP])
        dbg("aT", aT[:, 0, :])
        P.mark("fox_prompt")
        dbg("aTs", aT[:, 0, TP:NT])
        ar.release(s1)
        bT_off = ar.mark()
        bT = ar.bf(128, 4, NT)
        s1 = ar.mark()

        P.mark("fox_sample")
        wm = ar.bf(128, 8, 2048)
        P.dma("pool", wm[:, :, 0:1536], wv("w_in")[:, :, 1544:3080])
        P.dma("pool", wm[:, :, 1536:2048], wv("w_in")[:, :, 3088:3600])
        bmv = ar.f32(128, 512)
        P.dma("sp", bmv, I["b_row"][:, 2568:3080].partition_broadcast(128))
        CT = ar.f32(128, 4, 129)
        pc = ar.f32(128, 8, 515)
        qks = [ar.bf(128, 8, 512), ar.bf(128, 8, 512)]
        qkc = [qks[0]]
        sigo = ar.f32(128, 4, 512)
        rdb = ar.f32(128, 4, 512)
        caccs = [ar.f32(128, 512), ar.f32(128, 512)]
        ones3 = ones_f.unsqueeze(1).broadcast_to([128, 4, 128])
        msets = []
        for _ in range(2):
            msets.append(dict(vfull=ar.f32(128, 512), vw=ar.bf(128, 4, 129), wgb=ar.bf(128, 4, 128),
                              ktok=ar.bf(128, 4, 128), ET=ar.bf(128, 4, 128), Cq=ar.bf(128, 4, 128),
                              nbm=ar.bf(128, 4, 128), tden=ar.f32(128, 4, 128), hs=ar.f32(128, 4, 128),
                              hsq=ar.bf(128, 4, 128)))
            msets[-1]["rr"] = msets[-1]["tden"]
        B_v = PS[:, 0, :]
        B_tr = PS[:, 1, :].bitcast(BF16)
        B_S = PS[:, 2, :]
        B_k = [PS[:, 3, :], PS[:, 4, :]]
        B_Y = PS[:, 5, :]
        B_D = PS[:, 6, :]
        B_q = PS[:, 7, :]
        P.memset(CT, 0.0)
        P.memset(pc[:, :, 0:3], 0.0)
        wb3 = ar.f32(128, 8)
        for j in range(8):
            P.tt(wb3[:, j:j + 1], mconv_w[:, j, 3:4], bcol(1544 + 128 * j), ALU.mult)

        def ml_step(prev, cur):
            if prev is not None:
                tiP, ttP, LP, loP, SP = prev
                Dv = r3(B_D[:, 0:4 * LP], LP)
                Yv = r3(B_Y[:, 0:4 * LP], LP)
                td = SP["tden"][:, :, 0:LP]
                hs_ = SP["hs"][:, :, 0:LP]
                hq_ = SP["hsq"][:, :, 0:LP]
                rr_ = SP["rr"][:, :, 0:LP]
            if cur is not None:
                ti, tt0, L, lo, S = cur
                wg3 = wg_tok[0:L, ti, :].unsqueeze(2)
            if prev is not None:
                for h in range(4):
                    q_h = qkc[0][:, h, loP:loP + LP]
                    P.mm(B_Y[:, h * LP:(h + 1) * LP], SP["Cq"][:, h, :], q_h, start=True, stop=False)
                    P.mm(B_Y[:, h * LP:(h + 1) * LP], SP["vw"][0:LP, h, 0:128], SP["ET"][0:LP, h, 0:LP], start=False, stop=True)
                for h in range(4):
                    q_h = qkc[0][:, h, loP:loP + LP]
                    P.mm(B_D[:, h * LP:(h + 1) * LP], SP["nbm"][:, h, :], q_h, start=True, stop=False)
                    P.mm(B_D[:, h * LP:(h + 1) * LP], SP["wgb"][0:LP, h, :], SP["ET"][0:LP, h, 0:LP], start=False, stop=True)
            if cur is not None:
                for c in range(8):
                    P.mm(B_v[0:L, :], xnT[:, c, tt0:tt0 + L], wm[:, c, 1024:1536], start=(c == 0), stop=(c == 7))
            if prev is not None:
                P.act(td, Dv, AF.Abs)
                P.tt(td, td, rdb[:, :, loP:loP + LP], ALU.max)
                P.act(td, td, AF.Ln)
                P.act(td, td, AF.Exp, scale=-1.0)
            if cur is not None:
                P.tt(S["vfull"][0:L], B_v[0:L, :], bmv[0:L], ALU.add)
                P.tt(S["vw"][0:L, :, 0:128], r3(S["vfull"][0:L], 128), wg3.broadcast_to([L, 4, 128]), ALU.mult)
                P.copy(S["vw"][0:L, :, 128:129], wg3)
                P.tt(S["wgb"][0:L], ones_f[0:L].unsqueeze(1).broadcast_to([L, 4, 128]), wg3.broadcast_to([L, 4, 128]), ALU.mult)
                for h in range(4):
                    P.tr(B_tr[0:L, h * 128:(h + 1) * 128], qkc[0][:, 4 + h, lo:lo + L], ident_b)
                P.act(S["ktok"][0:L], r3(B_tr[0:L, 0:512], 128), AF.Copy)
                for h in range(4):
                    P.mm(B_S[0:L, h * L:(h + 1) * L], qkc[0][:, 4 + h, lo:lo + L], qkc[0][:, h, lo:lo + L])
            if prev is not None:
                P.tt(hs_, Yv, td, ALU.mult)
                P.tt(hs_, hs_, sigo[:, :, loP:loP + LP], ALU.mult)
                P.act(hq_, hs_, AF.Square)
                for h in range(4):
                    P.mm(B_q[:, h * LP:(h + 1) * LP], ones_b, SP["hsq"][:, h, 0:LP])
                P.act(rr_, r3(B_q[:, 0:4 * LP], LP), AF.Ln, bias=eps_col, scale=1.0 / 128)
                P.act(rr_, rr_, AF.Exp, scale=-0.5)
            if cur is not None:
                P.stt(S["ET"][0:L, :, 0:L], r3(B_S[0:L, 0:4 * L], L), QSC,
                      trimask[0:L, 0:L].unsqueeze(1).broadcast_to([L, 4, L]), ALU.mult, ALU.mult)
                for h in range(4):
                    P.mm(B_k[h // 2][:, (h % 2) * 129:(h % 2) * 129 + 129], S["ktok"][0:L, h, :], S["vw"][0:L, h, :])
                P.tt(CT, CT, dec_b[:, :, ti:ti + 1].broadcast_to([128, 4, 129]), ALU.mult)
                P.act(S["Cq"], CT[:, :, 0:128], AF.Copy, scale=QSC)
                P.stt(S["nbm"], ones3, QSC, CT[:, :, 128:129].broadcast_to([128, 4, 128]), ALU.mult, ALU.mult)
            if prev is not None:
                P.tt(hs_, hs_, rr_, ALU.mult)
                P.tt(bT[:, :, ttP:ttP + LP], hs_, mhn.unsqueeze(2).broadcast_to([128, 4, LP]), ALU.mult)
            if cur is not None:
                P.tt(CT[:, 0:2, :], CT[:, 0:2, :], r3(B_k[0][:, 0:258], 129), ALU.add)
                P.tt(CT[:, 2:4, :], CT[:, 2:4, :], r3(B_k[1][:, 0:258], 129), ALU.add)

        def ml_proj_parts(gi):
            t0, n = GROUPS[gi]
            qk = qks[gi % 2]

            def part(pj):
                if pj == 0 and gi == 4:
                    P.dma("sp", pc[:, :, 0:3], I["sconv"])
                for j in (2 * pj, 2 * pj + 1):
                    ps = nb()
                    for c in range(8):
                        P.mm(ps[:, 0:n], wm[:, c, j * 128:(j + 1) * 128], xnT[:, c, t0:t0 + n], start=(c == 0), stop=(c == 7))
                    P.act(pc[:, j, 3:3 + n], ps[:, 0:n], AF.Identity, bias=bcol(1544 + 128 * j))
                    P.act(caccs[j % 2][:, 0:n], ps[:, 0:n], AF.Identity, scale=mconv_w[:, j, 3:4], bias=wb3[:, j:j + 1])
                for j in (2 * pj, 2 * pj + 1):
                    cacc = caccs[j % 2]
                    for tap in (0, 1, 2):
                        P.stt(cacc[:, 0:n], pc[:, j, tap:tap + n], mconv_w[:, j, tap:tap + 1], cacc[:, 0:n], ALU.mult, ALU.add)
                    P.act(qk[:, j, 0:n], cacc[:, 0:n], AF.Silu, bias=mconv_b[:, j:j + 1])
                if pj == 3:
                    if gi == 3:
                        P.dma("sp", O["ml_convT"][0], pc[:, :, n:n + 3])
                    if gi == 4:
                        P.dma("sp", O["ml_convT"][1], pc[:, :, n:n + 3])
                    if gi < 3:
                        P.copy(pc[:, :, 0:3], pc[:, :, n:n + 3])
            return [lambda pj=pj: part(pj) for pj in range(4)]

        def ml_gates(gi):
            t0, n = GROUPS[gi]
            for h in range(4):
                ps = nb()
                for c in range(8):
                    P.mm(ps[:, 0:n], wm[:, c, 1536 + h * 128:1536 + (h + 1) * 128], xnT[:, c, t0:t0 + n], start=(c == 0), stop=(c == 7))
                P.act(sigo[:, h, 0:n], ps[:, 0:n], AF.Sigmoid, bias=bcol(3088 + 128 * h))
            for h in range(4):
                ps = nb()
                P.mm(ps[:, 0:n], sel4[:, h, :], rdl[:, t0:t0 + n])
                P.act(rdb[:, h, 0:n], ps[:, 0:n], AF.Exp)

        for p_ in ml_proj_parts(0):
            p_()
        ti_global = 0
        for gi, (t0, n) in enumerate(GROUPS):
            if gi == 4:
                P.dma("sp", O["ml_cT"][0], CT[:, :, 0:128])
                P.dma("sp", O["ml_n"][0], CT[:, :, 128])
                P.dma("sp", CT[:, :, 0:128], I["sC"])
                P.dma("sp", CT[:, :, 128], I["sn"])
            ml_gates(gi)
            qkc[0] = qks[gi % 2]
            nparts = ml_proj_parts(gi + 1) if gi + 1 < len(GROUPS) else []
            tiles = [(tt0, L) for (tt0, L) in TILES if t0 <= tt0 < t0 + n]
            prev = None
            for k_, (tt0, L) in enumerate(tiles):
                ti = ti_global
                ti_global += 1
                cur = (ti, tt0, L, tt0 - t0, msets[ti % 2])
                ml_step(prev, cur)
                prev = cur
                if k_ < len(nparts):
                    nparts[k_]()
            ml_step(prev, None)
        P.dma("sp", O["ml_cT"][1], CT[:, :, 0:128])
        P.dma("sp", O["ml_n"][1], CT[:, :, 128])
        ar.release(s1)
        dbg("bT", bT[:, 0, :])
        mT_off = ar.mark()
        mT = ar.bf(128, 4, NT)
        s1 = ar.mark()

        P.mark("mlstm")
        wq = ar.bf(128, 8, 512)
        P.dma("pool", wq, wv("w_in")[:, :, 3600:4112])
        wkv = ar.bf(128, 8, 1024)
        P.dma("pool", wkv, wv("w_mem_kv"))
        memx = ar.f32(128, 8, 256)
        P.dma("sp", memx, I["memT"].rearrange("(c p) n -> p c n", p=128))
        rbm = ar.f32(128, 256)
        rms_bcast(memx, 8, 256, 1024.0, rbm)
        memn = ar.bf(128, 8, 256)
        for c in range(8):
            P.stt(memn[:, c, :], memx[:, c, :], g_mem[:, c:c + 1], rbm, ALU.mult, ALU.mult)
        mkT = ar.bf(128, 2, 4, 256)
        mv = ar.bf(128, 2, 2, 512)
        tmpf = ar.f32(128, 512)
        for chh in range(4):
            ps = nb()
            for c in range(8):
                P.mm(ps[:, 0:256], wkv[:, c, chh * 128:(chh + 1) * 128], memn[:, c, :], start=(c == 0), stop=(c == 7))
            P.copy(tmpf[:, 0:256], ps[:, 0:256])
            P.copy(mkT[:, 0, chh, :], tmpf[:, 0:256])
            P.dma("sp", O["mem_kT"][chh * 128:(chh + 1) * 128, :], tmpf[:, 0:256])
        for mt in range(2):
            ps = nb()
            for c in range(8):
                P.mm(ps[:, :], memn[:, c, mt * 128:(mt + 1) * 128], wkv[:, c, 512:1024], start=(c == 0), stop=(c == 7))
            P.copy(tmpf, ps)
            P.copy(mv[:, 0, mt, :], tmpf)
            P.dma("sp", O["mem_v"][mt * 128:(mt + 1) * 128, :], tmpf)
        P.dma("pool", mkT[:, 1], I["cmkT"].rearrange("(c p) n -> p c n", p=128))
        P.dma("pool", mv[:, 1], I["cmv"].rearrange("(t p) f -> p t f", p=128))
        qhs = [ar.bf(128, 512), ar.bf(128, 512)]
        ptms = [ar.bf(128, 2, 512), ar.bf(128, 2, 512)]
        rlms = [ar.f32(128, 512), ar.f32(128, 512)]
        mits = [(gi, h) for gi in range(len(GROUPS)) for h in range(4)]

        def mm_A(i):
            gi, h = mits[i]
            t0, n = GROUPS[gi]
            ps = nb()
            for c in range(8):
                P.mm(ps[:, 0:n], wq[:, c, h * 128:(h + 1) * 128], xnT[:, c, t0:t0 + n], start=(c == 0), stop=(c == 7))
            P.act(qhs[i % 2][:, 0:n], ps[:, 0:n], AF.Identity, bias=bcol(3600 + 128 * h))

        def mm_B(i):
            gi, h = mits[i]
            t0, n = GROUPS[gi]
            seq = 0 if gi < 4 else 1
            qh, ptm, rlm = qhs[i % 2], ptms[i % 2], rlms[i % 2]
            for mt in range(2):
                sps = nb()
                P.mm(sps[:, 0:n], mkT[:, seq, h, mt * 128:(mt + 1) * 128], qh[:, 0:n])
                P.act(ptm[:, mt, 0:n], sps[:, 0:n], AF.Exp, scale=QSC)
            ops_ = nb()
            lps = nb()
            for mt in range(2):
                P.mm(ops_[:, 0:n], mv[:, seq, mt, h * 128:(h + 1) * 128], ptm[:, mt, 0:n], start=(mt == 0), stop=(mt == 1))
            for mt in range(2):
                P.mm(lps[:, 0:n], ones_b, ptm[:, mt, 0:n], start=(mt == 0), stop=(mt == 1))
            P.act(rlm[:, 0:n], lps[:, 0:n], AF.Ln)
            P.act(rlm[:, 0:n], rlm[:, 0:n], AF.Exp, scale=-1.0)
            P.tt(mT[:, h, t0:t0 + n], ops_[:, 0:n], rlm[:, 0:n], ALU.mult)

        mm_A(0)
        for i in range(len(mits)):
            if i + 1 < len(mits):
                mm_A(i + 1)
            mm_B(i)
        dbg("mT", mT[:, 0, :])
        ar.release(s1)

        P.mark("mem")
        mergedT = ar.bf(128, 8, NT)
        s2 = ar.mark()
        wgs = [[ar.bf(128, 8, 128) for b in range(3)] for _ in range(2)]
        wbs = [[ar.bf(128, 4, 128) for b in range(3)] for _ in range(2)]
        sg = [ar.f32(128, 512) for b in range(3)]
        macc = ar.f32(128, 512)
        mtmp = ar.f32(128, 512)
        brs = ("w_br_a", "w_br_b", "w_br_m")
        srcs = (aT, bT, mT)
        for oc in range(8):
            wg_ = wgs[oc % 2]
            wb_ = wbs[oc % 2]
            for b in range(3):
                g0 = 4112 + b * 1024 + oc * 128
                P.dma("pool", wg_[b], wv("w_in")[:, :, g0:g0 + 128])
                P.dma("pool", wb_[b], wv(brs[b])[:, :, oc * 128:(oc + 1) * 128])
            for (t0, n) in GROUPS:
                pp = []
                for b in range(3):
                    g0 = 4112 + b * 1024 + oc * 128
                    ps = nb()
                    for c in range(8):
                        P.mm(ps[:, 0:n], wg_[b][:, c, :], xnT[:, c, t0:t0 + n], start=(c == 0), stop=(c == 7))
                    P.act(sg[b][:, 0:n], ps[:, 0:n], AF.Sigmoid, bias=bcol(g0))
                for b in range(3):
                    ps = nb()
                    for c in range(4):
                        P.mm(ps[:, 0:n], wb_[b][:, c, :], srcs[b][:, c, t0:t0 + n], start=(c == 0), stop=(c == 3))
                    pp.append(ps)
                P.tt(macc[:, 0:n], sg[0][:, 0:n], pp[0][:, 0:n], ALU.mult)
                P.tt(mtmp[:, 0:n], sg[1][:, 0:n], pp[1][:, 0:n], ALU.mult)
                P.tt(macc[:, 0:n], macc[:, 0:n], mtmp[:, 0:n], ALU.add)
                P.tt(mtmp[:, 0:n], sg[2][:, 0:n], pp[2][:, 0:n], ALU.mult)
                P.tt(mergedT[:, oc, t0:t0 + n], macc[:, 0:n], mtmp[:, 0:n], ALU.add)
        dbg("mergedT", mergedT[:, 0, :])
        ar.release(s2)

        P.mark("s2_merge")
        wo = ar.bf(128, 8, 1024)
        P.dma("pool", wo, wv("w_out"))
        oTs = [ar.f32(128, 8, 512), ar.f32_at(aT_off, 128, 8, 512)]
        xss = [ar.f32(128, 8, 512), ar.f32_at(bT_off, 128, 8, 512)]
        rbs = [ar.f32(128, 512), ar.f32(128, 512)]
        sqs = [ar.bf(128, 8, 512), ar.bf_at(mT_off, 128, 8, 512)]
        x1s3w = x1scr.rearrange("(c p) n -> p c n", p=128)

        def ob_A(gi):
            t0, n = GROUPS[gi]
            oT, xs = oTs[gi % 2], xss[gi % 2]
            P.dma("sp", xs[:, :, 0:n], xT3[:, :, t0:t0 + n])
            for oc in range(8):
                ps = nb()
                for c in range(8):
                    P.mm(ps[:, 0:n], wo[:, c, oc * 128:(oc + 1) * 128], mergedT[:, c, t0:t0 + n], start=(c == 0), stop=(c == 7))
                P.act(oT[:, oc, 0:n], ps[:, 0:n], AF.Copy)
                P.act(sqs[gi % 2][:, oc, 0:n], ps[:, 0:n], AF.Square)

        def ob_B(gi):
            t0, n = GROUPS[gi]
            oT, xs, rb, sq = oTs[gi % 2], xss[gi % 2], rbs[gi % 2], sqs[gi % 2]
            ps = nb()
            for c in range(8):
                P.mm(ps[:, 0:n], ones_b, sq[:, c, 0:n], start=(c == 0), stop=(c == 7))
            P.act(rb[:, 0:n], ps[:, 0:n], AF.Ln, bias=eps_col, scale=1.0 / 1024.0)
            P.act(rb[:, 0:n], rb[:, 0:n], AF.Exp, scale=-0.5)
            ps2 = nb()
            for oc in range(8):
                P.stt(oT[:, oc, 0:n], oT[:, oc, 0:n], g_post[:, oc:oc + 1], rb[:, 0:n], ALU.mult, ALU.mult)
                P.tt(xs[:, oc, 0:n], xs[:, oc, 0:n], oT[:, oc, 0:n], ALU.add)
                P.act(sq[:, oc, 0:n], xs[:, oc, 0:n], AF.Square)
                P.mm(ps2[:, 0:n], ones_b, sq[:, oc, 0:n], start=(oc == 0), stop=(oc == 7))
            P.dma("sp", x1s3w[:, :, t0:t0 + n], xs[:, :, 0:n])
            P.act(rb[:, 0:n], ps2[:, 0:n], AF.Ln, bias=eps_col, scale=1.0 / 1024.0)
            P.act(rb[:, 0:n], rb[:, 0:n], AF.Exp, scale=-0.5)
            for c in range(8):
                P.stt(xnT[:, c, t0:t0 + n], xs[:, c, 0:n], g_fpre[:, c:c + 1], rb[:, 0:n], ALU.mult, ALU.mult)

        ob_A(0)
        for gi in range(len(GROUPS)):
            if gi + 1 < len(GROUPS):
                ob_A(gi + 1)
            ob_B(gi)
        dbg("x1nT", xnT[:, 0, :])
        ar.release(m_after_xn)

        P.mark("s2b_out")
        hidT = ar.bf(128, 22, NT)
        m_h = ar.mark()
        wus = [ar.bf(128, 8, 256), ar.bf(128, 8, 256)]
        apre_p = ar.f32(128, 2 + TP)
        bpre_p = ar.f32(128, 2 + TP)
        apre_s = ar.f32(128, 2 + TS)
        bpre_s = ar.f32(128, 2 + TS)
        accas = [ar.f32(128, 512) for _ in range(3)]
        accbs = [ar.f32(128, 512) for _ in range(3)]
        gas = [ar.f32(128, 512) for _ in range(3)]
        fit = [0]
        ftail = [None]
        w_up3 = wv("w_up")
        for c in range(22):
            wu = wus[c % 2]
            P.dma("pool", wu[:, :, 0:128], w_up3[:, :, c * 128:(c + 1) * 128])
            P.dma("pool", wu[:, :, 128:256], w_up3[:, :, 2816 + c * 128:2816 + (c + 1) * 128])
            ja, jb = c, 22 + c
            for gi, (t0, n) in enumerate(GROUPS):
                if gi < 4:
                    apre, bpre = apre_p[:, t0:t0 + n + 2], bpre_p[:, t0:t0 + n + 2]
                else:
                    apre, bpre = apre_s, bpre_s
                if gi == 0:
                    P.memset(apre_p[:, 0:2], 0.0)
                    P.memset(bpre_p[:, 0:2], 0.0)
                if gi == 4:
                    P.dma("sp", apre[:, 0:2], I["sfconv"][:, ja, :])
                    P.dma("sp", bpre[:, 0:2], I["sfconv"][:, jb, :])
                acca, accb, ga = accas[fit[0] % 3], accbs[fit[0] % 3], gas[fit[0] % 3]
                fit[0] += 1
                for (pre, off) in ((apre, 0), (bpre, 128)):
                    ps = nb()
                    for k in range(8):
                        P.mm(ps[:, 0:n], wu[:, k, off:off + 128], xnT[:, k, t0:t0 + n], start=(k == 0), stop=(k == 7))
                    P.act(pre[:, 2:2 + n], ps[:, 0:n], AF.Copy)
                    if off == 128:
                        P.act(accb[:, 0:n], ps[:, 0:n], AF.Identity, scale=fconv_w[:, jb, 2:3])
                    else:
                        P.act(acca[:, 0:n], ps[:, 0:n], AF.Identity, scale=fconv_w[:, ja, 2:3])
                P.stt(acca[:, 0:n], apre[:, 0:n], fconv_w[:, ja, 0:1], acca[:, 0:n], ALU.mult, ALU.add)
                P.stt(acca[:, 0:n], apre[:, 1:1 + n], fconv_w[:, ja, 1:2], acca[:, 0:n], ALU.mult, ALU.add)
                P.stt(accb[:, 0:n], bpre[:, 0:n], fconv_w[:, jb, 0:1], accb[:, 0:n], ALU.mult, ALU.add)
                P.stt(accb[:, 0:n], bpre[:, 1:1 + n], fconv_w[:, jb, 1:2], accb[:, 0:n], ALU.mult, ALU.add)
                if ftail[0] is not None:
                    ftail[0]()

                def _tail(ga=ga, acca=acca, accb=accb, n=n, ja=ja, jb=jb, c=c, t0=t0):
                    P.act(ga[:, 0:n], acca[:, 0:n], AF.Gelu_apprx_tanh, bias=fconv_b[:, ja:ja + 1])
                    P.stt(hidT[:, c, t0:t0 + n], accb[:, 0:n], fconv_b[:, jb:jb + 1], ga[:, 0:n], ALU.add, ALU.mult)
                ftail[0] = _tail
                if gi in (3, 4):
                    so = 0 if gi == 3 else 1
                    P.dma("sp", O["ffn_convT"][so][:, ja, :], apre[:, n:n + 2])
                    P.dma("sp", O["ffn_convT"][so][:, jb, :], bpre[:, n:n + 2])
        ftail[0]()
        dbg("hidT", hidT[:, 0, :])
        ar.release(m_h)
        P.mark("ffn_up")
        wd3 = wv("w_down")
        wd0 = ar.bf_at(xn_off, 128, 22, 512)
        wd1 = ar.bf(128, 22, 512)
        P.dma("pool", wd0, wd3[:, :, 0:512])
        P.dma("pool", wd1, wd3[:, :, 512:1024])
        oT = ar.f32(128, 8, 512)
        xs = ar.f32(128, 8, 512)
        rb = ar.f32(128, 512)
        x1s3 = x1scr.rearrange("(c p) n -> p c n", p=128)
        yT3 = O["yT"].rearrange("(c p) n -> p c n", p=128)
        for gi, (t0, n) in enumerate(GROUPS):
            P.dma("sp", xs[:, :, 0:n], x1s3[:, :, t0:t0 + n])
            for oc in range(8):
                wd = wd0 if oc < 4 else wd1
                o4 = oc % 4
                ps = nb()
                for c in range(22):
                    P.mm(ps[:, 0:n], wd[:, c, o4 * 128:(o4 + 1) * 128], hidT[:, c, t0:t0 + n], start=(c == 0), stop=(c == 21))
                P.act(oT[:, oc, 0:n], ps[:, 0:n], AF.Copy)
            rms_bcast(oT[:, :, 0:n], 8, n, 1024.0, rb[:, 0:n])
            for oc in range(8):
                P.stt(oT[:, oc, 0:n], oT[:, oc, 0:n], g_fpost[:, oc:oc + 1], rb[:, 0:n], ALU.mult, ALU.mult)
                P.tt(xs[:, oc, 0:n], xs[:, oc, 0:n], oT[:, oc, 0:n], ALU.add)
            P.dma("sp", yT3[:, :, t0:t0 + n], xs[:, :, 0:n])
        P.mark("ffn_down")
        P.emit(Kq={'pool': 3})
    except _Stop:
        pass
    return nc, DBG, P, 0


_CACHE = {}


def _get_nc(debug=()):
    key = tuple(debug)
    if key not in _CACHE:
        _CACHE[key] = build(debug)
    return _CACHE[key]


def _consts():
    ident = np.eye(128, dtype=np.float32)
    s = np.arange(128)
    trimask = (s[None, :] >= s[:, None]).astype(np.float32)
    negmask = np.where(s[:, None] <= s[None, :], 0.0, -30000.0).astype(np.float32)
    bd = np.zeros((128, 128), np.float32)
    bd[:64, :64] = 1.0
    bd[64:, 64:] = 1.0
    sel4 = np.zeros((4, 4, 128), np.float32)
    for h in range(4):
        sel4[h, h, :] = 1.0
    return dict(ident=ident, trimask=trimask, negmask=negmask, bdones=bd, sel4=sel4)


def _fm(v):
    return np.ascontiguousarray(v.reshape(-1, 128).T)


def _prep_shared(inp):
    f = lambda a: np.ascontiguousarray(np.asarray(a, dtype=np.float32))
    b_in = f(inp["b_in"])[0]
    d = dict(
        w_in=f(inp["w_in"])[0],
        b_fm=np.ascontiguousarray(np.stack([b_in[s:s + 128] for s in FM_STARTS], axis=1)),
        bg_fox=np.ascontiguousarray(b_in[1536:1544][:, None]),
        bg_i=np.ascontiguousarray(b_in[3080:3084][:, None]),
        bg_f=np.ascontiguousarray(b_in[3084:3088][:, None]),
        b_row=np.ascontiguousarray(b_in[None, :]),
        g_pre=_fm(f(inp["norm_mix_pre"])[0]),
        gq2=np.ascontiguousarray(np.concatenate([f(inp["fox_q_norm"])[0]] * 2)[:, None]),
        gk2=np.ascontiguousarray(np.concatenate([f(inp["fox_k_norm"])[0]] * 2)[:, None]),
        mconv_w=np.ascontiguousarray(f(inp["mlstm_conv_w"])[0].reshape(4, 8, 128).transpose(2, 1, 0)),
        mconv_b=_fm(f(inp["mlstm_conv_b"])[0]),
        mhn=np.ascontiguousarray(f(inp["mlstm_head_norm"])[0].T),
        g_mem=_fm(f(inp["norm_mem"])[0]),
        w_mem_kv=f(inp["w_mem_kv"])[0],
        w_br_a=f(inp["w_br_a"])[0], w_br_b=f(inp["w_br_b"])[0], w_br_m=f(inp["w_br_m"])[0],
        w_out=f(inp["w_out"])[0],
        g_post=_fm(f(inp["norm_mix_post"])[0]),
        g_fpre=_fm(f(inp["norm_ffn_pre"])[0]),
        w_up=f(inp["w_up"])[0],
        fconv_w=np.ascontiguousarray(f(inp["ffn_conv_w"])[0].reshape(3, 44, 128).transpose(2, 1, 0)),
        fconv_b=_fm(f(inp["ffn_conv_b"])[0]),
        w_down=f(inp["w_down"])[0],
        g_fpost=_fm(f(inp["norm_ffn_post"])[0]),
    )
    d.update(_consts())
    return d


def _prep_core(inp, b):
    f = lambda a: np.asarray(a, dtype=np.float32)
    c = np.ascontiguousarray
    return dict(
        xT=c(np.concatenate([f(inp["x_prompt"])[b].T, f(inp["x_sample"])[b].T], axis=1)),
        ckT=c(f(inp["cache_fox_k"])[0, b].reshape(PAST, 512).T),
        cv=c(f(inp["cache_fox_v"])[0, b].reshape(32, 128, 8, 64).transpose(2, 1, 0, 3)),
        clogfT=c(f(inp["cache_fox_logf"])[0, b].T),
        sC=c(f(inp["state_mlstm_c"])[0, b].transpose(2, 0, 1)),
        sn=c(f(inp["state_mlstm_n"])[0, b].T),
        sm=c(f(inp["state_mlstm_m"])[0, b][:, None]),
        sconv=c(f(inp["state_mlstm_conv"])[0, b].reshape(3, 8, 128).transpose(2, 1, 0)),
        cmkT=c(f(inp["cache_mem_k"])[0, b].reshape(256, 512).T),
        cmv=c(f(inp["cache_mem_v"])[0, b].reshape(256, 512)),
        sfconv=c(f(inp["state_ffn_conv"])[0, b].reshape(2, 44, 128).transpose(2, 1, 0)),
        memT=c(f(inp["mem_prompt"])[b].T),
    )


def _assemble(results):
    B = len(results)
    z = lambda *s: np.zeros(s, np.float32)
    y_p, y_s = z(B, TP, 1024), z(B, TS, 1024)
    fk_p, fv_p, fl_p = z(1, B, TP, 8, 64), z(1, B, TP, 8, 64), z(1, B, TP, 8)
    fk_s, fv_s, fl_s = z(1, B, TS, 8, 64), z(1, B, TS, 8, 64), z(1, B, TS, 8)
    c_p, n_p, m_p = z(1, B, 4, 128, 128), z(1, B, 4, 128), z(1, B, 4)
    c_s, n_s, m_s = z(1, B, 4, 128, 128), z(1, B, 4, 128), z(1, B, 4)
    cv_p, cv_s = z(1, B, 3, 1024), z(1, B, 3, 1024)
    fc_p, fc_s = z(1, B, 2, 5632), z(1, B, 2, 5632)
    mk_p, mv_p = z(1, B, 256, 4, 128), z(1, B, 256, 4, 128)
    for b, r in enumerate(results):
        yT = r["yT"]
        y_p[b] = yT[:, :TP].T
        y_s[b] = yT[:, TP:].T
        kT = r["fox_kT"]
        fk_p[0, b] = kT[:, :TP].T.reshape(TP, 8, 64)
        fk_s[0, b] = kT[:, TP:].T.reshape(TS, 8, 64)
        fv = r["fox_v"]
        fv_p[0, b] = fv[:TP].reshape(TP, 8, 64)
        fv_s[0, b] = fv[TP:].reshape(TS, 8, 64)
        lf = r["fox_logfT"]
        fl_p[0, b] = lf[:, :TP].T
        fl_s[0, b] = lf[:, TP:].T
        for (si, cc, nn, mm, cvv, fcc) in ((0, c_p, n_p, m_p, cv_p, fc_p), (1, c_s, n_s, m_s, cv_s, fc_s)):
            cc[0, b] = r["ml_cT"][si].transpose(1, 2, 0)
            nn[0, b] = r["ml_n"][si].T
            mm[0, b] = r["ml_m"][si][:, 0]
            cvv[0, b] = r["ml_convT"][si].transpose(2, 1, 0).reshape(3, 1024)
            fcc[0, b] = r["ffn_convT"][si].transpose(2, 1, 0).reshape(2, 5632)
        mk_p[0, b] = r["mem_kT"].T.reshape(256, 4, 128)
        mv_p[0, b] = r["mem_v"].reshape(256, 4, 128)
    return (y_p, y_s, fk_p, fv_p, fl_p, c_p, n_p, m_p, cv_p, fc_p, mk_p, mv_p,
            fk_s, fv_s, fl_s, c_s, n_s, m_s, cv_s, fc_s)


def kernel(**inputs):
    nc = _get_nc()[0]
    shared = _prep_shared(inputs)
    in_maps = []
    for b in range(8):
        d = dict(shared)
        d.update(_prep_core(inputs, b))
        in_maps.append(d)
    res = run_bass_kernel_spmd(nc, in_maps, core_ids=list(range(8)))
    return _assemble(res.results)
```

```python
import numpy as np
from contextlib import ExitStack
import concourse.bass as bass
import concourse.mybir as mybir

F32 = mybir.dt.float32
BF16 = mybir.dt.bfloat16
AF = mybir.ActivationFunctionType
ALU = mybir.AluOpType
AX = mybir.AxisListType


def _esize(dt):
    n = str(dt)
    if "float32" in n or "int32" in n:
        return 4
    if "bfloat16" in n or "float16" in n or "int16" in n:
        return 2
    if "int8" in n or "float8" in n:
        return 1
    if "64" in n:
        return 8
    raise ValueError(n)


def _boxes(ap):
    t = ap.tensor
    name = t.name
    es = _esize(ap.dtype)
    dims = [(int(s), int(n)) for s, n in ap.ap]
    off = int(ap.offset)
    if "DRAM" in str(ap.space).upper() or "HBM" in str(ap.space).upper():
        ext = sum((n - 1) * abs(s) for s, n in dims)
        return name, [(0, 1, off * es, (off + ext + 1) * es)]
    tes = _esize(t.dtype)
    rowbytes = int(np.prod([int(x) for x in t.shape[1:]])) * tes
    row = rowbytes // es
    p0 = off // row
    f0 = off % row
    pext = 0
    fd = []
    for s, n in dims:
        if n == 1:
            continue
        if s != 0 and s % row == 0:
            pext += (n - 1) * (s // row)
        elif s != 0:
            fd.append((abs(s), n))
    fd.sort(reverse=True)
    p1 = p0 + pext + 1
    if len(fd) >= 2:
        inner = sum((n - 1) * s for s, n in fd[1:]) + 1
        s0, n0 = fd[0]
        if s0 >= inner and n0 <= 64:
            return name, [(p0, p1, (f0 + i * s0) * es, (f0 + i * s0 + inner) * es) for i in range(n0)]
    ext = sum((n - 1) * s for s, n in fd) + 1
    return name, [(p0, p1, f0 * es, (f0 + ext) * es)]


def _ov(a, b):
    for x in a:
        for y in b:
            if x[0] < y[1] and y[0] < x[1] and x[2] < y[3] and y[2] < x[3]:
                return True
    return False


def _contained(a, b):
    for x in a:
        ok = False
        for y in b:
            if y[0] <= x[0] and x[1] <= y[1] and y[2] <= x[2] and x[3] <= y[3]:
                ok = True
                break
        if not ok:
            return False
    return True


class _Stop(Exception):
    pass


class Prog:
    ENGS = ("pe", "act", "dve", "pool", "sp")

    def __init__(self, nc):
        self.nc = nc
        self.ops = []
        self.hist = {}

    def add(self, eng, fn, reads=(), writes=(), dma=False):
        idx = len(self.ops)
        deps = set()
        rb = [_boxes(a) for a in reads]
        wb = [_boxes(a) for a in writes]
        for name, bx in rb:
            for e in self.hist.get(name, ()):
                if e[2] and _ov(e[0], bx):
                    deps.add(e[1])
        for name, bx in wb:
            for e in self.hist.get(name, ()):
                if _ov(e[0], bx):
                    deps.add(e[1])
        for name, bx in wb:
            lst = [e for e in self.hist.get(name, ()) if not _contained(e[0], bx)]
            lst.append((bx, idx, True, eng, dma))
            self.hist[name] = lst
        for name, bx in rb:
            lst = self.hist.setdefault(name, [])
            rep = False
            if not dma:
                for i, e in enumerate(lst):
                    if (not e[2]) and e[3] == eng and (not e[4]) and e[0] == bx:
                        lst[i] = (bx, idx, False, eng, dma)
                        rep = True
                        break
            if not rep:
                lst.append((bx, idx, False, eng, dma))
        deps.discard(idx)
        self.ops.append((eng, fn, deps, dma))
        return idx

    def mark(self, name):
        if not hasattr(self, 'marks'):
            self.marks = []
        self.marks.append((name, sum(1 for o in self.ops if o[0] == 'pe')))
        if getattr(self, 'stop_at', None) == name:
            self.emit()
            raise _Stop()

    def dma(self, q, out, in_, **kw):
        kw.setdefault('allow_slow_non_contiguous', True)
        return self.add(q, lambda e: e.dma_start(out=out, in_=in_, **kw), [in_], [out], dma=True)

    def mm(self, out, lhsT, rhs, start=True, stop=True, acc=False):
        rd = [lhsT, rhs] + ([out] if not start else [])
        return self.add("pe", lambda e: e.matmul(out, lhsT, rhs, start=start, stop=stop), rd, [out])

    def tr(self, out, in_, ident):
        return self.add("pe", lambda e: e.transpose(out, in_, ident), [in_, ident], [out])

    def act(self, out, in_, func, bias=None, scale=None, accum_out=None, eng="act"):
        rd = [in_]
        kw = {}
        if bias is not None:
            kw["bias"] = bias
            if not isinstance(bias, (int, float)):
                rd.append(bias)
        if scale is not None:
            kw["scale"] = scale
            if not isinstance(scale, (int, float)):
                rd.append(scale)
        wr = [out]
        if accum_out is not None:
            kw["accum_out"] = accum_out
            wr.append(accum_out)
        return self.add("act", lambda e: e.activation(out, in_, func, **kw), rd, wr)

    def tt(self, out, in0, in1, op, eng="dve"):
        return self.add(eng, lambda e: e.tensor_tensor(out, in0, in1, op), [in0, in1], [out])

    def ts(self, out, in0, s1, op0, s2=None, op1=None, eng="dve"):
        rd = [in0] + [s for s in (s1, s2) if s is not None and not isinstance(s, (int, float))]
        if op1 is None:
            return self.add(eng, lambda e: e.tensor_scalar(out, in0, s1, None, op0), rd, [out])
        return self.add(eng, lambda e: e.tensor_scalar(out, in0, s1, s2, op0, op1), rd, [out])

    def stt(self, out, in0, scalar, in1, op0, op1, eng="dve"):
        rd = [in0, in1] + ([] if isinstance(scalar, (int, float)) else [scalar])
        return self.add(eng, lambda e: e.scalar_tensor_tensor(out, in0, scalar, in1, op0, op1), rd, [out])

    def copy(self, out, in_, eng="dve"):
        return self.add(eng, lambda e: e.tensor_copy(out, in_), [in_], [out])

    def memset(self, out, val, eng="dve"):
        return self.add(eng, lambda e: e.memset(out, val), [], [out])

    def recip(self, out, in_):
        return self.add("dve", lambda e: e.reciprocal(out, in_), [in_], [out])

    def scan(self, out, d0, d1, init, op0, op1):
        rd = [d0, d1] + ([] if isinstance(init, (int, float)) else [init])
        return self.add("dve", lambda e: e.tensor_tensor_scan(out, d0, d1, init, op0, op1), rd, [out])

    def emit(self, R=20000, K=8, Kq=None):
        Kq = dict(Kq or {})
        KK = {e: Kq.get(e, K) for e in self.ENGS}
        nc = self.nc
        ops = self.ops
        needed = set()
        for eng, fn, deps, dma in ops:
            for d in deps:
                de, _, _, ddma = ops[d]
                if ddma:
                    continue
                if de == "pe" and eng == "pe" and not dma:
                    continue
                needed.add(d)
        sigidx = {}
        cnt = {e: 0 for e in self.ENGS}
        for i, (eng, fn, deps, dma) in enumerate(ops):
            if (not dma) and i in needed:
                sigidx[i] = cnt[eng]
                cnt[eng] += 1
        dmaidx = {}
        dcnt = {e: 0 for e in self.ENGS}
        for i, (eng, fn, deps, dma) in enumerate(ops):
            if dma:
                dmaidx[i] = dcnt[eng]
                dcnt[eng] += 1
        with ExitStack() as st:
            csem = {e: [st.enter_context(nc.semaphore(f"c_{e}_{j}")) for j in range(max(1, (cnt[e] + R - 1) // R))]
                    for e in self.ENGS}
            dsem = {e: [st.enter_context(nc.semaphore(f"d_{e}_{j}")) for j in range(min(KK[e], dcnt[e]))]
                    for e in self.ENGS}
            block = st.enter_context(nc.Block())

            def run(me, e):
                waited_c = {x: -1 for x in self.ENGS}
                waited_d = {}

                def wait_dma(d):
                    q = ops[d][0]
                    n = dmaidx[d]
                    K = KK[q]
                    sem = dsem[q][n % K]
                    val = 16 * (n // K + 1)
                    key = (q, n % K)
                    if waited_d.get(key, 0) >= val:
                        return
                    e.wait_ge(sem, val)
                    waited_d[key] = val

                for i, (eng, fn, deps, dma) in enumerate(ops):
                    if eng != me:
                        continue
                    for d in sorted(deps):
                        de, _, _, ddma = ops[d]
                        if ddma:
                            wait_dma(d)
                        else:
                            if de == "pe" and me == "pe" and not dma:
                                continue
                            g = sigidx[d]
                            if waited_c[de] >= g:
                                continue
                            e.wait_ge(csem[de][g // R], g % R + 1)
                            waited_c[de] = g
                    if dma:
                        n = dmaidx[i]
                        K = KK[me]
                        sem = dsem[me][n % K]
                        if n >= K:
                            key = (me, n % K)
                            val = 16 * (n // K)
                            if waited_d.get(key, 0) < val:
                                e.wait_ge(sem, val)
                                waited_d[key] = val
                        ins = fn(e)
                        ins.then_inc(sem, 16)
                    else:
                        ins = fn(e)
                        if i in sigidx:
                            g = sigidx[i]
                            ins.then_inc(csem[me][g // R], 1)
                K = KK[me]
                for j in range(min(K, dcnt[me])):
                    uses = (dcnt[me] - 1 - j) // K + 1
                    val = 16 * uses
                    if waited_d.get((me, j), 0) < val:
                        e.wait_ge(dsem[me][j], val)

            @block.sync
            def _(e):
                run("sp", e)

            @block.scalar
            def _(e):
                run("act", e)

            @block.vector
            def _(e):
                run("dve", e)

            @block.gpsimd
            def _(e):
                run("pool", e)

            @block.tensor
            def _(e):
                run("pe", e)

from concourse.bass_utils import run_bass_kernel_spmd

NT = 2112
TP = 2048
TS = 64
PAST = 4096
EPS = 1e-6
GROUPS = [(0, 512), (512, 512), (1024, 512), (1536, 512), (2048, 64)]
TILES = [(i * 128, 128) for i in range(16)] + [(2048, 64)]
QSC = 128.0 ** -0.5

FM_STARTS = ([0, 128, 256, 384] + [512, 640, 768, 896] + [1544 + 128 * i for i in range(8)]
             + [3088 + 128 * i for i in range(4)]
             + [3600 + 128 * i for i in range(4)] + [4112 + 128 * i for i in range(24)])
FM_COL = {s: i for i, s in enumerate(FM_STARTS)}

IN_SHAPES = dict(
    xT=[1024, NT], w_in=[1024, 7184], b_fm=[128, 48], bg_fox=[8, 1], bg_i=[4, 1], bg_f=[4, 1],
    b_row=[1, 7184], g_pre=[128, 8], gq2=[128, 1], gk2=[128, 1], mconv_w=[128, 8, 4], mconv_b=[128, 8],
    mhn=[128, 4], g_mem=[128, 8], w_mem_kv=[1024, 1024], w_br_a=[512, 1024], w_br_b=[512, 1024],
    w_br_m=[512, 1024], w_out=[1024, 1024], g_post=[128, 8], g_fpre=[128, 8], w_up=[1024, 5632],
    fconv_w=[128, 44, 3], fconv_b=[128, 44], w_down=[2816, 1024], g_fpost=[128, 8],
    ckT=[512, PAST], cv=[8, 128, 32, 64], clogfT=[8, PAST], sC=[128, 4, 128], sn=[128, 4], sm=[4, 1],
    sconv=[128, 8, 3], cmkT=[512, 256], cmv=[256, 512], sfconv=[128, 44, 2], memT=[1024, 256],
    ident=[128, 128], trimask=[128, 128], negmask=[128, 128], bdones=[128, 128], sel4=[4, 4, 128],
)
OUT_SHAPES = dict(
    yT=[1024, NT], fox_kT=[512, NT], fox_v=[NT, 512], fox_logfT=[8, NT],
    ml_cT=[2, 128, 4, 128], ml_n=[2, 128, 4], ml_m=[2, 4, 1], ml_convT=[2, 128, 8, 3],
    ffn_convT=[2, 128, 44, 2], mem_kT=[512, 256], mem_v=[256, 512],
)


class Arena:
    def __init__(self, A, cap):
        self.A = A
        self.cap = cap
        self.top = 0
        self.hw = 0

    def _alloc(self, words):
        off = self.top
        self.top += words
        assert self.top <= self.cap, f"arena overflow {self.top} > {self.cap}"
        self.hw = max(self.hw, self.top)
        return off

    def f32(self, *shape):
        n = int(np.prod(shape[1:]))
        off = self._alloc(n)
        return self._view(self.A[:, off:off + n], shape)

    def bf(self, *shape):
        n = int(np.prod(shape[1:]))
        words = (n + 1) // 2
        off = self._alloc(words)
        return self._view(self.A[:, off:off + words].bitcast(BF16)[:, 0:n], shape)

    @staticmethod
    def _view(v, shape):
        p = shape[0]
        if len(shape) == 3:
            v = v.rearrange("p (a b) -> p a b", b=shape[2])
        elif len(shape) == 4:
            v = v.rearrange("p (a b c) -> p a b c", b=shape[2], c=shape[3])
        if p < 128:
            v = v[0:p]
        return v

    def f32_at(self, off, *shape):
        n = int(np.prod(shape[1:]))
        return self._view(self.A[:, off:off + n], shape)

    def bf_at(self, off, *shape):
        n = int(np.prod(shape[1:]))
        words = (n + 1) // 2
        return self._view(self.A[:, off:off + words].bitcast(BF16)[:, 0:n], shape)

    def mark(self):
        return self.top

    def release(self, m):
        self.top = m


def build(debug=(), stop_at=None, salt=None):
    nc = bass.Bass("TRN2", target_bir_lowering=False)
    I = {k: nc.dram_tensor(k, list(v), F32, kind="ExternalInput").ap() for k, v in IN_SHAPES.items()}
    O = {k: nc.dram_tensor(k, list(v), F32, kind="ExternalOutput").ap() for k, v in OUT_SHAPES.items()}
    x1scr = nc.dram_tensor("x1scr", [1024, NT], F32, kind="Internal").ap()
    DBG = {}
    P = Prog(nc)
    P.stop_at = stop_at
    CAP = 53000
    try:
      with nc.sbuf_tensor("A", [128, CAP], F32) as A_, nc.psum_tensor("PS", [128, 8, 512], F32) as PS:
        ar = Arena(A_, CAP)
        hw_box = [ar]
        bank_ctr = [0]

        def nb():
            b = bank_ctr[0] % 8
            bank_ctr[0] += 1
            return PS[:, b, :]

        def dbg(name, ap):
            if name in debug:
                shp = [int(s) for s in ap.shape]
                d = nc.dram_tensor("dbg_" + name, shp, F32, kind="ExternalOutput").ap()
                DBG[name] = shp
                if ap.dtype == F32 and "PSUM" not in str(ap.space).upper():
                    P.dma("sp", d, ap)
                else:
                    m = ar.mark()
                    t = ar.f32(*([128] + shp[1:]))[0:shp[0]]
                    P.copy(t, ap)
                    P.dma("sp", d, t)
                    ar.release(m)

        wv = lambda name: I[name].rearrange("(c p) n -> p c n", p=128)
        r3 = lambda ap, b: ap.rearrange("p (a b) -> p a b", b=b)

        ident_f = ar.f32(128, 128)
        ident_b = ar.bf(128, 128)
        trimask = ar.bf(128, 128)
        negmask = ar.bf(128, 128)
        bdones = ar.bf(128, 128)
        ones_b = ar.bf(128, 128)
        ones_f = ar.f32(128, 128)
        sel4 = ar.f32(4, 4, 128)
        b_fm = ar.f32(128, 48)
        g_pre = ar.f32(128, 8)
        g_post = ar.f32(128, 8)
        g_fpre = ar.f32(128, 8)
        g_fpost = ar.f32(128, 8)
        g_mem = ar.f32(128, 8)
        gq2 = ar.f32(128, 1)
        gk2 = ar.f32(128, 1)
        mhn = ar.f32(128, 4)
        mconv_w = ar.f32(128, 8, 4)
        mconv_b = ar.f32(128, 8)
        fconv_w = ar.f32(128, 44, 3)
        fconv_b = ar.f32(128, 44)
        def load_consts():
            P.dma("sp", ident_f, I["ident"])
            P.dma("pool", ident_b, I["ident"])
            P.dma("pool", trimask, I["trimask"])
            P.dma("pool", negmask, I["negmask"])
            P.dma("pool", bdones, I["bdones"])
            P.dma("sp", sel4, I["sel4"])
            for t, n in ((b_fm, "b_fm"), (g_pre, "g_pre"), (g_post, "g_post"), (g_fpre, "g_fpre"), (g_fpost, "g_fpost"),
                         (g_mem, "g_mem"), (gq2, "gq2"), (gk2, "gk2"), (mhn, "mhn"), (mconv_w, "mconv_w"),
                         (mconv_b, "mconv_b"), (fconv_w, "fconv_w"), (fconv_b, "fconv_b")):
                P.dma("sp", t, I[n])
            P.ts(gq8, gq2, 0.125, ALU.mult)
        P.memset(ones_b, 1.0)
        P.memset(ones_f, 1.0)
        eps_col = ar.f32(128, 1)
        P.memset(eps_col, EPS)
        one_col = ar.f32(128, 1)
        P.memset(one_col, 1.0)
        gq8 = ar.f32(128, 1)
        bcol = lambda start: b_fm[:, FM_COL[start]:FM_COL[start] + 1]

        def rms_bcast(src, C, n, D, rb, sq=None):
            m = ar.mark()
            if sq is None:
                sq = ar.bf(128, C, n)
            P.act(sq, src, AF.Square)
            ps = nb()
            for c in range(C):
                P.mm(ps[:, 0:n], ones_b, sq[:, c, :], start=(c == 0), stop=(c == C - 1))
            P.act(rb, ps[:, 0:n], AF.Ln, bias=eps_col, scale=1.0 / D)
            P.act(rb, rb, AF.Exp, scale=-0.5)
            ar.release(m)

        xn_off = ar.mark()
        xnT = ar.bf(128, 8, NT)
        m_after_xn = ar.mark()
        aT_off = ar.mark()
        aT = ar.bf(128, 4, NT)
        chi = ar.bf(8, NT)
        clo = ar.bf(8, NT)
        rdl = ar.f32(4, NT)
        wg_tok = ar.f32(128, 17, 4)
        dec_b = ar.f32(128, 4, 17)

        xT3 = I["xT"].rearrange("(c p) n -> p c n", p=128)
        m = ar.mark()
        xs2 = [ar.f32(128, 8, 512), ar.f32(128, 8, 512)]
        rb2 = [ar.f32(128, 512), ar.f32(128, 512)]
        sq2 = [ar.bf(128, 8, 512), ar.bf(128, 8, 512)]
        for gi, (t0, n) in enumerate(GROUPS):
            xs = xs2[gi % 2][:, :, 0:n]
            P.dma("sp", xs, xT3[:, :, t0:t0 + n])
            if gi == 0:
                load_consts()
            rb = rb2[gi % 2][:, 0:n]
            rms_bcast(xs, 8, n, 1024.0, rb, sq=sq2[gi % 2][:, :, 0:n])
            for c in range(8):
                P.stt(xnT[:, c, t0:t0 + n], xs[:, c, :], g_pre[:, c:c + 1], rb, ALU.mult, ALU.mult)
        ar.release(m)
        dbg("xnT", xnT[:, 0, :])

        P.mark("s0_norm")
        m1a = ar.mark()
        wgf = ar.bf(128, 8, 8)
        wgi = ar.bf(128, 8, 4)
        wgm = ar.bf(128, 8, 4)
        P.dma("pool", wgf, wv("w_in")[:, :, 1536:1544])
        P.dma("pool", wgi, wv("w_in")[:, :, 3080:3084])
        P.dma("pool", wgm, wv("w_in")[:, :, 3084:3088])
        bgf = ar.f32(8, 1)
        bgi = ar.f32(4, 1)
        bgm = ar.f32(4, 1)
        P.dma("sp", bgf, I["bg_fox"])
        P.dma("sp", bgi, I["bg_i"])
        P.dma("sp", bgm, I["bg_f"])
        zrow = ar.f32(8, NT)
        P.memset(zrow, 0.0)
        flog = ar.f32(8, NT)
        cfox = ar.f32(8, NT)
        gi_r = ar.f32(4, NT)
        mlf = ar.f32(4, NT)
        A_r = ar.f32(4, NT)
        G_r = ar.f32(4, NT)
        wgT = ar.f32(4, NT)
        for (wt, bc, dst, rows, ls) in ((wgf, bgf, flog, 8, True), (wgi, bgi, gi_r, 4, False), (wgm, bgm, mlf, 4, True)):
            nbias = ar.f32(rows, 1)
            P.ts(nbias, bc, -1.0, ALU.mult)
            for (t0, n) in GROUPS:
                ps = nb()
                for c in range(8):
                    P.mm(ps[0:rows, 0:n], wt[:, c, :], xnT[:, c, t0:t0 + n], start=(c == 0), stop=(c == 7))
                if ls:
                    m = ar.mark()
                    e = ar.f32(rows, n)
                    P.act(e, ps[0:rows, 0:n], AF.Exp, bias=nbias, scale=-1.0)
                    P.act(e, e, AF.Ln, bias=one_col[0:rows], scale=1.0)
                    P.ts(dst[:, t0:t0 + n], e, -1.0, ALU.mult)
                    ar.release(m)
                else:
                    P.act(dst[:, t0:t0 + n], ps[0:rows, 0:n], AF.Identity, bias=bc)
        P.dma("sp", O["fox_logfT"], flog)
        P.scan(cfox[:, 0:TP], flog[:, 0:TP], zrow[:, 0:TP], 0.0, ALU.add, ALU.add)
        P.scan(cfox[:, TP:NT], flog[:, TP:NT], zrow[:, 0:TS], 0.0, ALU.add, ALU.add)
        P.copy(chi, cfox)
        P.tt(clo, cfox, chi, ALU.subtract)
        sm0 = ar.f32(4, 1)
        P.dma("sp", sm0, I["sm"])
        zr4 = zrow[0:4]
        Bc = ar.f32(4, NT)
        P.scan(Bc[:, 0:TP], mlf[:, 0:TP], zr4[:, 0:TP], 0.0, ALU.add, ALU.add)
        P.scan(Bc[:, TP:NT], mlf[:, TP:NT], zr4[:, 0:TS], 0.0, ALU.add, ALU.add)
        P.tt(A_r, gi_r, Bc, ALU.subtract)
        P.scan(G_r[:, 0:TP], A_r[:, 0:TP], A_r[:, 0:TP], 0.0, ALU.max, ALU.max)
        P.scan(G_r[:, TP:NT], A_r[:, TP:NT], A_r[:, TP:NT], sm0, ALU.max, ALU.max)
        gend = ar.f32(4, 17)
        mprev = ar.f32(4, 17)
        P.copy(gend[:, 0:16], G_r[:, 127:TP:128])
        P.copy(gend[:, 16:17], G_r[:, NT - 1:NT])
        P.memset(mprev[:, 0:1], 0.0)
        P.copy(mprev[:, 1:16], gend[:, 0:15])
        P.copy(mprev[:, 16:17], sm0)
        dec = ar.f32(4, 17)
        P.tt(dec, mprev, gend, ALU.subtract)
        P.act(dec, dec, AF.Exp)
        gend_b = gend[:, 0:16].unsqueeze(2).broadcast_to([4, 16, 128])
        P.tt(r3(wgT[:, 0:TP], 128), r3(A_r[:, 0:TP], 128), gend_b, ALU.subtract)
        P.ts(wgT[:, TP:NT], A_r[:, TP:NT], gend[:, 16:17], ALU.subtract)
        P.act(wgT, wgT, AF.Exp)
        P.tt(r3(rdl[:, 0:TP], 128), r3(Bc[:, 0:TP], 128), gend_b, ALU.add)
        P.ts(rdl[:, TP:NT], Bc[:, TP:NT], gend[:, 16:17], ALU.add)
        P.ts(rdl, rdl, -1.0, ALU.mult)
        mTo = ar.f32(4, 2)
        P.tt(mTo[:, 0:1], G_r[:, TP - 1:TP], Bc[:, TP - 1:TP], ALU.add)
        P.tt(mTo[:, 1:2], G_r[:, NT - 1:NT], Bc[:, NT - 1:NT], ALU.add)
        P.dma("sp", O["ml_m"][0], mTo[:, 0:1])
        P.dma("sp", O["ml_m"][1], mTo[:, 1:2])
        for ti, (t0, L) in enumerate(TILES):
            ps = nb()
            P.tr(ps[0:L, 0:4], wgT[:, t0:t0 + L], ident_f[0:4, 0:4])
            P.copy(wg_tok[0:L, ti, :], ps[0:L, 0:4])
        for h in range(4):
            ps = nb()
            P.mm(ps[:, 0:17], sel4[:, h, :], dec)
            P.copy(dec_b[:, h, :], ps[:, 0:17])
        dbg("cfox", cfox)
        dbg("G_r", G_r)
        dbg("wgT", wgT)
        dbg("rdl", rdl)
        ar.release(m1a)
        s1 = ar.mark()

        P.mark("s1a_gates")
        cchi = ar.bf(8, PAST)
        cclo = ar.bf(8, PAST)
        m_s = ar.mark()
        ccache = ar.f32(8, PAST)
        lcache = ar.f32(8, PAST)
        zc = ar.f32(8, PAST)
        P.memset(zc, 0.0)
        P.dma("sp", lcache, I["clogfT"])
        P.scan(ccache, lcache, zc, 0.0, ALU.add, ALU.add)
        ctot = ar.f32(8, 1)
        P.copy(ctot, ccache[:, PAST - 1:PAST])
        P.ts(ccache, ccache, ctot, ALU.subtract)
        P.copy(cchi, ccache)
        P.tt(cclo, ccache, cchi, ALU.subtract)
        ar.release(m_s)
        qT = ar.bf(128, 4, NT)
        kT = ar.bf(128, 4, NT)
        V = ar.bf(128, 17, 512)
        m_w = ar.mark()
        wf = ar.bf(128, 8, 1536)
        P.dma("pool", wf, wv("w_in")[:, :, 0:1536])
        bvrow = ar.f32(128, 512)
        P.dma("sp", bvrow, I["b_row"][:, 1024:1536].partition_broadcast(128))
        ptmp = [(ar.f32(128, 512), ar.bf(128, 512), ar.f32(128, 512), ar.f32(128, 512)) for _ in range(3)]
        its = [(t0, n, which, ch) for (t0, n) in GROUPS for which in range(2) for ch in range(4)]

        def fp_A(it):
            t0, n, which, ch = it
            col0 = which * 512 + ch * 128
            ps = nb()
            for c in range(8):
                P.mm(ps[:, 0:n], wf[:, c, col0:col0 + 128], xnT[:, c, t0:t0 + n], start=(c == 0), stop=(c == 7))
            return ps

        def fp_B(i, it, ps):
            t0, n, which, ch = it
            col0 = which * 512 + ch * 128
            z_, sq_, r_, kf_ = ptmp[i % 3]
            z = z_[:, 0:n]
            sq = sq_[:, 0:n]
            P.act(z, ps[:, 0:n], AF.Identity, bias=bcol(col0))
            P.act(sq, ps[:, 0:n], AF.Square, bias=bcol(col0))
            ps2 = nb()
            P.mm(ps2[:, 0:n], bdones, sq)
            r = r_[:, 0:n]
            P.act(r, ps2[:, 0:n], AF.Ln, bias=eps_col, scale=1.0 / 64)
            P.act(r, r, AF.Exp, scale=-0.5)
            if which == 0:
                P.stt(qT[:, ch, t0:t0 + n], z, gq8, r, ALU.mult, ALU.mult)
            else:
                kf = kf_[:, 0:n]
                P.stt(kf, z, gk2, r, ALU.mult, ALU.mult)
                P.copy(kT[:, ch, t0:t0 + n], kf)
                P.dma("sp", O["fox_kT"][ch * 128:(ch + 1) * 128, t0:t0 + n], kf)

        vtmp = [ar.f32(128, 512), ar.f32(128, 512)]

        def fp_V(ti):
            t0, L = TILES[ti]
            ps = nb()
            for c in range(8):
                P.mm(ps[0:L, :], xnT[:, c, t0:t0 + L], wf[:, c, 1024:1536], start=(c == 0), stop=(c == 7))
            vf = vtmp[ti % 2]
            P.tt(vf[0:L], ps[0:L, :], bvrow[0:L], ALU.add)
            P.copy(V[0:L, ti, :], vf[0:L])
            P.dma("sp", O["fox_v"][t0:t0 + L, :], vf[0:L])

        psq = [fp_A(its[0])]
        vnext = 0
        for i, it in enumerate(its):
            if i + 1 < len(its):
                psq.append(fp_A(its[i + 1]))
            if i % 2 == 1 and vnext < 17:
                fp_V(vnext)
                vnext += 1
            fp_B(i, it, psq[i])
        while vnext < 17:
            fp_V(vnext)
            vnext += 1
        ar.release(m_w)
        dbg("qT", qT[:, 0, :])
        dbg("kT", kT[:, 0, :])

        P.mark("s1b_foxproj")
        pbufs = [ar.bf(128, 2, 512) for _ in range(4)]
        pctr = [0]
        fctr = [0, 0]
        rbuf = ar.f32(128, 512)
        KMAX = PAST + TS
        qas = [ar.bf(128, NT), ar.bf(128, NT)]
        kas_p = [ar.bf(128, TP), ar.bf(128, TP)]
        kas_s = [ar.bf(128, KMAX), ar.bf(128, KMAX)]
        vexts_p = [ar.bf(128, 16, 128), ar.bf(128, 16, 128)]
        vexts_s = [ar.bf(128, 33, 128), ar.bf(128, 33, 128)]
        for i in range(2):
            P.memset(qas[i][64:68, :], -1.0)
            P.memset(kas_p[i][64:68, :], 1.0)
            P.memset(kas_s[i][64:68, :], 1.0)
        for ve in (vexts_p, vexts_s):
            P.memset(ve[0][:, :, 64:128], 1.0)
            P.memset(ve[1][:, :, 0:64], 1.0)
        ckT_d = I["ckT"]

        def fox_attend(h, qa, qc0, qn, keys, ka, vext):
            ch, half = h // 2, h % 2
            pb = half * 64
            po = (1 - half) * 64
            acc = PS[:, 6 + (fctr[0] % 2), :]
            fctr[0] += 1
            full = [k for k in keys if k[3] is None]
            diag = [k for k in keys if k[3] is not None]
            per = 2 if qn > 64 else 8
            units = [full[i:i + per] for i in range(0, len(full), per)] + [[k] for k in diag]
            nk = len(keys)
            done = [0]

            def emit_front(unit):
                base = 2 * (fctr[1] % 3)
                fctr[1] += 1
                pt = pbufs[pctr[0] % len(pbufs)]
                pctr[0] += 1
                pvs = []
                if len(unit) == 1:
                    kc0, vti, L, doff = unit[0]
                    q_lo = 0 if doff is None else doff
                    sps = PS[:, base, :]
                    P.mm(sps[0:L, q_lo:qn], ka[0:68, kc0:kc0 + L], qa[0:68, qc0 + q_lo:qc0 + qn], start=True, stop=(doff is None))
                    if doff is not None:
                        dq = min(L, qn - q_lo)
                        P.mm(sps[0:L, q_lo:q_lo + dq], ident_b[0:L, 0:L], negmask[0:L, 0:dq], start=False, stop=True)
                    P.act(pt[0:L, 0, q_lo:qn], sps[0:L, q_lo:qn], AF.Exp)
                    pvs.append((acc[:, q_lo:qn], vext[0:L, vti, :], pt[0:L, 0, q_lo:qn]))
                elif qn > 64:
                    for j, (kc0, vti, L, doff) in enumerate(unit):
                        P.mm(PS[:, base + j, 0:qn], ka[0:68, kc0:kc0 + L], qa[0:68, qc0:qc0 + qn])
                        pvs.append((acc[:, 0:qn], vext[0:L, vti, :], pt[:, j, 0:qn]))
                    P.act(pt[:, 0:2, 0:qn], PS[:, base:base + 2, 0:qn], AF.Exp)
                else:
                    m = len(unit)
                    for j, (kc0, vti, L, doff) in enumerate(unit):
                        P.mm(PS[:, base, j * qn:(j + 1) * qn], ka[0:68, kc0:kc0 + L], qa[0:68, qc0:qc0 + qn])
                        pvs.append((acc[:, 0:qn], vext[0:L, vti, :], pt[:, 0, j * qn:(j + 1) * qn]))
                    P.act(pt[:, 0, 0:m * qn], PS[:, base, 0:m * qn], AF.Exp)
                return pvs

            def emit_pv(pvs):
                for (o_, l_, r_) in pvs:
                    P.mm(o_, l_, r_, start=(done[0] == 0), stop=(done[0] == nk - 1))
                    done[0] += 1

            LA = 2
            q_ = [emit_front(units[u]) for u in range(min(LA, len(units)))]
            for u in range(len(units)):
                if u + LA < len(units):
                    q_.append(emit_front(units[u + LA]))
                emit_pv(q_[u])
            P.act(rbuf[pb:pb + 64, 0:qn], acc[po:po + 64, 0:qn], AF.Ln)
            P.act(rbuf[pb:pb + 64, 0:qn], rbuf[pb:pb + 64, 0:qn], AF.Exp, scale=-1.0)
            P.tt(aT[pb:pb + 64, ch, qc0:qc0 + qn], acc[pb:pb + 64, 0:qn], rbuf[pb:pb + 64, 0:qn], ALU.mult)

        def sample_loads(h):
            ch, half = h // 2, h % 2
            pb = half * 64
            ka, vext = kas_s[h % 2], vexts_s[h % 2]
            vo = 0 if half == 0 else 64
            P.dma("pool", ka[0:64, 0:PAST], ckT_d[h * 64:(h + 1) * 64, :])
            P.dma("sp", ka[64:65, 0:PAST], cchi[h:h + 1, :])
            P.dma("sp", ka[65:66, 0:PAST], cclo[h:h + 1, :])
            P.dma("sp", ka[0:64, PAST:KMAX], kT[pb:pb + 64, ch, TP:NT])
            P.dma("sp", ka[64:65, PAST:KMAX], chi[h:h + 1, TP:NT])
            P.dma("sp", ka[65:66, PAST:KMAX], clo[h:h + 1, TP:NT])
            P.dma("pool", vext[:, 0:32, vo:vo + 64], I["cv"][h])
            P.copy(vext[0:64, 32, vo:vo + 64], V[0:64, 16, h * 64:h * 64 + 64], eng="pool")

        sample_loads(0)
        keys_s = [(ti * 128, ti, 128, None) for ti in range(32)] + [(PAST, 32, 64, 0)]
        for h in range(8):
            ch, half = h // 2, h % 2
            pb = half * 64
            qa, ka, vext = qas[h % 2], kas_p[h % 2], vexts_p[h % 2]
            vo = 0 if half == 0 else 64
            P.dma("sp", qa[0:64, :], qT[pb:pb + 64, ch, :])
            P.dma("sp", qa[66:67, :], chi[h:h + 1, :])
            P.dma("sp", qa[67:68, :], clo[h:h + 1, :])
            P.dma("sp", ka[0:64, 0:TP], kT[pb:pb + 64, ch, 0:TP])
            P.dma("sp", ka[64:65, 0:TP], chi[h:h + 1, 0:TP])
            P.dma("sp", ka[65:66, 0:TP], clo[h:h + 1, 0:TP])
            P.copy(vext[:, 0:16, vo:vo + 64], V[:, 0:16, h * 64:h * 64 + 64], eng="pool")
            if h + 1 < 8:
                sample_loads(h + 1)
            for gi in range(4):
                keys = []
                for ti in range(gi * 4 + 4):
                    doff = None if ti < gi * 4 else (ti - gi * 4) * 128
                    keys.append((ti * 128, ti, 128, doff))
                fox_attend(h, qa, gi * 512, 512, keys, ka, vext)
            fox_attend(h, qa, TP, TS, keys_s, kas_s[h % 2], vexts_s[h % 2])
            if h == 0:
                dbg("aT0", aT[0:64, 0, 0:TP])
        dbg("aT", aT[:, 0, :])
        P.mark("fox_prompt")
        dbg("aTs", aT[:, 0, TP:NT])
        ar.release(s1)
        bT_off = ar.mark()
        bT = ar.bf(128, 4, NT)
        s1 = ar.mark()

        P.mark("fox_sample")
        wm = ar.bf(128, 8, 2048)
        P.dma("pool", wm[:, :, 0:1536], wv("w_in")[:, :, 1544:3080])
        P.dma("pool", wm[:, :, 1536:2048], wv("w_in")[:, :, 3088:3600])
        bmv = ar.f32(128, 512)
        P.dma("sp", bmv, I["b_row"][:, 2568:3080].partition_broadcast(128))
        CT = ar.f32(128, 4, 129)
        pc = ar.f32(128, 8, 515)
        qks = [ar.bf(128, 8, 512), ar.bf(128, 8, 512)]
        qkc = [qks[0]]
        sigo = ar.f32(128, 4, 512)
        rdb = ar.f32(128, 4, 512)
        caccs = [ar.f32(128, 512), ar.f32(128, 512)]
        ones3 = ones_f.unsqueeze(1).broadcast_to([128, 4, 128])
        msets = []
        for _ in range(2):
            msets.append(dict(vfull=ar.f32(128, 512), vw=ar.bf(128, 4, 129), wgb=ar.bf(128, 4, 128),
                              ktok=ar.bf(128, 4, 128), ET=ar.bf(128, 4, 128), Cq=ar.bf(128, 4, 128),
                              nbm=ar.bf(128, 4, 128), tden=ar.f32(128, 4, 128), hs=ar.f32(128, 4, 128),
                              hsq=ar.bf(128, 4, 128)))
            msets[-1]["rr"] = msets[-1]["tden"]
        B_v = PS[:, 0, :]
        B_tr = PS[:, 1, :].bitcast(BF16)
        B_S = PS[:, 2, :]
        B_k = [PS[:, 3, :], PS[:, 4, :]]
        B_Y = PS[:, 5, :]
        B_D = PS[:, 6, :]
        B_q = PS[:, 7, :]
        P.memset(CT, 0.0)
        P.memset(pc[:, :, 0:3], 0.0)
        wb3 = ar.f32(128, 8)
        for j in range(8):
            P.tt(wb3[:, j:j + 1], mconv_w[:, j, 3:4], bcol(1544 + 128 * j), ALU.mult)

        def ml_step(prev, cur):
            if prev is not None:
                tiP, ttP, LP, loP, SP = prev
                Dv = r3(B_D[:, 0:4 * LP], LP)
                Yv = r3(B_Y[:, 0:4 * LP], LP)
                td = SP["tden"][:, :, 0:LP]
                hs_ = SP["hs"][:, :, 0:LP]
                hq_ = SP["hsq"][:, :, 0:LP]
                rr_ = SP["rr"][:, :, 0:LP]
            if cur is not None:
                ti, tt0, L, lo, S = cur
                wg3 = wg_tok[0:L, ti, :].unsqueeze(2)
            if prev is not None:
                for h in range(4):
                    q_h = qkc[0][:, h, loP:loP + LP]
                    P.mm(B_Y[:, h * LP:(h + 1) * LP], SP["Cq"][:, h, :], q_h, start=True, stop=False)
                    P.mm(B_Y[:, h * LP:(h + 1) * LP], SP["vw"][0:LP, h, 0:128], SP["ET"][0:LP, h, 0:LP], start=False, stop=True)
                for h in range(4):
                    q_h = qkc[0][:, h, loP:loP + LP]
                    P.mm(B_D[:, h * LP:(h + 1) * LP], SP["nbm"][:, h, :], q_h, start=True, stop=False)
                    P.mm(B_D[:, h * LP:(h + 1) * LP], SP["wgb"][0:LP, h, :], SP["ET"][0:LP, h, 0:LP], start=False, stop=True)
            if cur is not None:
                for c in range(8):
                    P.mm(B_v[0:L, :], xnT[:, c, tt0:tt0 + L], wm[:, c, 1024:1536], start=(c == 0), stop=(c == 7))
            if prev is not None:
                P.act(td, Dv, AF.Abs)
                P.tt(td, td, rdb[:, :, loP:loP + LP], ALU.max)
                P.act(td, td, AF.Ln)
                P.act(td, td, AF.Exp, scale=-1.0)
            if cur is not None:
                P.tt(S["vfull"][0:L], B_v[0:L, :], bmv[0:L], ALU.add)
                P.tt(S["vw"][0:L, :, 0:128], r3(S["vfull"][0:L], 128), wg3.broadcast_to([L, 4, 128]), ALU.mult)
                P.copy(S["vw"][0:L, :, 128:129], wg3)
                P.tt(S["wgb"][0:L], ones_f[0:L].unsqueeze(1).broadcast_to([L, 4, 128]), wg3.broadcast_to([L, 4, 128]), ALU.mult)
                for h in range(4):
                    P.tr(B_tr[0:L, h * 128:(h + 1) * 128], qkc[0][:, 4 + h, lo:lo + L], ident_b)
                P.act(S["ktok"][0:L], r3(B_tr[0:L, 0:512], 128), AF.Copy)
                for h in range(4):
                    P.mm(B_S[0:L, h * L:(h + 1) * L], qkc[0][:, 4 + h, lo:lo + L], qkc[0][:, h, lo:lo + L])
            if prev is not None:
                P.tt(hs_, Yv, td, ALU.mult)
                P.tt(hs_, hs_, sigo[:, :, loP:loP + LP], ALU.mult)
                P.act(hq_, hs_, AF.Square)
                for h in range(4):
                    P.mm(B_q[:, h * LP:(h + 1) * LP], ones_b, SP["hsq"][:, h, 0:LP])
                P.act(rr_, r3(B_q[:, 0:4 * LP], LP), AF.Ln, bias=eps_col, scale=1.0 / 128)
                P.act(rr_, rr_, AF.Exp, scale=-0.5)
            if cur is not None:
                P.stt(S["ET"][0:L, :, 0:L], r3(B_S[0:L, 0:4 * L], L), QSC,
                      trimask[0:L, 0:L].unsqueeze(1).broadcast_to([L, 4, L]), ALU.mult, ALU.mult)
                for h in range(4):
                    P.mm(B_k[h // 2][:, (h % 2) * 129:(h % 2) * 129 + 129], S["ktok"][0:L, h, :], S["vw"][0:L, h, :])
                P.tt(CT, CT, dec_b[:, :, ti:ti + 1].broadcast_to([128, 4, 129]), ALU.mult)
                P.act(S["Cq"], CT[:, :, 0:128], AF.Copy, scale=QSC)
                P.stt(S["nbm"], ones3, QSC, CT[:, :, 128:129].broadcast_to([128, 4, 128]), ALU.mult, ALU.mult)
            if prev is not None:
                P.tt(hs_, hs_, rr_, ALU.mult)
                P.tt(bT[:, :, ttP:ttP + LP], hs_, mhn.unsqueeze(2).broadcast_to([128, 4, LP]), ALU.mult)
            if cur is not None:
                P.tt(CT[:, 0:2, :], CT[:, 0:2, :], r3(B_k[0][:, 0:258], 129), ALU.add)
                P.tt(CT[:, 2:4, :], CT[:, 2:4, :], r3(B_k[1][:, 0:258], 129), ALU.add)

        def ml_proj_parts(gi):
            t0, n = GROUPS[gi]
            qk = qks[gi % 2]

            def part(pj):
                if pj == 0 and gi == 4:
                    P.dma("sp", pc[:, :, 0:3], I["sconv"])
                for j in (2 * pj, 2 * pj + 1):
                    ps = nb()
                    for c in range(8):
                        P.mm(ps[:, 0:n], wm[:, c, j * 128:(j + 1) * 128], xnT[:, c, t0:t0 + n], start=(c == 0), stop=(c == 7))
                    P.act(pc[:, j, 3:3 + n], ps[:, 0:n], AF.Identity, bias=bcol(1544 + 128 * j))
                    P.act(caccs[j % 2][:, 0:n], ps[:, 0:n], AF.Identity, scale=mconv_w[:, j, 3:4], bias=wb3[:, j:j + 1])
                for j in (2 * pj, 2 * pj + 1):
                    cacc = caccs[j % 2]
                    for tap in (0, 1, 2):
                        P.stt(cacc[:, 0:n], pc[:, j, tap:tap + n], mconv_w[:, j, tap:tap + 1], cacc[:, 0:n], ALU.mult, ALU.add)
                    P.act(qk[:, j, 0:n], cacc[:, 0:n], AF.Silu, bias=mconv_b[:, j:j + 1])
                if pj == 3:
                    if gi == 3:
                        P.dma("sp", O["ml_convT"][0], pc[:, :, n:n + 3])
                    if gi == 4:
                        P.dma("sp", O["ml_convT"][1], pc[:, :, n:n + 3])
                    if gi < 3:
                        P.copy(pc[:, :, 0:3], pc[:, :, n:n + 3])
            return [lambda pj=pj: part(pj) for pj in range(4)]

        def ml_gates(gi):
            t0, n = GROUPS[gi]
            for h in range(4):
                ps = nb()
                for c in range(8):
                    P.mm(ps[:, 0:n], wm[:, c, 1536 + h * 128:1536 + (h + 1) * 128], xnT[:, c, t0:t0 + n], start=(c == 0), stop=(c == 7))
                P.act(sigo[:, h, 0:n], ps[:, 0:n], AF.Sigmoid, bias=bcol(3088 + 128 * h))
            for h in range(4):
                ps = nb()
                P.mm(ps[:, 0:n], sel4[:, h, :], rdl[:, t0:t0 + n])
                P.act(rdb[:, h, 0:n], ps[:, 0:n], AF.Exp)

        for p_ in ml_proj_parts(0):
            p_()
        ti_global = 0
        for gi, (t0, n) in enumerate(GROUPS):
            if gi == 4:
                P.dma("sp", O["ml_cT"][0], CT[:, :, 0:128])
                P.dma("sp", O["ml_n"][0], CT[:, :, 128])
                P.dma("sp", CT[:, :, 0:128], I["sC"])
                P.dma("sp", CT[:, :, 128], I["sn"])
            ml_gates(gi)
            qkc[0] = qks[gi % 2]
            nparts = ml_proj_parts(gi + 1) if gi + 1 < len(GROUPS) else []
            tiles = [(tt0, L) for (tt0, L) in TILES if t0 <= tt0 < t0 + n]
            prev = None
            for k_, (tt0, L) in enumerate(tiles):
                ti = ti_global
                ti_global += 1
                cur = (ti, tt0, L, tt0 - t0, msets[ti % 2])
                ml_step(prev, cur)
                prev = cur
                if k_ < len(nparts):
                    nparts[k_]()
            ml_step(prev, None)
        P.dma("sp", O["ml_cT"][1], CT[:, :, 0:128])
        P.dma("sp", O["ml_n"][1], CT[:, :, 128])
        ar.release(s1)
        dbg("bT", bT[:, 0, :])
        mT_off = ar.mark()
        mT = ar.bf(128, 4, NT)
        s1 = ar.mark()

        P.mark("mlstm")
        wq = ar.bf(128, 8, 512)
        P.dma("pool", wq, wv("w_in")[:, :, 3600:4112])
        wkv = ar.bf(128, 8, 1024)
        P.dma("pool", wkv, wv("w_mem_kv"))
        memx = ar.f32(128, 8, 256)
        P.dma("sp", memx, I["memT"].rearrange("(c p) n -> p c n", p=128))
        rbm = ar.f32(128, 256)
        rms_bcast(memx, 8, 256, 1024.0, rbm)
        memn = ar.bf(128, 8, 256)
        for c in range(8):
            P.stt(memn[:, c, :], memx[:, c, :], g_mem[:, c:c + 1], rbm, ALU.mult, ALU.mult)
        mkT = ar.bf(128, 2, 4, 256)
        mv = ar.bf(128, 2, 2, 512)
        tmpf = ar.f32(128, 512)
        for chh in range(4):
            ps = nb()
            for c in range(8):
                P.mm(ps[:, 0:256], wkv[:, c, chh * 128:(chh + 1) * 128], memn[:, c, :], start=(c == 0), stop=(c == 7))
            P.copy(tmpf[:, 0:256], ps[:, 0:256])
            P.copy(mkT[:, 0, chh, :], tmpf[:, 0:256])
            P.dma("sp", O["mem_kT"][chh * 128:(chh + 1) * 128, :], tmpf[:, 0:256])
        for mt in range(2):
            ps = nb()
            for c in range(8):
                P.mm(ps[:, :], memn[:, c, mt * 128:(mt + 1) * 128], wkv[:, c, 512:1024], start=(c == 0), stop=(c == 7))
            P.copy(tmpf, ps)
            P.copy(mv[:, 0, mt, :], tmpf)
            P.dma("sp", O["mem_v"][mt * 128:(mt + 1) * 128, :], tmpf)
        P.dma("pool", mkT[:, 1], I["cmkT"].rearrange("(c p) n -> p c n", p=128))
        P.dma("pool", mv[:, 1], I["cmv"].rearrange("(t p) f -> p t f", p=128))
        qhs = [ar.bf(128, 512), ar.bf(128, 512)]
        ptms = [ar.bf(128, 2, 512), ar.bf(128, 2, 512)]
        rlms = [ar.f32(128, 512), ar.f32(128, 512)]
        mits = [(gi, h) for gi in range(len(GROUPS)) for h in range(4)]

        def mm_A(i):
            gi, h = mits[i]
            t0, n = GROUPS[gi]
            ps = nb()
            for c in range(8):
                P.mm(ps[:, 0:n], wq[:, c, h * 128:(h + 1) * 128], xnT[:, c, t0:t0 + n], start=(c == 0), stop=(c == 7))
            P.act(qhs[i % 2][:, 0:n], ps[:, 0:n], AF.Identity, bias=bcol(3600 + 128 * h))

        def mm_B(i):
            gi, h = mits[i]
            t0, n = GROUPS[gi]
            seq = 0 if gi < 4 else 1
            qh, ptm, rlm = qhs[i % 2], ptms[i % 2], rlms[i % 2]
            for mt in range(2):
                sps = nb()
                P.mm(sps[:, 0:n], mkT[:, seq, h, mt * 128:(mt + 1) * 128], qh[:, 0:n])
                P.act(ptm[:, mt, 0:n], sps[:, 0:n], AF.Exp, scale=QSC)
            ops_ = nb()
            lps = nb()
            for mt in range(2):
                P.mm(ops_[:, 0:n], mv[:, seq, mt, h * 128:(h + 1) * 128], ptm[:, mt, 0:n], start=(mt == 0), stop=(mt == 1))
            for mt in range(2):
                P.mm(lps[:, 0:n], ones_b, ptm[:, mt, 0:n], start=(mt == 0), stop=(mt == 1))
            P.act(rlm[:, 0:n], lps[:, 0:n], AF.Ln)
            P.act(rlm[:, 0:n], rlm[:, 0:n], AF.Exp, scale=-1.0)
            P.tt(mT[:, h, t0:t0 + n], ops_[:, 0:n], rlm[:, 0:n], ALU.mult)

        mm_A(0)
        for i in range(len(mits)):
            if i + 1 < len(mits):
                mm_A(i + 1)
            mm_B(i)
        dbg("mT", mT[:, 0, :])
        ar.release(s1)

        P.mark("mem")
        mergedT = ar.bf(128, 8, NT)
        wo = ar.bf(128, 8, 1024)
        s2 = ar.mark()
        wgs = [[ar.bf(128, 8, 128) for b in range(3)] for _ in range(2)]
        wbs = [[ar.bf(128, 4, 128) for b in range(3)] for _ in range(2)]
        sg = [ar.f32(128, 512) for b in range(3)]
        macc = ar.f32(128, 512)
        mtmp = ar.f32(128, 512)
        brs = ("w_br_a", "w_br_b", "w_br_m")
        srcs = (aT, bT, mT)
        for oc in range(8):
            wg_ = wgs[oc % 2]
            wb_ = wbs[oc % 2]
            for b in range(3):
                g0 = 4112 + b * 1024 + oc * 128
                P.dma("pool", wg_[b], wv("w_in")[:, :, g0:g0 + 128])
                P.dma("pool", wb_[b], wv(brs[b])[:, :, oc * 128:(oc + 1) * 128])
            if oc == 2:
                P.dma("pool", wo, wv("w_out"))
            for (t0, n) in GROUPS:
                pp = []
                for b in range(3):
                    g0 = 4112 + b * 1024 + oc * 128
                    ps = nb()
                    for c in range(8):
                        P.mm(ps[:, 0:n], wg_[b][:, c, :], xnT[:, c, t0:t0 + n], start=(c == 0), stop=(c == 7))
                    P.act(sg[b][:, 0:n], ps[:, 0:n], AF.Sigmoid, bias=bcol(g0))
                for b in range(3):
                    ps = nb()
                    for c in range(4):
                        P.mm(ps[:, 0:n], wb_[b][:, c, :], srcs[b][:, c, t0:t0 + n], start=(c == 0), stop=(c == 3))
                    pp.append(ps)
                P.tt(macc[:, 0:n], sg[0][:, 0:n], pp[0][:, 0:n], ALU.mult)
                P.tt(mtmp[:, 0:n], sg[1][:, 0:n], pp[1][:, 0:n], ALU.mult)
                P.tt(macc[:, 0:n], macc[:, 0:n], mtmp[:, 0:n], ALU.add)
                P.tt(mtmp[:, 0:n], sg[2][:, 0:n], pp[2][:, 0:n], ALU.mult)
                P.tt(mergedT[:, oc, t0:t0 + n], macc[:, 0:n], mtmp[:, 0:n], ALU.add)
        dbg("mergedT", mergedT[:, 0, :])
        ar.release(s2)

        P.mark("s2_merge")
        oTs = [ar.f32(128, 8, 512), ar.f32_at(aT_off, 128, 8, 512)]
        xss = [ar.f32(128, 8, 512), ar.f32_at(bT_off, 128, 8, 512)]
        rbs = [ar.f32(128, 512), ar.f32(128, 512)]
        sqs = [ar.bf(128, 8, 512), ar.bf_at(mT_off, 128, 8, 512)]
        x1s3w = x1scr.rearrange("(c p) n -> p c n", p=128)

        def ob_A(gi):
            t0, n = GROUPS[gi]
            oT, xs = oTs[gi % 2], xss[gi % 2]
            P.dma("sp", xs[:, :, 0:n], xT3[:, :, t0:t0 + n])
            for oc in range(8):
                ps = nb()
                for c in range(8):
                    P.mm(ps[:, 0:n], wo[:, c, oc * 128:(oc + 1) * 128], mergedT[:, c, t0:t0 + n], start=(c == 0), stop=(c == 7))
                P.act(oT[:, oc, 0:n], ps[:, 0:n], AF.Copy)
                P.act(sqs[gi % 2][:, oc, 0:n], ps[:, 0:n], AF.Square)

        def ob_B(gi):
            t0, n = GROUPS[gi]
            oT, xs, rb, sq = oTs[gi % 2], xss[gi % 2], rbs[gi % 2], sqs[gi % 2]
            ps = nb()
            for c in range(8):
                P.mm(ps[:, 0:n], ones_b, sq[:, c, 0:n], start=(c == 0), stop=(c == 7))
            P.act(rb[:, 0:n], ps[:, 0:n], AF.Ln, bias=eps_col, scale=1.0 / 1024.0)
            P.act(rb[:, 0:n], rb[:, 0:n], AF.Exp, scale=-0.5)
            ps2 = nb()
            for oc in range(8):
                P.stt(oT[:, oc, 0:n], oT[:, oc, 0:n], g_post[:, oc:oc + 1], rb[:, 0:n], ALU.mult, ALU.mult)
                P.tt(xs[:, oc, 0:n], xs[:, oc, 0:n], oT[:, oc, 0:n], ALU.add)
                P.act(sq[:, oc, 0:n], xs[:, oc, 0:n], AF.Square)
                P.mm(ps2[:, 0:n], ones_b, sq[:, oc, 0:n], start=(oc == 0), stop=(oc == 7))
            P.dma("sp", x1s3w[:, :, t0:t0 + n], xs[:, :, 0:n])
            P.act(rb[:, 0:n], ps2[:, 0:n], AF.Ln, bias=eps_col, scale=1.0 / 1024.0)
            P.act(rb[:, 0:n], rb[:, 0:n], AF.Exp, scale=-0.5)
            for c in range(8):
                P.stt(xnT[:, c, t0:t0 + n], xs[:, c, 0:n], g_fpre[:, c:c + 1], rb[:, 0:n], ALU.mult, ALU.mult)

        ob_A(0)
        for gi in range(len(GROUPS)):
            if gi + 1 < len(GROUPS):
                ob_A(gi + 1)
            ob_B(gi)
        dbg("x1nT", xnT[:, 0, :])
        ar.release(m_after_xn)

        P.mark("s2b_out")
        hidT = ar.bf(128, 22, NT)
        wd3 = wv("w_down")
        wd1 = ar.bf(128, 22, 512)
        m_h = ar.mark()
        wus = [ar.bf(128, 8, 256), ar.bf(128, 8, 256)]
        apre_p = ar.f32(128, 2 + TP)
        bpre_p = ar.f32(128, 2 + TP)
        apre_s = ar.f32(128, 2 + TS)
        bpre_s = ar.f32(128, 2 + TS)
        accas = [ar.f32(128, 512) for _ in range(3)]
        accbs = [ar.f32(128, 512) for _ in range(3)]
        gas = [ar.f32(128, 512) for _ in range(3)]
        fit = [0]
        ftail = [None]
        w_up3 = wv("w_up")
        for c in range(22):
            wu = wus[c % 2]
            P.dma("pool", wu[:, :, 0:128], w_up3[:, :, c * 128:(c + 1) * 128])
            P.dma("pool", wu[:, :, 128:256], w_up3[:, :, 2816 + c * 128:2816 + (c + 1) * 128])
            if c == 2:
                P.dma("pool", wd1, wd3[:, :, 512:1024])
            ja, jb = c, 22 + c
            for gi, (t0, n) in enumerate(GROUPS):
                if gi < 4:
                    apre, bpre = apre_p[:, t0:t0 + n + 2], bpre_p[:, t0:t0 + n + 2]
                else:
                    apre, bpre = apre_s, bpre_s
                if gi == 0:
                    P.memset(apre_p[:, 0:2], 0.0)
                    P.memset(bpre_p[:, 0:2], 0.0)
                if gi == 4:
                    P.dma("sp", apre[:, 0:2], I["sfconv"][:, ja, :])
                    P.dma("sp", bpre[:, 0:2], I["sfconv"][:, jb, :])
                acca, accb, ga = accas[fit[0] % 3], accbs[fit[0] % 3], gas[fit[0] % 3]
                fit[0] += 1
                for (pre, off) in ((apre, 0), (bpre, 128)):
                    ps = nb()
                    for k in range(8):
                        P.mm(ps[:, 0:n], wu[:, k, off:off + 128], xnT[:, k, t0:t0 + n], start=(k == 0), stop=(k == 7))
                    P.act(pre[:, 2:2 + n], ps[:, 0:n], AF.Copy)
                    if off == 128:
                        P.act(accb[:, 0:n], ps[:, 0:n], AF.Identity, scale=fconv_w[:, jb, 2:3])
                    else:
                        P.act(acca[:, 0:n], ps[:, 0:n], AF.Identity, scale=fconv_w[:, ja, 2:3])
                P.stt(acca[:, 0:n], apre[:, 0:n], fconv_w[:, ja, 0:1], acca[:, 0:n], ALU.mult, ALU.add)
                P.stt(acca[:, 0:n], apre[:, 1:1 + n], fconv_w[:, ja, 1:2], acca[:, 0:n], ALU.mult, ALU.add)
                P.stt(accb[:, 0:n], bpre[:, 0:n], fconv_w[:, jb, 0:1], accb[:, 0:n], ALU.mult, ALU.add)
                P.stt(accb[:, 0:n], bpre[:, 1:1 + n], fconv_w[:, jb, 1:2], accb[:, 0:n], ALU.mult, ALU.add)
                if ftail[0] is not None:
                    ftail[0]()

                def _tail(ga=ga, acca=acca, accb=accb, n=n, ja=ja, jb=jb, c=c, t0=t0):
                    P.act(ga[:, 0:n], acca[:, 0:n], AF.Gelu_apprx_tanh, bias=fconv_b[:, ja:ja + 1])
                    P.stt(hidT[:, c, t0:t0 + n], accb[:, 0:n], fconv_b[:, jb:jb + 1], ga[:, 0:n], ALU.add, ALU.mult)
                ftail[0] = _tail
                if gi in (3, 4):
                    so = 0 if gi == 3 else 1
                    P.dma("sp", O["ffn_convT"][so][:, ja, :], apre[:, n:n + 2])
                    P.dma("sp", O["ffn_convT"][so][:, jb, :], bpre[:, n:n + 2])
        ftail[0]()
        dbg("hidT", hidT[:, 0, :])
        ar.release(m_h)
        P.mark("ffn_up")
        wd0 = ar.bf_at(xn_off, 128, 22, 512)
        P.dma("pool", wd0, wd3[:, :, 0:512])
        oT = ar.f32(128, 8, 512)
        xs = ar.f32(128, 8, 512)
        rb = ar.f32(128, 512)
        x1s3 = x1scr.rearrange("(c p) n -> p c n", p=128)
        yT3 = O["yT"].rearrange("(c p) n -> p c n", p=128)
        for gi, (t0, n) in enumerate(GROUPS):
            P.dma("sp", xs[:, :, 0:n], x1s3[:, :, t0:t0 + n])
            for oc in range(8):
                wd = wd0 if oc < 4 else wd1
                o4 = oc % 4
                ps = nb()
                for c in range(22):
                    P.mm(ps[:, 0:n], wd[:, c, o4 * 128:(o4 + 1) * 128], hidT[:, c, t0:t0 + n], start=(c == 0), stop=(c == 21))
                P.act(oT[:, oc, 0:n], ps[:, 0:n], AF.Copy)
            rms_bcast(oT[:, :, 0:n], 8, n, 1024.0, rb[:, 0:n])
            for oc in range(8):
                P.stt(oT[:, oc, 0:n], oT[:, oc, 0:n], g_fpost[:, oc:oc + 1], rb[:, 0:n], ALU.mult, ALU.mult)
                P.tt(xs[:, oc, 0:n], xs[:, oc, 0:n], oT[:, oc, 0:n], ALU.add)
            P.dma("sp", yT3[:, :, t0:t0 + n], xs[:, :, 0:n])
        P.mark("ffn_down")
        P.emit(Kq={'pool': 3})
    except _Stop:
        pass
    return nc, DBG, P, 0


_CACHE = {}


def _get_nc(debug=()):
    key = tuple(debug)
    if key not in _CACHE:
        _CACHE[key] = build(debug)
    return _CACHE[key]


def _consts():
    ident = np.eye(128, dtype=np.float32)
    s = np.arange(128)
    trimask = (s[None, :] >= s[:, None]).astype(np.float32)
    negmask = np.where(s[:, None] <= s[None, :], 0.0, -30000.0).astype(np.float32)
    bd = np.zeros((128, 128), np.float32)
    bd[:64, :64] = 1.0
    bd[64:, 64:] = 1.0
    sel4 = np.zeros((4, 4, 128), np.float32)
    for h in range(4):
        sel4[h, h, :] = 1.0
    return dict(ident=ident, trimask=trimask, negmask=negmask, bdones=bd, sel4=sel4)


def _fm(v):
    return np.ascontiguousarray(v.reshape(-1, 128).T)


def _prep_shared(inp):
    f = lambda a: np.ascontiguousarray(np.asarray(a, dtype=np.float32))
    b_in = f(inp["b_in"])[0]
    d = dict(
        w_in=f(inp["w_in"])[0],
        b_fm=np.ascontiguousarray(np.stack([b_in[s:s + 128] for s in FM_STARTS], axis=1)),
        bg_fox=np.ascontiguousarray(b_in[1536:1544][:, None]),
        bg_i=np.ascontiguousarray(b_in[3080:3084][:, None]),
        bg_f=np.ascontiguousarray(b_in[3084:3088][:, None]),
        b_row=np.ascontiguousarray(b_in[None, :]),
        g_pre=_fm(f(inp["norm_mix_pre"])[0]),
        gq2=np.ascontiguousarray(np.concatenate([f(inp["fox_q_norm"])[0]] * 2)[:, None]),
        gk2=np.ascontiguousarray(np.concatenate([f(inp["fox_k_norm"])[0]] * 2)[:, None]),
        mconv_w=np.ascontiguousarray(f(inp["mlstm_conv_w"])[0].reshape(4, 8, 128).transpose(2, 1, 0)),
        mconv_b=_fm(f(inp["mlstm_conv_b"])[0]),
        mhn=np.ascontiguousarray(f(inp["mlstm_head_norm"])[0].T),
        g_mem=_fm(f(inp["norm_mem"])[0]),
        w_mem_kv=f(inp["w_mem_kv"])[0],
        w_br_a=f(inp["w_br_a"])[0], w_br_b=f(inp["w_br_b"])[0], w_br_m=f(inp["w_br_m"])[0],
        w_out=f(inp["w_out"])[0],
        g_post=_fm(f(inp["norm_mix_post"])[0]),
        g_fpre=_fm(f(inp["norm_ffn_pre"])[0]),
        w_up=f(inp["w_up"])[0],
        fconv_w=np.ascontiguousarray(f(inp["ffn_conv_w"])[0].reshape(3, 44, 128).transpose(2, 1, 0)),
        fconv_b=_fm(f(inp["ffn_conv_b"])[0]),
        w_down=f(inp["w_down"])[0],
        g_fpost=_fm(f(inp["norm_ffn_post"])[0]),
    )
    d.update(_consts())
    return d


def _prep_core(inp, b):
    f = lambda a: np.asarray(a, dtype=np.float32)
    c = np.ascontiguousarray
    return dict(
        xT=c(np.concatenate([f(inp["x_prompt"])[b].T, f(inp["x_sample"])[b].T], axis=1)),
        ckT=c(f(inp["cache_fox_k"])[0, b].reshape(PAST, 512).T),
        cv=c(f(inp["cache_fox_v"])[0, b].reshape(32, 128, 8, 64).transpose(2, 1, 0, 3)),
        clogfT=c(f(inp["cache_fox_logf"])[0, b].T),
        sC=c(f(inp["state_mlstm_c"])[0, b].transpose(2, 0, 1)),
        sn=c(f(inp["state_mlstm_n"])[0, b].T),
        sm=c(f(inp["state_mlstm_m"])[0, b][:, None]),
        sconv=c(f(inp["state_mlstm_conv"])[0, b].reshape(3, 8, 128).transpose(2, 1, 0)),
        cmkT=c(f(inp["cache_mem_k"])[0, b].reshape(256, 512).T),
        cmv=c(f(inp["cache_mem_v"])[0, b].reshape(256, 512)),
        sfconv=c(f(inp["state_ffn_conv"])[0, b].reshape(2, 44, 128).transpose(2, 1, 0)),
        memT=c(f(inp["mem_prompt"])[b].T),
    )


def _assemble(results):
    B = len(results)
    z = lambda *s: np.zeros(s, np.float32)
    y_p, y_s = z(B, TP, 1024), z(B, TS, 1024)
    fk_p, fv_p, fl_p = z(1, B, TP, 8, 64), z(1, B, TP, 8, 64), z(1, B, TP, 8)
    fk_s, fv_s, fl_s = z(1, B, TS, 8, 64), z(1, B, TS, 8, 64), z(1, B, TS, 8)
    c_p, n_p, m_p = z(1, B, 4, 128, 128), z(1, B, 4, 128), z(1, B, 4)
    c_s, n_s, m_s = z(1, B, 4, 128, 128), z(1, B, 4, 128), z(1, B, 4)
    cv_p, cv_s = z(1, B, 3, 1024), z(1, B, 3, 1024)
    fc_p, fc_s = z(1, B, 2, 5632), z(1, B, 2, 5632)
    mk_p, mv_p = z(1, B, 256, 4, 128), z(1, B, 256, 4, 128)
    for b, r in enumerate(results):
        yT = r["yT"]
        y_p[b] = yT[:, :TP].T
        y_s[b] = yT[:, TP:].T
        kT = r["fox_kT"]
        fk_p[0, b] = kT[:, :TP].T.reshape(TP, 8, 64)
        fk_s[0, b] = kT[:, TP:].T.reshape(TS, 8, 64)
        fv = r["fox_v"]
        fv_p[0, b] = fv[:TP].reshape(TP, 8, 64)
        fv_s[0, b] = fv[TP:].reshape(TS, 8, 64)
        lf = r["fox_logfT"]
        fl_p[0, b] = lf[:, :TP].T
        fl_s[0, b] = lf[:, TP:].T
        for (si, cc, nn, mm, cvv, fcc) in ((0, c_p, n_p, m_p, cv_p, fc_p), (1, c_s, n_s, m_s, cv_s, fc_s)):
            cc[0, b] = r["ml_cT"][si].transpose(1, 2, 0)
            nn[0, b] = r["ml_n"][si].T
            mm[0, b] = r["ml_m"][si][:, 0]
            cvv[0, b] = r["ml_convT"][si].transpose(2, 1, 0).reshape(3, 1024)
            fcc[0, b] = r["ffn_convT"][si].transpose(2, 1, 0).reshape(2, 5632)
        mk_p[0, b] = r["mem_kT"].T.reshape(256, 4, 128)
        mv_p[0, b] = r["mem_v"].reshape(256, 4, 128)
    return (y_p, y_s, fk_p, fv_p, fl_p, c_p, n_p, m_p, cv_p, fc_p, mk_p, mv_p,
            fk_s, fv_s, fl_s, c_s, n_s, m_s, cv_s, fc_s)


def kernel(**inputs):
    nc = _get_nc()[0]
    shared = _prep_shared(inputs)
    in_maps = []
    for b in range(8):
        d = dict(shared)
        d.update(_prep_core(inputs, b))
        in_maps.append(d)
    res = run_bass_kernel_spmd(nc, in_maps, core_ids=list(range(8)))
    return _assemble(res.results)
```

```python
import numpy as np
from contextlib import ExitStack
import concourse.bass as bass
import concourse.mybir as mybir

F32 = mybir.dt.float32
BF16 = mybir.dt.bfloat16
AF = mybir.ActivationFunctionType
ALU = mybir.AluOpType
AX = mybir.AxisListType


def _esize(dt):
    n = str(dt)
    if "float32" in n or "int32" in n:
        return 4
    if "bfloat16" in n or "float16" in n or "int16" in n:
        return 2
    if "int8" in n or "float8" in n:
        return 1
    if "64" in n:
        return 8
    raise ValueError(n)


def _boxes(ap):
    t = ap.tensor
    name = t.name
    es = _esize(ap.dtype)
    dims = [(int(s), int(n)) for s, n in ap.ap]
    off = int(ap.offset)
    if "DRAM" in str(ap.space).upper() or "HBM" in str(ap.space).upper():
        ext = sum((n - 1) * abs(s) for s, n in dims)
        return name, [(0, 1, off * es, (off + ext + 1) * es)]
    tes = _esize(t.dtype)
    rowbytes = int(np.prod([int(x) for x in t.shape[1:]])) * tes
    row = rowbytes // es
    p0 = off // row
    f0 = off % row
    pext = 0
    fd = []
    for s, n in dims:
        if n == 1:
            continue
        if s != 0 and s % row == 0:
            pext += (n - 1) * (s // row)
        elif s != 0:
            fd.append((abs(s), n))
    fd.sort(reverse=True)
    p1 = p0 + pext + 1
    if len(fd) >= 2:
        inner = sum((n - 1) * s for s, n in fd[1:]) + 1
        s0, n0 = fd[0]
        if s0 >= inner and n0 <= 64:
            return name, [(p0, p1, (f0 + i * s0) * es, (f0 + i * s0 + inner) * es) for i in range(n0)]
    ext = sum((n - 1) * s for s, n in fd) + 1
    return name, [(p0, p1, f0 * es, (f0 + ext) * es)]


def _ov(a, b):
    for x in a:
        for y in b:
            if x[0] < y[1] and y[0] < x[1] and x[2] < y[3] and y[2] < x[3]:
                return True
    return False


def _contained(a, b):
    for x in a:
        ok = False
        for y in b:
            if y[0] <= x[0] and x[1] <= y[1] and y[2] <= x[2] and x[3] <= y[3]:
                ok = True
                break
        if not ok:
            return False
    return True


class _Stop(Exception):
    pass


class Prog:
    ENGS = ("pe", "act", "dve", "pool", "sp")

    def __init__(self, nc):
        self.nc = nc
        self.ops = []
        self.hist = {}

    def add(self, eng, fn, reads=(), writes=(), dma=False):
        idx = len(self.ops)
        deps = set()
        rb = [_boxes(a) for a in reads]
        wb = [_boxes(a) for a in writes]
        for name, bx in rb:
            for e in self.hist.get(name, ()):
                if e[2] and _ov(e[0], bx):
                    deps.add(e[1])
        for name, bx in wb:
            for e in self.hist.get(name, ()):
                if _ov(e[0], bx):
                    deps.add(e[1])
        for name, bx in wb:
            lst = [e for e in self.hist.get(name, ()) if not _contained(e[0], bx)]
            lst.append((bx, idx, True, eng, dma))
            self.hist[name] = lst
        for name, bx in rb:
            lst = self.hist.setdefault(name, [])
            rep = False
            if not dma:
                for i, e in enumerate(lst):
                    if (not e[2]) and e[3] == eng and (not e[4]) and e[0] == bx:
                        lst[i] = (bx, idx, False, eng, dma)
                        rep = True
                        break
            if not rep:
                lst.append((bx, idx, False, eng, dma))
        deps.discard(idx)
        self.ops.append((eng, fn, deps, dma))
        return idx

    def mark(self, name):
        if not hasattr(self, 'marks'):
            self.marks = []
        self.marks.append((name, sum(1 for o in self.ops if o[0] == 'pe')))
        if getattr(self, 'stop_at', None) == name:
            self.emit()
            raise _Stop()

    def dma(self, q, out, in_, **kw):
        kw.setdefault('allow_slow_non_contiguous', True)
        return self.add(q, lambda e: e.dma_start(out=out, in_=in_, **kw), [in_], [out], dma=True)

    def mm(self, out, lhsT, rhs, start=True, stop=True, acc=False):
        rd = [lhsT, rhs] + ([out] if not start else [])
        return self.add("pe", lambda e: e.matmul(out, lhsT, rhs, start=start, stop=stop), rd, [out])

    def tr(self, out, in_, ident):
        return self.add("pe", lambda e: e.transpose(out, in_, ident), [in_, ident], [out])

    def act(self, out, in_, func, bias=None, scale=None, accum_out=None, eng="act"):
        rd = [in_]
        kw = {}
        if bias is not None:
            kw["bias"] = bias
            if not isinstance(bias, (int, float)):
                rd.append(bias)
        if scale is not None:
            kw["scale"] = scale
            if not isinstance(scale, (int, float)):
                rd.append(scale)
        wr = [out]
        if accum_out is not None:
            kw["accum_out"] = accum_out
            wr.append(accum_out)
        return self.add("act", lambda e: e.activation(out, in_, func, **kw), rd, wr)

    def tt(self, out, in0, in1, op, eng="dve"):
        return self.add(eng, lambda e: e.tensor_tensor(out, in0, in1, op), [in0, in1], [out])

    def ts(self, out, in0, s1, op0, s2=None, op1=None, eng="dve"):
        rd = [in0] + [s for s in (s1, s2) if s is not None and not isinstance(s, (int, float))]
        if op1 is None:
            return self.add(eng, lambda e: e.tensor_scalar(out, in0, s1, None, op0), rd, [out])
        return self.add(eng, lambda e: e.tensor_scalar(out, in0, s1, s2, op0, op1), rd, [out])

    def stt(self, out, in0, scalar, in1, op0, op1, eng="dve"):
        rd = [in0, in1] + ([] if isinstance(scalar, (int, float)) else [scalar])
        return self.add(eng, lambda e: e.scalar_tensor_tensor(out, in0, scalar, in1, op0, op1), rd, [out])

    def copy(self, out, in_, eng="dve"):
        return self.add(eng, lambda e: e.tensor_copy(out, in_), [in_], [out])

    def memset(self, out, val, eng="dve"):
        return self.add(eng, lambda e: e.memset(out, val), [], [out])

    def recip(self, out, in_):
        return self.add("dve", lambda e: e.reciprocal(out, in_), [in_], [out])

    def scan(self, out, d0, d1, init, op0, op1):
        rd = [d0, d1] + ([] if isinstance(init, (int, float)) else [init])
        return self.add("dve", lambda e: e.tensor_tensor_scan(out, d0, d1, init, op0, op1), rd, [out])

    def emit(self, R=20000, K=8, Kq=None):
        Kq = dict(Kq or {})
        KK = {e: Kq.get(e, K) for e in self.ENGS}
        nc = self.nc
        ops = self.ops
        needed = set()
        for eng, fn, deps, dma in ops:
            for d in deps:
                de, _, _, ddma = ops[d]
                if ddma:
                    continue
                if de == "pe" and eng == "pe" and not dma:
                    continue
                needed.add(d)
        sigidx = {}
        cnt = {e: 0 for e in self.ENGS}
        for i, (eng, fn, deps, dma) in enumerate(ops):
            if (not dma) and i in needed:
                sigidx[i] = cnt[eng]
                cnt[eng] += 1
        dmaidx = {}
        dcnt = {e: 0 for e in self.ENGS}
        for i, (eng, fn, deps, dma) in enumerate(ops):
            if dma:
                dmaidx[i] = dcnt[eng]
                dcnt[eng] += 1
        with ExitStack() as st:
            csem = {e: [st.enter_context(nc.semaphore(f"c_{e}_{j}")) for j in range(max(1, (cnt[e] + R - 1) // R))]
                    for e in self.ENGS}
            dsem = {e: [st.enter_context(nc.semaphore(f"d_{e}_{j}")) for j in range(min(KK[e], dcnt[e]))]
                    for e in self.ENGS}
            block = st.enter_context(nc.Block())

            def run(me, e):
                waited_c = {x: -1 for x in self.ENGS}
                waited_d = {}

                def wait_dma(d):
                    q = ops[d][0]
                    n = dmaidx[d]
                    K = KK[q]
                    sem = dsem[q][n % K]
                    val = 16 * (n // K + 1)
                    key = (q, n % K)
                    if waited_d.get(key, 0) >= val:
                        return
                    e.wait_ge(sem, val)
                    waited_d[key] = val

                for i, (eng, fn, deps, dma) in enumerate(ops):
                    if eng != me:
                        continue
                    for d in sorted(deps):
                        de, _, _, ddma = ops[d]
                        if ddma:
                            wait_dma(d)
                        else:
                            if de == "pe" and me == "pe" and not dma:
                                continue
                            g = sigidx[d]
                            if waited_c[de] >= g:
                                continue
                            e.wait_ge(csem[de][g // R], g % R + 1)
                            waited_c[de] = g
                    if dma:
                        n = dmaidx[i]
                        K = KK[me]
                        sem = dsem[me][n % K]
                        if n >= K:
                            key = (me, n % K)
                            val = 16 * (n // K)
                            if waited_d.get(key, 0) < val:
                                e.wait_ge(sem, val)
                                waited_d[key] = val
                        ins = fn(e)
                        ins.then_inc(sem, 16)
                    else:
                        ins = fn(e)
                        if i in sigidx:
                            g = sigidx[i]
                            ins.then_inc(csem[me][g // R], 1)
                K = KK[me]
                for j in range(min(K, dcnt[me])):
                    uses = (dcnt[me] - 1 - j) // K + 1
                    val = 16 * uses
                    if waited_d.get((me, j), 0) < val:
                        e.wait_ge(dsem[me][j], val)

            @block.sync
            def _(e):
                run("sp", e)

            @block.scalar
            def _(e):
                run("act", e)

            @block.vector
            def _(e):
                run("dve", e)

            @block.gpsimd
            def _(e):
                run("pool", e)

            @block.tensor
            def _(e):
                run("pe", e)

from concourse.bass_utils import run_bass_kernel_spmd

NT = 2112
TP = 2048
TS = 64
PAST = 4096
EPS = 1e-6
GROUPS = [(0, 512), (512, 512), (1024, 512), (1536, 512), (2048, 64)]
TILES = [(i * 128, 128) for i in range(16)] + [(2048, 64)]
QSC = 128.0 ** -0.5

FM_STARTS = ([0, 128, 256, 384] + [512, 640, 768, 896] + [1544 + 128 * i for i in range(8)]
             + [3088 + 128 * i for i in range(4)]
             + [3600 + 128 * i for i in range(4)] + [4112 + 128 * i for i in range(24)])
FM_COL = {s: i for i, s in enumerate(FM_STARTS)}

IN_SHAPES = dict(
    xT=[1024, NT], w_in=[1024, 7184], b_fm=[128, 48], bg_fox=[8, 1], bg_i=[4, 1], bg_f=[4, 1],
    b_row=[1, 7184], g_pre=[128, 8], gq2=[128, 1], gk2=[128, 1], mconv_w=[128, 8, 4], mconv_b=[128, 8],
    mhn=[128, 4], g_mem=[128, 8], w_mem_kv=[1024, 1024], w_br_a=[512, 1024], w_br_b=[512, 1024],
    w_br_m=[512, 1024], w_out=[1024, 1024], g_post=[128, 8], g_fpre=[128, 8], w_up=[1024, 5632],
    fconv_w=[128, 44, 3], fconv_b=[128, 44], w_down=[2816, 1024], g_fpost=[128, 8],
    ckT=[512, PAST], cv=[8, 128, 32, 64], clogfT=[8, PAST], sC=[128, 4, 128], sn=[128, 4], sm=[4, 1],
    sconv=[128, 8, 3], cmkT=[512, 256], cmv=[256, 512], sfconv=[128, 44, 2], memT=[1024, 256],
    ident=[128, 128], trimask=[128, 128], negmask=[128, 128], bdones=[128, 128], sel4=[4, 4, 128],
)
OUT_SHAPES = dict(
    yT=[1024, NT], fox_kT=[512, NT], fox_v=[NT, 512], fox_logfT=[8, NT],
    ml_cT=[2, 128, 4, 128], ml_n=[2, 128, 4], ml_m=[2, 4, 1], ml_convT=[2, 128, 8, 3],
    ffn_convT=[2, 128, 44, 2], mem_kT=[512, 256], mem_v=[256, 512],
)


class Arena:
    def __init__(self, A, cap):
        self.A = A
        self.cap = cap
        self.top = 0
        self.hw = 0

    def _alloc(self, words):
        off = self.top
        self.top += words
        assert self.top <= self.cap, f"arena overflow {self.top} > {self.cap}"
        self.hw = max(self.hw, self.top)
        return off

    def f32(self, *shape):
        n = int(np.prod(shape[1:]))
        off = self._alloc(n)
        return self._view(self.A[:, off:off + n], shape)

    def bf(self, *shape):
        n = int(np.prod(shape[1:]))
        words = (n + 1) // 2
        off = self._alloc(words)
        return self._view(self.A[:, off:off + words].bitcast(BF16)[:, 0:n], shape)

    @staticmethod
    def _view(v, shape):
        p = shape[0]
        if len(shape) == 3:
            v = v.rearrange("p (a b) -> p a b", b=shape[2])
        elif len(shape) == 4:
            v = v.rearrange("p (a b c) -> p a b c", b=shape[2], c=shape[3])
        if p < 128:
            v = v[0:p]
        return v

    def f32_at(self, off, *shape):
        n = int(np.prod(shape[1:]))
        return self._view(self.A[:, off:off + n], shape)

    def bf_at(self, off, *shape):
        n = int(np.prod(shape[1:]))
        words = (n + 1) // 2
        return self._view(self.A[:, off:off + words].bitcast(BF16)[:, 0:n], shape)

    def mark(self):
        return self.top

    def release(self, m):
        self.top = m


def build(debug=(), stop_at=None, salt=None):
    nc = bass.Bass("TRN2", target_bir_lowering=False)
    I = {k: nc.dram_tensor(k, list(v), F32, kind="ExternalInput").ap() for k, v in IN_SHAPES.items()}
    O = {k: nc.dram_tensor(k, list(v), F32, kind="ExternalOutput").ap() for k, v in OUT_SHAPES.items()}
    x1scr = nc.dram_tensor("x1scr", [1024, NT], F32, kind="Internal").ap()
    DBG = {}
    P = Prog(nc)
    P.stop_at = stop_at
    CAP = 53000
    try:
      with nc.sbuf_tensor("A", [128, CAP], F32) as A_, nc.psum_tensor("PS", [128, 8, 512], F32) as PS:
        ar = Arena(A_, CAP)
        hw_box = [ar]
        bank_ctr = [0]

        def nb():
            b = bank_ctr[0] % 8
            bank_ctr[0] += 1
            return PS[:, b, :]

        def dbg(name, ap):
            if name in debug:
                shp = [int(s) for s in ap.shape]
                d = nc.dram_tensor("dbg_" + name, shp, F32, kind="ExternalOutput").ap()
                DBG[name] = shp
                if ap.dtype == F32 and "PSUM" not in str(ap.space).upper():
                    P.dma("sp", d, ap)
                else:
                    m = ar.mark()
                    t = ar.f32(*([128] + shp[1:]))[0:shp[0]]
                    P.copy(t, ap)
                    P.dma("sp", d, t)
                    ar.release(m)

        wv = lambda name: I[name].rearrange("(c p) n -> p c n", p=128)
        r3 = lambda ap, b: ap.rearrange("p (a b) -> p a b", b=b)

        ident_f = ar.f32(128, 128)
        ident_b = ar.bf(128, 128)
        trimask = ar.bf(128, 128)
        negmask = ar.bf(128, 128)
        bdones = ar.bf(128, 128)
        ones_b = ar.bf(128, 128)
        ones_f = ar.f32(128, 128)
        sel4 = ar.f32(4, 4, 128)
        b_fm = ar.f32(128, 48)
        g_pre = ar.f32(128, 8)
        g_post = ar.f32(128, 8)
        g_fpre = ar.f32(128, 8)
        g_fpost = ar.f32(128, 8)
        g_mem = ar.f32(128, 8)
        gq2 = ar.f32(128, 1)
        gk2 = ar.f32(128, 1)
        mhn = ar.f32(128, 4)
        mconv_w = ar.f32(128, 8, 4)
        mconv_b = ar.f32(128, 8)
        fconv_w = ar.f32(128, 44, 3)
        fconv_b = ar.f32(128, 44)
        def load_consts():
            P.dma("sp", ident_f, I["ident"])
            P.dma("pool", ident_b, I["ident"])
            P.dma("pool", trimask, I["trimask"])
            P.dma("pool", negmask, I["negmask"])
            P.dma("pool", bdones, I["bdones"])
            P.dma("sp", sel4, I["sel4"])
            for t, n in ((b_fm, "b_fm"), (g_pre, "g_pre"), (g_post, "g_post"), (g_fpre, "g_fpre"), (g_fpost, "g_fpost"),
                         (g_mem, "g_mem"), (gq2, "gq2"), (gk2, "gk2"), (mhn, "mhn"), (mconv_w, "mconv_w"),
                         (mconv_b, "mconv_b"), (fconv_w, "fconv_w"), (fconv_b, "fconv_b")):
                P.dma("sp", t, I[n])
            P.ts(gq8, gq2, 0.125, ALU.mult)
        P.memset(ones_b, 1.0)
        P.memset(ones_f, 1.0)
        eps_col = ar.f32(128, 1)
        P.memset(eps_col, EPS)
        one_col = ar.f32(128, 1)
        P.memset(one_col, 1.0)
        gq8 = ar.f32(128, 1)
        bcol = lambda start: b_fm[:, FM_COL[start]:FM_COL[start] + 1]

        def rms_bcast(src, C, n, D, rb, sq=None):
            m = ar.mark()
            if sq is None:
                sq = ar.bf(128, C, n)
            P.act(sq, src, AF.Square)
            ps = nb()
            for c in range(C):
                P.mm(ps[:, 0:n], ones_b, sq[:, c, :], start=(c == 0), stop=(c == C - 1))
            P.act(rb, ps[:, 0:n], AF.Ln, bias=eps_col, scale=1.0 / D)
            P.act(rb, rb, AF.Exp, scale=-0.5)
            ar.release(m)

        xn_off = ar.mark()
        xnT = ar.bf(128, 8, NT)
        m_after_xn = ar.mark()
        aT_off = ar.mark()
        aT = ar.bf(128, 4, NT)
        chi = ar.bf(8, NT)
        clo = ar.bf(8, NT)
        rdl = ar.f32(4, NT)
        wg_tok = ar.f32(128, 17, 4)
        dec_b = ar.f32(128, 4, 17)

        xT3 = I["xT"].rearrange("(c p) n -> p c n", p=128)
        m = ar.mark()
        xs2 = [ar.f32(128, 8, 512), ar.f32(128, 8, 512)]
        rb2 = [ar.f32(128, 512), ar.f32(128, 512)]
        sq2 = [ar.bf(128, 8, 512), ar.bf(128, 8, 512)]
        for gi, (t0, n) in enumerate(GROUPS):
            xs = xs2[gi % 2][:, :, 0:n]
            P.dma("sp", xs, xT3[:, :, t0:t0 + n])
            if gi == 0:
                load_consts()
            rb = rb2[gi % 2][:, 0:n]
            rms_bcast(xs, 8, n, 1024.0, rb, sq=sq2[gi % 2][:, :, 0:n])
            for c in range(8):
                P.stt(xnT[:, c, t0:t0 + n], xs[:, c, :], g_pre[:, c:c + 1], rb, ALU.mult, ALU.mult)
        ar.release(m)
        dbg("xnT", xnT[:, 0, :])

        P.mark("s0_norm")
        m1a = ar.mark()
        wgf = ar.bf(128, 8, 8)
        wgi = ar.bf(128, 8, 4)
        wgm = ar.bf(128, 8, 4)
        P.dma("pool", wgf, wv("w_in")[:, :, 1536:1544])
        P.dma("pool", wgi, wv("w_in")[:, :, 3080:3084])
        P.dma("pool", wgm, wv("w_in")[:, :, 3084:3088])
        bgf = ar.f32(8, 1)
        bgi = ar.f32(4, 1)
        bgm = ar.f32(4, 1)
        P.dma("sp", bgf, I["bg_fox"])
        P.dma("sp", bgi, I["bg_i"])
        P.dma("sp", bgm, I["bg_f"])
        zrow = ar.f32(8, NT)
        P.memset(zrow, 0.0)
        flog = ar.f32(8, NT)
        cfox = ar.f32(8, NT)
        gi_r = ar.f32(4, NT)
        mlf = ar.f32(4, NT)
        A_r = ar.f32(4, NT)
        G_r = ar.f32(4, NT)
        wgT = ar.f32(4, NT)
        for (wt, bc, dst, rows, ls) in ((wgf, bgf, flog, 8, True), (wgi, bgi, gi_r, 4, False), (wgm, bgm, mlf, 4, True)):
            nbias = ar.f32(rows, 1)
            P.ts(nbias, bc, -1.0, ALU.mult)
            for (t0, n) in GROUPS:
                ps = nb()
                for c in range(8):
                    P.mm(ps[0:rows, 0:n], wt[:, c, :], xnT[:, c, t0:t0 + n], start=(c == 0), stop=(c == 7))
                if ls:
                    m = ar.mark()
                    e = ar.f32(rows, n)
                    P.act(e, ps[0:rows, 0:n], AF.Exp, bias=nbias, scale=-1.0)
                    P.act(e, e, AF.Ln, bias=one_col[0:rows], scale=1.0)
                    P.ts(dst[:, t0:t0 + n], e, -1.0, ALU.mult)
                    ar.release(m)
                else:
                    P.act(dst[:, t0:t0 + n], ps[0:rows, 0:n], AF.Identity, bias=bc)
        P.dma("sp", O["fox_logfT"], flog)
        P.scan(cfox[:, 0:TP], flog[:, 0:TP], zrow[:, 0:TP], 0.0, ALU.add, ALU.add)
        P.scan(cfox[:, TP:NT], flog[:, TP:NT], zrow[:, 0:TS], 0.0, ALU.add, ALU.add)
        P.copy(chi, cfox)
        P.tt(clo, cfox, chi, ALU.subtract)
        sm0 = ar.f32(4, 1)
        P.dma("sp", sm0, I["sm"])
        zr4 = zrow[0:4]
        Bc = ar.f32(4, NT)
        P.scan(Bc[:, 0:TP], mlf[:, 0:TP], zr4[:, 0:TP], 0.0, ALU.add, ALU.add)
        P.scan(Bc[:, TP:NT], mlf[:, TP:NT], zr4[:, 0:TS], 0.0, ALU.add, ALU.add)
        P.tt(A_r, gi_r, Bc, ALU.subtract)
        P.scan(G_r[:, 0:TP], A_r[:, 0:TP], A_r[:, 0:TP], 0.0, ALU.max, ALU.max)
        P.scan(G_r[:, TP:NT], A_r[:, TP:NT], A_r[:, TP:NT], sm0, ALU.max, ALU.max)
        gend = ar.f32(4, 17)
        mprev = ar.f32(4, 17)
        P.copy(gend[:, 0:16], G_r[:, 127:TP:128])
        P.copy(gend[:, 16:17], G_r[:, NT - 1:NT])
        P.memset(mprev[:, 0:1], 0.0)
        P.copy(mprev[:, 1:16], gend[:, 0:15])
        P.copy(mprev[:, 16:17], sm0)
        dec = ar.f32(4, 17)
        P.tt(dec, mprev, gend, ALU.subtract)
        P.act(dec, dec, AF.Exp)
        gend_b = gend[:, 0:16].unsqueeze(2).broadcast_to([4, 16, 128])
        P.tt(r3(wgT[:, 0:TP], 128), r3(A_r[:, 0:TP], 128), gend_b, ALU.subtract)
        P.ts(wgT[:, TP:NT], A_r[:, TP:NT], gend[:, 16:17], ALU.subtract)
        P.act(wgT, wgT, AF.Exp)
        P.tt(r3(rdl[:, 0:TP], 128), r3(Bc[:, 0:TP], 128), gend_b, ALU.add)
        P.ts(rdl[:, TP:NT], Bc[:, TP:NT], gend[:, 16:17], ALU.add)
        P.ts(rdl, rdl, -1.0, ALU.mult)
        mTo = ar.f32(4, 2)
        P.tt(mTo[:, 0:1], G_r[:, TP - 1:TP], Bc[:, TP - 1:TP], ALU.add)
        P.tt(mTo[:, 1:2], G_r[:, NT - 1:NT], Bc[:, NT - 1:NT], ALU.add)
        P.dma("sp", O["ml_m"][0], mTo[:, 0:1])
        P.dma("sp", O["ml_m"][1], mTo[:, 1:2])
        for ti, (t0, L) in enumerate(TILES):
            ps = nb()
            P.tr(ps[0:L, 0:4], wgT[:, t0:t0 + L], ident_f[0:4, 0:4])
            P.copy(wg_tok[0:L, ti, :], ps[0:L, 0:4])
        for h in range(4):
            ps = nb()
            P.mm(ps[:, 0:17], sel4[:, h, :], dec)
            P.copy(dec_b[:, h, :], ps[:, 0:17])
        dbg("cfox", cfox)
        dbg("G_r", G_r)
        dbg("wgT", wgT)
        dbg("rdl", rdl)
        ar.release(m1a)
        s1 = ar.mark()

        P.mark("s1a_gates")
        cchi = ar.bf(8, PAST)
        cclo = ar.bf(8, PAST)
        m_s = ar.mark()
        ccache = ar.f32(8, PAST)
        lcache = ar.f32(8, PAST)
        zc = ar.f32(8, PAST)
        P.memset(zc, 0.0)
        P.dma("sp", lcache, I["clogfT"])
        P.scan(ccache, lcache, zc, 0.0, ALU.add, ALU.add)
        ctot = ar.f32(8, 1)
        P.copy(ctot, ccache[:, PAST - 1:PAST])
        P.ts(ccache, ccache, ctot, ALU.subtract)
        P.copy(cchi, ccache)
        P.tt(cclo, ccache, cchi, ALU.subtract)
        ar.release(m_s)
        qT = ar.bf(128, 4, NT)
        kT = ar.bf(128, 4, NT)
        V = ar.bf(128, 17, 512)
        m_w = ar.mark()
        wf = ar.bf(128, 8, 1536)
        P.dma("pool", wf, wv("w_in")[:, :, 0:1536])
        bvrow = ar.f32(128, 512)
        P.dma("sp", bvrow, I["b_row"][:, 1024:1536].partition_broadcast(128))
        ptmp = [(ar.f32(128, 512), ar.bf(128, 512), ar.f32(128, 512), ar.f32(128, 512)) for _ in range(3)]
        its = [(t0, n, which, ch) for (t0, n) in GROUPS for which in range(2) for ch in range(4)]

        def fp_A(it):
            t0, n, which, ch = it
            col0 = which * 512 + ch * 128
            ps = nb()
            for c in range(8):
                P.mm(ps[:, 0:n], wf[:, c, col0:col0 + 128], xnT[:, c, t0:t0 + n], start=(c == 0), stop=(c == 7))
            return ps

        def fp_B(i, it, ps):
            t0, n, which, ch = it
            col0 = which * 512 + ch * 128
            z_, sq_, r_, kf_ = ptmp[i % 3]
            z = z_[:, 0:n]
            sq = sq_[:, 0:n]
            P.act(z, ps[:, 0:n], AF.Identity, bias=bcol(col0))
            P.act(sq, ps[:, 0:n], AF.Square, bias=bcol(col0))
            ps2 = nb()
            P.mm(ps2[:, 0:n], bdones, sq)
            r = r_[:, 0:n]
            P.act(r, ps2[:, 0:n], AF.Ln, bias=eps_col, scale=1.0 / 64)
            P.act(r, r, AF.Exp, scale=-0.5)
            if which == 0:
                P.stt(qT[:, ch, t0:t0 + n], z, gq8, r, ALU.mult, ALU.mult)
            else:
                kf = kf_[:, 0:n]
                P.stt(kf, z, gk2, r, ALU.mult, ALU.mult)
                P.copy(kT[:, ch, t0:t0 + n], kf)
                P.dma("sp", O["fox_kT"][ch * 128:(ch + 1) * 128, t0:t0 + n], kf)

        vtmp = [ar.f32(128, 512), ar.f32(128, 512)]

        def fp_V(ti):
            t0, L = TILES[ti]
            ps = nb()
            for c in range(8):
                P.mm(ps[0:L, :], xnT[:, c, t0:t0 + L], wf[:, c, 1024:1536], start=(c == 0), stop=(c == 7))
            vf = vtmp[ti % 2]
            P.tt(vf[0:L], ps[0:L, :], bvrow[0:L], ALU.add)
            P.copy(V[0:L, ti, :], vf[0:L])
            P.dma("sp", O["fox_v"][t0:t0 + L, :], vf[0:L])

        psq = [fp_A(its[0])]
        vnext = 0
        for i, it in enumerate(its):
            if i + 1 < len(its):
                psq.append(fp_A(its[i + 1]))
            if i % 2 == 1 and vnext < 17:
                fp_V(vnext)
                vnext += 1
            fp_B(i, it, psq[i])
        while vnext < 17:
            fp_V(vnext)
            vnext += 1
        ar.release(m_w)
        dbg("qT", qT[:, 0, :])
        dbg("kT", kT[:, 0, :])

        P.mark("s1b_foxproj")
        pbufs = [ar.bf(128, 2, 512) for _ in range(4)]
        pctr = [0]
        fctr = [0, 0]
        rbuf = ar.f32(128, 512)
        KMAX = PAST + TS
        qas = [ar.bf(128, NT), ar.bf(128, NT)]
        kas_p = [ar.bf(128, TP), ar.bf(128, TP)]
        kas_s = [ar.bf(128, KMAX), ar.bf(128, KMAX)]
        vexts_p = [ar.bf(128, 16, 128), ar.bf(128, 16, 128)]
        vexts_s = [ar.bf(128, 33, 128), ar.bf(128, 33, 128)]
        for i in range(2):
            P.memset(qas[i][64:68, :], -1.0)
            P.memset(kas_p[i][64:68, :], 1.0)
            P.memset(kas_s[i][64:68, :], 1.0)
        for ve in (vexts_p, vexts_s):
            P.memset(ve[0][:, :, 64:128], 1.0)
            P.memset(ve[1][:, :, 0:64], 1.0)
        ckT_d = I["ckT"]

        def fox_attend(h, qa, qc0, qn, keys, ka, vext):
            ch, half = h // 2, h % 2
            pb = half * 64
            po = (1 - half) * 64
            acc = PS[:, 6 + (fctr[0] % 2), :]
            fctr[0] += 1
            full = [k for k in keys if k[3] is None]
            diag = [k for k in keys if k[3] is not None]
            per = 2 if qn > 64 else 8
            units = [full[i:i + per] for i in range(0, len(full), per)] + [[k] for k in diag]
            nk = len(keys)
            done = [0]

            def emit_front(unit):
                base = 2 * (fctr[1] % 3)
                fctr[1] += 1
                pt = pbufs[pctr[0] % len(pbufs)]
                pctr[0] += 1
                pvs = []
                if len(unit) == 1:
                    kc0, vti, L, doff = unit[0]
                    q_lo = 0 if doff is None else doff
                    sps = PS[:, base, :]
                    P.mm(sps[0:L, q_lo:qn], ka[0:68, kc0:kc0 + L], qa[0:68, qc0 + q_lo:qc0 + qn], start=True, stop=(doff is None))
                    if doff is not None:
                        dq = min(L, qn - q_lo)
                        P.mm(sps[0:L, q_lo:q_lo + dq], ident_b[0:L, 0:L], negmask[0:L, 0:dq], start=False, stop=True)
                    P.act(pt[0:L, 0, q_lo:qn], sps[0:L, q_lo:qn], AF.Exp)
                    pvs.append((acc[:, q_lo:qn], vext[0:L, vti, :], pt[0:L, 0, q_lo:qn]))
                elif qn > 64:
                    for j, (kc0, vti, L, doff) in enumerate(unit):
                        P.mm(PS[:, base + j, 0:qn], ka[0:68, kc0:kc0 + L], qa[0:68, qc0:qc0 + qn])
                        pvs.append((acc[:, 0:qn], vext[0:L, vti, :], pt[:, j, 0:qn]))
                    P.act(pt[:, 0:2, 0:qn], PS[:, base:base + 2, 0:qn], AF.Exp)
                else:
                    m = len(unit)
                    for j, (kc0, vti, L, doff) in enumerate(unit):
                        P.mm(PS[:, base, j * qn:(j + 1) * qn], ka[0:68, kc0:kc0 + L], qa[0:68, qc0:qc0 + qn])
                        pvs.append((acc[:, 0:qn], vext[0:L, vti, :], pt[:, 0, j * qn:(j + 1) * qn]))
                    P.act(pt[:, 0, 0:m * qn], PS[:, base, 0:m * qn], AF.Exp)
                return pvs

            def emit_pv(pvs):
                for (o_, l_, r_) in pvs:
                    P.mm(o_, l_, r_, start=(done[0] == 0), stop=(done[0] == nk - 1))
                    done[0] += 1

            LA = 2
            q_ = [emit_front(units[u]) for u in range(min(LA, len(units)))]
            for u in range(len(units)):
                if u + LA < len(units):
                    q_.append(emit_front(units[u + LA]))
                emit_pv(q_[u])
            P.act(rbuf[pb:pb + 64, 0:qn], acc[po:po + 64, 0:qn], AF.Ln)
            P.act(rbuf[pb:pb + 64, 0:qn], rbuf[pb:pb + 64, 0:qn], AF.Exp, scale=-1.0)
            P.tt(aT[pb:pb + 64, ch, qc0:qc0 + qn], acc[pb:pb + 64, 0:qn], rbuf[pb:pb + 64, 0:qn], ALU.mult)

        def sample_loads(h):
            ch, half = h // 2, h % 2
            pb = half * 64
            ka, vext = kas_s[h % 2], vexts_s[h % 2]
            vo = 0 if half == 0 else 64
            P.dma("pool", ka[0:64, 0:PAST], ckT_d[h * 64:(h + 1) * 64, :])
            P.dma("sp", ka[64:65, 0:PAST], cchi[h:h + 1, :])
            P.dma("sp", ka[65:66, 0:PAST], cclo[h:h + 1, :])
            P.dma("sp", ka[0:64, PAST:KMAX], kT[pb:pb + 64, ch, TP:NT])
            P.dma("sp", ka[64:65, PAST:KMAX], chi[h:h + 1, TP:NT])
            P.dma("sp", ka[65:66, PAST:KMAX], clo[h:h + 1, TP:NT])
            P.dma("pool", vext[:, 0:32, vo:vo + 64], I["cv"][h])
            P.copy(vext[0:64, 32, vo:vo + 64], V[0:64, 16, h * 64:h * 64 + 64], eng="pool")

        sample_loads(0)
        keys_s = [(ti * 128, ti, 128, None) for ti in range(32)] + [(PAST, 32, 64, 0)]
        for h in range(8):
            ch, half = h // 2, h % 2
            pb = half * 64
            qa, ka, vext = qas[h % 2], kas_p[h % 2], vexts_p[h % 2]
            vo = 0 if half == 0 else 64
            P.dma("sp", qa[0:64, :], qT[pb:pb + 64, ch, :])
            P.dma("sp", qa[66:67, :], chi[h:h + 1, :])
            P.dma("sp", qa[67:68, :], clo[h:h + 1, :])
            P.dma("sp", ka[0:64, 0:TP], kT[pb:pb + 64, ch, 0:TP])
            P.dma("sp", ka[64:65, 0:TP], chi[h:h + 1, 0:TP])
            P.dma("sp", ka[65:66, 0:TP], clo[h:h + 1, 0:TP])
            P.copy(vext[:, 0:16, vo:vo + 64], V[:, 0:16, h * 64:h * 64 + 64], eng="pool")
            if h + 1 < 8:
                sample_loads(h + 1)
            for gi in range(4):
                keys = []
                for ti in range(gi * 4 + 4):
                    doff = None if ti < gi * 4 else (ti - gi * 4) * 128
                    keys.append((ti * 128, ti, 128, doff))
                fox_attend(h, qa, gi * 512, 512, keys, ka, vext)
            fox_attend(h, qa, TP, TS, keys_s, kas_s[h % 2], vexts_s[h % 2])
            if h == 0:
                dbg("aT0", aT[0:64, 0, 0:TP])
        dbg("aT", aT[:, 0, :])
        P.mark("fox_prompt")
        dbg("aTs", aT[:, 0, TP:NT])
        ar.release(s1)
        bT_off = ar.mark()
        bT = ar.bf(128, 4, NT)
        s1 = ar.mark()

        P.mark("fox_sample")
        wm = ar.bf(128, 8, 2048)
        P.dma("pool", wm[:, :, 0:1536], wv("w_in")[:, :, 1544:3080])
        P.dma("pool", wm[:, :, 1536:2048], wv("w_in")[:, :, 3088:3600])
        bmv = ar.f32(128, 512)
        P.dma("sp", bmv, I["b_row"][:, 2568:3080].partition_broadcast(128))
        CT = ar.f32(128, 4, 129)
        pc = ar.f32(128, 8, 515)
        qks = [ar.bf(128, 8, 512), ar.bf(128, 8, 512)]
        qkc = [qks[0]]
        sigos = [ar.f32(128, 4, 512), ar.f32(128, 4, 512)]
        sigc = [sigos[0]]
        caccs = [ar.f32(128, 512), ar.f32(128, 512)]
        ones3 = ones_f.unsqueeze(1).broadcast_to([128, 4, 128])
        msets = []
        for _ in range(2):
            msets.append(dict(vfull=ar.f32(128, 512), vw=ar.bf(128, 4, 129), wgb=ar.bf(128, 4, 128),
                              ktok=ar.bf(128, 4, 128), ET=ar.bf(128, 4, 128), Cq=ar.bf(128, 4, 128),
                              nbm=ar.bf(128, 4, 128), tden=ar.f32(128, 4, 128), hs=ar.f32(128, 4, 128),
                              hsq=ar.bf(128, 4, 128), rdbt=ar.f32(128, 4, 128)))
            msets[-1]["rr"] = msets[-1]["tden"]
        B_v = PS[:, 0, :]
        B_tr = PS[:, 1, :].bitcast(BF16)
        B_S = PS[:, 2, :]
        B_k = [PS[:, 3, :], PS[:, 4, :]]
        B_Y = PS[:, 5, :]
        B_D = PS[:, 6, :]
        B_q = PS[:, 7, :]
        P.memset(CT, 0.0)
        P.memset(pc[:, :, 0:3], 0.0)
        wb3 = ar.f32(128, 8)
        for j in range(8):
            P.tt(wb3[:, j:j + 1], mconv_w[:, j, 3:4], bcol(1544 + 128 * j), ALU.mult)

        def ml_step(prev, cur):
            if prev is not None:
                tiP, ttP, LP, loP, SP = prev
                Dv = r3(B_D[:, 0:4 * LP], LP)
                Yv = r3(B_Y[:, 0:4 * LP], LP)
                td = SP["tden"][:, :, 0:LP]
                hs_ = SP["hs"][:, :, 0:LP]
                hq_ = SP["hsq"][:, :, 0:LP]
                rr_ = SP["rr"][:, :, 0:LP]
            if cur is not None:
                ti, tt0, L, lo, S = cur
                wg3 = wg_tok[0:L, ti, :].unsqueeze(2)
            if prev is not None:
                for h in range(4):
                    q_h = qkc[0][:, h, loP:loP + LP]
                    P.mm(B_Y[:, h * LP:(h + 1) * LP], SP["Cq"][:, h, :], q_h, start=True, stop=False)
                    P.mm(B_Y[:, h * LP:(h + 1) * LP], SP["vw"][0:LP, h, 0:128], SP["ET"][0:LP, h, 0:LP], start=False, stop=True)
                for h in range(4):
                    q_h = qkc[0][:, h, loP:loP + LP]
                    P.mm(B_D[:, h * LP:(h + 1) * LP], SP["nbm"][:, h, :], q_h, start=True, stop=False)
                    P.mm(B_D[:, h * LP:(h + 1) * LP], SP["wgb"][0:LP, h, :], SP["ET"][0:LP, h, 0:LP], start=False, stop=True)
            if cur is not None:
                for c in range(8):
                    P.mm(B_v[0:L, :], xnT[:, c, tt0:tt0 + L], wm[:, c, 1024:1536], start=(c == 0), stop=(c == 7))
            if prev is not None:
                P.act(td, Dv, AF.Abs)
                P.tt(td, td, SP["rdbt"][:, :, 0:LP], ALU.max)
                P.act(td, td, AF.Ln)
                P.act(td, td, AF.Exp, scale=-1.0)
            if cur is not None:
                P.tt(S["vfull"][0:L], B_v[0:L, :], bmv[0:L], ALU.add)
                P.tt(S["vw"][0:L, :, 0:128], r3(S["vfull"][0:L], 128), wg3.broadcast_to([L, 4, 128]), ALU.mult)
                P.copy(S["vw"][0:L, :, 128:129], wg3)
                P.tt(S["wgb"][0:L], ones_f[0:L].unsqueeze(1).broadcast_to([L, 4, 128]), wg3.broadcast_to([L, 4, 128]), ALU.mult)
                for h in range(4):
                    P.tr(B_tr[0:L, h * 128:(h + 1) * 128], qkc[0][:, 4 + h, lo:lo + L], ident_b)
                P.act(S["ktok"][0:L], r3(B_tr[0:L, 0:512], 128), AF.Copy)
                for h in range(4):
                    P.mm(B_S[0:L, h * L:(h + 1) * L], qkc[0][:, 4 + h, lo:lo + L], qkc[0][:, h, lo:lo + L])
            if prev is not None:
                P.tt(hs_, Yv, td, ALU.mult)
                P.tt(hs_, hs_, sigc[0][:, :, loP:loP + LP], ALU.mult)
                P.act(hq_, hs_, AF.Square)
                for h in range(4):
                    P.mm(B_q[:, h * LP:(h + 1) * LP], ones_b, SP["hsq"][:, h, 0:LP])
                P.act(rr_, r3(B_q[:, 0:4 * LP], LP), AF.Ln, bias=eps_col, scale=1.0 / 128)
                P.act(rr_, rr_, AF.Exp, scale=-0.5)
            if cur is not None:
                for h in range(4):
                    P.mm(B_q[:, h * L:(h + 1) * L], sel4[:, h, :], rdl[:, tt0:tt0 + L])
                P.act(S["rdbt"][:, :, 0:L], r3(B_q[:, 0:4 * L], L), AF.Exp)
            if cur is not None:
                P.stt(S["ET"][0:L, :, 0:L], r3(B_S[0:L, 0:4 * L], L), QSC,
                      trimask[0:L, 0:L].unsqueeze(1).broadcast_to([L, 4, L]), ALU.mult, ALU.mult)
                for h in range(4):
                    P.mm(B_k[h // 2][:, (h % 2) * 129:(h % 2) * 129 + 129], S["ktok"][0:L, h, :], S["vw"][0:L, h, :])
                P.tt(CT, CT, dec_b[:, :, ti:ti + 1].broadcast_to([128, 4, 129]), ALU.mult)
                P.act(S["Cq"], CT[:, :, 0:128], AF.Copy, scale=QSC)
                P.stt(S["nbm"], ones3, QSC, CT[:, :, 128:129].broadcast_to([128, 4, 128]), ALU.mult, ALU.mult)
            if prev is not None:
                P.tt(hs_, hs_, rr_, ALU.mult)
                P.tt(bT[:, :, ttP:ttP + LP], hs_, mhn.unsqueeze(2).broadcast_to([128, 4, LP]), ALU.mult)
            if cur is not None:
                P.tt(CT[:, 0:2, :], CT[:, 0:2, :], r3(B_k[0][:, 0:258], 129), ALU.add)
                P.tt(CT[:, 2:4, :], CT[:, 2:4, :], r3(B_k[1][:, 0:258], 129), ALU.add)

        def ml_proj_parts(gi):
            t0, n = GROUPS[gi]
            qk = qks[gi % 2]

            def part(pj):
                if pj == 0 and gi == 4:
                    P.dma("sp", pc[:, :, 0:3], I["sconv"])
                for j in (2 * pj, 2 * pj + 1):
                    ps = nb()
                    for c in range(8):
                        P.mm(ps[:, 0:n], wm[:, c, j * 128:(j + 1) * 128], xnT[:, c, t0:t0 + n], start=(c == 0), stop=(c == 7))
                    P.act(pc[:, j, 3:3 + n], ps[:, 0:n], AF.Identity, bias=bcol(1544 + 128 * j))
                    P.act(caccs[j % 2][:, 0:n], ps[:, 0:n], AF.Identity, scale=mconv_w[:, j, 3:4], bias=wb3[:, j:j + 1])
                for j in (2 * pj, 2 * pj + 1):
                    cacc = caccs[j % 2]
                    for tap in (0, 1, 2):
                        P.stt(cacc[:, 0:n], pc[:, j, tap:tap + n], mconv_w[:, j, tap:tap + 1], cacc[:, 0:n], ALU.mult, ALU.add)
                    P.act(qk[:, j, 0:n], cacc[:, 0:n], AF.Silu, bias=mconv_b[:, j:j + 1])
                if pj == 3:
                    if gi == 3:
                        P.dma("sp", O["ml_convT"][0], pc[:, :, n:n + 3])
                    if gi == 4:
                        P.dma("sp", O["ml_convT"][1], pc[:, :, n:n + 3])
                    if gi < 3:
                        P.copy(pc[:, :, 0:3], pc[:, :, n:n + 3])
            return [lambda pj=pj: part(pj) for pj in range(4)]

        def ml_gates(gi):
            t0, n = GROUPS[gi]
            for h in range(4):
                ps = nb()
                for c in range(8):
                    P.mm(ps[:, 0:n], wm[:, c, 1536 + h * 128:1536 + (h + 1) * 128], xnT[:, c, t0:t0 + n], start=(c == 0), stop=(c == 7))
                P.act(sigos[gi % 2][:, h, 0:n], ps[:, 0:n], AF.Sigmoid, bias=bcol(3088 + 128 * h))

        for p_ in ml_proj_parts(0):
            p_()
        ti_global = 0
        for gi, (t0, n) in enumerate(GROUPS):
            if gi == 4:
                P.dma("sp", O["ml_cT"][0], CT[:, :, 0:128])
                P.dma("sp", O["ml_n"][0], CT[:, :, 128])
                P.dma("sp", CT[:, :, 0:128], I["sC"])
                P.dma("sp", CT[:, :, 128], I["sn"])
            if gi == 0:
                ml_gates(0)
            qkc[0] = qks[gi % 2]
            sigc[0] = sigos[gi % 2]
            nparts = ml_proj_parts(gi + 1) if gi + 1 < len(GROUPS) else []
            tiles = [(tt0, L) for (tt0, L) in TILES if t0 <= tt0 < t0 + n]
            prev = None
            for k_, (tt0, L) in enumerate(tiles):
                ti = ti_global
                ti_global += 1
                cur = (ti, tt0, L, tt0 - t0, msets[ti % 2])
                ml_step(prev, cur)
                prev = cur
                if k_ < len(nparts):
                    nparts[k_]()
                if k_ == min(1, len(tiles) - 1) and gi + 1 < len(GROUPS):
                    ml_gates(gi + 1)
            ml_step(prev, None)
        P.dma("sp", O["ml_cT"][1], CT[:, :, 0:128])
        P.dma("sp", O["ml_n"][1], CT[:, :, 128])
        ar.release(s1)
        dbg("bT", bT[:, 0, :])
        mT_off = ar.mark()
        mT = ar.bf(128, 4, NT)
        s1 = ar.mark()

        P.mark("mlstm")
        wq = ar.bf(128, 8, 512)
        P.dma("pool", wq, wv("w_in")[:, :, 3600:4112])
        wkv = ar.bf(128, 8, 1024)
        P.dma("pool", wkv, wv("w_mem_kv"))
        memx = ar.f32(128, 8, 256)
        P.dma("sp", memx, I["memT"].rearrange("(c p) n -> p c n", p=128))
        rbm = ar.f32(128, 256)
        rms_bcast(memx, 8, 256, 1024.0, rbm)
        memn = ar.bf(128, 8, 256)
        for c in range(8):
            P.stt(memn[:, c, :], memx[:, c, :], g_mem[:, c:c + 1], rbm, ALU.mult, ALU.mult)
        mkT = ar.bf(128, 2, 4, 256)
        mv = ar.bf(128, 2, 2, 512)
        tmpf = ar.f32(128, 512)
        for chh in range(4):
            ps = nb()
            for c in range(8):
                P.mm(ps[:, 0:256], wkv[:, c, chh * 128:(chh + 1) * 128], memn[:, c, :], start=(c == 0), stop=(c == 7))
            P.copy(tmpf[:, 0:256], ps[:, 0:256])
            P.copy(mkT[:, 0, chh, :], tmpf[:, 0:256])
            P.dma("sp", O["mem_kT"][chh * 128:(chh + 1) * 128, :], tmpf[:, 0:256])
        for mt in range(2):
            ps = nb()
            for c in range(8):
                P.mm(ps[:, :], memn[:, c, mt * 128:(mt + 1) * 128], wkv[:, c, 512:1024], start=(c == 0), stop=(c == 7))
            P.copy(tmpf, ps)
            P.copy(mv[:, 0, mt, :], tmpf)
            P.dma("sp", O["mem_v"][mt * 128:(mt + 1) * 128, :], tmpf)
        P.dma("pool", mkT[:, 1], I["cmkT"].rearrange("(c p) n -> p c n", p=128))
        P.dma("pool", mv[:, 1], I["cmv"].rearrange("(t p) f -> p t f", p=128))
        qhs = [ar.bf(128, 512), ar.bf(128, 512)]
        ptms = [ar.bf(128, 2, 512), ar.bf(128, 2, 512)]
        rlms = [ar.f32(128, 512), ar.f32(128, 512)]
        mits = [(gi, h) for gi in range(len(GROUPS)) for h in range(4)]

        def mm_A(i):
            gi, h = mits[i]
            t0, n = GROUPS[gi]
            ps = nb()
            for c in range(8):
                P.mm(ps[:, 0:n], wq[:, c, h * 128:(h + 1) * 128], xnT[:, c, t0:t0 + n], start=(c == 0), stop=(c == 7))
            P.act(qhs[i % 2][:, 0:n], ps[:, 0:n], AF.Identity, bias=bcol(3600 + 128 * h))

        def mm_B(i):
            gi, h = mits[i]
            t0, n = GROUPS[gi]
            seq = 0 if gi < 4 else 1
            qh, ptm, rlm = qhs[i % 2], ptms[i % 2], rlms[i % 2]
            for mt in range(2):
                sps = nb()
                P.mm(sps[:, 0:n], mkT[:, seq, h, mt * 128:(mt + 1) * 128], qh[:, 0:n])
                P.act(ptm[:, mt, 0:n], sps[:, 0:n], AF.Exp, scale=QSC)
            ops_ = nb()
            lps = nb()
            for mt in range(2):
                P.mm(ops_[:, 0:n], mv[:, seq, mt, h * 128:(h + 1) * 128], ptm[:, mt, 0:n], start=(mt == 0), stop=(mt == 1))
            for mt in range(2):
                P.mm(lps[:, 0:n], ones_b, ptm[:, mt, 0:n], start=(mt == 0), stop=(mt == 1))
            P.act(rlm[:, 0:n], lps[:, 0:n], AF.Ln)
            P.act(rlm[:, 0:n], rlm[:, 0:n], AF.Exp, scale=-1.0)
            P.tt(mT[:, h, t0:t0 + n], ops_[:, 0:n], rlm[:, 0:n], ALU.mult)

        mm_A(0)
        for i in range(len(mits)):
            if i + 1 < len(mits):
                mm_A(i + 1)
            mm_B(i)
        dbg("mT", mT[:, 0, :])
        ar.release(s1)

        P.mark("mem")
        mergedT = ar.bf(128, 8, NT)
        wo = ar.bf(128, 8, 1024)
        s2 = ar.mark()
        wgs = [[ar.bf(128, 8, 128) for b in range(3)] for _ in range(2)]
        wbs = [[ar.bf(128, 4, 128) for b in range(3)] for _ in range(2)]
        sg = [ar.f32(128, 512) for b in range(3)]
        macc = ar.f32(128, 512)
        mtmp = ar.f32(128, 512)
        brs = ("w_br_a", "w_br_b", "w_br_m")
        srcs = (aT, bT, mT)
        for oc in range(8):
            wg_ = wgs[oc % 2]
            wb_ = wbs[oc % 2]
            for b in range(3):
                g0 = 4112 + b * 1024 + oc * 128
                P.dma("pool", wg_[b], wv("w_in")[:, :, g0:g0 + 128])
                P.dma("pool", wb_[b], wv(brs[b])[:, :, oc * 128:(oc + 1) * 128])
            if oc == 2:
                P.dma("pool", wo, wv("w_out"))
            for (t0, n) in GROUPS:
                pp = []
                for b in range(3):
                    g0 = 4112 + b * 1024 + oc * 128
                    ps = nb()
                    for c in range(8):
                        P.mm(ps[:, 0:n], wg_[b][:, c, :], xnT[:, c, t0:t0 + n], start=(c == 0), stop=(c == 7))
                    P.act(sg[b][:, 0:n], ps[:, 0:n], AF.Sigmoid, bias=bcol(g0))
                for b in range(3):
                    ps = nb()
                    for c in range(4):
                        P.mm(ps[:, 0:n], wb_[b][:, c, :], srcs[b][:, c, t0:t0 + n], start=(c == 0), stop=(c == 3))
                    pp.append(ps)
                P.tt(macc[:, 0:n], sg[0][:, 0:n], pp[0][:, 0:n], ALU.mult)
                P.tt(mtmp[:, 0:n], sg[1][:, 0:n], pp[1][:, 0:n], ALU.mult)
                P.tt(macc[:, 0:n], macc[:, 0:n], mtmp[:, 0:n], ALU.add)
                P.tt(mtmp[:, 0:n], sg[2][:, 0:n], pp[2][:, 0:n], ALU.mult)
                P.tt(mergedT[:, oc, t0:t0 + n], macc[:, 0:n], mtmp[:, 0:n], ALU.add)
        dbg("mergedT", mergedT[:, 0, :])
        ar.release(s2)

        P.mark("s2_merge")
        oTs = [ar.f32(128, 8, 512), ar.f32_at(aT_off, 128, 8, 512)]
        xss = [ar.f32(128, 8, 512), ar.f32_at(bT_off, 128, 8, 512)]
        rbs = [ar.f32(128, 512), ar.f32(128, 512)]
        sqs = [ar.bf(128, 8, 512), ar.bf_at(mT_off, 128, 8, 512)]
        x1s3w = x1scr.rearrange("(c p) n -> p c n", p=128)

        def ob_A(gi):
            t0, n = GROUPS[gi]
            oT, xs = oTs[gi % 2], xss[gi % 2]
            P.dma("sp", xs[:, :, 0:n], xT3[:, :, t0:t0 + n])
            for oc in range(8):
                ps = nb()
                for c in range(8):
                    P.mm(ps[:, 0:n], wo[:, c, oc * 128:(oc + 1) * 128], mergedT[:, c, t0:t0 + n], start=(c == 0), stop=(c == 7))
                P.act(oT[:, oc, 0:n], ps[:, 0:n], AF.Copy)
                P.act(sqs[gi % 2][:, oc, 0:n], ps[:, 0:n], AF.Square)

        def ob_B(gi):
            t0, n = GROUPS[gi]
            oT, xs, rb, sq = oTs[gi % 2], xss[gi % 2], rbs[gi % 2], sqs[gi % 2]
            ps = nb()
            for c in range(8):
                P.mm(ps[:, 0:n], ones_b, sq[:, c, 0:n], start=(c == 0), stop=(c == 7))
            P.act(rb[:, 0:n], ps[:, 0:n], AF.Ln, bias=eps_col, scale=1.0 / 1024.0)
            P.act(rb[:, 0:n], rb[:, 0:n], AF.Exp, scale=-0.5)
            ps2 = nb()
            for oc in range(8):
                P.stt(oT[:, oc, 0:n], oT[:, oc, 0:n], g_post[:, oc:oc + 1], rb[:, 0:n], ALU.mult, ALU.mult)
                P.tt(xs[:, oc, 0:n], xs[:, oc, 0:n], oT[:, oc, 0:n], ALU.add)
                P.act(sq[:, oc, 0:n], xs[:, oc, 0:n], AF.Square)
                P.mm(ps2[:, 0:n], ones_b, sq[:, oc, 0:n], start=(oc == 0), stop=(oc == 7))
            P.dma("sp", x1s3w[:, :, t0:t0 + n], xs[:, :, 0:n])
            P.act(rb[:, 0:n], ps2[:, 0:n], AF.Ln, bias=eps_col, scale=1.0 / 1024.0)
            P.act(rb[:, 0:n], rb[:, 0:n], AF.Exp, scale=-0.5)
            for c in range(8):
                P.stt(xnT[:, c, t0:t0 + n], xs[:, c, 0:n], g_fpre[:, c:c + 1], rb[:, 0:n], ALU.mult, ALU.mult)

        ob_A(0)
        for gi in range(len(GROUPS)):
            if gi + 1 < len(GROUPS):
                ob_A(gi + 1)
            ob_B(gi)
        dbg("x1nT", xnT[:, 0, :])
        ar.release(m_after_xn)

        P.mark("s2b_out")
        hidT = ar.bf(128, 22, NT)
        wd3 = wv("w_down")
        wd1 = ar.bf(128, 22, 512)
        m_h = ar.mark()
        wus = [ar.bf(128, 8, 256), ar.bf(128, 8, 256)]
        apre_p = ar.f32(128, 2 + TP)
        bpre_p = ar.f32(128, 2 + TP)
        apre_s = ar.f32(128, 2 + TS)
        bpre_s = ar.f32(128, 2 + TS)
        accas = [ar.f32(128, 512) for _ in range(3)]
        accbs = [ar.f32(128, 512) for _ in range(3)]
        gas = [ar.f32(128, 512) for _ in range(3)]
        fit = [0]
        ftail = [None]
        w_up3 = wv("w_up")
        for c in range(22):
            wu = wus[c % 2]
            P.dma("pool", wu[:, :, 0:128], w_up3[:, :, c * 128:(c + 1) * 128])
            P.dma("pool", wu[:, :, 128:256], w_up3[:, :, 2816 + c * 128:2816 + (c + 1) * 128])
            if c == 2:
                P.dma("pool", wd1, wd3[:, :, 512:1024])
            ja, jb = c, 22 + c
            for gi, (t0, n) in enumerate(GROUPS):
                if gi < 4:
                    apre, bpre = apre_p[:, t0:t0 + n + 2], bpre_p[:, t0:t0 + n + 2]
                else:
                    apre, bpre = apre_s, bpre_s
                if gi == 0:
                    P.memset(apre_p[:, 0:2], 0.0)
                    P.memset(bpre_p[:, 0:2], 0.0)
                if gi == 4:
                    P.dma("sp", apre[:, 0:2], I["sfconv"][:, ja, :])
                    P.dma("sp", bpre[:, 0:2], I["sfconv"][:, jb, :])
                acca, accb, ga = accas[fit[0] % 3], accbs[fit[0] % 3], gas[fit[0] % 3]
                fit[0] += 1
                for (pre, off) in ((apre, 0), (bpre, 128)):
                    ps = nb()
                    for k in range(8):
                        P.mm(ps[:, 0:n], wu[:, k, off:off + 128], xnT[:, k, t0:t0 + n], start=(k == 0), stop=(k == 7))
                    P.act(pre[:, 2:2 + n], ps[:, 0:n], AF.Copy)
                    if off == 128:
                        P.act(accb[:, 0:n], ps[:, 0:n], AF.Identity, scale=fconv_w[:, jb, 2:3])
                    else:
                        P.act(acca[:, 0:n], ps[:, 0:n], AF.Identity, scale=fconv_w[:, ja, 2:3])
                P.stt(acca[:, 0:n], apre[:, 0:n], fconv_w[:, ja, 0:1], acca[:, 0:n], ALU.mult, ALU.add)
                P.stt(acca[:, 0:n], apre[:, 1:1 + n], fconv_w[:, ja, 1:2], acca[:, 0:n], ALU.mult, ALU.add)
                P.stt(accb[:, 0:n], bpre[:, 0:n], fconv_w[:, jb, 0:1], accb[:, 0:n], ALU.mult, ALU.add)
                P.stt(accb[:, 0:n], bpre[:, 1:1 + n], fconv_w[:, jb, 1:2], accb[:, 0:n], ALU.mult, ALU.add)
                if ftail[0] is not None:
                    ftail[0]()

                def _tail(ga=ga, acca=acca, accb=accb, n=n, ja=ja, jb=jb, c=c, t0=t0):
                    P.act(ga[:, 0:n], acca[:, 0:n], AF.Gelu_apprx_tanh, bias=fconv_b[:, ja:ja + 1])
                    P.stt(hidT[:, c, t0:t0 + n], accb[:, 0:n], fconv_b[:, jb:jb + 1], ga[:, 0:n], ALU.add, ALU.mult)
                ftail[0] = _tail
                if gi in (3, 4):
                    so = 0 if gi == 3 else 1
                    P.dma("sp", O["ffn_convT"][so][:, ja, :], apre[:, n:n + 2])
                    P.dma("sp", O["ffn_convT"][so][:, jb, :], bpre[:, n:n + 2])
        ftail[0]()
        dbg("hidT", hidT[:, 0, :])
        ar.release(m_h)
        P.mark("ffn_up")
        wd0 = ar.bf_at(xn_off, 128, 22, 512)
        P.dma("pool", wd0, wd3[:, :, 0:512])
        oT = ar.f32(128, 8, 512)
        xs = ar.f32(128, 8, 512)
        rb = ar.f32(128, 512)
        x1s3 = x1scr.rearrange("(c p) n -> p c n", p=128)
        yT3 = O["yT"].rearrange("(c p) n -> p c n", p=128)
        for gi, (t0, n) in enumerate(GROUPS):
            P.dma("sp", xs[:, :, 0:n], x1s3[:, :, t0:t0 + n])
            for oc in range(8):
                wd = wd0 if oc < 4 else wd1
                o4 = oc % 4
                ps = nb()
                for c in range(22):
                    P.mm(ps[:, 0:n], wd[:, c, o4 * 128:(o4 + 1) * 128], hidT[:, c, t0:t0 + n], start=(c == 0), stop=(c == 21))
                P.act(oT[:, oc, 0:n], ps[:, 0:n], AF.Copy)
            rms_bcast(oT[:, :, 0:n], 8, n, 1024.0, rb[:, 0:n])
            for oc in range(8):
                P.stt(oT[:, oc, 0:n], oT[:, oc, 0:n], g_fpost[:, oc:oc + 1], rb[:, 0:n], ALU.mult, ALU.mult)
                P.tt(xs[:, oc, 0:n], xs[:, oc, 0:n], oT[:, oc, 0:n], ALU.add)
            P.dma("sp", yT3[:, :, t0:t0 + n], xs[:, :, 0:n])
        P.mark("ffn_down")
        P.emit(Kq={'pool': 3})
    except _Stop:
        pass
    return nc, DBG, P, 0


_CACHE = {}


def _get_nc(debug=()):
    key = tuple(debug)
    if key not in _CACHE:
        _CACHE[key] = build(debug)
    return _CACHE[key]


def _consts():
    ident = np.eye(128, dtype=np.float32)
    s = np.arange(128)
    trimask = (s[None, :] >= s[:, None]).astype(np.float32)
    negmask = np.where(s[:, None] <= s[None, :], 0.0, -30000.0).astype(np.float32)
    bd = np.zeros((128, 128), np.float32)
    bd[:64, :64] = 1.0
    bd[64:, 64:] = 1.0
    sel4 = np.zeros((4, 4, 128), np.float32)
    for h in range(4):
        sel4[h, h, :] = 1.0
    return dict(ident=ident, trimask=trimask, negmask=negmask, bdones=bd, sel4=sel4)


def _fm(v):
    return np.ascontiguousarray(v.reshape(-1, 128).T)


def _prep_shared(inp):
    f = lambda a: np.ascontiguousarray(np.asarray(a, dtype=np.float32))
    b_in = f(inp["b_in"])[0]
    d = dict(
        w_in=f(inp["w_in"])[0],
        b_fm=np.ascontiguousarray(np.stack([b_in[s:s + 128] for s in FM_STARTS], axis=1)),
        bg_fox=np.ascontiguousarray(b_in[1536:1544][:, None]),
        bg_i=np.ascontiguousarray(b_in[3080:3084][:, None]),
        bg_f=np.ascontiguousarray(b_in[3084:3088][:, None]),
        b_row=np.ascontiguousarray(b_in[None, :]),
        g_pre=_fm(f(inp["norm_mix_pre"])[0]),
        gq2=np.ascontiguousarray(np.concatenate([f(inp["fox_q_norm"])[0]] * 2)[:, None]),
        gk2=np.ascontiguousarray(np.concatenate([f(inp["fox_k_norm"])[0]] * 2)[:, None]),
        mconv_w=np.ascontiguousarray(f(inp["mlstm_conv_w"])[0].reshape(4, 8, 128).transpose(2, 1, 0)),
        mconv_b=_fm(f(inp["mlstm_conv_b"])[0]),
        mhn=np.ascontiguousarray(f(inp["mlstm_head_norm"])[0].T),
        g_mem=_fm(f(inp["norm_mem"])[0]),
        w_mem_kv=f(inp["w_mem_kv"])[0],
        w_br_a=f(inp["w_br_a"])[0], w_br_b=f(inp["w_br_b"])[0], w_br_m=f(inp["w_br_m"])[0],
        w_out=f(inp["w_out"])[0],
        g_post=_fm(f(inp["norm_mix_post"])[0]),
        g_fpre=_fm(f(inp["norm_ffn_pre"])[0]),
        w_up=f(inp["w_up"])[0],
        fconv_w=np.ascontiguousarray(f(inp["ffn_conv_w"])[0].reshape(3, 44, 128).transpose(2, 1, 0)),
        fconv_b=_fm(f(inp["ffn_conv_b"])[0]),
        w_down=f(inp["w_down"])[0],
        g_fpost=_fm(f(inp["norm_ffn_post"])[0]),
    )
    d.update(_consts())
    return d


def _prep_core(inp, b):
    f = lambda a: np.asarray(a, dtype=np.float32)
    c = np.ascontiguousarray
    return dict(
        xT=c(np.concatenate([f(inp["x_prompt"])[b].T, f(inp["x_sample"])[b].T], axis=1)),
        ckT=c(f(inp["cache_fox_k"])[0, b].reshape(PAST, 512).T),
        cv=c(f(inp["cache_fox_v"])[0, b].reshape(32, 128, 8, 64).transpose(2, 1, 0, 3)),
        clogfT=c(f(inp["cache_fox_logf"])[0, b].T),
        sC=c(f(inp["state_mlstm_c"])[0, b].transpose(2, 0, 1)),
        sn=c(f(inp["state_mlstm_n"])[0, b].T),
        sm=c(f(inp["state_mlstm_m"])[0, b][:, None]),
        sconv=c(f(inp["state_mlstm_conv"])[0, b].reshape(3, 8, 128).transpose(2, 1, 0)),
        cmkT=c(f(inp["cache_mem_k"])[0, b].reshape(256, 512).T),
        cmv=c(f(inp["cache_mem_v"])[0, b].reshape(256, 512)),
        sfconv=c(f(inp["state_ffn_conv"])[0, b].reshape(2, 44, 128).transpose(2, 1, 0)),
        memT=c(f(inp["mem_prompt"])[b].T),
    )


def _assemble(results):
    B = len(results)
    z = lambda *s: np.zeros(s, np.float32)
    y_p, y_s = z(B, TP, 1024), z(B, TS, 1024)
    fk_p, fv_p, fl_p = z(1, B, TP, 8, 64), z(1, B, TP, 8, 64), z(1, B, TP, 8)
    fk_s, fv_s, fl_s = z(1, B, TS, 8, 64), z(1, B, TS, 8, 64), z(1, B, TS, 8)
    c_p, n_p, m_p = z(1, B, 4, 128, 128), z(1, B, 4, 128), z(1, B, 4)
    c_s, n_s, m_s = z(1, B, 4, 128, 128), z(1, B, 4, 128), z(1, B, 4)
    cv_p, cv_s = z(1, B, 3, 1024), z(1, B, 3, 1024)
    fc_p, fc_s = z(1, B, 2, 5632), z(1, B, 2, 5632)
    mk_p, mv_p = z(1, B, 256, 4, 128), z(1, B, 256, 4, 128)
    for b, r in enumerate(results):
        yT = r["yT"]
        y_p[b] = yT[:, :TP].T
        y_s[b] = yT[:, TP:].T
        kT = r["fox_kT"]
        fk_p[0, b] = kT[:, :TP].T.reshape(TP, 8, 64)
        fk_s[0, b] = kT[:, TP:].T.reshape(TS, 8, 64)
        fv = r["fox_v"]
        fv_p[0, b] = fv[:TP].reshape(TP, 8, 64)
        fv_s[0, b] = fv[TP:].reshape(TS, 8, 64)
        lf = r["fox_logfT"]
        fl_p[0, b] = lf[:, :TP].T
        fl_s[0, b] = lf[:, TP:].T
        for (si, cc, nn, mm, cvv, fcc) in ((0, c_p, n_p, m_p, cv_p, fc_p), (1, c_s, n_s, m_s, cv_s, fc_s)):
            cc[0, b] = r["ml_cT"][si].transpose(1, 2, 0)
            nn[0, b] = r["ml_n"][si].T
            mm[0, b] = r["ml_m"][si][:, 0]
            cvv[0, b] = r["ml_convT"][si].transpose(2, 1, 0).reshape(3, 1024)
            fcc[0, b] = r["ffn_convT"][si].transpose(2, 1, 0).reshape(2, 5632)
        mk_p[0, b] = r["mem_kT"].T.reshape(256, 4, 128)
        mv_p[0, b] = r["mem_v"].reshape(256, 4, 128)
    return (y_p, y_s, fk_p, fv_p, fl_p, c_p, n_p, m_p, cv_p, fc_p, mk_p, mv_p,
            fk_s, fv_s, fl_s, c_s, n_s, m_s, cv_s, fc_s)


def kernel(**inputs):
    nc = _get_nc()[0]
    shared = _prep_shared(inputs)
    in_maps = []
    for b in range(8):
        d = dict(shared)
        d.update(_prep_core(inputs, b))
        in_maps.append(d)
    res = run_bass_kernel_spmd(nc, in_maps, core_ids=list(range(8)))
    return _assemble(res.results)
```

```python
import numpy as np
from contextlib import ExitStack
import concourse.bass as bass
import concourse.mybir as mybir

F32 = mybir.dt.float32
BF16 = mybir.dt.bfloat16
AF = mybir.ActivationFunctionType
ALU = mybir.AluOpType
AX = mybir.AxisListType


def _esize(dt):
    n = str(dt)
    if "float32" in n or "int32" in n:
        return 4
    if "bfloat16" in n or "float16" in n or "int16" in n:
        return 2
    if "int8" in n or "float8" in n:
        return 1
    if "64" in n:
        return 8
    raise ValueError(n)


def _boxes(ap):
    t = ap.tensor
    name = t.name
    es = _esize(ap.dtype)
    dims = [(int(s), int(n)) for s, n in ap.ap]
    off = int(ap.offset)
    if "DRAM" in str(ap.space).upper() or "HBM" in str(ap.space).upper():
        ext = sum((n - 1) * abs(s) for s, n in dims)
        return name, [(0, 1, off * es, (off + ext + 1) * es)]
    tes = _esize(t.dtype)
    rowbytes = int(np.prod([int(x) for x in t.shape[1:]])) * tes
    row = rowbytes // es
    p0 = off // row
    f0 = off % row
    pext = 0
    fd = []
    for s, n in dims:
        if n == 1:
            continue
        if s != 0 and s % row == 0:
            pext += (n - 1) * (s // row)
        elif s != 0:
            fd.append((abs(s), n))
    fd.sort(reverse=True)
    p1 = p0 + pext + 1
    if len(fd) >= 2:
        inner = sum((n - 1) * s for s, n in fd[1:]) + 1
        s0, n0 = fd[0]
        if s0 >= inner and n0 <= 64:
            return name, [(p0, p1, (f0 + i * s0) * es, (f0 + i * s0 + inner) * es) for i in range(n0)]
    ext = sum((n - 1) * s for s, n in fd) + 1
    return name, [(p0, p1, f0 * es, (f0 + ext) * es)]


def _ov(a, b):
    for x in a:
        for y in b:
            if x[0] < y[1] and y[0] < x[1] and x[2] < y[3] and y[2] < x[3]:
                return True
    return False


def _contained(a, b):
    for x in a:
        ok = False
        for y in b:
            if y[0] <= x[0] and x[1] <= y[1] and y[2] <= x[2] and x[3] <= y[3]:
                ok = True
                break
        if not ok:
            return False
    return True


class _Stop(Exception):
    pass


class Prog:
    ENGS = ("pe", "act", "dve", "pool", "sp")

    def __init__(self, nc):
        self.nc = nc
        self.ops = []
        self.hist = {}

    def add(self, eng, fn, reads=(), writes=(), dma=False):
        idx = len(self.ops)
        deps = set()
        rb = [_boxes(a) for a in reads]
        wb = [_boxes(a) for a in writes]
        for name, bx in rb:
            for e in self.hist.get(name, ()):
                if e[2] and _ov(e[0], bx):
                    deps.add(e[1])
        for name, bx in wb:
            for e in self.hist.get(name, ()):
                if _ov(e[0], bx):
                    deps.add(e[1])
        for name, bx in wb:
            lst = [e for e in self.hist.get(name, ()) if not _contained(e[0], bx)]
            lst.append((bx, idx, True, eng, dma))
            self.hist[name] = lst
        for name, bx in rb:
            lst = self.hist.setdefault(name, [])
            rep = False
            if not dma:
                for i, e in enumerate(lst):
                    if (not e[2]) and e[3] == eng and (not e[4]) and e[0] == bx:
                        lst[i] = (bx, idx, False, eng, dma)
                        rep = True
                        break
            if not rep:
                lst.append((bx, idx, False, eng, dma))
        deps.discard(idx)
        self.ops.append((eng, fn, deps, dma))
        return idx

    def mark(self, name):
        if not hasattr(self, 'marks'):
            self.marks = []
        self.marks.append((name, sum(1 for o in self.ops if o[0] == 'pe')))
        if getattr(self, 'stop_at', None) == name:
            self.emit()
            raise _Stop()

    def dma(self, q, out, in_, **kw):
        kw.setdefault('allow_slow_non_contiguous', True)
        return self.add(q, lambda e: e.dma_start(out=out, in_=in_, **kw), [in_], [out], dma=True)

    def mm(self, out, lhsT, rhs, start=True, stop=True, acc=False):
        rd = [lhsT, rhs] + ([out] if not start else [])
        return self.add("pe", lambda e: e.matmul(out, lhsT, rhs, start=start, stop=stop), rd, [out])

    def tr(self, out, in_, ident):
        return self.add("pe", lambda e: e.transpose(out, in_, ident), [in_, ident], [out])

    def act(self, out, in_, func, bias=None, scale=None, accum_out=None, eng="act"):
        rd = [in_]
        kw = {}
        if bias is not None:
            kw["bias"] = bias
            if not isinstance(bias, (int, float)):
                rd.append(bias)
        if scale is not None:
            kw["scale"] = scale
            if not isinstance(scale, (int, float)):
                rd.append(scale)
        wr = [out]
        if accum_out is not None:
            kw["accum_out"] = accum_out
            wr.append(accum_out)
        return self.add("act", lambda e: e.activation(out, in_, func, **kw), rd, wr)

    def tt(self, out, in0, in1, op, eng="dve"):
        return self.add(eng, lambda e: e.tensor_tensor(out, in0, in1, op), [in0, in1], [out])

    def ts(self, out, in0, s1, op0, s2=None, op1=None, eng="dve"):
        rd = [in0] + [s for s in (s1, s2) if s is not None and not isinstance(s, (int, float))]
        if op1 is None:
            return self.add(eng, lambda e: e.tensor_scalar(out, in0, s1, None, op0), rd, [out])
        return self.add(eng, lambda e: e.tensor_scalar(out, in0, s1, s2, op0, op1), rd, [out])

    def stt(self, out, in0, scalar, in1, op0, op1, eng="dve"):
        rd = [in0, in1] + ([] if isinstance(scalar, (int, float)) else [scalar])
        return self.add(eng, lambda e: e.scalar_tensor_tensor(out, in0, scalar, in1, op0, op1), rd, [out])

    def copy(self, out, in_, eng="dve"):
        return self.add(eng, lambda e: e.tensor_copy(out, in_), [in_], [out])

    def memset(self, out, val, eng="dve"):
        return self.add(eng, lambda e: e.memset(out, val), [], [out])

    def recip(self, out, in_):
        return self.add("dve", lambda e: e.reciprocal(out, in_), [in_], [out])

    def scan(self, out, d0, d1, init, op0, op1):
        rd = [d0, d1] + ([] if isinstance(init, (int, float)) else [init])
        return self.add("dve", lambda e: e.tensor_tensor_scan(out, d0, d1, init, op0, op1), rd, [out])

    def emit(self, R=20000, K=8, Kq=None):
        Kq = dict(Kq or {})
        KK = {e: Kq.get(e, K) for e in self.ENGS}
        nc = self.nc
        ops = self.ops
        needed = set()
        for eng, fn, deps, dma in ops:
            for d in deps:
                de, _, _, ddma = ops[d]
                if ddma:
                    continue
                if de == "pe" and eng == "pe" and not dma:
                    continue
                needed.add(d)
        sigidx = {}
        cnt = {e: 0 for e in self.ENGS}
        for i, (eng, fn, deps, dma) in enumerate(ops):
            if (not dma) and i in needed:
                sigidx[i] = cnt[eng]
                cnt[eng] += 1
        dmaidx = {}
        dcnt = {e: 0 for e in self.ENGS}
        for i, (eng, fn, deps, dma) in enumerate(ops):
            if dma:
                dmaidx[i] = dcnt[eng]
                dcnt[eng] += 1
        with ExitStack() as st:
            csem = {e: [st.enter_context(nc.semaphore(f"c_{e}_{j}")) for j in range(max(1, (cnt[e] + R - 1) // R))]
                    for e in self.ENGS}
            dsem = {e: [st.enter_context(nc.semaphore(f"d_{e}_{j}")) for j in range(min(KK[e], dcnt[e]))]
                    for e in self.ENGS}
            block = st.enter_context(nc.Block())

            def run(me, e):
                waited_c = {x: -1 for x in self.ENGS}
                waited_d = {}

                def wait_dma(d):
                    q = ops[d][0]
                    n = dmaidx[d]
                    K = KK[q]
                    sem = dsem[q][n % K]
                    val = 16 * (n // K + 1)
                    key = (q, n % K)
                    if waited_d.get(key, 0) >= val:
                        return
                    e.wait_ge(sem, val)
                    waited_d[key] = val

                for i, (eng, fn, deps, dma) in enumerate(ops):
                    if eng != me:
                        continue
                    for d in sorted(deps):
                        de, _, _, ddma = ops[d]
                        if ddma:
                            wait_dma(d)
                        else:
                            if de == "pe" and me == "pe" and not dma:
                                continue
                            g = sigidx[d]
                            if waited_c[de] >= g:
                                continue
                            e.wait_ge(csem[de][g // R], g % R + 1)
                            waited_c[de] = g
                    if dma:
                        n = dmaidx[i]
                        K = KK[me]
                        sem = dsem[me][n % K]
                        if n >= K:
                            key = (me, n % K)
                            val = 16 * (n // K)
                            if waited_d.get(key, 0) < val:
                                e.wait_ge(sem, val)
                                waited_d[key] = val
                        ins = fn(e)
                        ins.then_inc(sem, 16)
                    else:
                        ins = fn(e)
                        if i in sigidx:
                            g = sigidx[i]
                            ins.then_inc(csem[me][g // R], 1)
                K = KK[me]
                for j in range(min(K, dcnt[me])):
                    uses = (dcnt[me] - 1 - j) // K + 1
                    val = 16 * uses
                    if waited_d.get((me, j), 0) < val:
                        e.wait_ge(dsem[me][j], val)

            @block.sync
            def _(e):
                run("sp", e)

            @block.scalar
            def _(e):
                run("act", e)

            @block.vector
            def _(e):
                run("dve", e)

            @block.gpsimd
            def _(e):
                run("pool", e)

            @block.tensor
            def _(e):
                run("pe", e)

from concourse.bass_utils import run_bass_kernel_spmd

NT = 2112
TP = 2048
TS = 64
PAST = 4096
EPS = 1e-6
GROUPS = [(0, 512), (512, 512), (1024, 512), (1536, 512), (2048, 64)]
TILES = [(i * 128, 128) for i in range(16)] + [(2048, 64)]
QSC = 128.0 ** -0.5

FM_STARTS = ([0, 128, 256, 384] + [512, 640, 768, 896] + [1544 + 128 * i for i in range(8)]
             + [3088 + 128 * i for i in range(4)]
             + [3600 + 128 * i for i in range(4)] + [4112 + 128 * i for i in range(24)])
FM_COL = {s: i for i, s in enumerate(FM_STARTS)}

IN_SHAPES = dict(
    xT=[1024, NT], w_in=[1024, 7184], b_fm=[128, 48], bg_fox=[8, 1], bg_i=[4, 1], bg_f=[4, 1],
    b_row=[1, 7184], g_pre=[128, 8], gq2=[128, 1], gk2=[128, 1], mconv_w=[128, 8, 4], mconv_b=[128, 8],
    mhn=[128, 4], g_mem=[128, 8], w_mem_kv=[1024, 1024], w_br_a=[512, 1024], w_br_b=[512, 1024],
    w_br_m=[512, 1024], w_out=[1024, 1024], g_post=[128, 8], g_fpre=[128, 8], w_up=[1024, 5632],
    fconv_w=[128, 44, 3], fconv_b=[128, 44], w_down=[2816, 1024], g_fpost=[128, 8],
    ckT=[512, PAST], cv=[8, 128, 32, 64], clogfT=[8, PAST], sC=[128, 4, 128], sn=[128, 4], sm=[4, 1],
    sconv=[128, 8, 3], cmkT=[512, 256], cmv=[256, 512], sfconv=[128, 44, 2], memT=[1024, 256],
    ident=[128, 128], trimask=[128, 128], negmask=[128, 128], bdones=[128, 128], sel4=[4, 4, 128],
)
OUT_SHAPES = dict(
    yT=[1024, NT], fox_kT=[512, NT], fox_v=[NT, 512], fox_logfT=[8, NT],
    ml_cT=[2, 128, 4, 128], ml_n=[2, 128, 4], ml_m=[2, 4, 1], ml_convT=[2, 128, 8, 3],
    ffn_convT=[2, 128, 44, 2], mem_kT=[512, 256], mem_v=[256, 512],
)


class Arena:
    def __init__(self, A, cap):
        self.A = A
        self.cap = cap
        self.top = 0
        self.hw = 0

    def _alloc(self, words):
        off = self.top
        self.top += words
        assert self.top <= self.cap, f"arena overflow {self.top} > {self.cap}"
        self.hw = max(self.hw, self.top)
        return off

    def f32(self, *shape):
        n = int(np.prod(shape[1:]))
        off = self._alloc(n)
        return self._view(self.A[:, off:off + n], shape)

    def bf(self, *shape):
        n = int(np.prod(shape[1:]))
        words = (n + 1) // 2
        off = self._alloc(words)
        return self._view(self.A[:, off:off + words].bitcast(BF16)[:, 0:n], shape)

    @staticmethod
    def _view(v, shape):
        p = shape[0]
        if len(shape) == 3:
            v = v.rearrange("p (a b) -> p a b", b=shape[2])
        elif len(shape) == 4:
            v = v.rearrange("p (a b c) -> p a b c", b=shape[2], c=shape[3])
        if p < 128:
            v = v[0:p]
        return v

    def f32_at(self, off, *shape):
        n = int(np.prod(shape[1:]))
        return self._view(self.A[:, off:off + n], shape)

    def bf_at(self, off, *shape):
        n = int(np.prod(shape[1:]))
        words = (n + 1) // 2
        return self._view(self.A[:, off:off + words].bitcast(BF16)[:, 0:n], shape)

    def mark(self):
        return self.top

    def release(self, m):
        self.top = m


def build(debug=(), stop_at=None, salt=None):
    nc = bass.Bass("TRN2", target_bir_lowering=False)
    I = {k: nc.dram_tensor(k, list(v), F32, kind="ExternalInput").ap() for k, v in IN_SHAPES.items()}
    O = {k: nc.dram_tensor(k, list(v), F32, kind="ExternalOutput").ap() for k, v in OUT_SHAPES.items()}
    x1scr = nc.dram_tensor("x1scr", [1024, NT], F32, kind="Internal").ap()
    DBG = {}
    P = Prog(nc)
    P.stop_at = stop_at
    CAP = 53000
    try:
      with nc.sbuf_tensor("A", [128, CAP], F32) as A_, nc.psum_tensor("PS", [128, 8, 512], F32) as PS:
        ar = Arena(A_, CAP)
        hw_box = [ar]
        bank_ctr = [0]

        def nb():
            b = bank_ctr[0] % 8
            bank_ctr[0] += 1
            return PS[:, b, :]

        def dbg(name, ap):
            if name in debug:
                shp = [int(s) for s in ap.shape]
                d = nc.dram_tensor("dbg_" + name, shp, F32, kind="ExternalOutput").ap()
                DBG[name] = shp
                if ap.dtype == F32 and "PSUM" not in str(ap.space).upper():
                    P.dma("sp", d, ap)
                else:
                    m = ar.mark()
                    t = ar.f32(*([128] + shp[1:]))[0:shp[0]]
                    P.copy(t, ap)
                    P.dma("sp", d, t)
                    ar.release(m)

        wv = lambda name: I[name].rearrange("(c p) n -> p c n", p=128)
        r3 = lambda ap, b: ap.rearrange("p (a b) -> p a b", b=b)

        ident_f = ar.f32(128, 128)
        ident_b = ar.bf(128, 128)
        trimask = ar.bf(128, 128)
        negmask = ar.bf(128, 128)
        bdones = ar.bf(128, 128)
        ones_b = ar.bf(128, 128)
        ones_f = ar.f32(128, 128)
        sel4 = ar.f32(4, 4, 128)
        b_fm = ar.f32(128, 48)
        g_pre = ar.f32(128, 8)
        g_post = ar.f32(128, 8)
        g_fpre = ar.f32(128, 8)
        g_fpost = ar.f32(128, 8)
        g_mem = ar.f32(128, 8)
        gq2 = ar.f32(128, 1)
        gk2 = ar.f32(128, 1)
        mhn = ar.f32(128, 4)
        mconv_w = ar.f32(128, 8, 4)
        mconv_b = ar.f32(128, 8)
        fconv_w = ar.f32(128, 44, 3)
        fconv_b = ar.f32(128, 44)
        def load_consts():
            P.dma("sp", ident_f, I["ident"])
            P.dma("pool", ident_b, I["ident"])
            P.dma("pool", trimask, I["trimask"])
            P.dma("pool", negmask, I["negmask"])
            P.dma("pool", bdones, I["bdones"])
            P.dma("sp", sel4, I["sel4"])
            for t, n in ((b_fm, "b_fm"), (g_pre, "g_pre"), (g_post, "g_post"), (g_fpre, "g_fpre"), (g_fpost, "g_fpost"),
                         (g_mem, "g_mem"), (gq2, "gq2"), (gk2, "gk2"), (mhn, "mhn"), (mconv_w, "mconv_w"),
                         (mconv_b, "mconv_b"), (fconv_w, "fconv_w"), (fconv_b, "fconv_b")):
                P.dma("sp", t, I[n])
            P.ts(gq8, gq2, 0.125, ALU.mult)
        P.memset(ones_b, 1.0)
        P.memset(ones_f, 1.0)
        eps_col = ar.f32(128, 1)
        P.memset(eps_col, EPS)
        one_col = ar.f32(128, 1)
        P.memset(one_col, 1.0)
        gq8 = ar.f32(128, 1)
        bcol = lambda start: b_fm[:, FM_COL[start]:FM_COL[start] + 1]

        def rms_bcast(src, C, n, D, rb, sq=None):
            m = ar.mark()
            if sq is None:
                sq = ar.bf(128, C, n)
            ps = nb()
            for c in range(C):
                P.act(sq[:, c, :], src[:, c, :], AF.Square)
                P.mm(ps[:, 0:n], ones_b, sq[:, c, :], start=(c == 0), stop=(c == C - 1))
            P.act(rb, ps[:, 0:n], AF.Ln, bias=eps_col, scale=1.0 / D)
            P.act(rb, rb, AF.Exp, scale=-0.5)
            ar.release(m)

        xn_off = ar.mark()
        xnT = ar.bf(128, 8, NT)
        m_after_xn = ar.mark()
        aT_off = ar.mark()
        aT = ar.bf(128, 4, NT)
        chi = ar.bf(8, NT)
        clo = ar.bf(8, NT)
        rdl = ar.f32(4, NT)
        wg_tok = ar.f32(128, 17, 4)
        dec_b = ar.f32(128, 4, 17)

        xT3 = I["xT"].rearrange("(c p) n -> p c n", p=128)
        m = ar.mark()
        xs2 = [ar.f32(128, 8, 512), ar.f32(128, 8, 512)]
        rb2 = [ar.f32(128, 512), ar.f32(128, 512)]
        sq2 = [ar.bf(128, 8, 512), ar.bf(128, 8, 512)]
        for gi, (t0, n) in enumerate(GROUPS):
            xs = xs2[gi % 2][:, :, 0:n]
            P.dma("sp", xs, xT3[:, :, t0:t0 + n])
            if gi == 0:
                load_consts()
            rb = rb2[gi % 2][:, 0:n]
            rms_bcast(xs, 8, n, 1024.0, rb, sq=sq2[gi % 2][:, :, 0:n])
            for c in range(8):
                P.stt(xnT[:, c, t0:t0 + n], xs[:, c, :], g_pre[:, c:c + 1], rb, ALU.mult, ALU.mult)
        ar.release(m)
        dbg("xnT", xnT[:, 0, :])

        P.mark("s0_norm")
        m1a = ar.mark()
        wgf = ar.bf(128, 8, 8)
        wgi = ar.bf(128, 8, 4)
        wgm = ar.bf(128, 8, 4)
        P.dma("pool", wgf, wv("w_in")[:, :, 1536:1544])
        P.dma("pool", wgi, wv("w_in")[:, :, 3080:3084])
        P.dma("pool", wgm, wv("w_in")[:, :, 3084:3088])
        bgf = ar.f32(8, 1)
        bgi = ar.f32(4, 1)
        bgm = ar.f32(4, 1)
        P.dma("sp", bgf, I["bg_fox"])
        P.dma("sp", bgi, I["bg_i"])
        P.dma("sp", bgm, I["bg_f"])
        zrow = ar.f32(8, NT)
        P.memset(zrow, 0.0)
        flog = ar.f32(8, NT)
        cfox = ar.f32(8, NT)
        gi_r = ar.f32(4, NT)
        mlf = ar.f32(4, NT)
        A_r = ar.f32(4, NT)
        G_r = ar.f32(4, NT)
        wgT = ar.f32(4, NT)
        for (wt, bc, dst, rows, ls) in ((wgf, bgf, flog, 8, True), (wgi, bgi, gi_r, 4, False), (wgm, bgm, mlf, 4, True)):
            nbias = ar.f32(rows, 1)
            P.ts(nbias, bc, -1.0, ALU.mult)
            for (t0, n) in GROUPS:
                ps = nb()
                for c in range(8):
                    P.mm(ps[0:rows, 0:n], wt[:, c, :], xnT[:, c, t0:t0 + n], start=(c == 0), stop=(c == 7))
                if ls:
                    m = ar.mark()
                    e = ar.f32(rows, n)
                    P.act(e, ps[0:rows, 0:n], AF.Exp, bias=nbias, scale=-1.0)
                    P.act(e, e, AF.Ln, bias=one_col[0:rows], scale=1.0)
                    P.ts(dst[:, t0:t0 + n], e, -1.0, ALU.mult)
                    ar.release(m)
                else:
                    P.act(dst[:, t0:t0 + n], ps[0:rows, 0:n], AF.Identity, bias=bc)
        P.dma("sp", O["fox_logfT"], flog)
        P.scan(cfox[:, 0:TP], flog[:, 0:TP], zrow[:, 0:TP], 0.0, ALU.add, ALU.add)
        P.scan(cfox[:, TP:NT], flog[:, TP:NT], zrow[:, 0:TS], 0.0, ALU.add, ALU.add)
        P.copy(chi, cfox)
        P.tt(clo, cfox, chi, ALU.subtract)
        sm0 = ar.f32(4, 1)
        P.dma("sp", sm0, I["sm"])
        zr4 = zrow[0:4]
        Bc = ar.f32(4, NT)
        P.scan(Bc[:, 0:TP], mlf[:, 0:TP], zr4[:, 0:TP], 0.0, ALU.add, ALU.add)
        P.scan(Bc[:, TP:NT], mlf[:, TP:NT], zr4[:, 0:TS], 0.0, ALU.add, ALU.add)
        P.tt(A_r, gi_r, Bc, ALU.subtract)
        P.scan(G_r[:, 0:TP], A_r[:, 0:TP], A_r[:, 0:TP], 0.0, ALU.max, ALU.max)
        P.scan(G_r[:, TP:NT], A_r[:, TP:NT], A_r[:, TP:NT], sm0, ALU.max, ALU.max)
        gend = ar.f32(4, 17)
        mprev = ar.f32(4, 17)
        P.copy(gend[:, 0:16], G_r[:, 127:TP:128])
        P.copy(gend[:, 16:17], G_r[:, NT - 1:NT])
        P.memset(mprev[:, 0:1], 0.0)
        P.copy(mprev[:, 1:16], gend[:, 0:15])
        P.copy(mprev[:, 16:17], sm0)
        dec = ar.f32(4, 17)
        P.tt(dec, mprev, gend, ALU.subtract)
        P.act(dec, dec, AF.Exp)
        gend_b = gend[:, 0:16].unsqueeze(2).broadcast_to([4, 16, 128])
        P.tt(r3(wgT[:, 0:TP], 128), r3(A_r[:, 0:TP], 128), gend_b, ALU.subtract)
        P.ts(wgT[:, TP:NT], A_r[:, TP:NT], gend[:, 16:17], ALU.subtract)
        P.act(wgT, wgT, AF.Exp)
        P.tt(r3(rdl[:, 0:TP], 128), r3(Bc[:, 0:TP], 128), gend_b, ALU.add)
        P.ts(rdl[:, TP:NT], Bc[:, TP:NT], gend[:, 16:17], ALU.add)
        P.ts(rdl, rdl, -1.0, ALU.mult)
        mTo = ar.f32(4, 2)
        P.tt(mTo[:, 0:1], G_r[:, TP - 1:TP], Bc[:, TP - 1:TP], ALU.add)
        P.tt(mTo[:, 1:2], G_r[:, NT - 1:NT], Bc[:, NT - 1:NT], ALU.add)
        P.dma("sp", O["ml_m"][0], mTo[:, 0:1])
        P.dma("sp", O["ml_m"][1], mTo[:, 1:2])
        for ti, (t0, L) in enumerate(TILES):
            ps = nb()
            P.tr(ps[0:L, 0:4], wgT[:, t0:t0 + L], ident_f[0:4, 0:4])
            P.copy(wg_tok[0:L, ti, :], ps[0:L, 0:4])
        for h in range(4):
            ps = nb()
            P.mm(ps[:, 0:17], sel4[:, h, :], dec)
            P.copy(dec_b[:, h, :], ps[:, 0:17])
        dbg("cfox", cfox)
        dbg("G_r", G_r)
        dbg("wgT", wgT)
        dbg("rdl", rdl)
        ar.release(m1a)
        s1 = ar.mark()

        P.mark("s1a_gates")
        cchi = ar.bf(8, PAST)
        cclo = ar.bf(8, PAST)
        m_s = ar.mark()
        ccache = ar.f32(8, PAST)
        lcache = ar.f32(8, PAST)
        zc = ar.f32(8, PAST)
        P.memset(zc, 0.0)
        P.dma("sp", lcache, I["clogfT"])
        P.scan(ccache, lcache, zc, 0.0, ALU.add, ALU.add)
        ctot = ar.f32(8, 1)
        P.copy(ctot, ccache[:, PAST - 1:PAST])
        P.ts(ccache, ccache, ctot, ALU.subtract)
        P.copy(cchi, ccache)
        P.tt(cclo, ccache, cchi, ALU.subtract)
        ar.release(m_s)
        qT = ar.bf(128, 4, NT)
        kT = ar.bf(128, 4, NT)
        V = ar.bf(128, 17, 512)
        m_w = ar.mark()
        wf = ar.bf(128, 8, 1536)
        P.dma("pool", wf, wv("w_in")[:, :, 0:1536])
        bvrow = ar.f32(128, 512)
        P.dma("sp", bvrow, I["b_row"][:, 1024:1536].partition_broadcast(128))
        ptmp = [(ar.f32(128, 512), ar.bf(128, 512), ar.f32(128, 512), ar.f32(128, 512)) for _ in range(3)]
        its = [(t0, n, which, ch) for (t0, n) in GROUPS for which in range(2) for ch in range(4)]

        def fp_A(it):
            t0, n, which, ch = it
            col0 = which * 512 + ch * 128
            ps = nb()
            for c in range(8):
                P.mm(ps[:, 0:n], wf[:, c, col0:col0 + 128], xnT[:, c, t0:t0 + n], start=(c == 0), stop=(c == 7))
            return ps

        def fp_B(i, it, ps):
            t0, n, which, ch = it
            col0 = which * 512 + ch * 128
            z_, sq_, r_, kf_ = ptmp[i % 3]
            z = z_[:, 0:n]
            sq = sq_[:, 0:n]
            P.act(z, ps[:, 0:n], AF.Identity, bias=bcol(col0))
            P.act(sq, ps[:, 0:n], AF.Square, bias=bcol(col0))
            ps2 = nb()
            P.mm(ps2[:, 0:n], bdones, sq)
            r = r_[:, 0:n]
            P.act(r, ps2[:, 0:n], AF.Ln, bias=eps_col, scale=1.0 / 64)
            P.act(r, r, AF.Exp, scale=-0.5)
            if which == 0:
                P.stt(qT[:, ch, t0:t0 + n], z, gq8, r, ALU.mult, ALU.mult)
            else:
                kf = kf_[:, 0:n]
                P.stt(kf, z, gk2, r, ALU.mult, ALU.mult)
                P.copy(kT[:, ch, t0:t0 + n], kf)
                P.dma("sp", O["fox_kT"][ch * 128:(ch + 1) * 128, t0:t0 + n], kf)

        vtmp = [ar.f32(128, 512), ar.f32(128, 512)]

        def fp_V(ti):
            t0, L = TILES[ti]
            ps = nb()
            for c in range(8):
                P.mm(ps[0:L, :], xnT[:, c, t0:t0 + L], wf[:, c, 1024:1536], start=(c == 0), stop=(c == 7))
            vf = vtmp[ti % 2]
            P.tt(vf[0:L], ps[0:L, :], bvrow[0:L], ALU.add)
            P.copy(V[0:L, ti, :], vf[0:L])
            P.dma("sp", O["fox_v"][t0:t0 + L, :], vf[0:L])

        psq = [fp_A(its[0])]
        vnext = 0
        for i, it in enumerate(its):
            if i + 1 < len(its):
                psq.append(fp_A(its[i + 1]))
            if i % 2 == 1 and vnext < 17:
                fp_V(vnext)
                vnext += 1
            fp_B(i, it, psq[i])
        while vnext < 17:
            fp_V(vnext)
            vnext += 1
        ar.release(m_w)
        dbg("qT", qT[:, 0, :])
        dbg("kT", kT[:, 0, :])

        P.mark("s1b_foxproj")
        pbufs = [ar.bf(128, 2, 512) for _ in range(4)]
        pctr = [0]
        fctr = [0, 0]
        rbuf = ar.f32(128, 512)
        KMAX = PAST + TS
        qas = [ar.bf(128, NT), ar.bf(128, NT)]
        kas_p = [ar.bf(128, TP), ar.bf(128, TP)]
        kas_s = [ar.bf(128, KMAX), ar.bf(128, KMAX)]
        vexts_p = [ar.bf(128, 16, 128), ar.bf(128, 16, 128)]
        vexts_s = [ar.bf(128, 33, 128), ar.bf(128, 33, 128)]
        for i in range(2):
            P.memset(qas[i][64:68, :], -1.0)
            P.memset(kas_p[i][64:68, :], 1.0)
            P.memset(kas_s[i][64:68, :], 1.0)
        for ve in (vexts_p, vexts_s):
            P.memset(ve[0][:, :, 64:128], 1.0)
            P.memset(ve[1][:, :, 0:64], 1.0)
        ckT_d = I["ckT"]

        def fox_attend(h, qa, qc0, qn, keys, ka, vext):
            ch, half = h // 2, h % 2
            pb = half * 64
            po = (1 - half) * 64
            acc = PS[:, 6 + (fctr[0] % 2), :]
            fctr[0] += 1
            full = [k for k in keys if k[3] is None]
            diag = [k for k in keys if k[3] is not None]
            per = 2 if qn > 64 else 8
            units = [full[i:i + per] for i in range(0, len(full), per)] + [[k] for k in diag]
            nk = len(keys)
            done = [0]

            def emit_front(unit):
                base = 2 * (fctr[1] % 3)
                fctr[1] += 1
                pt = pbufs[pctr[0] % len(pbufs)]
                pctr[0] += 1
                pvs = []
                if len(unit) == 1:
                    kc0, vti, L, doff = unit[0]
                    q_lo = 0 if doff is None else doff
                    sps = PS[:, base, :]
                    P.mm(sps[0:L, q_lo:qn], ka[0:68, kc0:kc0 + L], qa[0:68, qc0 + q_lo:qc0 + qn], start=True, stop=(doff is None))
                    if doff is not None:
                        dq = min(L, qn - q_lo)
                        P.mm(sps[0:L, q_lo:q_lo + dq], ident_b[0:L, 0:L], negmask[0:L, 0:dq], start=False, stop=True)
                    P.act(pt[0:L, 0, q_lo:qn], sps[0:L, q_lo:qn], AF.Exp)
                    pvs.append((acc[:, q_lo:qn], vext[0:L, vti, :], pt[0:L, 0, q_lo:qn]))
                elif qn > 64:
                    for j, (kc0, vti, L, doff) in enumerate(unit):
                        P.mm(PS[:, base + j, 0:qn], ka[0:68, kc0:kc0 + L], qa[0:68, qc0:qc0 + qn])
                        pvs.append((acc[:, 0:qn], vext[0:L, vti, :], pt[:, j, 0:qn]))
                    P.act(pt[:, 0:2, 0:qn], PS[:, base:base + 2, 0:qn], AF.Exp)
                else:
                    m = len(unit)
                    for j, (kc0, vti, L, doff) in enumerate(unit):
                        P.mm(PS[:, base, j * qn:(j + 1) * qn], ka[0:68, kc0:kc0 + L], qa[0:68, qc0:qc0 + qn])
                        pvs.append((acc[:, 0:qn], vext[0:L, vti, :], pt[:, 0, j * qn:(j + 1) * qn]))
                    P.act(pt[:, 0, 0:m * qn], PS[:, base, 0:m * qn], AF.Exp)
                return pvs

            def emit_pv(pvs):
                for (o_, l_, r_) in pvs:
                    P.mm(o_, l_, r_, start=(done[0] == 0), stop=(done[0] == nk - 1))
                    done[0] += 1

            LA = 2
            q_ = [emit_front(units[u]) for u in range(min(LA, len(units)))]
            for u in range(len(units)):
                if u + LA < len(units):
                    q_.append(emit_front(units[u + LA]))
                emit_pv(q_[u])
            rb_ = rbuf
            P.recip(rb_[pb:pb + 64, 0:qn], acc[po:po + 64, 0:qn])
            P.tt(aT[pb:pb + 64, ch, qc0:qc0 + qn], acc[pb:pb + 64, 0:qn], rb_[pb:pb + 64, 0:qn], ALU.mult)

        def sample_loads(h):
            ch, half = h // 2, h % 2
            pb = half * 64
            ka, vext = kas_s[h % 2], vexts_s[h % 2]
            vo = 0 if half == 0 else 64
            P.dma("pool", ka[0:64, 0:PAST], ckT_d[h * 64:(h + 1) * 64, :])
            P.dma("sp", ka[64:65, 0:PAST], cchi[h:h + 1, :])
            P.dma("sp", ka[65:66, 0:PAST], cclo[h:h + 1, :])
            P.dma("sp", ka[0:64, PAST:KMAX], kT[pb:pb + 64, ch, TP:NT])
            P.dma("sp", ka[64:65, PAST:KMAX], chi[h:h + 1, TP:NT])
            P.dma("sp", ka[65:66, PAST:KMAX], clo[h:h + 1, TP:NT])
            P.dma("pool", vext[:, 0:32, vo:vo + 64], I["cv"][h])
            P.copy(vext[0:64, 32, vo:vo + 64], V[0:64, 16, h * 64:h * 64 + 64], eng="pool")

        sample_loads(0)
        keys_s = [(ti * 128, ti, 128, None) for ti in range(32)] + [(PAST, 32, 64, 0)]
        for h in range(8):
            ch, half = h // 2, h % 2
            pb = half * 64
            qa, ka, vext = qas[h % 2], kas_p[h % 2], vexts_p[h % 2]
            vo = 0 if half == 0 else 64
            P.dma("sp", qa[0:64, :], qT[pb:pb + 64, ch, :])
            P.dma("sp", qa[66:67, :], chi[h:h + 1, :])
            P.dma("sp", qa[67:68, :], clo[h:h + 1, :])
            P.dma("sp", ka[0:64, 0:TP], kT[pb:pb + 64, ch, 0:TP])
            P.dma("sp", ka[64:65, 0:TP], chi[h:h + 1, 0:TP])
            P.dma("sp", ka[65:66, 0:TP], clo[h:h + 1, 0:TP])
            P.copy(vext[:, 0:16, vo:vo + 64], V[:, 0:16, h * 64:h * 64 + 64], eng="pool")
            if h + 1 < 8:
                sample_loads(h + 1)
            for gi in range(4):
                keys = []
                for ti in range(gi * 4 + 4):
                    doff = None if ti < gi * 4 else (ti - gi * 4) * 128
                    keys.append((ti * 128, ti, 128, doff))
                fox_attend(h, qa, gi * 512, 512, keys, ka, vext)
            fox_attend(h, qa, TP, TS, keys_s, kas_s[h % 2], vexts_s[h % 2])
            if h == 0:
                dbg("aT0", aT[0:64, 0, 0:TP])
        dbg("aT", aT[:, 0, :])
        P.mark("fox_prompt")
        dbg("aTs", aT[:, 0, TP:NT])
        ar.release(s1)
        bT_off = ar.mark()
        bT = ar.bf(128, 4, NT)
        s1 = ar.mark()

        P.mark("fox_sample")
        wm = ar.bf(128, 8, 2048)
        P.dma("pool", wm[:, :, 0:1536], wv("w_in")[:, :, 1544:3080])
        P.dma("pool", wm[:, :, 1536:2048], wv("w_in")[:, :, 3088:3600])
        bmv = ar.f32(128, 512)
        P.dma("sp", bmv, I["b_row"][:, 2568:3080].partition_broadcast(128))
        CT = ar.f32(128, 4, 129)
        pc = ar.f32(128, 8, 515)
        qks = [ar.bf(128, 8, 512), ar.bf(128, 8, 512)]
        qkc = [qks[0]]
        sigo = ar.f32(128, 4, 512)
        rdb = ar.f32(128, 4, 512)
        caccs = [ar.f32(128, 512), ar.f32(128, 512)]
        ones3 = ones_f.unsqueeze(1).broadcast_to([128, 4, 128])
        msets = []
        for _ in range(2):
            msets.append(dict(vfull=ar.f32(128, 512), vw=ar.bf(128, 4, 129), wgb=ar.bf(128, 4, 128),
                              ktok=ar.bf(128, 4, 128), ET=ar.bf(128, 4, 128), Cq=ar.bf(128, 4, 128),
                              nbm=ar.bf(128, 4, 128), tden=ar.f32(128, 4, 128), hs=ar.f32(128, 4, 128),
                              hsq=ar.bf(128, 4, 128)))
            msets[-1]["rr"] = msets[-1]["tden"]
        B_v = PS[:, 0, :]
        B_tr = PS[:, 1, :].bitcast(BF16)
        B_S = PS[:, 2, :]
        B_k = [PS[:, 3, :], PS[:, 4, :]]
        B_Y = PS[:, 5, :]
        B_D = PS[:, 6, :]
        B_q = PS[:, 7, :]
        P.memset(CT, 0.0)
        P.memset(pc[:, :, 0:3], 0.0)
        wb3 = ar.f32(128, 8)
        for j in range(8):
            P.tt(wb3[:, j:j + 1], mconv_w[:, j, 3:4], bcol(1544 + 128 * j), ALU.mult)

        def ml_step(prev, cur):
            if prev is not None:
                tiP, ttP, LP, loP, SP = prev
                Dv = r3(B_D[:, 0:4 * LP], LP)
                Yv = r3(B_Y[:, 0:4 * LP], LP)
                td = SP["tden"][:, :, 0:LP]
                hs_ = SP["hs"][:, :, 0:LP]
                hq_ = SP["hsq"][:, :, 0:LP]
                rr_ = SP["rr"][:, :, 0:LP]
            if cur is not None:
                ti, tt0, L, lo, S = cur
                wg3 = wg_tok[0:L, ti, :].unsqueeze(2)
            if prev is not None:
                for h in range(4):
                    q_h = qkc[0][:, h, loP:loP + LP]
                    P.mm(B_Y[:, h * LP:(h + 1) * LP], SP["Cq"][:, h, :], q_h, start=True, stop=False)
                    P.mm(B_Y[:, h * LP:(h + 1) * LP], SP["vw"][0:LP, h, 0:128], SP["ET"][0:LP, h, 0:LP], start=False, stop=True)
                for h in range(4):
                    q_h = qkc[0][:, h, loP:loP + LP]
                    P.mm(B_D[:, h * LP:(h + 1) * LP], SP["nbm"][:, h, :], q_h, start=True, stop=False)
                    P.mm(B_D[:, h * LP:(h + 1) * LP], SP["wgb"][0:LP, h, :], SP["ET"][0:LP, h, 0:LP], start=False, stop=True)
            if cur is not None:
                for c in range(8):
                    P.mm(B_v[0:L, :], xnT[:, c, tt0:tt0 + L], wm[:, c, 1024:1536], start=(c == 0), stop=(c == 7))
            if prev is not None:
                P.act(td, Dv, AF.Abs)
                P.tt(td, td, rdb[:, :, loP:loP + LP], ALU.max)
                P.act(td, td, AF.Ln)
                P.act(td, td, AF.Exp, scale=-1.0)
            if cur is not None:
                P.tt(S["vfull"][0:L], B_v[0:L, :], bmv[0:L], ALU.add)
                P.tt(S["vw"][0:L, :, 0:128], r3(S["vfull"][0:L], 128), wg3.broadcast_to([L, 4, 128]), ALU.mult)
                P.copy(S["vw"][0:L, :, 128:129], wg3)
                P.tt(S["wgb"][0:L], ones_f[0:L].unsqueeze(1).broadcast_to([L, 4, 128]), wg3.broadcast_to([L, 4, 128]), ALU.mult)
                for h in range(4):
                    P.tr(B_tr[0:L, h * 128:(h + 1) * 128], qkc[0][:, 4 + h, lo:lo + L], ident_b)
                P.act(S["ktok"][0:L], r3(B_tr[0:L, 0:512], 128), AF.Copy)
                for h in range(4):
                    P.mm(B_S[0:L, h * L:(h + 1) * L], qkc[0][:, 4 + h, lo:lo + L], qkc[0][:, h, lo:lo + L])
            if prev is not None:
                P.tt(hs_, Yv, td, ALU.mult)
                P.tt(hs_, hs_, sigo[:, :, loP:loP + LP], ALU.mult)
                P.act(hq_, hs_, AF.Square)
                for h in range(4):
                    P.mm(B_q[:, h * LP:(h + 1) * LP], ones_b, SP["hsq"][:, h, 0:LP])
                P.act(rr_, r3(B_q[:, 0:4 * LP], LP), AF.Ln, bias=eps_col, scale=1.0 / 128)
                P.act(rr_, rr_, AF.Exp, scale=-0.5)
            if cur is not None:
                P.stt(S["ET"][0:L, :, 0:L], r3(B_S[0:L, 0:4 * L], L), QSC,
                      trimask[0:L, 0:L].unsqueeze(1).broadcast_to([L, 4, L]), ALU.mult, ALU.mult)
                for h in range(4):
                    P.mm(B_k[h // 2][:, (h % 2) * 129:(h % 2) * 129 + 129], S["ktok"][0:L, h, :], S["vw"][0:L, h, :])
                P.tt(CT, CT, dec_b[:, :, ti:ti + 1].broadcast_to([128, 4, 129]), ALU.mult)
                P.act(S["Cq"], CT[:, :, 0:128], AF.Copy, scale=QSC)
                P.stt(S["nbm"], ones3, QSC, CT[:, :, 128:129].broadcast_to([128, 4, 128]), ALU.mult, ALU.mult)
            if prev is not None:
                P.tt(hs_, hs_, rr_, ALU.mult)
                P.tt(bT[:, :, ttP:ttP + LP], hs_, mhn.unsqueeze(2).broadcast_to([128, 4, LP]), ALU.mult)
            if cur is not None:
                P.tt(CT[:, 0:2, :], CT[:, 0:2, :], r3(B_k[0][:, 0:258], 129), ALU.add)
                P.tt(CT[:, 2:4, :], CT[:, 2:4, :], r3(B_k[1][:, 0:258], 129), ALU.add)

        def ml_proj_parts(gi):
            t0, n = GROUPS[gi]
            qk = qks[gi % 2]

            def part(pj):
                if pj == 0 and gi == 4:
                    P.dma("sp", pc[:, :, 0:3], I["sconv"])
                for j in (2 * pj, 2 * pj + 1):
                    ps = nb()
                    for c in range(8):
                        P.mm(ps[:, 0:n], wm[:, c, j * 128:(j + 1) * 128], xnT[:, c, t0:t0 + n], start=(c == 0), stop=(c == 7))
                    P.act(pc[:, j, 3:3 + n], ps[:, 0:n], AF.Identity, bias=bcol(1544 + 128 * j))
                    P.act(caccs[j % 2][:, 0:n], ps[:, 0:n], AF.Identity, scale=mconv_w[:, j, 3:4], bias=wb3[:, j:j + 1])
                for j in (2 * pj, 2 * pj + 1):
                    cacc = caccs[j % 2]
                    for tap in (0, 1, 2):
                        P.stt(cacc[:, 0:n], pc[:, j, tap:tap + n], mconv_w[:, j, tap:tap + 1], cacc[:, 0:n], ALU.mult, ALU.add)
                    P.act(qk[:, j, 0:n], cacc[:, 0:n], AF.Silu, bias=mconv_b[:, j:j + 1])
                if pj == 3:
                    if gi == 3:
                        P.dma("sp", O["ml_convT"][0], pc[:, :, n:n + 3])
                    if gi == 4:
                        P.dma("sp", O["ml_convT"][1], pc[:, :, n:n + 3])
                    if gi < 3:
                        P.copy(pc[:, :, 0:3], pc[:, :, n:n + 3])
            return [lambda pj=pj: part(pj) for pj in range(4)]

        def ml_gates(gi):
            t0, n = GROUPS[gi]
            for h in range(4):
                ps = nb()
                for c in range(8):
                    P.mm(ps[:, 0:n], wm[:, c, 1536 + h * 128:1536 + (h + 1) * 128], xnT[:, c, t0:t0 + n], start=(c == 0), stop=(c == 7))
                P.act(sigo[:, h, 0:n], ps[:, 0:n], AF.Sigmoid, bias=bcol(3088 + 128 * h))
            for h in range(4):
                ps = nb()
                P.mm(ps[:, 0:n], sel4[:, h, :], rdl[:, t0:t0 + n])
                P.act(rdb[:, h, 0:n], ps[:, 0:n], AF.Exp)

        for p_ in ml_proj_parts(0):
            p_()
        ti_global = 0
        for gi, (t0, n) in enumerate(GROUPS):
            if gi == 4:
                P.dma("sp", O["ml_cT"][0], CT[:, :, 0:128])
                P.dma("sp", O["ml_n"][0], CT[:, :, 128])
                P.dma("sp", CT[:, :, 0:128], I["sC"])
                P.dma("sp", CT[:, :, 128], I["sn"])
            ml_gates(gi)
            qkc[0] = qks[gi % 2]
            nparts = ml_proj_parts(gi + 1) if gi + 1 < len(GROUPS) else []
            tiles = [(tt0, L) for (tt0, L) in TILES if t0 <= tt0 < t0 + n]
            prev = None
            for k_, (tt0, L) in enumerate(tiles):
                ti = ti_global
                ti_global += 1
                cur = (ti, tt0, L, tt0 - t0, msets[ti % 2])
                ml_step(prev, cur)
                prev = cur
                if k_ < len(nparts):
                    nparts[k_]()
            ml_step(prev, None)
        P.dma("sp", O["ml_cT"][1], CT[:, :, 0:128])
        P.dma("sp", O["ml_n"][1], CT[:, :, 128])
        ar.release(s1)
        dbg("bT", bT[:, 0, :])
        mT_off = ar.mark()
        mT = ar.bf(128, 4, NT)
        s1 = ar.mark()

        P.mark("mlstm")
        wq = ar.bf(128, 8, 512)
        P.dma("pool", wq, wv("w_in")[:, :, 3600:4112])
        wkv = ar.bf(128, 8, 1024)
        P.dma("pool", wkv, wv("w_mem_kv"))
        memx = ar.f32(128, 8, 256)
        P.dma("sp", memx, I["memT"].rearrange("(c p) n -> p c n", p=128))
        rbm = ar.f32(128, 256)
        rms_bcast(memx, 8, 256, 1024.0, rbm)
        memn = ar.bf(128, 8, 256)
        for c in range(8):
            P.stt(memn[:, c, :], memx[:, c, :], g_mem[:, c:c + 1], rbm, ALU.mult, ALU.mult)
        mkT = ar.bf(128, 2, 4, 256)
        mv = ar.bf(128, 2, 2, 512)
        tmpf = ar.f32(128, 512)
        for chh in range(4):
            ps = nb()
            for c in range(8):
                P.mm(ps[:, 0:256], wkv[:, c, chh * 128:(chh + 1) * 128], memn[:, c, :], start=(c == 0), stop=(c == 7))
            P.copy(tmpf[:, 0:256], ps[:, 0:256])
            P.copy(mkT[:, 0, chh, :], tmpf[:, 0:256])
            P.dma("sp", O["mem_kT"][chh * 128:(chh + 1) * 128, :], tmpf[:, 0:256])
        for mt in range(2):
            ps = nb()
            for c in range(8):
                P.mm(ps[:, :], memn[:, c, mt * 128:(mt + 1) * 128], wkv[:, c, 512:1024], start=(c == 0), stop=(c == 7))
            P.copy(tmpf, ps)
            P.copy(mv[:, 0, mt, :], tmpf)
            P.dma("sp", O["mem_v"][mt * 128:(mt + 1) * 128, :], tmpf)
        P.dma("pool", mkT[:, 1], I["cmkT"].rearrange("(c p) n -> p c n", p=128))
        P.dma("pool", mv[:, 1], I["cmv"].rearrange("(t p) f -> p t f", p=128))
        qhs = [ar.bf(128, 512), ar.bf(128, 512)]
        ptms = [ar.bf(128, 2, 512), ar.bf(128, 2, 512)]
        rlms = [ar.f32(128, 512), ar.f32(128, 512)]
        mits = [(gi, h) for gi in range(len(GROUPS)) for h in range(4)]

        def mm_A(i):
            gi, h = mits[i]
            t0, n = GROUPS[gi]
            ps = nb()
            for c in range(8):
                P.mm(ps[:, 0:n], wq[:, c, h * 128:(h + 1) * 128], xnT[:, c, t0:t0 + n], start=(c == 0), stop=(c == 7))
            P.act(qhs[i % 2][:, 0:n], ps[:, 0:n], AF.Identity, bias=bcol(3600 + 128 * h))

        def mm_B(i):
            gi, h = mits[i]
            t0, n = GROUPS[gi]
            seq = 0 if gi < 4 else 1
            qh, ptm, rlm = qhs[i % 2], ptms[i % 2], rlms[i % 2]
            for mt in range(2):
                sps = nb()
                P.mm(sps[:, 0:n], mkT[:, seq, h, mt * 128:(mt + 1) * 128], qh[:, 0:n])
                P.act(ptm[:, mt, 0:n], sps[:, 0:n], AF.Exp, scale=QSC)
            ops_ = nb()
            lps = nb()
            for mt in range(2):
                P.mm(ops_[:, 0:n], mv[:, seq, mt, h * 128:(h + 1) * 128], ptm[:, mt, 0:n], start=(mt == 0), stop=(mt == 1))
            for mt in range(2):
                P.mm(lps[:, 0:n], ones_b, ptm[:, mt, 0:n], start=(mt == 0), stop=(mt == 1))
            P.act(rlm[:, 0:n], lps[:, 0:n], AF.Ln)
            P.act(rlm[:, 0:n], rlm[:, 0:n], AF.Exp, scale=-1.0)
            P.tt(mT[:, h, t0:t0 + n], ops_[:, 0:n], rlm[:, 0:n], ALU.mult)

        mm_A(0)
        for i in range(len(mits)):
            if i + 1 < len(mits):
                mm_A(i + 1)
            mm_B(i)
        dbg("mT", mT[:, 0, :])
        ar.release(s1)

        P.mark("mem")
        mergedT = ar.bf(128, 8, NT)
        wo = ar.bf(128, 8, 1024)
        s2 = ar.mark()
        wgs = [[ar.bf(128, 8, 128) for b in range(3)] for _ in range(2)]
        wbs = [[ar.bf(128, 4, 128) for b in range(3)] for _ in range(2)]
        sg = [ar.f32(128, 512) for b in range(3)]
        macc = ar.f32(128, 512)
        mtmp = ar.f32(128, 512)
        brs = ("w_br_a", "w_br_b", "w_br_m")
        srcs = (aT, bT, mT)
        for oc in range(8):
            wg_ = wgs[oc % 2]
            wb_ = wbs[oc % 2]
            for b in range(3):
                g0 = 4112 + b * 1024 + oc * 128
                P.dma("pool", wg_[b], wv("w_in")[:, :, g0:g0 + 128])
                P.dma("pool", wb_[b], wv(brs[b])[:, :, oc * 128:(oc + 1) * 128])
            if oc == 2:
                P.dma("pool", wo, wv("w_out"))
            for (t0, n) in GROUPS:
                pp = []
                for b in range(3):
                    g0 = 4112 + b * 1024 + oc * 128
                    ps = nb()
                    for c in range(8):
                        P.mm(ps[:, 0:n], wg_[b][:, c, :], xnT[:, c, t0:t0 + n], start=(c == 0), stop=(c == 7))
                    P.act(sg[b][:, 0:n], ps[:, 0:n], AF.Sigmoid, bias=bcol(g0))
                for b in range(3):
                    ps = nb()
                    for c in range(4):
                        P.mm(ps[:, 0:n], wb_[b][:, c, :], srcs[b][:, c, t0:t0 + n], start=(c == 0), stop=(c == 3))
                    pp.append(ps)
                P.tt(macc[:, 0:n], sg[0][:, 0:n], pp[0][:, 0:n], ALU.mult)
                P.tt(mtmp[:, 0:n], sg[1][:, 0:n], pp[1][:, 0:n], ALU.mult)
                P.tt(macc[:, 0:n], macc[:, 0:n], mtmp[:, 0:n], ALU.add)
                P.tt(mtmp[:, 0:n], sg[2][:, 0:n], pp[2][:, 0:n], ALU.mult)
                P.tt(mergedT[:, oc, t0:t0 + n], macc[:, 0:n], mtmp[:, 0:n], ALU.add)
        dbg("mergedT", mergedT[:, 0, :])
        ar.release(s2)

        P.mark("s2_merge")
        oTs = [ar.f32(128, 8, 512), ar.f32_at(aT_off, 128, 8, 512)]
        xss = [ar.f32(128, 8, 512), ar.f32_at(bT_off, 128, 8, 512)]
        rbs = [ar.f32(128, 512), ar.f32(128, 512)]
        sqs = [ar.bf(128, 8, 512), ar.bf_at(mT_off, 128, 8, 512)]
        x1s3w = x1scr.rearrange("(c p) n -> p c n", p=128)

        def ob_A(gi):
            t0, n = GROUPS[gi]
            oT, xs = oTs[gi % 2], xss[gi % 2]
            P.dma("sp", xs[:, :, 0:n], xT3[:, :, t0:t0 + n])
            for oc in range(8):
                ps = nb()
                for c in range(8):
                    P.mm(ps[:, 0:n], wo[:, c, oc * 128:(oc + 1) * 128], mergedT[:, c, t0:t0 + n], start=(c == 0), stop=(c == 7))
                P.act(oT[:, oc, 0:n], ps[:, 0:n], AF.Copy)
                P.act(sqs[gi % 2][:, oc, 0:n], ps[:, 0:n], AF.Square)

        def ob_B(gi):
            t0, n = GROUPS[gi]
            oT, xs, rb, sq = oTs[gi % 2], xss[gi % 2], rbs[gi % 2], sqs[gi % 2]
            ps = nb()
            for c in range(8):
                P.mm(ps[:, 0:n], ones_b, sq[:, c, 0:n], start=(c == 0), stop=(c == 7))
            P.act(rb[:, 0:n], ps[:, 0:n], AF.Ln, bias=eps_col, scale=1.0 / 1024.0)
            P.act(rb[:, 0:n], rb[:, 0:n], AF.Exp, scale=-0.5)
            ps2 = nb()
            for oc in range(8):
                P.stt(oT[:, oc, 0:n], oT[:, oc, 0:n], g_post[:, oc:oc + 1], rb[:, 0:n], ALU.mult, ALU.mult)
                P.tt(xs[:, oc, 0:n], xs[:, oc, 0:n], oT[:, oc, 0:n], ALU.add)
                P.act(sq[:, oc, 0:n], xs[:, oc, 0:n], AF.Square)
                P.mm(ps2[:, 0:n], ones_b, sq[:, oc, 0:n], start=(oc == 0), stop=(oc == 7))
            P.dma("sp", x1s3w[:, :, t0:t0 + n], xs[:, :, 0:n])
            P.act(rb[:, 0:n], ps2[:, 0:n], AF.Ln, bias=eps_col, scale=1.0 / 1024.0)
            P.act(rb[:, 0:n], rb[:, 0:n], AF.Exp, scale=-0.5)
            for c in range(8):
                P.stt(xnT[:, c, t0:t0 + n], xs[:, c, 0:n], g_fpre[:, c:c + 1], rb[:, 0:n], ALU.mult, ALU.mult)

        ob_A(0)
        for gi in range(len(GROUPS)):
            if gi + 1 < len(GROUPS):
                ob_A(gi + 1)
            ob_B(gi)
        dbg("x1nT", xnT[:, 0, :])
        ar.release(m_after_xn)

        P.mark("s2b_out")
        hidT = ar.bf(128, 22, NT)
        wd3 = wv("w_down")
        wd1 = ar.bf(128, 22, 512)
        m_h = ar.mark()
        wus = [ar.bf(128, 8, 256), ar.bf(128, 8, 256)]
        apre_p = ar.f32(128, 2 + TP)
        bpre_p = ar.f32(128, 2 + TP)
        apre_s = ar.f32(128, 2 + TS)
        bpre_s = ar.f32(128, 2 + TS)
        accas = [ar.f32(128, 512) for _ in range(3)]
        accbs = [ar.f32(128, 512) for _ in range(3)]
        gas = [ar.f32(128, 512) for _ in range(3)]
        fit = [0]
        ftail = [None]
        w_up3 = wv("w_up")
        for c in range(22):
            wu = wus[c % 2]
            P.dma("pool", wu[:, :, 0:128], w_up3[:, :, c * 128:(c + 1) * 128])
            P.dma("pool", wu[:, :, 128:256], w_up3[:, :, 2816 + c * 128:2816 + (c + 1) * 128])
            if c == 2:
                P.dma("pool", wd1, wd3[:, :, 512:1024])
            ja, jb = c, 22 + c
            for gi, (t0, n) in enumerate(GROUPS):
                if gi < 4:
                    apre, bpre = apre_p[:, t0:t0 + n + 2], bpre_p[:, t0:t0 + n + 2]
                else:
                    apre, bpre = apre_s, bpre_s
                if gi == 0:
                    P.memset(apre_p[:, 0:2], 0.0)
                    P.memset(bpre_p[:, 0:2], 0.0)
                if gi == 4:
                    P.dma("sp", apre[:, 0:2], I["sfconv"][:, ja, :])
                    P.dma("sp", bpre[:, 0:2], I["sfconv"][:, jb, :])
                acca, accb, ga = accas[fit[0] % 3], accbs[fit[0] % 3], gas[fit[0] % 3]
                fit[0] += 1
                for (pre, off) in ((apre, 0), (bpre, 128)):
                    ps = nb()
                    for k in range(8):
                        P.mm(ps[:, 0:n], wu[:, k, off:off + 128], xnT[:, k, t0:t0 + n], start=(k == 0), stop=(k == 7))
                    P.act(pre[:, 2:2 + n], ps[:, 0:n], AF.Copy)
                    if off == 128:
                        P.act(accb[:, 0:n], ps[:, 0:n], AF.Identity, scale=fconv_w[:, jb, 2:3])
                    else:
                        P.act(acca[:, 0:n], ps[:, 0:n], AF.Identity, scale=fconv_w[:, ja, 2:3])
                P.stt(acca[:, 0:n], apre[:, 0:n], fconv_w[:, ja, 0:1], acca[:, 0:n], ALU.mult, ALU.add)
                P.stt(acca[:, 0:n], apre[:, 1:1 + n], fconv_w[:, ja, 1:2], acca[:, 0:n], ALU.mult, ALU.add)
                P.stt(accb[:, 0:n], bpre[:, 0:n], fconv_w[:, jb, 0:1], accb[:, 0:n], ALU.mult, ALU.add)
                P.stt(accb[:, 0:n], bpre[:, 1:1 + n], fconv_w[:, jb, 1:2], accb[:, 0:n], ALU.mult, ALU.add)
                if ftail[0] is not None:
                    ftail[0]()

                def _tail(ga=ga, acca=acca, accb=accb, n=n, ja=ja, jb=jb, c=c, t0=t0):
                    P.act(ga[:, 0:n], acca[:, 0:n], AF.Gelu_apprx_tanh, bias=fconv_b[:, ja:ja + 1])
                    P.stt(hidT[:, c, t0:t0 + n], accb[:, 0:n], fconv_b[:, jb:jb + 1], ga[:, 0:n], ALU.add, ALU.mult)
                ftail[0] = _tail
                if gi in (3, 4):
                    so = 0 if gi == 3 else 1
                    P.dma("sp", O["ffn_convT"][so][:, ja, :], apre[:, n:n + 2])
                    P.dma("sp", O["ffn_convT"][so][:, jb, :], bpre[:, n:n + 2])
        ftail[0]()
        dbg("hidT", hidT[:, 0, :])
        ar.release(m_h)
        P.mark("ffn_up")
        wd0 = ar.bf_at(xn_off, 128, 22, 512)
        P.dma("pool", wd0, wd3[:, :, 0:512])
        oT = ar.f32(128, 8, 512)
        xs = ar.f32(128, 8, 512)
        rb = ar.f32(128, 512)
        sqd = ar.bf(128, 8, 512)
        x1s3 = x1scr.rearrange("(c p) n -> p c n", p=128)
        yT3 = O["yT"].rearrange("(c p) n -> p c n", p=128)
        for gi, (t0, n) in enumerate(GROUPS):
            P.dma("sp", xs[:, :, 0:n], x1s3[:, :, t0:t0 + n])
            for oc in range(8):
                wd = wd0 if oc < 4 else wd1
                o4 = oc % 4
                ps = nb()
                for c in range(22):
                    P.mm(ps[:, 0:n], wd[:, c, o4 * 128:(o4 + 1) * 128], hidT[:, c, t0:t0 + n], start=(c == 0), stop=(c == 21))
                P.act(oT[:, oc, 0:n], ps[:, 0:n], AF.Copy)
                P.act(sqd[:, oc, 0:n], ps[:, 0:n], AF.Square)
            psr = nb()
            for oc in range(8):
                P.mm(psr[:, 0:n], ones_b, sqd[:, oc, 0:n], start=(oc == 0), stop=(oc == 7))
            P.act(rb[:, 0:n], psr[:, 0:n], AF.Ln, bias=eps_col, scale=1.0 / 1024.0)
            P.act(rb[:, 0:n], rb[:, 0:n], AF.Exp, scale=-0.5)
            for oc in range(8):
                P.stt(oT[:, oc, 0:n], oT[:, oc, 0:n], g_fpost[:, oc:oc + 1], rb[:, 0:n], ALU.mult, ALU.mult)
                P.tt(xs[:, oc, 0:n], xs[:, oc, 0:n], oT[:, oc, 0:n], ALU.add)
            P.dma("sp", yT3[:, :, t0:t0 + n], xs[:, :, 0:n])
        P.mark("ffn_down")
        P.emit(Kq={'pool': 3})
    except _Stop:
        pass
    return nc, DBG, P, 0


_CACHE = {}


def _get_nc(debug=()):
    key = tuple(debug)
    if key not in _CACHE:
        _CACHE[key] = build(debug)
    return _CACHE[key]


def _consts():
    ident = np.eye(128, dtype=np.float32)
    s = np.arange(128)
    trimask = (s[None, :] >= s[:, None]).astype(np.float32)
    negmask = np.where(s[:, None] <= s[None, :], 0.0, -30000.0).astype(np.float32)
    bd = np.zeros((128, 128), np.float32)
    bd[:64, :64] = 1.0
    bd[64:, 64:] = 1.0
    sel4 = np.zeros((4, 4, 128), np.float32)
    for h in range(4):
        sel4[h, h, :] = 1.0
    return dict(ident=ident, trimask=trimask, negmask=negmask, bdones=bd, sel4=sel4)


def _fm(v):
    return np.ascontiguousarray(v.reshape(-1, 128).T)


def _prep_shared(inp):
    f = lambda a: np.ascontiguousarray(np.asarray(a, dtype=np.float32))
    b_in = f(inp["b_in"])[0]
    d = dict(
        w_in=f(inp["w_in"])[0],
        b_fm=np.ascontiguousarray(np.stack([b_in[s:s + 128] for s in FM_STARTS], axis=1)),
        bg_fox=np.ascontiguousarray(b_in[1536:1544][:, None]),
        bg_i=np.ascontiguousarray(b_in[3080:3084][:, None]),
        bg_f=np.ascontiguousarray(b_in[3084:3088][:, None]),
        b_row=np.ascontiguousarray(b_in[None, :]),
        g_pre=_fm(f(inp["norm_mix_pre"])[0]),
        gq2=np.ascontiguousarray(np.concatenate([f(inp["fox_q_norm"])[0]] * 2)[:, None]),
        gk2=np.ascontiguousarray(np.concatenate([f(inp["fox_k_norm"])[0]] * 2)[:, None]),
        mconv_w=np.ascontiguousarray(f(inp["mlstm_conv_w"])[0].reshape(4, 8, 128).transpose(2, 1, 0)),
        mconv_b=_fm(f(inp["mlstm_conv_b"])[0]),
        mhn=np.ascontiguousarray(f(inp["mlstm_head_norm"])[0].T),
        g_mem=_fm(f(inp["norm_mem"])[0]),
        w_mem_kv=f(inp["w_mem_kv"])[0],
        w_br_a=f(inp["w_br_a"])[0], w_br_b=f(inp["w_br_b"])[0], w_br_m=f(inp["w_br_m"])[0],
        w_out=f(inp["w_out"])[0],
        g_post=_fm(f(inp["norm_mix_post"])[0]),
        g_fpre=_fm(f(inp["norm_ffn_pre"])[0]),
        w_up=f(inp["w_up"])[0],
        fconv_w=np.ascontiguousarray(f(inp["ffn_conv_w"])[0].reshape(3, 44, 128).transpose(2, 1, 0)),
        fconv_b=_fm(f(inp["ffn_conv_b"])[0]),
        w_down=f(inp["w_down"])[0],
        g_fpost=_fm(f(inp["norm_ffn_post"])[0]),
    )
    d.update(_consts())
    return d


def _prep_core(inp, b):
    f = lambda a: np.asarray(a, dtype=np.float32)
    c = np.ascontiguousarray
    return dict(
        xT=c(np.concatenate([f(inp["x_prompt"])[b].T, f(inp["x_sample"])[b].T], axis=1)),
        ckT=c(f(inp["cache_fox_k"])[0, b].reshape(PAST, 512).T),
        cv=c(f(inp["cache_fox_v"])[0, b].reshape(32, 128, 8, 64).transpose(2, 1, 0, 3)),
        clogfT=c(f(inp["cache_fox_logf"])[0, b].T),
        sC=c(f(inp["state_mlstm_c"])[0, b].transpose(2, 0, 1)),
        sn=c(f(inp["state_mlstm_n"])[0, b].T),
        sm=c(f(inp["state_mlstm_m"])[0, b][:, None]),
        sconv=c(f(inp["state_mlstm_conv"])[0, b].reshape(3, 8, 128).transpose(2, 1, 0)),
        cmkT=c(f(inp["cache_mem_k"])[0, b].reshape(256, 512).T),
        cmv=c(f(inp["cache_mem_v"])[0, b].reshape(256, 512)),
        sfconv=c(f(inp["state_ffn_conv"])[0, b].reshape(2, 44, 128).transpose(2, 1, 0)),
        memT=c(f(inp["mem_prompt"])[b].T),
    )


def _assemble(results):
    B = len(results)
    z = lambda *s: np.zeros(s, np.float32)
    y_p, y_s = z(B, TP, 1024), z(B, TS, 1024)
    fk_p, fv_p, fl_p = z(1, B, TP, 8, 64), z(1, B, TP, 8, 64), z(1, B, TP, 8)
    fk_s, fv_s, fl_s = z(1, B, TS, 8, 64), z(1, B, TS, 8, 64), z(1, B, TS, 8)
    c_p, n_p, m_p = z(1, B, 4, 128, 128), z(1, B, 4, 128), z(1, B, 4)
    c_s, n_s, m_s = z(1, B, 4, 128, 128), z(1, B, 4, 128), z(1, B, 4)
    cv_p, cv_s = z(1, B, 3, 1024), z(1, B, 3, 1024)
    fc_p, fc_s = z(1, B, 2, 5632), z(1, B, 2, 5632)
    mk_p, mv_p = z(1, B, 256, 4, 128), z(1, B, 256, 4, 128)
    for b, r in enumerate(results):
        yT = r["yT"]
        y_p[b] = yT[:, :TP].T
        y_s[b] = yT[:, TP:].T
        kT = r["fox_kT"]
        fk_p[0, b] = kT[:, :TP].T.reshape(TP, 8, 64)
        fk_s[0, b] = kT[:, TP:].T.reshape(TS, 8, 64)
        fv = r["fox_v"]
        fv_p[0, b] = fv[:TP].reshape(TP, 8, 64)
        fv_s[0, b] = fv[TP:].reshape(TS, 8, 64)
        lf = r["fox_logfT"]
        fl_p[0, b] = lf[:, :TP].T
        fl_s[0, b] = lf[:, TP:].T
        for (si, cc, nn, mm, cvv, fcc) in ((0, c_p, n_p, m_p, cv_p, fc_p), (1, c_s, n_s, m_s, cv_s, fc_s)):
            cc[0, b] = r["ml_cT"][si].transpose(1, 2, 0)
            nn[0, b] = r["ml_n"][si].T
            mm[0, b] = r["ml_m"][si][:, 0]
            cvv[0, b] = r["ml_convT"][si].transpose(2, 1, 0).reshape(3, 1024)
            fcc[0, b] = r["ffn_convT"][si].transpose(2, 1, 0).reshape(2, 5632)
        mk_p[0, b] = r["mem_kT"].T.reshape(256, 4, 128)
        mv_p[0, b] = r["mem_v"].reshape(256, 4, 128)
    return (y_p, y_s, fk_p, fv_p, fl_p, c_p, n_p, m_p, cv_p, fc_p, mk_p, mv_p,
            fk_s, fv_s, fl_s, c_s, n_s, m_s, cv_s, fc_s)


def kernel(**inputs):
    nc = _get_nc()[0]
    shared = _prep_shared(inputs)
    in_maps = []
    for b in range(8):
        d = dict(shared)
        d.update(_prep_core(inputs, b))
        in_maps.append(d)
    res = run_bass_kernel_spmd(nc, in_maps, core_ids=list(range(8)))
    return _assemble(res.results)
```

```python
import numpy as np
from contextlib import ExitStack
import concourse.bass as bass
import concourse.mybir as mybir

F32 = mybir.dt.float32
BF16 = mybir.dt.bfloat16
AF = mybir.ActivationFunctionType
ALU = mybir.AluOpType
AX = mybir.AxisListType


def _esize(dt):
    n = str(dt)
    if "float32" in n or "int32" in n:
        return 4
    if "bfloat16" in n or "float16" in n or "int16" in n:
        return 2
    if "int8" in n or "float8" in n:
        return 1
    if "64" in n:
        return 8
    raise ValueError(n)


def _boxes(ap):
    t = ap.tensor
    name = t.name
    es = _esize(ap.dtype)
    dims = [(int(s), int(n)) for s, n in ap.ap]
    off = int(ap.offset)
    if "DRAM" in str(ap.space).upper() or "HBM" in str(ap.space).upper():
        ext = sum((n - 1) * abs(s) for s, n in dims)
        return name, [(0, 1, off * es, (off + ext + 1) * es)]
    tes = _esize(t.dtype)
    rowbytes = int(np.prod([int(x) for x in t.shape[1:]])) * tes
    row = rowbytes // es
    p0 = off // row
    f0 = off % row
    pext = 0
    fd = []
    for s, n in dims:
        if n == 1:
            continue
        if s != 0 and s % row == 0:
            pext += (n - 1) * (s // row)
        elif s != 0:
            fd.append((abs(s), n))
    fd.sort(reverse=True)
    p1 = p0 + pext + 1
    if len(fd) >= 2:
        inner = sum((n - 1) * s for s, n in fd[1:]) + 1
        s0, n0 = fd[0]
        if s0 >= inner and n0 <= 64:
            return name, [(p0, p1, (f0 + i * s0) * es, (f0 + i * s0 + inner) * es) for i in range(n0)]
    ext = sum((n - 1) * s for s, n in fd) + 1
    return name, [(p0, p1, f0 * es, (f0 + ext) * es)]


def _ov(a, b):
    for x in a:
        for y in b:
            if x[0] < y[1] and y[0] < x[1] and x[2] < y[3] and y[2] < x[3]:
                return True
    return False


def _contained(a, b):
    for x in a:
        ok = False
        for y in b:
            if y[0] <= x[0] and x[1] <= y[1] and y[2] <= x[2] and x[3] <= y[3]:
                ok = True
                break
        if not ok:
            return False
    return True


class _Stop(Exception):
    pass


class Prog:
    ENGS = ("pe", "act", "dve", "pool", "sp")

    def __init__(self, nc):
        self.nc = nc
        self.ops = []
        self.hist = {}

    def add(self, eng, fn, reads=(), writes=(), dma=False):
        idx = len(self.ops)
        deps = set()
        rb = [_boxes(a) for a in reads]
        wb = [_boxes(a) for a in writes]
        for name, bx in rb:
            for e in self.hist.get(name, ()):
                if e[2] and _ov(e[0], bx):
                    deps.add(e[1])
        for name, bx in wb:
            for e in self.hist.get(name, ()):
                if _ov(e[0], bx):
                    deps.add(e[1])
        for name, bx in wb:
            lst = [e for e in self.hist.get(name, ()) if not _contained(e[0], bx)]
            lst.append((bx, idx, True, eng, dma))
            self.hist[name] = lst
        for name, bx in rb:
            lst = self.hist.setdefault(name, [])
            rep = False
            if not dma:
                for i, e in enumerate(lst):
                    if (not e[2]) and e[3] == eng and (not e[4]) and e[0] == bx:
                        lst[i] = (bx, idx, False, eng, dma)
                        rep = True
                        break
            if not rep:
                lst.append((bx, idx, False, eng, dma))
        deps.discard(idx)
        self.ops.append((eng, fn, deps, dma))
        return idx

    def mark(self, name):
        if not hasattr(self, 'marks'):
            self.marks = []
        self.marks.append((name, sum(1 for o in self.ops if o[0] == 'pe')))
        if getattr(self, 'stop_at', None) == name:
            self.emit()
            raise _Stop()

    def dma(self, q, out, in_, **kw):
        kw.setdefault('allow_slow_non_contiguous', True)
        return self.add(q, lambda e: e.dma_start(out=out, in_=in_, **kw), [in_], [out], dma=True)

    def mm(self, out, lhsT, rhs, start=True, stop=True, acc=False):
        rd = [lhsT, rhs] + ([out] if not start else [])
        return self.add("pe", lambda e: e.matmul(out, lhsT, rhs, start=start, stop=stop), rd, [out])

    def tr(self, out, in_, ident):
        return self.add("pe", lambda e: e.transpose(out, in_, ident), [in_, ident], [out])

    def act(self, out, in_, func, bias=None, scale=None, accum_out=None, eng="act"):
        rd = [in_]
        kw = {}
        if bias is not None:
            kw["bias"] = bias
            if not isinstance(bias, (int, float)):
                rd.append(bias)
        if scale is not None:
            kw["scale"] = scale
            if not isinstance(scale, (int, float)):
                rd.append(scale)
        wr = [out]
        if accum_out is not None:
            kw["accum_out"] = accum_out
            wr.append(accum_out)
        return self.add("act", lambda e: e.activation(out, in_, func, **kw), rd, wr)

    def tt(self, out, in0, in1, op, eng="dve"):
        return self.add(eng, lambda e: e.tensor_tensor(out, in0, in1, op), [in0, in1], [out])

    def ts(self, out, in0, s1, op0, s2=None, op1=None, eng="dve"):
        rd = [in0] + [s for s in (s1, s2) if s is not None and not isinstance(s, (int, float))]
        if op1 is None:
            return self.add(eng, lambda e: e.tensor_scalar(out, in0, s1, None, op0), rd, [out])
        return self.add(eng, lambda e: e.tensor_scalar(out, in0, s1, s2, op0, op1), rd, [out])

    def stt(self, out, in0, scalar, in1, op0, op1, eng="dve"):
        rd = [in0, in1] + ([] if isinstance(scalar, (int, float)) else [scalar])
        return self.add(eng, lambda e: e.scalar_tensor_tensor(out, in0, scalar, in1, op0, op1), rd, [out])

    def copy(self, out, in_, eng="dve"):
        return self.add(eng, lambda e: e.tensor_copy(out, in_), [in_], [out])

    def memset(self, out, val, eng="dve"):
        return self.add(eng, lambda e: e.memset(out, val), [], [out])

    def recip(self, out, in_):
        return self.add("dve", lambda e: e.reciprocal(out, in_), [in_], [out])

    def scan(self, out, d0, d1, init, op0, op1):
        rd = [d0, d1] + ([] if isinstance(init, (int, float)) else [init])
        return self.add("dve", lambda e: e.tensor_tensor_scan(out, d0, d1, init, op0, op1), rd, [out])

    def emit(self, R=20000, K=8, Kq=None):
        Kq = dict(Kq or {})
        KK = {e: Kq.get(e, K) for e in self.ENGS}
        nc = self.nc
        ops = self.ops
        needed = set()
        for eng, fn, deps, dma in ops:
            for d in deps:
                de, _, _, ddma = ops[d]
                if ddma:
                    continue
                if de == "pe" and eng == "pe" and not dma:
                    continue
                needed.add(d)
        sigidx = {}
        cnt = {e: 0 for e in self.ENGS}
        for i, (eng, fn, deps, dma) in enumerate(ops):
            if (not dma) and i in needed:
                sigidx[i] = cnt[eng]
                cnt[eng] += 1
        dmaidx = {}
        dcnt = {e: 0 for e in self.ENGS}
        for i, (eng, fn, deps, dma) in enumerate(ops):
            if dma:
                dmaidx[i] = dcnt[eng]
                dcnt[eng] += 1
        with ExitStack() as st:
            csem = {e: [st.enter_context(nc.semaphore(f"c_{e}_{j}")) for j in range(max(1, (cnt[e] + R - 1) // R))]
                    for e in self.ENGS}
            dsem = {e: [st.enter_context(nc.semaphore(f"d_{e}_{j}")) for j in range(min(KK[e], dcnt[e]))]
                    for e in self.ENGS}
            block = st.enter_context(nc.Block())

            def run(me, e):
                waited_c = {x: -1 for x in self.ENGS}
                waited_d = {}

                def wait_dma(d):
                    q = ops[d][0]
                    n = dmaidx[d]
                    K = KK[q]
                    sem = dsem[q][n % K]
                    val = 16 * (n // K + 1)
                    key = (q, n % K)
                    if waited_d.get(key, 0) >= val:
                        return
                    e.wait_ge(sem, val)
                    waited_d[key] = val

                for i, (eng, fn, deps, dma) in enumerate(ops):
                    if eng != me:
                        continue
                    for d in sorted(deps):
                        de, _, _, ddma = ops[d]
                        if ddma:
                            wait_dma(d)
                        else:
                            if de == "pe" and me == "pe" and not dma:
                                continue
                            g = sigidx[d]
                            if waited_c[de] >= g:
                                continue
                            e.wait_ge(csem[de][g // R], g % R + 1)
                            waited_c[de] = g
                    if dma:
                        n = dmaidx[i]
                        K = KK[me]
                        sem = dsem[me][n % K]
                        if n >= K:
                            key = (me, n % K)
                            val = 16 * (n // K)
                            if waited_d.get(key, 0) < val:
                                e.wait_ge(sem, val)
                                waited_d[key] = val
                        ins = fn(e)
                        ins.then_inc(sem, 16)
                    else:
                        ins = fn(e)
                        if i in sigidx:
                            g = sigidx[i]
                            ins.then_inc(csem[me][g // R], 1)
                K = KK[me]
                for j in range(min(K, dcnt[me])):
                    uses = (dcnt[me] - 1 - j) // K + 1
                    val = 16 * uses
                    if waited_d.get((me, j), 0) < val:
                        e.wait_ge(dsem[me][j], val)

            @block.sync
            def _(e):
                run("sp", e)

            @block.scalar
            def _(e):
                run("act", e)

            @block.vector
            def _(e):
                run("dve", e)

            @block.gpsimd
            def _(e):
                run("pool", e)

            @block.tensor
            def _(e):
                run("pe", e)

from concourse.bass_utils import run_bass_kernel_spmd

NT = 2112
TP = 2048
TS = 64
PAST = 4096
EPS = 1e-6
GROUPS = [(0, 512), (512, 512), (1024, 512), (1536, 512), (2048, 64)]
TILES = [(i * 128, 128) for i in range(16)] + [(2048, 64)]
QSC = 128.0 ** -0.5

FM_STARTS = ([0, 128, 256, 384] + [512, 640, 768, 896] + [1544 + 128 * i for i in range(8)]
             + [3088 + 128 * i for i in range(4)]
             + [3600 + 128 * i for i in range(4)] + [4112 + 128 * i for i in range(24)])
FM_COL = {s: i for i, s in enumerate(FM_STARTS)}

IN_SHAPES = dict(
    xT=[1024, NT], w_in=[1024, 7184], b_fm=[128, 48], bg_fox=[8, 1], bg_i=[4, 1], bg_f=[4, 1],
    b_row=[1, 7184], g_pre=[128, 8], gq2=[128, 1], gk2=[128, 1], mconv_w=[128, 8, 4], mconv_b=[128, 8],
    mhn=[128, 4], g_mem=[128, 8], w_mem_kv=[1024, 1024], w_br_a=[512, 1024], w_br_b=[512, 1024],
    w_br_m=[512, 1024], w_out=[1024, 1024], g_post=[128, 8], g_fpre=[128, 8], w_up=[1024, 5632],
    fconv_w=[128, 44, 3], fconv_b=[128, 44], w_down=[2816, 1024], g_fpost=[128, 8],
    ckT=[512, PAST], cv=[8, 128, 32, 64], clogfT=[8, PAST], sC=[128, 4, 128], sn=[128, 4], sm=[4, 1],
    sconv=[128, 8, 3], cmkT=[512, 256], cmv=[256, 512], sfconv=[128, 44, 2], memT=[1024, 256],
    ident=[128, 128], trimask=[128, 128], negmask=[128, 128], bdones=[128, 128], sel4=[4, 4, 128],
)
OUT_SHAPES = dict(
    yT=[1024, NT], fox_kT=[512, NT], fox_v=[NT, 512], fox_logfT=[8, NT],
    ml_cT=[2, 128, 4, 128], ml_n=[2, 128, 4], ml_m=[2, 4, 1], ml_convT=[2, 128, 8, 3],
    ffn_convT=[2, 128, 44, 2], mem_kT=[512, 256], mem_v=[256, 512],
)


class Arena:
    def __init__(self, A, cap):
        self.A = A
        self.cap = cap
        self.top = 0
        self.hw = 0

    def _alloc(self, words):
        off = self.top
        self.top += words
        assert self.top <= self.cap, f"arena overflow {self.top} > {self.cap}"
        self.hw = max(self.hw, self.top)
        return off

    def f32(self, *shape):
        n = int(np.prod(shape[1:]))
        off = self._alloc(n)
        return self._view(self.A[:, off:off + n], shape)

    def bf(self, *shape):
        n = int(np.prod(shape[1:]))
        words = (n + 1) // 2
        off = self._alloc(words)
        return self._view(self.A[:, off:off + words].bitcast(BF16)[:, 0:n], shape)

    @staticmethod
    def _view(v, shape):
        p = shape[0]
        if len(shape) == 3:
            v = v.rearrange("p (a b) -> p a b", b=shape[2])
        elif len(shape) == 4:
            v = v.rearrange("p (a b c) -> p a b c", b=shape[2], c=shape[3])
        if p < 128:
            v = v[0:p]
        return v

    def f32_at(self, off, *shape):
        n = int(np.prod(shape[1:]))
        return self._view(self.A[:, off:off + n], shape)

    def bf_at(self, off, *shape):
        n = int(np.prod(shape[1:]))
        words = (n + 1) // 2
        return self._view(self.A[:, off:off + words].bitcast(BF16)[:, 0:n], shape)

    def mark(self):
        return self.top

    def release(self, m):
        self.top = m


def build(debug=(), stop_at=None, salt=None):
    nc = bass.Bass("TRN2", target_bir_lowering=False)
    I = {k: nc.dram_tensor(k, list(v), F32, kind="ExternalInput").ap() for k, v in IN_SHAPES.items()}
    O = {k: nc.dram_tensor(k, list(v), F32, kind="ExternalOutput").ap() for k, v in OUT_SHAPES.items()}
    x1scr = nc.dram_tensor("x1scr", [1024, NT], F32, kind="Internal").ap()
    DBG = {}
    P = Prog(nc)
    P.stop_at = stop_at
    CAP = 53000
    try:
      with nc.sbuf_tensor("A", [128, CAP], F32) as A_, nc.psum_tensor("PS", [128, 8, 512], F32) as PS:
        ar = Arena(A_, CAP)
        hw_box = [ar]
        bank_ctr = [0]

        def nb():
            b = bank_ctr[0] % 8
            bank_ctr[0] += 1
            return PS[:, b, :]

        def dbg(name, ap):
            if name in debug:
                shp = [int(s) for s in ap.shape]
                d = nc.dram_tensor("dbg_" + name, shp, F32, kind="ExternalOutput").ap()
                DBG[name] = shp
                if ap.dtype == F32 and "PSUM" not in str(ap.space).upper():
                    P.dma("sp", d, ap)
                else:
                    m = ar.mark()
                    t = ar.f32(*([128] + shp[1:]))[0:shp[0]]
                    P.copy(t, ap)
                    P.dma("sp", d, t)
                    ar.release(m)

        wv = lambda name: I[name].rearrange("(c p) n -> p c n", p=128)
        r3 = lambda ap, b: ap.rearrange("p (a b) -> p a b", b=b)

        ident_f = ar.f32(128, 128)
        ident_b = ar.bf(128, 128)
        trimask = ar.bf(128, 128)
        negmask = ar.bf(128, 128)
        bdones = ar.bf(128, 128)
        ones_b = ar.bf(128, 128)
        ones_f = ar.f32(128, 128)
        sel4 = ar.f32(4, 4, 128)
        b_fm = ar.f32(128, 48)
        g_pre = ar.f32(128, 8)
        g_post = ar.f32(128, 8)
        g_fpre = ar.f32(128, 8)
        g_fpost = ar.f32(128, 8)
        g_mem = ar.f32(128, 8)
        gq2 = ar.f32(128, 1)
        gk2 = ar.f32(128, 1)
        mhn = ar.f32(128, 4)
        mconv_w = ar.f32(128, 8, 4)
        mconv_b = ar.f32(128, 8)
        fconv_w = ar.f32(128, 44, 3)
        fconv_b = ar.f32(128, 44)
        def load_consts():
            P.dma("sp", ident_f, I["ident"])
            P.dma("pool", ident_b, I["ident"])
            P.dma("pool", trimask, I["trimask"])
            P.dma("pool", negmask, I["negmask"])
            P.dma("pool", bdones, I["bdones"])
            P.dma("sp", sel4, I["sel4"])
            for t, n in ((b_fm, "b_fm"), (g_pre, "g_pre"), (g_post, "g_post"), (g_fpre, "g_fpre"), (g_fpost, "g_fpost"),
                         (g_mem, "g_mem"), (gq2, "gq2"), (gk2, "gk2"), (mhn, "mhn"), (mconv_w, "mconv_w"),
                         (mconv_b, "mconv_b"), (fconv_w, "fconv_w"), (fconv_b, "fconv_b")):
                P.dma("sp", t, I[n])
            P.ts(gq8, gq2, 0.125, ALU.mult)
        P.memset(ones_b, 1.0)
        P.memset(ones_f, 1.0)
        eps_col = ar.f32(128, 1)
        P.memset(eps_col, EPS)
        one_col = ar.f32(128, 1)
        P.memset(one_col, 1.0)
        gq8 = ar.f32(128, 1)
        bcol = lambda start: b_fm[:, FM_COL[start]:FM_COL[start] + 1]

        def rms_bcast(src, C, n, D, rb, sq=None):
            m = ar.mark()
            if sq is None:
                sq = ar.bf(128, C, n)
            ps = nb()
            for c in range(C):
                P.act(sq[:, c, :], src[:, c, :], AF.Square)
                P.mm(ps[:, 0:n], ones_b, sq[:, c, :], start=(c == 0), stop=(c == C - 1))
            P.act(rb, ps[:, 0:n], AF.Ln, bias=eps_col, scale=1.0 / D)
            P.act(rb, rb, AF.Exp, scale=-0.5)
            ar.release(m)

        xn_off = ar.mark()
        xnT = ar.bf(128, 8, NT)
        m_after_xn = ar.mark()
        aT_off = ar.mark()
        aT = ar.bf(128, 4, NT)
        chi = ar.bf(8, NT)
        clo = ar.bf(8, NT)
        rdl = ar.f32(4, NT)
        wg_tok = ar.f32(128, 17, 4)
        dec_b = ar.f32(128, 4, 17)

        xT3 = I["xT"].rearrange("(c p) n -> p c n", p=128)
        m = ar.mark()
        xs2 = [ar.f32(128, 8, 512), ar.f32(128, 8, 512)]
        rb2 = [ar.f32(128, 512), ar.f32(128, 512)]
        sq2 = [ar.bf(128, 8, 512), ar.bf(128, 8, 512)]
        for gi, (t0, n) in enumerate(GROUPS):
            xs = xs2[gi % 2][:, :, 0:n]
            P.dma("sp", xs, xT3[:, :, t0:t0 + n])
            if gi == 0:
                load_consts()
            rb = rb2[gi % 2][:, 0:n]
            rms_bcast(xs, 8, n, 1024.0, rb, sq=sq2[gi % 2][:, :, 0:n])
            for c in range(8):
                P.stt(xnT[:, c, t0:t0 + n], xs[:, c, :], g_pre[:, c:c + 1], rb, ALU.mult, ALU.mult)
        ar.release(m)
        dbg("xnT", xnT[:, 0, :])

        P.mark("s0_norm")
        m1a = ar.mark()
        wgf = ar.bf(128, 8, 8)
        wgi = ar.bf(128, 8, 4)
        wgm = ar.bf(128, 8, 4)
        P.dma("pool", wgf, wv("w_in")[:, :, 1536:1544])
        P.dma("pool", wgi, wv("w_in")[:, :, 3080:3084])
        P.dma("pool", wgm, wv("w_in")[:, :, 3084:3088])
        bgf = ar.f32(8, 1)
        bgi = ar.f32(4, 1)
        bgm = ar.f32(4, 1)
        P.dma("sp", bgf, I["bg_fox"])
        P.dma("sp", bgi, I["bg_i"])
        P.dma("sp", bgm, I["bg_f"])
        zrow = ar.f32(8, NT)
        P.memset(zrow, 0.0)
        flog = ar.f32(8, NT)
        cfox = ar.f32(8, NT)
        gi_r = ar.f32(4, NT)
        mlf = ar.f32(4, NT)
        A_r = ar.f32(4, NT)
        G_r = ar.f32(4, NT)
        wgT = ar.f32(4, NT)
        for (wt, bc, dst, rows, ls) in ((wgf, bgf, flog, 8, True), (wgi, bgi, gi_r, 4, False), (wgm, bgm, mlf, 4, True)):
            nbias = ar.f32(rows, 1)
            P.ts(nbias, bc, -1.0, ALU.mult)
            for (t0, n) in GROUPS:
                ps = nb()
                for c in range(8):
                    P.mm(ps[0:rows, 0:n], wt[:, c, :], xnT[:, c, t0:t0 + n], start=(c == 0), stop=(c == 7))
                if ls:
                    m = ar.mark()
                    e = ar.f32(rows, n)
                    P.act(e, ps[0:rows, 0:n], AF.Exp, bias=nbias, scale=-1.0)
                    P.act(e, e, AF.Ln, bias=one_col[0:rows], scale=1.0)
                    P.ts(dst[:, t0:t0 + n], e, -1.0, ALU.mult)
                    ar.release(m)
                else:
                    P.act(dst[:, t0:t0 + n], ps[0:rows, 0:n], AF.Identity, bias=bc)
        P.dma("sp", O["fox_logfT"], flog)
        P.scan(cfox[:, 0:TP], flog[:, 0:TP], zrow[:, 0:TP], 0.0, ALU.add, ALU.add)
        P.scan(cfox[:, TP:NT], flog[:, TP:NT], zrow[:, 0:TS], 0.0, ALU.add, ALU.add)
        P.copy(chi, cfox)
        P.tt(clo, cfox, chi, ALU.subtract)
        sm0 = ar.f32(4, 1)
        P.dma("sp", sm0, I["sm"])
        zr4 = zrow[0:4]
        Bc = ar.f32(4, NT)
        P.scan(Bc[:, 0:TP], mlf[:, 0:TP], zr4[:, 0:TP], 0.0, ALU.add, ALU.add)
        P.scan(Bc[:, TP:NT], mlf[:, TP:NT], zr4[:, 0:TS], 0.0, ALU.add, ALU.add)
        P.tt(A_r, gi_r, Bc, ALU.subtract)
        P.scan(G_r[:, 0:TP], A_r[:, 0:TP], A_r[:, 0:TP], 0.0, ALU.max, ALU.max)
        P.scan(G_r[:, TP:NT], A_r[:, TP:NT], A_r[:, TP:NT], sm0, ALU.max, ALU.max)
        gend = ar.f32(4, 17)
        mprev = ar.f32(4, 17)
        P.copy(gend[:, 0:16], G_r[:, 127:TP:128])
        P.copy(gend[:, 16:17], G_r[:, NT - 1:NT])
        P.memset(mprev[:, 0:1], 0.0)
        P.copy(mprev[:, 1:16], gend[:, 0:15])
        P.copy(mprev[:, 16:17], sm0)
        dec = ar.f32(4, 17)
        P.tt(dec, mprev, gend, ALU.subtract)
        P.act(dec, dec, AF.Exp)
        gend_b = gend[:, 0:16].unsqueeze(2).broadcast_to([4, 16, 128])
        P.tt(r3(wgT[:, 0:TP], 128), r3(A_r[:, 0:TP], 128), gend_b, ALU.subtract)
        P.ts(wgT[:, TP:NT], A_r[:, TP:NT], gend[:, 16:17], ALU.subtract)
        P.act(wgT, wgT, AF.Exp)
        P.tt(r3(rdl[:, 0:TP], 128), r3(Bc[:, 0:TP], 128), gend_b, ALU.add)
        P.ts(rdl[:, TP:NT], Bc[:, TP:NT], gend[:, 16:17], ALU.add)
        P.ts(rdl, rdl, -1.0, ALU.mult)
        mTo = ar.f32(4, 2)
        P.tt(mTo[:, 0:1], G_r[:, TP - 1:TP], Bc[:, TP - 1:TP], ALU.add)
        P.tt(mTo[:, 1:2], G_r[:, NT - 1:NT], Bc[:, NT - 1:NT], ALU.add)
        P.dma("sp", O["ml_m"][0], mTo[:, 0:1])
        P.dma("sp", O["ml_m"][1], mTo[:, 1:2])
        for ti, (t0, L) in enumerate(TILES):
            ps = nb()
            P.tr(ps[0:L, 0:4], wgT[:, t0:t0 + L], ident_f[0:4, 0:4])
            P.copy(wg_tok[0:L, ti, :], ps[0:L, 0:4])
        for h in range(4):
            ps = nb()
            P.mm(ps[:, 0:17], sel4[:, h, :], dec)
            P.copy(dec_b[:, h, :], ps[:, 0:17])
        dbg("cfox", cfox)
        dbg("G_r", G_r)
        dbg("wgT", wgT)
        dbg("rdl", rdl)
        ar.release(m1a)
        s1 = ar.mark()

        P.mark("s1a_gates")
        cchi = ar.bf(8, PAST)
        cclo = ar.bf(8, PAST)
        m_s = ar.mark()
        ccache = ar.f32(8, PAST)
        lcache = ar.f32(8, PAST)
        zc = ar.f32(8, PAST)
        P.memset(zc, 0.0)
        P.dma("sp", lcache, I["clogfT"])
        P.scan(ccache, lcache, zc, 0.0, ALU.add, ALU.add)
        ctot = ar.f32(8, 1)
        P.copy(ctot, ccache[:, PAST - 1:PAST])
        P.ts(ccache, ccache, ctot, ALU.subtract)
        P.copy(cchi, ccache)
        P.tt(cclo, ccache, cchi, ALU.subtract)
        ar.release(m_s)
        qT = ar.bf(128, 4, NT)
        kT = ar.bf(128, 4, NT)
        V = ar.bf(128, 17, 512)
        m_w = ar.mark()
        wf = ar.bf(128, 8, 1536)
        P.dma("pool", wf, wv("w_in")[:, :, 0:1536])
        bvrow = ar.f32(128, 512)
        P.dma("sp", bvrow, I["b_row"][:, 1024:1536].partition_broadcast(128))
        ptmp = [(ar.f32(128, 512), ar.bf(128, 512), ar.f32(128, 512), ar.f32(128, 512)) for _ in range(3)]
        its = [(t0, n, which, ch) for (t0, n) in GROUPS for which in range(2) for ch in range(4)]

        def fp_A(it):
            t0, n, which, ch = it
            col0 = which * 512 + ch * 128
            ps = nb()
            for c in range(8):
                P.mm(ps[:, 0:n], wf[:, c, col0:col0 + 128], xnT[:, c, t0:t0 + n], start=(c == 0), stop=(c == 7))
            return ps

        def fp_B(i, it, ps):
            t0, n, which, ch = it
            col0 = which * 512 + ch * 128
            z_, sq_, r_, kf_ = ptmp[i % 3]
            z = z_[:, 0:n]
            sq = sq_[:, 0:n]
            P.act(z, ps[:, 0:n], AF.Identity, bias=bcol(col0))
            P.act(sq, ps[:, 0:n], AF.Square, bias=bcol(col0))
            ps2 = nb()
            P.mm(ps2[:, 0:n], bdones, sq)
            r = r_[:, 0:n]
            P.act(r, ps2[:, 0:n], AF.Ln, bias=eps_col, scale=1.0 / 64)
            P.act(r, r, AF.Exp, scale=-0.5)
            if which == 0:
                P.stt(qT[:, ch, t0:t0 + n], z, gq8, r, ALU.mult, ALU.mult)
            else:
                kf = kf_[:, 0:n]
                P.stt(kf, z, gk2, r, ALU.mult, ALU.mult)
                P.copy(kT[:, ch, t0:t0 + n], kf)
                P.dma("sp", O["fox_kT"][ch * 128:(ch + 1) * 128, t0:t0 + n], kf)

        vtmp = [ar.f32(128, 512), ar.f32(128, 512)]

        def fp_V(ti):
            t0, L = TILES[ti]
            ps = nb()
            for c in range(8):
                P.mm(ps[0:L, :], xnT[:, c, t0:t0 + L], wf[:, c, 1024:1536], start=(c == 0), stop=(c == 7))
            vf = vtmp[ti % 2]
            P.tt(vf[0:L], ps[0:L, :], bvrow[0:L], ALU.add)
            P.copy(V[0:L, ti, :], vf[0:L])
            P.dma("sp", O["fox_v"][t0:t0 + L, :], vf[0:L])

        psq = [fp_A(its[0])]
        vnext = 0
        for i, it in enumerate(its):
            if i + 1 < len(its):
                psq.append(fp_A(its[i + 1]))
            if i % 2 == 1 and vnext < 17:
                fp_V(vnext)
                vnext += 1
            fp_B(i, it, psq[i])
        while vnext < 17:
            fp_V(vnext)
            vnext += 1
        ar.release(m_w)
        dbg("qT", qT[:, 0, :])
        dbg("kT", kT[:, 0, :])

        P.mark("s1b_foxproj")
        pbufs = [ar.bf(128, 2, 512) for _ in range(4)]
        pctr = [0]
        fctr = [0, 0]
        rbuf = ar.f32(128, 512)
        KMAX = PAST + TS
        qas = [ar.bf(128, NT), ar.bf(128, NT)]
        kas_p = [ar.bf(128, TP), ar.bf(128, TP)]
        kas_s = [ar.bf(128, KMAX), ar.bf(128, KMAX)]
        vexts_p = [ar.bf(128, 16, 128), ar.bf(128, 16, 128)]
        vexts_s = [ar.bf(128, 33, 128), ar.bf(128, 33, 128)]
        for i in range(2):
            P.memset(qas[i][64:68, :], -1.0)
            P.memset(kas_p[i][64:68, :], 1.0)
            P.memset(kas_s[i][64:68, :], 1.0)
        for ve in (vexts_p, vexts_s):
            P.memset(ve[0][:, :, 64:128], 1.0)
            P.memset(ve[1][:, :, 0:64], 1.0)
        ckT_d = I["ckT"]

        def fox_attend(h, qa, qc0, qn, keys, ka, vext):
            ch, half = h // 2, h % 2
            pb = half * 64
            po = (1 - half) * 64
            acc = PS[:, 6 + (fctr[0] % 2), :]
            fctr[0] += 1
            full = [k for k in keys if k[3] is None]
            diag = [k for k in keys if k[3] is not None]
            per = 2 if qn > 64 else 8
            units = [full[i:i + per] for i in range(0, len(full), per)] + [[k] for k in diag]
            nk = len(keys)
            done = [0]

            def emit_front(unit):
                base = 2 * (fctr[1] % 3)
                fctr[1] += 1
                pt = pbufs[pctr[0] % len(pbufs)]
                pctr[0] += 1
                pvs = []
                if len(unit) == 1:
                    kc0, vti, L, doff = unit[0]
                    q_lo = 0 if doff is None else doff
                    sps = PS[:, base, :]
                    P.mm(sps[0:L, q_lo:qn], ka[0:68, kc0:kc0 + L], qa[0:68, qc0 + q_lo:qc0 + qn], start=True, stop=(doff is None))
                    if doff is not None:
                        dq = min(L, qn - q_lo)
                        P.mm(sps[0:L, q_lo:q_lo + dq], ident_b[0:L, 0:L], negmask[0:L, 0:dq], start=False, stop=True)
                    P.act(pt[0:L, 0, q_lo:qn], sps[0:L, q_lo:qn], AF.Exp)
                    pvs.append((acc[:, q_lo:qn], vext[0:L, vti, :], pt[0:L, 0, q_lo:qn]))
                elif qn > 64:
                    for j, (kc0, vti, L, doff) in enumerate(unit):
                        P.mm(PS[:, base + j, 0:qn], ka[0:68, kc0:kc0 + L], qa[0:68, qc0:qc0 + qn])
                        pvs.append((acc[:, 0:qn], vext[0:L, vti, :], pt[:, j, 0:qn]))
                    P.act(pt[:, 0:2, 0:qn], PS[:, base:base + 2, 0:qn], AF.Exp)
                else:
                    m = len(unit)
                    for j, (kc0, vti, L, doff) in enumerate(unit):
                        P.mm(PS[:, base, j * qn:(j + 1) * qn], ka[0:68, kc0:kc0 + L], qa[0:68, qc0:qc0 + qn])
                        pvs.append((acc[:, 0:qn], vext[0:L, vti, :], pt[:, 0, j * qn:(j + 1) * qn]))
                    P.act(pt[:, 0, 0:m * qn], PS[:, base, 0:m * qn], AF.Exp)
                return pvs

            def emit_pv(pvs):
                for (o_, l_, r_) in pvs:
                    P.mm(o_, l_, r_, start=(done[0] == 0), stop=(done[0] == nk - 1))
                    done[0] += 1

            LA = 2
            q_ = [emit_front(units[u]) for u in range(min(LA, len(units)))]
            for u in range(len(units)):
                if u + LA < len(units):
                    q_.append(emit_front(units[u + LA]))
                emit_pv(q_[u])
            rb_ = rbuf
            P.recip(rb_[pb:pb + 64, 0:qn], acc[po:po + 64, 0:qn])
            P.tt(aT[pb:pb + 64, ch, qc0:qc0 + qn], acc[pb:pb + 64, 0:qn], rb_[pb:pb + 64, 0:qn], ALU.mult)

        def sample_loads(h):
            ch, half = h // 2, h % 2
            pb = half * 64
            ka, vext = kas_s[h % 2], vexts_s[h % 2]
            vo = 0 if half == 0 else 64
            P.dma("pool", ka[0:64, 0:PAST], ckT_d[h * 64:(h + 1) * 64, :])
            P.dma("sp", ka[64:65, 0:PAST], cchi[h:h + 1, :])
            P.dma("sp", ka[65:66, 0:PAST], cclo[h:h + 1, :])
            P.dma("sp", ka[0:64, PAST:KMAX], kT[pb:pb + 64, ch, TP:NT])
            P.dma("sp", ka[64:65, PAST:KMAX], chi[h:h + 1, TP:NT])
            P.dma("sp", ka[65:66, PAST:KMAX], clo[h:h + 1, TP:NT])
            P.dma("pool", vext[:, 0:32, vo:vo + 64], I["cv"][h])
            P.copy(vext[0:64, 32, vo:vo + 64], V[0:64, 16, h * 64:h * 64 + 64], eng="pool")

        sample_loads(0)
        keys_s = [(ti * 128, ti, 128, None) for ti in range(32)] + [(PAST, 32, 64, 0)]
        for h in range(8):
            ch, half = h // 2, h % 2
            pb = half * 64
            qa, ka, vext = qas[h % 2], kas_p[h % 2], vexts_p[h % 2]
            vo = 0 if half == 0 else 64
            P.dma("sp", qa[0:64, :], qT[pb:pb + 64, ch, :])
            P.dma("sp", qa[66:67, :], chi[h:h + 1, :])
            P.dma("sp", qa[67:68, :], clo[h:h + 1, :])
            P.dma("sp", ka[0:64, 0:TP], kT[pb:pb + 64, ch, 0:TP])
            P.dma("sp", ka[64:65, 0:TP], chi[h:h + 1, 0:TP])
            P.dma("sp", ka[65:66, 0:TP], clo[h:h + 1, 0:TP])
            P.copy(vext[:, 0:16, vo:vo + 64], V[:, 0:16, h * 64:h * 64 + 64], eng="pool")
            if h + 1 < 8:
                sample_loads(h + 1)
            for gi in range(4):
                keys = []
                for ti in range(gi * 4 + 4):
                    doff = None if ti < gi * 4 else (ti - gi * 4) * 128
                    keys.append((ti * 128, ti, 128, doff))
                fox_attend(h, qa, gi * 512, 512, keys, ka, vext)
            fox_attend(h, qa, TP, TS, keys_s, kas_s[h % 2], vexts_s[h % 2])
            if h == 0:
                dbg("aT0", aT[0:64, 0, 0:TP])
        dbg("aT", aT[:, 0, :])
        P.mark("fox_prompt")
        dbg("aTs", aT[:, 0, TP:NT])
        ar.release(s1)
        bT_off = ar.mark()
        bT = ar.bf(128, 4, NT)
        s1 = ar.mark()

        P.mark("fox_sample")
        wm = ar.bf(128, 8, 2048)
        P.dma("pool", wm[:, :, 0:1536], wv("w_in")[:, :, 1544:3080])
        P.dma("pool", wm[:, :, 1536:2048], wv("w_in")[:, :, 3088:3600])
        bmv = ar.f32(128, 512)
        P.dma("sp", bmv, I["b_row"][:, 2568:3080].partition_broadcast(128))
        CT = ar.f32(128, 4, 129)
        pc = ar.f32(128, 8, 515)
        qks = [ar.bf(128, 8, 512), ar.bf(128, 8, 512)]
        qkc = [qks[0]]
        sigo = ar.f32(128, 4, 512)
        rdb = ar.f32(128, 4, 512)
        caccs = [ar.f32(128, 512), ar.f32(128, 512)]
        ones3 = ones_f.unsqueeze(1).broadcast_to([128, 4, 128])
        msets = []
        for _ in range(2):
            msets.append(dict(vfull=ar.f32(128, 512), vw=ar.bf(128, 4, 129), wgb=ar.bf(128, 4, 128),
                              ktok=ar.bf(128, 4, 128), ET=ar.bf(128, 4, 128), Cq=ar.bf(128, 4, 128),
                              nbm=ar.bf(128, 4, 128), tden=ar.f32(128, 4, 128), hs=ar.f32(128, 4, 128),
                              hsq=ar.bf(128, 4, 128)))
            msets[-1]["rr"] = msets[-1]["tden"]
        B_v = PS[:, 0, :]
        B_tr = PS[:, 1, :].bitcast(BF16)
        B_S = PS[:, 2, :]
        B_k = [PS[:, 3, :], PS[:, 4, :]]
        B_Y = PS[:, 5, :]
        B_D = PS[:, 6, :]
        B_q = PS[:, 7, :]
        P.memset(CT, 0.0)
        P.memset(pc[:, :, 0:3], 0.0)
        wb3 = ar.f32(128, 8)
        for j in range(8):
            P.tt(wb3[:, j:j + 1], mconv_w[:, j, 3:4], bcol(1544 + 128 * j), ALU.mult)

        def ml_step(prev, cur):
            if prev is not None:
                tiP, ttP, LP, loP, SP = prev
                Dv = r3(B_D[:, 0:4 * LP], LP)
                Yv = r3(B_Y[:, 0:4 * LP], LP)
                td = SP["tden"][:, :, 0:LP]
                hs_ = SP["hs"][:, :, 0:LP]
                hq_ = SP["hsq"][:, :, 0:LP]
                rr_ = SP["rr"][:, :, 0:LP]
            if cur is not None:
                ti, tt0, L, lo, S = cur
                wg3 = wg_tok[0:L, ti, :].unsqueeze(2)
            if prev is not None:
                for h in range(4):
                    q_h = qkc[0][:, h, loP:loP + LP]
                    P.mm(B_Y[:, h * LP:(h + 1) * LP], SP["Cq"][:, h, :], q_h, start=True, stop=False)
                    P.mm(B_Y[:, h * LP:(h + 1) * LP], SP["vw"][0:LP, h, 0:128], SP["ET"][0:LP, h, 0:LP], start=False, stop=True)
                for h in range(4):
                    q_h = qkc[0][:, h, loP:loP + LP]
                    P.mm(B_D[:, h * LP:(h + 1) * LP], SP["nbm"][:, h, :], q_h, start=True, stop=False)
                    P.mm(B_D[:, h * LP:(h + 1) * LP], SP["wgb"][0:LP, h, :], SP["ET"][0:LP, h, 0:LP], start=False, stop=True)
            if cur is not None:
                for c in range(8):
                    P.mm(B_v[0:L, :], xnT[:, c, tt0:tt0 + L], wm[:, c, 1024:1536], start=(c == 0), stop=(c == 7))
            if prev is not None:
                P.act(td, Dv, AF.Abs)
                P.tt(td, td, rdb[:, :, loP:loP + LP], ALU.max)
                P.act(td, td, AF.Ln)
                P.act(td, td, AF.Exp, scale=-1.0)
            if cur is not None:
                P.tt(S["vfull"][0:L], B_v[0:L, :], bmv[0:L], ALU.add)
                P.tt(S["vw"][0:L, :, 0:128], r3(S["vfull"][0:L], 128), wg3.broadcast_to([L, 4, 128]), ALU.mult)
                P.copy(S["vw"][0:L, :, 128:129], wg3)
                P.tt(S["wgb"][0:L], ones_f[0:L].unsqueeze(1).broadcast_to([L, 4, 128]), wg3.broadcast_to([L, 4, 128]), ALU.mult)
                for h in range(4):
                    P.tr(B_tr[0:L, h * 128:(h + 1) * 128], qkc[0][:, 4 + h, lo:lo + L], ident_b)
                P.act(S["ktok"][0:L], r3(B_tr[0:L, 0:512], 128), AF.Copy)
                for h in range(4):
                    P.mm(B_S[0:L, h * L:(h + 1) * L], qkc[0][:, 4 + h, lo:lo + L], qkc[0][:, h, lo:lo + L])
            if prev is not None:
                P.tt(hs_, Yv, td, ALU.mult)
                P.tt(hs_, hs_, sigo[:, :, loP:loP + LP], ALU.mult)
                P.act(hq_, hs_, AF.Square)
                for h in range(4):
                    P.mm(B_q[:, h * LP:(h + 1) * LP], ones_b, SP["hsq"][:, h, 0:LP])
                P.act(rr_, r3(B_q[:, 0:4 * LP], LP), AF.Ln, bias=eps_col, scale=1.0 / 128)
                P.act(rr_, rr_, AF.Exp, scale=-0.5)
            if cur is not None:
                P.stt(S["ET"][0:L, :, 0:L], r3(B_S[0:L, 0:4 * L], L), QSC,
                      trimask[0:L, 0:L].unsqueeze(1).broadcast_to([L, 4, L]), ALU.mult, ALU.mult)
                for h in range(4):
                    P.mm(B_k[h // 2][:, (h % 2) * 129:(h % 2) * 129 + 129], S["ktok"][0:L, h, :], S["vw"][0:L, h, :])
                P.tt(CT, CT, dec_b[:, :, ti:ti + 1].broadcast_to([128, 4, 129]), ALU.mult)
                P.act(S["Cq"], CT[:, :, 0:128], AF.Copy, scale=QSC)
                P.stt(S["nbm"], ones3, QSC, CT[:, :, 128:129].broadcast_to([128, 4, 128]), ALU.mult, ALU.mult)
            if prev is not None:
                P.tt(hs_, hs_, rr_, ALU.mult)
                P.tt(bT[:, :, ttP:ttP + LP], hs_, mhn.unsqueeze(2).broadcast_to([128, 4, LP]), ALU.mult)
            if cur is not None:
                P.tt(CT[:, 0:2, :], CT[:, 0:2, :], r3(B_k[0][:, 0:258], 129), ALU.add)
                P.tt(CT[:, 2:4, :], CT[:, 2:4, :], r3(B_k[1][:, 0:258], 129), ALU.add)

        def ml_proj_parts(gi):
            t0, n = GROUPS[gi]
            qk = qks[gi % 2]

            def part(pj):
                if pj == 0 and gi == 4:
                    P.dma("sp", pc[:, :, 0:3], I["sconv"])
                for j in (2 * pj, 2 * pj + 1):
                    ps = nb()
                    for c in range(8):
                        P.mm(ps[:, 0:n], wm[:, c, j * 128:(j + 1) * 128], xnT[:, c, t0:t0 + n], start=(c == 0), stop=(c == 7))
                    P.act(pc[:, j, 3:3 + n], ps[:, 0:n], AF.Identity, bias=bcol(1544 + 128 * j))
                    P.act(caccs[j % 2][:, 0:n], ps[:, 0:n], AF.Identity, scale=mconv_w[:, j, 3:4], bias=wb3[:, j:j + 1])
                for j in (2 * pj, 2 * pj + 1):
                    cacc = caccs[j % 2]
                    for tap in (0, 1, 2):
                        P.stt(cacc[:, 0:n], pc[:, j, tap:tap + n], mconv_w[:, j, tap:tap + 1], cacc[:, 0:n], ALU.mult, ALU.add)
                    P.act(qk[:, j, 0:n], cacc[:, 0:n], AF.Silu, bias=mconv_b[:, j:j + 1])
                if pj == 3:
                    if gi == 3:
                        P.dma("sp", O["ml_convT"][0], pc[:, :, n:n + 3])
                    if gi == 4:
                        P.dma("sp", O["ml_convT"][1], pc[:, :, n:n + 3])
                    if gi < 3:
                        P.copy(pc[:, :, 0:3], pc[:, :, n:n + 3])
            return [lambda pj=pj: part(pj) for pj in range(4)]

        def ml_gates(gi):
            t0, n = GROUPS[gi]
            for h in range(4):
                ps = nb()
                for c in range(8):
                    P.mm(ps[:, 0:n], wm[:, c, 1536 + h * 128:1536 + (h + 1) * 128], xnT[:, c, t0:t0 + n], start=(c == 0), stop=(c == 7))
                P.act(sigo[:, h, 0:n], ps[:, 0:n], AF.Sigmoid, bias=bcol(3088 + 128 * h))
            for h in range(4):
                ps = nb()
                P.mm(ps[:, 0:n], sel4[:, h, :], rdl[:, t0:t0 + n])
                P.act(rdb[:, h, 0:n], ps[:, 0:n], AF.Exp)

        for p_ in ml_proj_parts(0):
            p_()
        ti_global = 0
        for gi, (t0, n) in enumerate(GROUPS):
            if gi == 4:
                P.dma("sp", O["ml_cT"][0], CT[:, :, 0:128])
                P.dma("sp", O["ml_n"][0], CT[:, :, 128])
                P.dma("sp", CT[:, :, 0:128], I["sC"])
                P.dma("sp", CT[:, :, 128], I["sn"])
            ml_gates(gi)
            qkc[0] = qks[gi % 2]
            nparts = ml_proj_parts(gi + 1) if gi + 1 < len(GROUPS) else []
            tiles = [(tt0, L) for (tt0, L) in TILES if t0 <= tt0 < t0 + n]
            prev = None
            for k_, (tt0, L) in enumerate(tiles):
                ti = ti_global
                ti_global += 1
                cur = (ti, tt0, L, tt0 - t0, msets[ti % 2])
                ml_step(prev, cur)
                prev = cur
                if k_ < len(nparts):
                    nparts[k_]()
            ml_step(prev, None)
        P.dma("sp", O["ml_cT"][1], CT[:, :, 0:128])
        P.dma("sp", O["ml_n"][1], CT[:, :, 128])
        ar.release(s1)
        dbg("bT", bT[:, 0, :])
        mT_off = ar.mark()
        mT = ar.bf(128, 4, NT)
        s1 = ar.mark()

        P.mark("mlstm")
        wq = ar.bf(128, 8, 512)
        P.dma("pool", wq, wv("w_in")[:, :, 3600:4112])
        wkv = ar.bf(128, 8, 1024)
        P.dma("pool", wkv, wv("w_mem_kv"))
        memx = ar.f32(128, 8, 256)
        P.dma("sp", memx, I["memT"].rearrange("(c p) n -> p c n", p=128))
        rbm = ar.f32(128, 256)
        rms_bcast(memx, 8, 256, 1024.0, rbm)
        memn = ar.bf(128, 8, 256)
        for c in range(8):
            P.stt(memn[:, c, :], memx[:, c, :], g_mem[:, c:c + 1], rbm, ALU.mult, ALU.mult)
        mkT = ar.bf(128, 2, 4, 256)
        mv = ar.bf(128, 2, 2, 512)
        tmpf = ar.f32(128, 512)
        for chh in range(4):
            ps = nb()
            for c in range(8):
                P.mm(ps[:, 0:256], wkv[:, c, chh * 128:(chh + 1) * 128], memn[:, c, :], start=(c == 0), stop=(c == 7))
            P.copy(tmpf[:, 0:256], ps[:, 0:256])
            P.copy(mkT[:, 0, chh, :], tmpf[:, 0:256])
            P.dma("sp", O["mem_kT"][chh * 128:(chh + 1) * 128, :], tmpf[:, 0:256])
        for mt in range(2):
            ps = nb()
            for c in range(8):
                P.mm(ps[:, :], memn[:, c, mt * 128:(mt + 1) * 128], wkv[:, c, 512:1024], start=(c == 0), stop=(c == 7))
            P.copy(tmpf, ps)
            P.copy(mv[:, 0, mt, :], tmpf)
            P.dma("sp", O["mem_v"][mt * 128:(mt + 1) * 128, :], tmpf)
        P.dma("pool", mkT[:, 1], I["cmkT"].rearrange("(c p) n -> p c n", p=128))
        P.dma("pool", mv[:, 1], I["cmv"].rearrange("(t p) f -> p t f", p=128))
        qhs = [ar.bf(128, 512), ar.bf(128, 512)]
        ptms = [ar.bf(128, 2, 512), ar.bf(128, 2, 512)]
        rlms = [ar.f32(128, 512), ar.f32(128, 512)]
        mits = [(gi, h) for gi in range(len(GROUPS)) for h in range(4)]

        def mm_A(i):
            gi, h = mits[i]
            t0, n = GROUPS[gi]
            ps = nb()
            for c in range(8):
                P.mm(ps[:, 0:n], wq[:, c, h * 128:(h + 1) * 128], xnT[:, c, t0:t0 + n], start=(c == 0), stop=(c == 7))
            P.act(qhs[i % 2][:, 0:n], ps[:, 0:n], AF.Identity, bias=bcol(3600 + 128 * h))

        def mm_B(i):
            gi, h = mits[i]
            t0, n = GROUPS[gi]
            seq = 0 if gi < 4 else 1
            qh, ptm, rlm = qhs[i % 2], ptms[i % 2], rlms[i % 2]
            for mt in range(2):
                sps = nb()
                P.mm(sps[:, 0:n], mkT[:, seq, h, mt * 128:(mt + 1) * 128], qh[:, 0:n])
                P.act(ptm[:, mt, 0:n], sps[:, 0:n], AF.Exp, scale=QSC)
            ops_ = nb()
            lps = nb()
            for mt in range(2):
                P.mm(ops_[:, 0:n], mv[:, seq, mt, h * 128:(h + 1) * 128], ptm[:, mt, 0:n], start=(mt == 0), stop=(mt == 1))
            for mt in range(2):
                P.mm(lps[:, 0:n], ones_b, ptm[:, mt, 0:n], start=(mt == 0), stop=(mt == 1))
            P.act(rlm[:, 0:n], lps[:, 0:n], AF.Ln)
            P.act(rlm[:, 0:n], rlm[:, 0:n], AF.Exp, scale=-1.0)
            P.tt(mT[:, h, t0:t0 + n], ops_[:, 0:n], rlm[:, 0:n], ALU.mult)

        mm_A(0)
        for i in range(len(mits)):
            if i + 1 < len(mits):
                mm_A(i + 1)
            mm_B(i)
        dbg("mT", mT[:, 0, :])
        ar.release(s1)

        P.mark("mem")
        mergedT = ar.bf(128, 8, NT)
        wo = ar.bf(128, 8, 1024)
        s2 = ar.mark()
        wgs = [[ar.bf(128, 8, 128) for b in range(3)] for _ in range(2)]
        wbs = [[ar.bf(128, 4, 128) for b in range(3)] for _ in range(2)]
        sg = [ar.f32(128, 512) for b in range(3)]
        macc = ar.f32(128, 512)
        mtmp = ar.f32(128, 512)
        brs = ("w_br_a", "w_br_b", "w_br_m")
        srcs = (aT, bT, mT)
        for oc in range(8):
            wg_ = wgs[oc % 2]
            wb_ = wbs[oc % 2]
            for b in range(3):
                g0 = 4112 + b * 1024 + oc * 128
                P.dma("pool", wg_[b], wv("w_in")[:, :, g0:g0 + 128])
                P.dma("pool", wb_[b], wv(brs[b])[:, :, oc * 128:(oc + 1) * 128])
            if oc == 2:
                P.dma("pool", wo, wv("w_out"))
            for (t0, n) in GROUPS:
                pp = []
                for b in range(3):
                    g0 = 4112 + b * 1024 + oc * 128
                    ps = nb()
                    for c in range(8):
                        P.mm(ps[:, 0:n], wg_[b][:, c, :], xnT[:, c, t0:t0 + n], start=(c == 0), stop=(c == 7))
                    P.act(sg[b][:, 0:n], ps[:, 0:n], AF.Sigmoid, bias=bcol(g0))
                for b in range(3):
                    ps = nb()
                    for c in range(4):
                        P.mm(ps[:, 0:n], wb_[b][:, c, :], srcs[b][:, c, t0:t0 + n], start=(c == 0), stop=(c == 3))
                    pp.append(ps)
                P.tt(macc[:, 0:n], sg[0][:, 0:n], pp[0][:, 0:n], ALU.mult)
                P.tt(mtmp[:, 0:n], sg[1][:, 0:n], pp[1][:, 0:n], ALU.mult)
                P.tt(macc[:, 0:n], macc[:, 0:n], mtmp[:, 0:n], ALU.add)
                P.tt(mtmp[:, 0:n], sg[2][:, 0:n], pp[2][:, 0:n], ALU.mult)
                P.tt(mergedT[:, oc, t0:t0 + n], macc[:, 0:n], mtmp[:, 0:n], ALU.add)
        dbg("mergedT", mergedT[:, 0, :])
        ar.release(s2)

        P.mark("s2_merge")
        oTs = [ar.f32(128, 8, 512), ar.f32_at(aT_off, 128, 8, 512)]
        xss = [ar.f32(128, 8, 512), ar.f32_at(bT_off, 128, 8, 512)]
        rbs = [ar.f32(128, 512), ar.f32(128, 512)]
        sqs = [ar.bf(128, 8, 512), ar.bf_at(mT_off, 128, 8, 512)]
        x1s3w = x1scr.rearrange("(c p) n -> p c n", p=128)

        def ob_A(gi):
            t0, n = GROUPS[gi]
            oT, xs = oTs[gi % 2], xss[gi % 2]
            P.dma("sp", xs[:, :, 0:n], xT3[:, :, t0:t0 + n])
            for oc in range(8):
                ps = nb()
                for c in range(8):
                    P.mm(ps[:, 0:n], wo[:, c, oc * 128:(oc + 1) * 128], mergedT[:, c, t0:t0 + n], start=(c == 0), stop=(c == 7))
                P.act(oT[:, oc, 0:n], ps[:, 0:n], AF.Copy)
                P.act(sqs[gi % 2][:, oc, 0:n], ps[:, 0:n], AF.Square)

        def ob_B(gi):
            t0, n = GROUPS[gi]
            oT, xs, rb, sq = oTs[gi % 2], xss[gi % 2], rbs[gi % 2], sqs[gi % 2]
            ps = nb()
            for c in range(8):
                P.mm(ps[:, 0:n], ones_b, sq[:, c, 0:n], start=(c == 0), stop=(c == 7))
            P.act(rb[:, 0:n], ps[:, 0:n], AF.Ln, bias=eps_col, scale=1.0 / 1024.0)
            P.act(rb[:, 0:n], rb[:, 0:n], AF.Exp, scale=-0.5)
            ps2 = nb()
            for oc in range(8):
                P.stt(oT[:, oc, 0:n], oT[:, oc, 0:n], g_post[:, oc:oc + 1], rb[:, 0:n], ALU.mult, ALU.mult)
                P.tt(xs[:, oc, 0:n], xs[:, oc, 0:n], oT[:, oc, 0:n], ALU.add)
                P.act(sq[:, oc, 0:n], xs[:, oc, 0:n], AF.Square)
                P.mm(ps2[:, 0:n], ones_b, sq[:, oc, 0:n], start=(oc == 0), stop=(oc == 7))
            P.dma("sp", x1s3w[:, :, t0:t0 + n], xs[:, :, 0:n])
            P.act(rb[:, 0:n], ps2[:, 0:n], AF.Ln, bias=eps_col, scale=1.0 / 1024.0)
            P.act(rb[:, 0:n], rb[:, 0:n], AF.Exp, scale=-0.5)
            for c in range(8):
                P.stt(xnT[:, c, t0:t0 + n], xs[:, c, 0:n], g_fpre[:, c:c + 1], rb[:, 0:n], ALU.mult, ALU.mult)

        ob_A(0)
        for gi in range(len(GROUPS)):
            if gi + 1 < len(GROUPS):
                ob_A(gi + 1)
            ob_B(gi)
        dbg("x1nT", xnT[:, 0, :])
        ar.release(m_after_xn)

        P.mark("s2b_out")
        hidT = ar.bf(128, 22, NT)
        wd3 = wv("w_down")
        wd1 = ar.bf(128, 22, 512)
        m_h = ar.mark()
        wus = [ar.bf(128, 8, 256), ar.bf(128, 8, 256)]
        apre_p = ar.f32(128, 2 + TP)
        bpre_p = ar.f32(128, 2 + TP)
        apre_s = ar.f32(128, 2 + TS)
        bpre_s = ar.f32(128, 2 + TS)
        accas = [ar.f32(128, 512) for _ in range(3)]
        accbs = [ar.f32(128, 512) for _ in range(3)]
        gas = [ar.f32(128, 512) for _ in range(3)]
        fit = [0]
        ftail = [None]
        w_up3 = wv("w_up")
        for c in range(22):
            wu = wus[c % 2]
            P.dma("pool", wu[:, :, 0:128], w_up3[:, :, c * 128:(c + 1) * 128])
            P.dma("pool", wu[:, :, 128:256], w_up3[:, :, 2816 + c * 128:2816 + (c + 1) * 128])
            if c == 2:
                P.dma("pool", wd1, wd3[:, :, 512:1024])
            ja, jb = c, 22 + c
            for gi, (t0, n) in enumerate(GROUPS):
                if gi < 4:
                    apre, bpre = apre_p[:, t0:t0 + n + 2], bpre_p[:, t0:t0 + n + 2]
                else:
                    apre, bpre = apre_s, bpre_s
                if gi == 0:
                    P.memset(apre_p[:, 0:2], 0.0)
                    P.memset(bpre_p[:, 0:2], 0.0)
                if gi == 4:
                    P.dma("sp", apre[:, 0:2], I["sfconv"][:, ja, :])
                    P.dma("sp", bpre[:, 0:2], I["sfconv"][:, jb, :])
                acca, accb, ga = accas[fit[0] % 3], accbs[fit[0] % 3], gas[fit[0] % 3]
                fit[0] += 1
                for (pre, off) in ((apre, 0), (bpre, 128)):
                    ps = nb()
                    for k in range(8):
                        P.mm(ps[:, 0:n], wu[:, k, off:off + 128], xnT[:, k, t0:t0 + n], start=(k == 0), stop=(k == 7))
                    P.act(pre[:, 2:2 + n], ps[:, 0:n], AF.Copy)
                    if off == 128:
                        P.act(accb[:, 0:n], ps[:, 0:n], AF.Identity, scale=fconv_w[:, jb, 2:3])
                    else:
                        P.act(acca[:, 0:n], ps[:, 0:n], AF.Identity, scale=fconv_w[:, ja, 2:3])
                P.stt(acca[:, 0:n], apre[:, 0:n], fconv_w[:, ja, 0:1], acca[:, 0:n], ALU.mult, ALU.add)
                P.stt(acca[:, 0:n], apre[:, 1:1 + n], fconv_w[:, ja, 1:2], acca[:, 0:n], ALU.mult, ALU.add)
                P.stt(accb[:, 0:n], bpre[:, 0:n], fconv_w[:, jb, 0:1], accb[:, 0:n], ALU.mult, ALU.add)
                P.stt(accb[:, 0:n], bpre[:, 1:1 + n], fconv_w[:, jb, 1:2], accb[:, 0:n], ALU.mult, ALU.add)
                if ftail[0] is not None:
                    ftail[0]()

                def _tail(ga=ga, acca=acca, accb=accb, n=n, ja=ja, jb=jb, c=c, t0=t0):
                    P.act(ga[:, 0:n], acca[:, 0:n], AF.Gelu_apprx_tanh, bias=fconv_b[:, ja:ja + 1])
                    P.stt(hidT[:, c, t0:t0 + n], accb[:, 0:n], fconv_b[:, jb:jb + 1], ga[:, 0:n], ALU.add, ALU.mult)
                ftail[0] = _tail
                if gi in (3, 4):
                    so = 0 if gi == 3 else 1
                    P.dma("sp", O["ffn_convT"][so][:, ja, :], apre[:, n:n + 2])
                    P.dma("sp", O["ffn_convT"][so][:, jb, :], bpre[:, n:n + 2])
        ftail[0]()
        dbg("hidT", hidT[:, 0, :])
        ar.release(m_h)
        P.mark("ffn_up")
        wd0 = ar.bf_at(xn_off, 128, 22, 512)
        P.dma("pool", wd0, wd3[:, :, 0:512])
        oT = ar.f32(128, 8, 512)
        xs = ar.f32(128, 8, 512)
        rb = ar.f32(128, 512)
        sqd = ar.bf(128, 8, 512)
        x1s3 = x1scr.rearrange("(c p) n -> p c n", p=128)
        yT3 = O["yT"].rearrange("(c p) n -> p c n", p=128)
        for gi, (t0, n) in enumerate(GROUPS):
            P.dma("sp", xs[:, :, 0:n], x1s3[:, :, t0:t0 + n])
            for oc in (4, 5, 6, 7, 0, 1, 2, 3):
                wd = wd0 if oc < 4 else wd1
                o4 = oc % 4
                ps = nb()
                for c in range(22):
                    P.mm(ps[:, 0:n], wd[:, c, o4 * 128:(o4 + 1) * 128], hidT[:, c, t0:t0 + n], start=(c == 0), stop=(c == 21))
                P.act(oT[:, oc, 0:n], ps[:, 0:n], AF.Copy)
                P.act(sqd[:, oc, 0:n], ps[:, 0:n], AF.Square)
            psr = nb()
            for oc in range(8):
                P.mm(psr[:, 0:n], ones_b, sqd[:, oc, 0:n], start=(oc == 0), stop=(oc == 7))
            P.act(rb[:, 0:n], psr[:, 0:n], AF.Ln, bias=eps_col, scale=1.0 / 1024.0)
            P.act(rb[:, 0:n], rb[:, 0:n], AF.Exp, scale=-0.5)
            for oc in range(8):
                P.stt(oT[:, oc, 0:n], oT[:, oc, 0:n], g_fpost[:, oc:oc + 1], rb[:, 0:n], ALU.mult, ALU.mult)
                P.tt(xs[:, oc, 0:n], xs[:, oc, 0:n], oT[:, oc, 0:n], ALU.add)
            P.dma("sp", yT3[:, :, t0:t0 + n], xs[:, :, 0:n])
        P.mark("ffn_down")
        P.emit(Kq={'pool': 3})
    except _Stop:
        pass
    return nc, DBG, P, 0


_CACHE = {}


def _get_nc(debug=()):
    key = tuple(debug)
    if key not in _CACHE:
        _CACHE[key] = build(debug)
    return _CACHE[key]


def _consts():
    ident = np.eye(128, dtype=np.float32)
    s = np.arange(128)
    trimask = (s[None, :] >= s[:, None]).astype(np.float32)
    negmask = np.where(s[:, None] <= s[None, :], 0.0, -30000.0).astype(np.float32)
    bd = np.zeros((128, 128), np.float32)
    bd[:64, :64] = 1.0
    bd[64:, 64:] = 1.0
    sel4 = np.zeros((4, 4, 128), np.float32)
    for h in range(4):
        sel4[h, h, :] = 1.0
    return dict(ident=ident, trimask=trimask, negmask=negmask, bdones=bd, sel4=sel4)


def _fm(v):
    return np.ascontiguousarray(v.reshape(-1, 128).T)


def _prep_shared(inp):
    f = lambda a: np.ascontiguousarray(np.asarray(a, dtype=np.float32))
    b_in = f(inp["b_in"])[0]
    d = dict(
        w_in=f(inp["w_in"])[0],
        b_fm=np.ascontiguousarray(np.stack([b_in[s:s + 128] for s in FM_STARTS], axis=1)),
        bg_fox=np.ascontiguousarray(b_in[1536:1544][:, None]),
        bg_i=np.ascontiguousarray(b_in[3080:3084][:, None]),
        bg_f=np.ascontiguousarray(b_in[3084:3088][:, None]),
        b_row=np.ascontiguousarray(b_in[None, :]),
        g_pre=_fm(f(inp["norm_mix_pre"])[0]),
        gq2=np.ascontiguousarray(np.concatenate([f(inp["fox_q_norm"])[0]] * 2)[:, None]),
        gk2=np.ascontiguousarray(np.concatenate([f(inp["fox_k_norm"])[0]] * 2)[:, None]),
        mconv_w=np.ascontiguousarray(f(inp["mlstm_conv_w"])[0].reshape(4, 8, 128).transpose(2, 1, 0)),
        mconv_b=_fm(f(inp["mlstm_conv_b"])[0]),
        mhn=np.ascontiguousarray(f(inp["mlstm_head_norm"])[0].T),
        g_mem=_fm(f(inp["norm_mem"])[0]),
        w_mem_kv=f(inp["w_mem_kv"])[0],
        w_br_a=f(inp["w_br_a"])[0], w_br_b=f(inp["w_br_b"])[0], w_br_m=f(inp["w_br_m"])[0],
        w_out=f(inp["w_out"])[0],
        g_post=_fm(f(inp["norm_mix_post"])[0]),
        g_fpre=_fm(f(inp["norm_ffn_pre"])[0]),
        w_up=f(inp["w_up"])[0],
        fconv_w=np.ascontiguousarray(f(inp["ffn_conv_w"])[0].reshape(3, 44, 128).transpose(2, 1, 0)),
        fconv_b=_fm(f(inp["ffn_conv_b"])[0]),
        w_down=f(inp["w_down"])[0],
        g_fpost=_fm(f(inp["norm_ffn_post"])[0]),
    )
    d.update(_consts())
    return d


def _prep_core(inp, b):
    f = lambda a: np.asarray(a, dtype=np.float32)
    c = np.ascontiguousarray
    return dict(
        xT=c(np.concatenate([f(inp["x_prompt"])[b].T, f(inp["x_sample"])[b].T], axis=1)),
        ckT=c(f(inp["cache_fox_k"])[0, b].reshape(PAST, 512).T),
        cv=c(f(inp["cache_fox_v"])[0, b].reshape(32, 128, 8, 64).transpose(2, 1, 0, 3)),
        clogfT=c(f(inp["cache_fox_logf"])[0, b].T),
        sC=c(f(inp["state_mlstm_c"])[0, b].transpose(2, 0, 1)),
        sn=c(f(inp["state_mlstm_n"])[0, b].T),
        sm=c(f(inp["state_mlstm_m"])[0, b][:, None]),
        sconv=c(f(inp["state_mlstm_conv"])[0, b].reshape(3, 8, 128).transpose(2, 1, 0)),
        cmkT=c(f(inp["cache_mem_k"])[0, b].reshape(256, 512).T),
        cmv=c(f(inp["cache_mem_v"])[0, b].reshape(256, 512)),
        sfconv=c(f(inp["state_ffn_conv"])[0, b].reshape(2, 44, 128).transpose(2, 1, 0)),
        memT=c(f(inp["mem_prompt"])[b].T),
    )


def _assemble(results):
    B = len(results)
    z = lambda *s: np.zeros(s, np.float32)
    y_p, y_s = z(B, TP, 1024), z(B, TS, 1024)
    fk_p, fv_p, fl_p = z(1, B, TP, 8, 64), z(1, B, TP, 8, 64), z(1, B, TP, 8)
    fk_s, fv_s, fl_s = z(1, B, TS, 8, 64), z(1, B, TS, 8, 64), z(1, B, TS, 8)
    c_p, n_p, m_p = z(1, B, 4, 128, 128), z(1, B, 4, 128), z(1, B, 4)
    c_s, n_s, m_s = z(1, B, 4, 128, 128), z(1, B, 4, 128), z(1, B, 4)
    cv_p, cv_s = z(1, B, 3, 1024), z(1, B, 3, 1024)
    fc_p, fc_s = z(1, B, 2, 5632), z(1, B, 2, 5632)
    mk_p, mv_p = z(1, B, 256, 4, 128), z(1, B, 256, 4, 128)
    for b, r in enumerate(results):
        yT = r["yT"]
        y_p[b] = yT[:, :TP].T
        y_s[b] = yT[:, TP:].T
        kT = r["fox_kT"]
        fk_p[0, b] = kT[:, :TP].T.reshape(TP, 8, 64)
        fk_s[0, b] = kT[:, TP:].T.reshape(TS, 8, 64)
        fv = r["fox_v"]
        fv_p[0, b] = fv[:TP].reshape(TP, 8, 64)
        fv_s[0, b] = fv[TP:].reshape(TS, 8, 64)
        lf = r["fox_logfT"]
        fl_p[0, b] = lf[:, :TP].T
        fl_s[0, b] = lf[:, TP:].T
        for (si, cc, nn, mm, cvv, fcc) in ((0, c_p, n_p, m_p, cv_p, fc_p), (1, c_s, n_s, m_s, cv_s, fc_s)):
            cc[0, b] = r["ml_cT"][si].transpose(1, 2, 0)
            nn[0, b] = r["ml_n"][si].T
            mm[0, b] = r["ml_m"][si][:, 0]
            cvv[0, b] = r["ml_convT"][si].transpose(2, 1, 0).reshape(3, 1024)
            fcc[0, b] = r["ffn_convT"][si].transpose(2, 1, 0).reshape(2, 5632)
        mk_p[0, b] = r["mem_kT"].T.reshape(256, 4, 128)
        mv_p[0, b] = r["mem_v"].reshape(256, 4, 128)
    return (y_p, y_s, fk_p, fv_p, fl_p, c_p, n_p, m_p, cv_p, fc_p, mk_p, mv_p,
            fk_s, fv_s, fl_s, c_s, n_s, m_s, cv_s, fc_s)


def kernel(**inputs):
    nc = _get_nc()[0]
    shared = _prep_shared(inputs)
    in_maps = []
    for b in range(8):
        d = dict(shared)
        d.update(_prep_core(inputs, b))
        in_maps.append(d)
    res = run_bass_kernel_spmd(nc, in_maps, core_ids=list(range(8)))
    return _assemble(res.results)
```

```python
import numpy as np
from contextlib import ExitStack
import concourse.bass as bass
import concourse.mybir as mybir

F32 = mybir.dt.float32
BF16 = mybir.dt.bfloat16
AF = mybir.ActivationFunctionType
ALU = mybir.AluOpType
AX = mybir.AxisListType


def _esize(dt):
    n = str(dt)
    if "float32" in n or "int32" in n:
        return 4
    if "bfloat16" in n or "float16" in n or "int16" in n:
        return 2
    if "int8" in n or "float8" in n:
        return 1
    if "64" in n:
        return 8
    raise ValueError(n)


def _boxes(ap):
    t = ap.tensor
    name = t.name
    es = _esize(ap.dtype)
    dims = [(int(s), int(n)) for s, n in ap.ap]
    off = int(ap.offset)
    if "DRAM" in str(ap.space).upper() or "HBM" in str(ap.space).upper():
        ext = sum((n - 1) * abs(s) for s, n in dims)
        return name, [(0, 1, off * es, (off + ext + 1) * es)]
    tes = _esize(t.dtype)
    rowbytes = int(np.prod([int(x) for x in t.shape[1:]])) * tes
    row = rowbytes // es
    p0 = off // row
    f0 = off % row
    pext = 0
    fd = []
    for s, n in dims:
        if n == 1:
            continue
        if s != 0 and s % row == 0:
            pext += (n - 1) * (s // row)
        elif s != 0:
            fd.append((abs(s), n))
    fd.sort(reverse=True)
    p1 = p0 + pext + 1
    if len(fd) >= 2:
        inner = sum((n - 1) * s for s, n in fd[1:]) + 1
        s0, n0 = fd[0]
        if s0 >= inner and n0 <= 64:
            return name, [(p0, p1, (f0 + i * s0) * es, (f0 + i * s0 + inner) * es) for i in range(n0)]
    ext = sum((n - 1) * s for s, n in fd) + 1
    return name, [(p0, p1, f0 * es, (f0 + ext) * es)]


def _ov(a, b):
    for x in a:
        for y in b:
            if x[0] < y[1] and y[0] < x[1] and x[2] < y[3] and y[2] < x[3]:
                return True
    return False


def _contained(a, b):
    for x in a:
        ok = False
        for y in b:
            if y[0] <= x[0] and x[1] <= y[1] and y[2] <= x[2] and x[3] <= y[3]:
                ok = True
                break
        if not ok:
            return False
    return True


class _Stop(Exception):
    pass


class Prog:
    ENGS = ("pe", "act", "dve", "pool", "sp")

    def __init__(self, nc):
        self.nc = nc
        self.ops = []
        self.hist = {}

    def add(self, eng, fn, reads=(), writes=(), dma=False):
        idx = len(self.ops)
        deps = set()
        rb = [_boxes(a) for a in reads]
        wb = [_boxes(a) for a in writes]
        for name, bx in rb:
            for e in self.hist.get(name, ()):
                if e[2] and _ov(e[0], bx):
                    deps.add(e[1])
        for name, bx in wb:
            for e in self.hist.get(name, ()):
                if _ov(e[0], bx):
                    deps.add(e[1])
        for name, bx in wb:
            lst = [e for e in self.hist.get(name, ()) if not _contained(e[0], bx)]
            lst.append((bx, idx, True, eng, dma))
            self.hist[name] = lst
        for name, bx in rb:
            lst = self.hist.setdefault(name, [])
            rep = False
            if not dma:
                for i, e in enumerate(lst):
                    if (not e[2]) and e[3] == eng and (not e[4]) and e[0] == bx:
                        lst[i] = (bx, idx, False, eng, dma)
                        rep = True
                        break
            if not rep:
                lst.append((bx, idx, False, eng, dma))
        deps.discard(idx)
        self.ops.append((eng, fn, deps, dma))
        return idx

    def mark(self, name):
        if not hasattr(self, 'marks'):
            self.marks = []
        self.marks.append((name, sum(1 for o in self.ops if o[0] == 'pe')))
        if getattr(self, 'stop_at', None) == name:
            self.emit()
            raise _Stop()

    def dma(self, q, out, in_, **kw):
        kw.setdefault('allow_slow_non_contiguous', True)
        return self.add(q, lambda e: e.dma_start(out=out, in_=in_, **kw), [in_], [out], dma=True)

    def mm(self, out, lhsT, rhs, start=True, stop=True, acc=False):
        rd = [lhsT, rhs] + ([out] if not start else [])
        return self.add("pe", lambda e: e.matmul(out, lhsT, rhs, start=start, stop=stop), rd, [out])

    def tr(self, out, in_, ident):
        return self.add("pe", lambda e: e.transpose(out, in_, ident), [in_, ident], [out])

    def act(self, out, in_, func, bias=None, scale=None, accum_out=None, eng="act"):
        rd = [in_]
        kw = {}
        if bias is not None:
            kw["bias"] = bias
            if not isinstance(bias, (int, float)):
                rd.append(bias)
        if scale is not None:
            kw["scale"] = scale
            if not isinstance(scale, (int, float)):
                rd.append(scale)
        wr = [out]
        if accum_out is not None:
            kw["accum_out"] = accum_out
            wr.append(accum_out)
        return self.add("act", lambda e: e.activation(out, in_, func, **kw), rd, wr)

    def tt(self, out, in0, in1, op, eng="dve"):
        return self.add(eng, lambda e: e.tensor_tensor(out, in0, in1, op), [in0, in1], [out])

    def ts(self, out, in0, s1, op0, s2=None, op1=None, eng="dve"):
        rd = [in0] + [s for s in (s1, s2) if s is not None and not isinstance(s, (int, float))]
        if op1 is None:
            return self.add(eng, lambda e: e.tensor_scalar(out, in0, s1, None, op0), rd, [out])
        return self.add(eng, lambda e: e.tensor_scalar(out, in0, s1, s2, op0, op1), rd, [out])

    def stt(self, out, in0, scalar, in1, op0, op1, eng="dve"):
        rd = [in0, in1] + ([] if isinstance(scalar, (int, float)) else [scalar])
        return self.add(eng, lambda e: e.scalar_tensor_tensor(out, in0, scalar, in1, op0, op1), rd, [out])

    def copy(self, out, in_, eng="dve"):
        return self.add(eng, lambda e: e.tensor_copy(out, in_), [in_], [out])

    def memset(self, out, val, eng="dve"):
        return self.add(eng, lambda e: e.memset(out, val), [], [out])

    def recip(self, out, in_):
        return self.add("dve", lambda e: e.reciprocal(out, in_), [in_], [out])

    def scan(self, out, d0, d1, init, op0, op1):
        rd = [d0, d1] + ([] if isinstance(init, (int, float)) else [init])
        return self.add("dve", lambda e: e.tensor_tensor_scan(out, d0, d1, init, op0, op1), rd, [out])

    def emit(self, R=20000, K=8, Kq=None):
        Kq = dict(Kq or {})
        KK = {e: Kq.get(e, K) for e in self.ENGS}
        nc = self.nc
        ops = self.ops
        needed = set()
        for eng, fn, deps, dma in ops:
            for d in deps:
                de, _, _, ddma = ops[d]
                if ddma:
                    continue
                if de == "pe" and eng == "pe" and not dma:
                    continue
                needed.add(d)
        sigidx = {}
        cnt = {e: 0 for e in self.ENGS}
        for i, (eng, fn, deps, dma) in enumerate(ops):
            if (not dma) and i in needed:
                sigidx[i] = cnt[eng]
                cnt[eng] += 1
        dmaidx = {}
        dcnt = {e: 0 for e in self.ENGS}
        for i, (eng, fn, deps, dma) in enumerate(ops):
            if dma:
                dmaidx[i] = dcnt[eng]
                dcnt[eng] += 1
        with ExitStack() as st:
            csem = {e: [st.enter_context(nc.semaphore(f"c_{e}_{j}")) for j in range(max(1, (cnt[e] + R - 1) // R))]
                    for e in self.ENGS}
            dsem = {e: [st.enter_context(nc.semaphore(f"d_{e}_{j}")) for j in range(min(KK[e], dcnt[e]))]
                    for e in self.ENGS}
            block = st.enter_context(nc.Block())

            def run(me, e):
                waited_c = {x: -1 for x in self.ENGS}
                waited_d = {}

                def wait_dma(d):
                    q = ops[d][0]
                    n = dmaidx[d]
                    K = KK[q]
                    sem = dsem[q][n % K]
                    val = 16 * (n // K + 1)
                    key = (q, n % K)
                    if waited_d.get(key, 0) >= val:
                        return
                    e.wait_ge(sem, val)
                    waited_d[key] = val

                for i, (eng, fn, deps, dma) in enumerate(ops):
                    if eng != me:
                        continue
                    for d in sorted(deps):
                        de, _, _, ddma = ops[d]
                        if ddma:
                            wait_dma(d)
                        else:
                            if de == "pe" and me == "pe" and not dma:
                                continue
                            g = sigidx[d]
                            if waited_c[de] >= g:
                                continue
                            e.wait_ge(csem[de][g // R], g % R + 1)
                            waited_c[de] = g
                    if dma:
                        n = dmaidx[i]
                        K = KK[me]
                        sem = dsem[me][n % K]
                        if n >= K:
                            key = (me, n % K)
                            val = 16 * (n // K)
                            if waited_d.get(key, 0) < val:
                                e.wait_ge(sem, val)
                                waited_d[key] = val
                        ins = fn(e)
                        ins.then_inc(sem, 16)
                    else:
                        ins = fn(e)
                        if i in sigidx:
                            g = sigidx[i]
                            ins.then_inc(csem[me][g // R], 1)
                K = KK[me]
                for j in range(min(K, dcnt[me])):
                    uses = (dcnt[me] - 1 - j) // K + 1
                    val = 16 * uses
                    if waited_d.get((me, j), 0) < val:
                        e.wait_ge(dsem[me][j], val)

            @block.sync
            def _(e):
                run("sp", e)

            @block.scalar
            def _(e):
                run("act", e)

            @block.vector
            def _(e):
                run("dve", e)

            @block.gpsimd
            def _(e):
                run("pool", e)

            @block.tensor
            def _(e):
                run("pe", e)

from concourse.bass_utils import run_bass_kernel_spmd

NT = 2112
TP = 2048
TS = 64
PAST = 4096
EPS = 1e-6
GROUPS = [(0, 512), (512, 512), (1024, 512), (1536, 512), (2048, 64)]
TILES = [(i * 128, 128) for i in range(16)] + [(2048, 64)]
QSC = 128.0 ** -0.5

FM_STARTS = ([0, 128, 256, 384] + [512, 640, 768, 896] + [1544 + 128 * i for i in range(8)]
             + [3088 + 128 * i for i in range(4)]
             + [3600 + 128 * i for i in range(4)] + [4112 + 128 * i for i in range(24)])
FM_COL = {s: i for i, s in enumerate(FM_STARTS)}

IN_SHAPES = dict(
    xT=[1024, NT], w_in=[1024, 7184], b_fm=[128, 48], bg_fox=[8, 1], bg_i=[4, 1], bg_f=[4, 1],
    b_row=[1, 7184], g_pre=[128, 8], gq2=[128, 1], gk2=[128, 1], mconv_w=[128, 8, 4], mconv_b=[128, 8],
    mhn=[128, 4], g_mem=[128, 8], w_mem_kv=[1024, 1024], w_br_a=[512, 1024], w_br_b=[512, 1024],
    w_br_m=[512, 1024], w_out=[1024, 1024], g_post=[128, 8], g_fpre=[128, 8], w_up=[1024, 5632],
    fconv_w=[128, 44, 3], fconv_b=[128, 44], w_down=[2816, 1024], g_fpost=[128, 8],
    ckT=[512, PAST], cv=[8, 128, 32, 64], clogfT=[8, PAST], sC=[128, 4, 128], sn=[128, 4], sm=[4, 1],
    sconv=[128, 8, 3], cmkT=[512, 256], cmv=[256, 512], sfconv=[128, 44, 2], memT=[1024, 256],
    ident=[128, 128], trimask=[128, 128], negmask=[128, 128], bdones=[128, 128], sel4=[4, 4, 128],
)
OUT_SHAPES = dict(
    yT=[1024, NT], fox_kT=[512, NT], fox_v=[NT, 512], fox_logfT=[8, NT],
    ml_cT=[2, 128, 4, 128], ml_n=[2, 128, 4], ml_m=[2, 4, 1], ml_convT=[2, 128, 8, 3],
    ffn_convT=[2, 128, 44, 2], mem_kT=[512, 256], mem_v=[256, 512],
)


class Arena:
    def __init__(self, A, cap):
        self.A = A
        self.cap = cap
        self.top = 0
        self.hw = 0

    def _alloc(self, words):
        off = self.top
        self.top += words
        assert self.top <= self.cap, f"arena overflow {self.top} > {self.cap}"
        self.hw = max(self.hw, self.top)
        return off

    def f32(self, *shape):
        n = int(np.prod(shape[1:]))
        off = self._alloc(n)
        return self._view(self.A[:, off:off + n], shape)

    def bf(self, *shape):
        n = int(np.prod(shape[1:]))
        words = (n + 1) // 2
        off = self._alloc(words)
        return self._view(self.A[:, off:off + words].bitcast(BF16)[:, 0:n], shape)

    @staticmethod
    def _view(v, shape):
        p = shape[0]
        if len(shape) == 3:
            v = v.rearrange("p (a b) -> p a b", b=shape[2])
        elif len(shape) == 4:
            v = v.rearrange("p (a b c) -> p a b c", b=shape[2], c=shape[3])
        if p < 128:
            v = v[0:p]
        return v

    def f32_at(self, off, *shape):
        n = int(np.prod(shape[1:]))
        return self._view(self.A[:, off:off + n], shape)

    def bf_at(self, off, *shape):
        n = int(np.prod(shape[1:]))
        words = (n + 1) // 2
        return self._view(self.A[:, off:off + words].bitcast(BF16)[:, 0:n], shape)

    def mark(self):
        return self.top

    def release(self, m):
        self.top = m


def build(debug=(), stop_at=None, salt=None):
    nc = bass.Bass("TRN2", target_bir_lowering=False)
    I = {k: nc.dram_tensor(k, list(v), F32, kind="ExternalInput").ap() for k, v in IN_SHAPES.items()}
    O = {k: nc.dram_tensor(k, list(v), F32, kind="ExternalOutput").ap() for k, v in OUT_SHAPES.items()}
    x1scr = nc.dram_tensor("x1scr", [1024, NT], F32, kind="Internal").ap()
    DBG = {}
    P = Prog(nc)
    P.stop_at = stop_at
    CAP = 53000
    try:
      with nc.sbuf_tensor("A", [128, CAP], F32) as A_, nc.psum_tensor("PS", [128, 8, 512], F32) as PS:
        ar = Arena(A_, CAP)
        hw_box = [ar]
        bank_ctr = [0]

        def nb():
            b = bank_ctr[0] % 8
            bank_ctr[0] += 1
            return PS[:, b, :]

        def dbg(name, ap):
            if name in debug:
                shp = [int(s) for s in ap.shape]
                d = nc.dram_tensor("dbg_" + name, shp, F32, kind="ExternalOutput").ap()
                DBG[name] = shp
                if ap.dtype == F32 and "PSUM" not in str(ap.space).upper():
                    P.dma("sp", d, ap)
                else:
                    m = ar.mark()
                    t = ar.f32(*([128] + shp[1:]))[0:shp[0]]
                    P.copy(t, ap)
                    P.dma("sp", d, t)
                    ar.release(m)

        wv = lambda name: I[name].rearrange("(c p) n -> p c n", p=128)
        r3 = lambda ap, b: ap.rearrange("p (a b) -> p a b", b=b)

        ident_f = ar.f32(128, 128)
        ident_b = ar.bf(128, 128)
        trimask = ar.bf(128, 128)
        negmask = ar.bf(128, 128)
        bdones = ar.bf(128, 128)
        ones_b = ar.bf(128, 128)
        ones_f = ar.f32(128, 128)
        sel4 = ar.f32(4, 4, 128)
        b_fm = ar.f32(128, 48)
        g_pre = ar.f32(128, 8)
        g_post = ar.f32(128, 8)
        g_fpre = ar.f32(128, 8)
        g_fpost = ar.f32(128, 8)
        g_mem = ar.f32(128, 8)
        gq2 = ar.f32(128, 1)
        gk2 = ar.f32(128, 1)
        mhn = ar.f32(128, 4)
        mconv_w = ar.f32(128, 8, 4)
        mconv_b = ar.f32(128, 8)
        fconv_w = ar.f32(128, 44, 3)
        fconv_b = ar.f32(128, 44)
        def load_consts():
            P.dma("sp", ident_f, I["ident"])
            P.dma("pool", ident_b, I["ident"])
            P.dma("pool", trimask, I["trimask"])
            P.dma("pool", negmask, I["negmask"])
            P.dma("pool", bdones, I["bdones"])
            P.dma("sp", sel4, I["sel4"])
            for t, n in ((b_fm, "b_fm"), (g_pre, "g_pre"), (g_post, "g_post"), (g_fpre, "g_fpre"), (g_fpost, "g_fpost"),
                         (g_mem, "g_mem"), (gq2, "gq2"), (gk2, "gk2"), (mhn, "mhn"), (mconv_w, "mconv_w"),
                         (mconv_b, "mconv_b"), (fconv_w, "fconv_w"), (fconv_b, "fconv_b")):
                P.dma("sp", t, I[n])
            P.ts(gq8, gq2, 0.125, ALU.mult)
        P.memset(ones_b, 1.0)
        P.memset(ones_f, 1.0)
        eps_col = ar.f32(128, 1)
        P.memset(eps_col, EPS)
        one_col = ar.f32(128, 1)
        P.memset(one_col, 1.0)
        gq8 = ar.f32(128, 1)
        bcol = lambda start: b_fm[:, FM_COL[start]:FM_COL[start] + 1]

        def rms_bcast(src, C, n, D, rb, sq=None):
            m = ar.mark()
            if sq is None:
                sq = ar.bf(128, C, n)
            ps = nb()
            for c in range(C):
                P.act(sq[:, c, :], src[:, c, :], AF.Square)
                P.mm(ps[:, 0:n], ones_b, sq[:, c, :], start=(c == 0), stop=(c == C - 1))
            P.act(rb, ps[:, 0:n], AF.Ln, bias=eps_col, scale=1.0 / D)
            P.act(rb, rb, AF.Exp, scale=-0.5)
            ar.release(m)

        xn_off = ar.mark()
        xnT = ar.bf(128, 8, NT)
        m_after_xn = ar.mark()
        aT_off = ar.mark()
        aT = ar.bf(128, 4, NT)
        chi = ar.bf(8, NT)
        clo = ar.bf(8, NT)
        rdl = ar.f32(4, NT)
        wg_tok = ar.f32(128, 17, 4)
        dec_b = ar.f32(128, 4, 17)

        xT3 = I["xT"].rearrange("(c p) n -> p c n", p=128)
        m = ar.mark()
        xs2 = [ar.f32(128, 8, 512), ar.f32(128, 8, 512)]
        rb2 = [ar.f32(128, 512), ar.f32(128, 512)]
        sq2 = [ar.bf(128, 8, 512), ar.bf(128, 8, 512)]
        for gi, (t0, n) in enumerate(GROUPS):
            xs = xs2[gi % 2][:, :, 0:n]
            P.dma("sp", xs, xT3[:, :, t0:t0 + n])
            if gi == 0:
                load_consts()
            rb = rb2[gi % 2][:, 0:n]
            rms_bcast(xs, 8, n, 1024.0, rb, sq=sq2[gi % 2][:, :, 0:n])
            for c in range(8):
                P.stt(xnT[:, c, t0:t0 + n], xs[:, c, :], g_pre[:, c:c + 1], rb, ALU.mult, ALU.mult)
        ar.release(m)
        dbg("xnT", xnT[:, 0, :])

        P.mark("s0_norm")
        m1a = ar.mark()
        wgf = ar.bf(128, 8, 8)
        wgi = ar.bf(128, 8, 4)
        wgm = ar.bf(128, 8, 4)
        P.dma("pool", wgf, wv("w_in")[:, :, 1536:1544])
        P.dma("pool", wgi, wv("w_in")[:, :, 3080:3084])
        P.dma("pool", wgm, wv("w_in")[:, :, 3084:3088])
        bgf = ar.f32(8, 1)
        bgi = ar.f32(4, 1)
        bgm = ar.f32(4, 1)
        P.dma("sp", bgf, I["bg_fox"])
        P.dma("sp", bgi, I["bg_i"])
        P.dma("sp", bgm, I["bg_f"])
        zrow = ar.f32(8, NT)
        P.memset(zrow, 0.0)
        flog = ar.f32(8, NT)
        cfox = ar.f32(8, NT)
        gi_r = ar.f32(4, NT)
        mlf = ar.f32(4, NT)
        A_r = ar.f32(4, NT)
        G_r = ar.f32(4, NT)
        wgT = ar.f32(4, NT)
        for (wt, bc, dst, rows, ls) in ((wgf, bgf, flog, 8, True), (wgi, bgi, gi_r, 4, False), (wgm, bgm, mlf, 4, True)):
            nbias = ar.f32(rows, 1)
            P.ts(nbias, bc, -1.0, ALU.mult)
            for (t0, n) in GROUPS:
                ps = nb()
                for c in range(8):
                    P.mm(ps[0:rows, 0:n], wt[:, c, :], xnT[:, c, t0:t0 + n], start=(c == 0), stop=(c == 7))
                if ls:
                    m = ar.mark()
                    e = ar.f32(rows, n)
                    P.act(e, ps[0:rows, 0:n], AF.Exp, bias=nbias, scale=-1.0)
                    P.act(e, e, AF.Ln, bias=one_col[0:rows], scale=1.0)
                    P.ts(dst[:, t0:t0 + n], e, -1.0, ALU.mult)
                    ar.release(m)
                else:
                    P.act(dst[:, t0:t0 + n], ps[0:rows, 0:n], AF.Identity, bias=bc)
        P.dma("sp", O["fox_logfT"], flog)
        P.scan(cfox[:, 0:TP], flog[:, 0:TP], zrow[:, 0:TP], 0.0, ALU.add, ALU.add)
        P.scan(cfox[:, TP:NT], flog[:, TP:NT], zrow[:, 0:TS], 0.0, ALU.add, ALU.add)
        P.copy(chi, cfox)
        P.tt(clo, cfox, chi, ALU.subtract)
        sm0 = ar.f32(4, 1)
        P.dma("sp", sm0, I["sm"])
        zr4 = zrow[0:4]
        Bc = ar.f32(4, NT)
        P.scan(Bc[:, 0:TP], mlf[:, 0:TP], zr4[:, 0:TP], 0.0, ALU.add, ALU.add)
        P.scan(Bc[:, TP:NT], mlf[:, TP:NT], zr4[:, 0:TS], 0.0, ALU.add, ALU.add)
        P.tt(A_r, gi_r, Bc, ALU.subtract)
        P.scan(G_r[:, 0:TP], A_r[:, 0:TP], A_r[:, 0:TP], 0.0, ALU.max, ALU.max)
        P.scan(G_r[:, TP:NT], A_r[:, TP:NT], A_r[:, TP:NT], sm0, ALU.max, ALU.max)
        gend = ar.f32(4, 17)
        mprev = ar.f32(4, 17)
        P.copy(gend[:, 0:16], G_r[:, 127:TP:128])
        P.copy(gend[:, 16:17], G_r[:, NT - 1:NT])
        P.memset(mprev[:, 0:1], 0.0)
        P.copy(mprev[:, 1:16], gend[:, 0:15])
        P.copy(mprev[:, 16:17], sm0)
        dec = ar.f32(4, 17)
        P.tt(dec, mprev, gend, ALU.subtract)
        P.act(dec, dec, AF.Exp)
        gend_b = gend[:, 0:16].unsqueeze(2).broadcast_to([4, 16, 128])
        P.tt(r3(wgT[:, 0:TP], 128), r3(A_r[:, 0:TP], 128), gend_b, ALU.subtract)
        P.ts(wgT[:, TP:NT], A_r[:, TP:NT], gend[:, 16:17], ALU.subtract)
        P.act(wgT, wgT, AF.Exp)
        P.tt(r3(rdl[:, 0:TP], 128), r3(Bc[:, 0:TP], 128), gend_b, ALU.add)
        P.ts(rdl[:, TP:NT], Bc[:, TP:NT], gend[:, 16:17], ALU.add)
        P.ts(rdl, rdl, -1.0, ALU.mult)
        mTo = ar.f32(4, 2)
        P.tt(mTo[:, 0:1], G_r[:, TP - 1:TP], Bc[:, TP - 1:TP], ALU.add)
        P.tt(mTo[:, 1:2], G_r[:, NT - 1:NT], Bc[:, NT - 1:NT], ALU.add)
        P.dma("sp", O["ml_m"][0], mTo[:, 0:1])
        P.dma("sp", O["ml_m"][1], mTo[:, 1:2])
        for ti, (t0, L) in enumerate(TILES):
            ps = nb()
            P.tr(ps[0:L, 0:4], wgT[:, t0:t0 + L], ident_f[0:4, 0:4])
            P.copy(wg_tok[0:L, ti, :], ps[0:L, 0:4])
        for h in range(4):
            ps = nb()
            P.mm(ps[:, 0:17], sel4[:, h, :], dec)
            P.copy(dec_b[:, h, :], ps[:, 0:17])
        dbg("cfox", cfox)
        dbg("G_r", G_r)
        dbg("wgT", wgT)
        dbg("rdl", rdl)
        ar.release(m1a)
        s1 = ar.mark()

        P.mark("s1a_gates")
        cchi = ar.bf(8, PAST)
        cclo = ar.bf(8, PAST)
        m_s = ar.mark()
        ccache = ar.f32(8, PAST)
        lcache = ar.f32(8, PAST)
        zc = ar.f32(8, PAST)
        P.memset(zc, 0.0)
        P.dma("sp", lcache, I["clogfT"])
        P.scan(ccache, lcache, zc, 0.0, ALU.add, ALU.add)
        ctot = ar.f32(8, 1)
        P.copy(ctot, ccache[:, PAST - 1:PAST])
        P.ts(ccache, ccache, ctot, ALU.subtract)
        P.copy(cchi, ccache)
        P.tt(cclo, ccache, cchi, ALU.subtract)
        ar.release(m_s)
        qT = ar.bf(128, 4, NT)
        kT = ar.bf(128, 4, NT)
        V = ar.bf(128, 17, 512)
        m_w = ar.mark()
        wf = ar.bf(128, 8, 1536)
        for c0_ in (0, 512, 1024):
            P.dma("pool", wf[:, :, c0_:c0_ + 512], wv("w_in")[:, :, c0_:c0_ + 512])
        bvrow = ar.f32(128, 512)
        P.dma("sp", bvrow, I["b_row"][:, 1024:1536].partition_broadcast(128))
        ptmp = [(ar.f32(128, 512), ar.bf(128, 512), ar.f32(128, 512), ar.f32(128, 512)) for _ in range(3)]
        its = [(t0, n, which, ch) for (t0, n) in GROUPS for which in range(2) for ch in range(4)]

        def fp_A(it):
            t0, n, which, ch = it
            col0 = which * 512 + ch * 128
            ps = nb()
            for c in range(8):
                P.mm(ps[:, 0:n], wf[:, c, col0:col0 + 128], xnT[:, c, t0:t0 + n], start=(c == 0), stop=(c == 7))
            return ps

        def fp_B(i, it, ps):
            t0, n, which, ch = it
            col0 = which * 512 + ch * 128
            z_, sq_, r_, kf_ = ptmp[i % 3]
            z = z_[:, 0:n]
            sq = sq_[:, 0:n]
            P.act(z, ps[:, 0:n], AF.Identity, bias=bcol(col0))
            P.act(sq, ps[:, 0:n], AF.Square, bias=bcol(col0))
            ps2 = nb()
            P.mm(ps2[:, 0:n], bdones, sq)
            r = r_[:, 0:n]
            P.act(r, ps2[:, 0:n], AF.Ln, bias=eps_col, scale=1.0 / 64)
            P.act(r, r, AF.Exp, scale=-0.5)
            if which == 0:
                P.stt(qT[:, ch, t0:t0 + n], z, gq8, r, ALU.mult, ALU.mult)
            else:
                kf = kf_[:, 0:n]
                P.stt(kf, z, gk2, r, ALU.mult, ALU.mult)
                P.copy(kT[:, ch, t0:t0 + n], kf)
                P.dma("sp", O["fox_kT"][ch * 128:(ch + 1) * 128, t0:t0 + n], kf)

        vtmp = [ar.f32(128, 512), ar.f32(128, 512)]

        def fp_V(ti):
            t0, L = TILES[ti]
            ps = nb()
            for c in range(8):
                P.mm(ps[0:L, :], xnT[:, c, t0:t0 + L], wf[:, c, 1024:1536], start=(c == 0), stop=(c == 7))
            vf = vtmp[ti % 2]
            P.tt(vf[0:L], ps[0:L, :], bvrow[0:L], ALU.add)
            P.copy(V[0:L, ti, :], vf[0:L])
            P.dma("sp", O["fox_v"][t0:t0 + L, :], vf[0:L])

        psq = [fp_A(its[0])]
        vnext = 0
        for i, it in enumerate(its):
            if i + 1 < len(its):
                psq.append(fp_A(its[i + 1]))
            if i % 2 == 1 and vnext < 17:
                fp_V(vnext)
                vnext += 1
            fp_B(i, it, psq[i])
        while vnext < 17:
            fp_V(vnext)
            vnext += 1
        ar.release(m_w)
        dbg("qT", qT[:, 0, :])
        dbg("kT", kT[:, 0, :])

        P.mark("s1b_foxproj")
        pbufs = [ar.bf(128, 2, 512) for _ in range(4)]
        pctr = [0]
        fctr = [0, 0]
        rbuf = ar.f32(128, 512)
        KMAX = PAST + TS
        qas = [ar.bf(128, NT), ar.bf(128, NT)]
        kas_p = [ar.bf(128, TP), ar.bf(128, TP)]
        kas_s = [ar.bf(128, KMAX), ar.bf(128, KMAX)]
        vexts_p = [ar.bf(128, 16, 128), ar.bf(128, 16, 128)]
        vexts_s = [ar.bf(128, 33, 128), ar.bf(128, 33, 128)]
        for i in range(2):
            P.memset(qas[i][64:68, :], -1.0)
            P.memset(kas_p[i][64:68, :], 1.0)
            P.memset(kas_s[i][64:68, :], 1.0)
        for ve in (vexts_p, vexts_s):
            P.memset(ve[0][:, :, 64:128], 1.0)
            P.memset(ve[1][:, :, 0:64], 1.0)
        ckT_d = I["ckT"]

        def fox_attend(h, qa, qc0, qn, keys, ka, vext):
            ch, half = h // 2, h % 2
            pb = half * 64
            po = (1 - half) * 64
            acc = PS[:, 6 + (fctr[0] % 2), :]
            fctr[0] += 1
            full = [k for k in keys if k[3] is None]
            diag = [k for k in keys if k[3] is not None]
            per = 2 if qn > 64 else 8
            units = [full[i:i + per] for i in range(0, len(full), per)] + [[k] for k in diag]
            nk = len(keys)
            done = [0]

            def emit_front(unit):
                base = 2 * (fctr[1] % 3)
                fctr[1] += 1
                pt = pbufs[pctr[0] % len(pbufs)]
                pctr[0] += 1
                pvs = []
                if len(unit) == 1:
                    kc0, vti, L, doff = unit[0]
                    q_lo = 0 if doff is None else doff
                    sps = PS[:, base, :]
                    P.mm(sps[0:L, q_lo:qn], ka[0:68, kc0:kc0 + L], qa[0:68, qc0 + q_lo:qc0 + qn], start=True, stop=(doff is None))
                    if doff is not None:
                        dq = min(L, qn - q_lo)
                        P.mm(sps[0:L, q_lo:q_lo + dq], ident_b[0:L, 0:L], negmask[0:L, 0:dq], start=False, stop=True)
                    P.act(pt[0:L, 0, q_lo:qn], sps[0:L, q_lo:qn], AF.Exp)
                    pvs.append((acc[:, q_lo:qn], vext[0:L, vti, :], pt[0:L, 0, q_lo:qn]))
                elif qn > 64:
                    for j, (kc0, vti, L, doff) in enumerate(unit):
                        P.mm(PS[:, base + j, 0:qn], ka[0:68, kc0:kc0 + L], qa[0:68, qc0:qc0 + qn])
                        pvs.append((acc[:, 0:qn], vext[0:L, vti, :], pt[:, j, 0:qn]))
                    P.act(pt[:, 0:2, 0:qn], PS[:, base:base + 2, 0:qn], AF.Exp)
                else:
                    m = len(unit)
                    for j, (kc0, vti, L, doff) in enumerate(unit):
                        P.mm(PS[:, base, j * qn:(j + 1) * qn], ka[0:68, kc0:kc0 + L], qa[0:68, qc0:qc0 + qn])
                        pvs.append((acc[:, 0:qn], vext[0:L, vti, :], pt[:, 0, j * qn:(j + 1) * qn]))
                    P.act(pt[:, 0, 0:m * qn], PS[:, base, 0:m * qn], AF.Exp)
                return pvs

            def emit_pv(pvs):
                for (o_, l_, r_) in pvs:
                    P.mm(o_, l_, r_, start=(done[0] == 0), stop=(done[0] == nk - 1))
                    done[0] += 1

            LA = 2
            q_ = [emit_front(units[u]) for u in range(min(LA, len(units)))]
            for u in range(len(units)):
                if u + LA < len(units):
                    q_.append(emit_front(units[u + LA]))
                emit_pv(q_[u])
            rb_ = rbuf
            P.recip(rb_[pb:pb + 64, 0:qn], acc[po:po + 64, 0:qn])
            P.tt(aT[pb:pb + 64, ch, qc0:qc0 + qn], acc[pb:pb + 64, 0:qn], rb_[pb:pb + 64, 0:qn], ALU.mult)

        def sample_loads(h):
            ch, half = h // 2, h % 2
            pb = half * 64
            ka, vext = kas_s[h % 2], vexts_s[h % 2]
            vo = 0 if half == 0 else 64
            P.dma("pool", ka[0:64, 0:PAST], ckT_d[h * 64:(h + 1) * 64, :])
            P.dma("sp", ka[64:65, 0:PAST], cchi[h:h + 1, :])
            P.dma("sp", ka[65:66, 0:PAST], cclo[h:h + 1, :])
            P.dma("sp", ka[0:64, PAST:KMAX], kT[pb:pb + 64, ch, TP:NT])
            P.dma("sp", ka[64:65, PAST:KMAX], chi[h:h + 1, TP:NT])
            P.dma("sp", ka[65:66, PAST:KMAX], clo[h:h + 1, TP:NT])
            P.dma("pool", vext[:, 0:32, vo:vo + 64], I["cv"][h])
            P.copy(vext[0:64, 32, vo:vo + 64], V[0:64, 16, h * 64:h * 64 + 64], eng="pool")

        sample_loads(0)
        keys_s = [(ti * 128, ti, 128, None) for ti in range(32)] + [(PAST, 32, 64, 0)]
        for h in range(8):
            ch, half = h // 2, h % 2
            pb = half * 64
            qa, ka, vext = qas[h % 2], kas_p[h % 2], vexts_p[h % 2]
            vo = 0 if half == 0 else 64
            P.dma("sp", qa[0:64, :], qT[pb:pb + 64, ch, :])
            P.dma("sp", qa[66:67, :], chi[h:h + 1, :])
            P.dma("sp", qa[67:68, :], clo[h:h + 1, :])
            P.dma("sp", ka[0:64, 0:TP], kT[pb:pb + 64, ch, 0:TP])
            P.dma("sp", ka[64:65, 0:TP], chi[h:h + 1, 0:TP])
            P.dma("sp", ka[65:66, 0:TP], clo[h:h + 1, 0:TP])
            P.copy(vext[:, 0:16, vo:vo + 64], V[:, 0:16, h * 64:h * 64 + 64], eng="pool")
            if h + 1 < 8:
                sample_loads(h + 1)
            for gi in range(4):
                keys = []
                for ti in range(gi * 4 + 4):
                    doff = None if ti < gi * 4 else (ti - gi * 4) * 128
                    keys.append((ti * 128, ti, 128, doff))
                fox_attend(h, qa, gi * 512, 512, keys, ka, vext)
            fox_attend(h, qa, TP, TS, keys_s, kas_s[h % 2], vexts_s[h % 2])
            if h == 0:
                dbg("aT0", aT[0:64, 0, 0:TP])
        dbg("aT", aT[:, 0, :])
        P.mark("fox_prompt")
        dbg("aTs", aT[:, 0, TP:NT])
        ar.release(s1)
        bT_off = ar.mark()
        bT = ar.bf(128, 4, NT)
        s1 = ar.mark()

        P.mark("fox_sample")
        wm = ar.bf(128, 8, 2048)
        P.dma("pool", wm[:, :, 0:1024], wv("w_in")[:, :, 1544:2568])
        P.dma("pool", wm[:, :, 1024:1536], wv("w_in")[:, :, 2568:3080])
        P.dma("pool", wm[:, :, 1536:2048], wv("w_in")[:, :, 3088:3600])
        bmv = ar.f32(128, 512)
        P.dma("sp", bmv, I["b_row"][:, 2568:3080].partition_broadcast(128))
        CT = ar.f32(128, 4, 129)
        pc = ar.f32(128, 8, 515)
        qks = [ar.bf(128, 8, 512), ar.bf(128, 8, 512)]
        qkc = [qks[0]]
        sigo = ar.f32(128, 4, 512)
        rdb = ar.f32(128, 4, 512)
        caccs = [ar.f32(128, 512), ar.f32(128, 512)]
        ones3 = ones_f.unsqueeze(1).broadcast_to([128, 4, 128])
        msets = []
        for _ in range(2):
            msets.append(dict(vfull=ar.f32(128, 512), vw=ar.bf(128, 4, 129), wgb=ar.bf(128, 4, 128),
                              ktok=ar.bf(128, 4, 128), ET=ar.bf(128, 4, 128), Cq=ar.bf(128, 4, 128),
                              nbm=ar.bf(128, 4, 128), tden=ar.f32(128, 4, 128), hs=ar.f32(128, 4, 128),
                              hsq=ar.bf(128, 4, 128)))
            msets[-1]["rr"] = msets[-1]["tden"]
        B_v = PS[:, 0, :]
        B_tr = PS[:, 1, :].bitcast(BF16)
        B_S = PS[:, 2, :]
        B_k = [PS[:, 3, :], PS[:, 4, :]]
        B_Y = PS[:, 5, :]
        B_D = PS[:, 6, :]
        B_q = PS[:, 7, :]
        P.memset(CT, 0.0)
        P.memset(pc[:, :, 0:3], 0.0)
        wb3 = ar.f32(128, 8)
        for j in range(8):
            P.tt(wb3[:, j:j + 1], mconv_w[:, j, 3:4], bcol(1544 + 128 * j), ALU.mult)

        def ml_step(prev, cur):
            if prev is not None:
                tiP, ttP, LP, loP, SP = prev
                Dv = r3(B_D[:, 0:4 * LP], LP)
                Yv = r3(B_Y[:, 0:4 * LP], LP)
                td = SP["tden"][:, :, 0:LP]
                hs_ = SP["hs"][:, :, 0:LP]
                hq_ = SP["hsq"][:, :, 0:LP]
                rr_ = SP["rr"][:, :, 0:LP]
            if cur is not None:
                ti, tt0, L, lo, S = cur
                wg3 = wg_tok[0:L, ti, :].unsqueeze(2)
            if prev is not None:
                for h in range(4):
                    q_h = qkc[0][:, h, loP:loP + LP]
                    P.mm(B_Y[:, h * LP:(h + 1) * LP], SP["Cq"][:, h, :], q_h, start=True, stop=False)
                    P.mm(B_Y[:, h * LP:(h + 1) * LP], SP["vw"][0:LP, h, 0:128], SP["ET"][0:LP, h, 0:LP], start=False, stop=True)
                for h in range(4):
                    q_h = qkc[0][:, h, loP:loP + LP]
                    P.mm(B_D[:, h * LP:(h + 1) * LP], SP["nbm"][:, h, :], q_h, start=True, stop=False)
                    P.mm(B_D[:, h * LP:(h + 1) * LP], SP["wgb"][0:LP, h, :], SP["ET"][0:LP, h, 0:LP], start=False, stop=True)
            if cur is not None:
                for c in range(8):
                    P.mm(B_v[0:L, :], xnT[:, c, tt0:tt0 + L], wm[:, c, 1024:1536], start=(c == 0), stop=(c == 7))
            if prev is not None:
                P.act(td, Dv, AF.Abs)
                P.tt(td, td, rdb[:, :, loP:loP + LP], ALU.max)
                P.act(td, td, AF.Ln)
                P.act(td, td, AF.Exp, scale=-1.0)
            if cur is not None:
                P.tt(S["vfull"][0:L], B_v[0:L, :], bmv[0:L], ALU.add)
                P.tt(S["vw"][0:L, :, 0:128], r3(S["vfull"][0:L], 128), wg3.broadcast_to([L, 4, 128]), ALU.mult)
                P.copy(S["vw"][0:L, :, 128:129], wg3)
                P.tt(S["wgb"][0:L], ones_f[0:L].unsqueeze(1).broadcast_to([L, 4, 128]), wg3.broadcast_to([L, 4, 128]), ALU.mult)
                for h in range(4):
                    P.tr(B_tr[0:L, h * 128:(h + 1) * 128], qkc[0][:, 4 + h, lo:lo + L], ident_b)
                P.act(S["ktok"][0:L], r3(B_tr[0:L, 0:512], 128), AF.Copy)
                for h in range(4):
                    P.mm(B_S[0:L, h * L:(h + 1) * L], qkc[0][:, 4 + h, lo:lo + L], qkc[0][:, h, lo:lo + L])
            if prev is not None:
                P.tt(hs_, Yv, td, ALU.mult)
                P.tt(hs_, hs_, sigo[:, :, loP:loP + LP], ALU.mult)
                P.act(hq_, hs_, AF.Square)
                for h in range(4):
                    P.mm(B_q[:, h * LP:(h + 1) * LP], ones_b, SP["hsq"][:, h, 0:LP])
                P.act(rr_, r3(B_q[:, 0:4 * LP], LP), AF.Ln, bias=eps_col, scale=1.0 / 128)
                P.act(rr_, rr_, AF.Exp, scale=-0.5)
            if cur is not None:
                P.stt(S["ET"][0:L, :, 0:L], r3(B_S[0:L, 0:4 * L], L), QSC,
                      trimask[0:L, 0:L].unsqueeze(1).broadcast_to([L, 4, L]), ALU.mult, ALU.mult)
                for h in range(4):
                    P.mm(B_k[h // 2][:, (h % 2) * 129:(h % 2) * 129 + 129], S["ktok"][0:L, h, :], S["vw"][0:L, h, :])
                P.tt(CT, CT, dec_b[:, :, ti:ti + 1].broadcast_to([128, 4, 129]), ALU.mult)
                P.act(S["Cq"], CT[:, :, 0:128], AF.Copy, scale=QSC)
                P.stt(S["nbm"], ones3, QSC, CT[:, :, 128:129].broadcast_to([128, 4, 128]), ALU.mult, ALU.mult)
            if prev is not None:
                P.tt(hs_, hs_, rr_, ALU.mult)
                P.tt(bT[:, :, ttP:ttP + LP], hs_, mhn.unsqueeze(2).broadcast_to([128, 4, LP]), ALU.mult)
            if cur is not None:
                P.tt(CT[:, 0:2, :], CT[:, 0:2, :], r3(B_k[0][:, 0:258], 129), ALU.add)
                P.tt(CT[:, 2:4, :], CT[:, 2:4, :], r3(B_k[1][:, 0:258], 129), ALU.add)

        def ml_proj_parts(gi):
            t0, n = GROUPS[gi]
            qk = qks[gi % 2]

            def part(pj):
                if pj == 0 and gi == 4:
                    P.dma("sp", pc[:, :, 0:3], I["sconv"])
                for j in (2 * pj, 2 * pj + 1):
                    ps = nb()
                    for c in range(8):
                        P.mm(ps[:, 0:n], wm[:, c, j * 128:(j + 1) * 128], xnT[:, c, t0:t0 + n], start=(c == 0), stop=(c == 7))
                    P.act(pc[:, j, 3:3 + n], ps[:, 0:n], AF.Identity, bias=bcol(1544 + 128 * j))
                    P.act(caccs[j % 2][:, 0:n], ps[:, 0:n], AF.Identity, scale=mconv_w[:, j, 3:4], bias=wb3[:, j:j + 1])
                for j in (2 * pj, 2 * pj + 1):
                    cacc = caccs[j % 2]
                    for tap in (0, 1, 2):
                        P.stt(cacc[:, 0:n], pc[:, j, tap:tap + n], mconv_w[:, j, tap:tap + 1], cacc[:, 0:n], ALU.mult, ALU.add)
                    P.act(qk[:, j, 0:n], cacc[:, 0:n], AF.Silu, bias=mconv_b[:, j:j + 1])
                if pj == 3:
                    if gi == 3:
                        P.dma("sp", O["ml_convT"][0], pc[:, :, n:n + 3])
                    if gi == 4:
                        P.dma("sp", O["ml_convT"][1], pc[:, :, n:n + 3])
                    if gi < 3:
                        P.copy(pc[:, :, 0:3], pc[:, :, n:n + 3])
            return [lambda pj=pj: part(pj) for pj in range(4)]

        def ml_gates(gi):
            t0, n = GROUPS[gi]
            for h in range(4):
                ps = nb()
                for c in range(8):
                    P.mm(ps[:, 0:n], wm[:, c, 1536 + h * 128:1536 + (h + 1) * 128], xnT[:, c, t0:t0 + n], start=(c == 0), stop=(c == 7))
                P.act(sigo[:, h, 0:n], ps[:, 0:n], AF.Sigmoid, bias=bcol(3088 + 128 * h))
            for h in range(4):
                ps = nb()
                P.mm(ps[:, 0:n], sel4[:, h, :], rdl[:, t0:t0 + n])
                P.act(rdb[:, h, 0:n], ps[:, 0:n], AF.Exp)

        for p_ in ml_proj_parts(0):
            p_()
        ti_global = 0
        for gi, (t0, n) in enumerate(GROUPS):
            if gi == 4:
                P.dma("sp", O["ml_cT"][0], CT[:, :, 0:128])
                P.dma("sp", O["ml_n"][0], CT[:, :, 128])
                P.dma("sp", CT[:, :, 0:128], I["sC"])
                P.dma("sp", CT[:, :, 128], I["sn"])
            ml_gates(gi)
            qkc[0] = qks[gi % 2]
            nparts = ml_proj_parts(gi + 1) if gi + 1 < len(GROUPS) else []
            tiles = [(tt0, L) for (tt0, L) in TILES if t0 <= tt0 < t0 + n]
            prev = None
            for k_, (tt0, L) in enumerate(tiles):
                ti = ti_global
                ti_global += 1
                cur = (ti, tt0, L, tt0 - t0, msets[ti % 2])
                ml_step(prev, cur)
                prev = cur
                if k_ < len(nparts):
                    nparts[k_]()
            ml_step(prev, None)
        P.dma("sp", O["ml_cT"][1], CT[:, :, 0:128])
        P.dma("sp", O["ml_n"][1], CT[:, :, 128])
        ar.release(s1)
        dbg("bT", bT[:, 0, :])
        mT_off = ar.mark()
        mT = ar.bf(128, 4, NT)
        s1 = ar.mark()

        P.mark("mlstm")
        wq = ar.bf(128, 8, 512)
        wkv = ar.bf(128, 8, 1024)
        P.dma("pool", wkv, wv("w_mem_kv"))
        P.dma("pool", wq, wv("w_in")[:, :, 3600:4112])
        memx = ar.f32(128, 8, 256)
        P.dma("sp", memx, I["memT"].rearrange("(c p) n -> p c n", p=128))
        rbm = ar.f32(128, 256)
        rms_bcast(memx, 8, 256, 1024.0, rbm)
        memn = ar.bf(128, 8, 256)
        for c in range(8):
            P.stt(memn[:, c, :], memx[:, c, :], g_mem[:, c:c + 1], rbm, ALU.mult, ALU.mult)
        mkT = ar.bf(128, 2, 4, 256)
        mv = ar.bf(128, 2, 2, 512)
        tmpf = ar.f32(128, 512)
        for chh in range(4):
            ps = nb()
            for c in range(8):
                P.mm(ps[:, 0:256], wkv[:, c, chh * 128:(chh + 1) * 128], memn[:, c, :], start=(c == 0), stop=(c == 7))
            P.copy(tmpf[:, 0:256], ps[:, 0:256])
            P.copy(mkT[:, 0, chh, :], tmpf[:, 0:256])
            P.dma("sp", O["mem_kT"][chh * 128:(chh + 1) * 128, :], tmpf[:, 0:256])
        for mt in range(2):
            ps = nb()
            for c in range(8):
                P.mm(ps[:, :], memn[:, c, mt * 128:(mt + 1) * 128], wkv[:, c, 512:1024], start=(c == 0), stop=(c == 7))
            P.copy(tmpf, ps)
            P.copy(mv[:, 0, mt, :], tmpf)
            P.dma("sp", O["mem_v"][mt * 128:(mt + 1) * 128, :], tmpf)
        P.dma("pool", mkT[:, 1], I["cmkT"].rearrange("(c p) n -> p c n", p=128))
        P.dma("pool", mv[:, 1], I["cmv"].rearrange("(t p) f -> p t f", p=128))
        qhs = [ar.bf(128, 512), ar.bf(128, 512)]
        ptms = [ar.bf(128, 2, 512), ar.bf(128, 2, 512)]
        rlms = [ar.f32(128, 512), ar.f32(128, 512)]
        mits = [(gi, h) for gi in range(len(GROUPS)) for h in range(4)]

        def mm_A(i):
            gi, h = mits[i]
            t0, n = GROUPS[gi]
            ps = nb()
            for c in range(8):
                P.mm(ps[:, 0:n], wq[:, c, h * 128:(h + 1) * 128], xnT[:, c, t0:t0 + n], start=(c == 0), stop=(c == 7))
            P.act(qhs[i % 2][:, 0:n], ps[:, 0:n], AF.Identity, bias=bcol(3600 + 128 * h))

        def mm_B(i):
            gi, h = mits[i]
            t0, n = GROUPS[gi]
            seq = 0 if gi < 4 else 1
            qh, ptm, rlm = qhs[i % 2], ptms[i % 2], rlms[i % 2]
            for mt in range(2):
                sps = nb()
                P.mm(sps[:, 0:n], mkT[:, seq, h, mt * 128:(mt + 1) * 128], qh[:, 0:n])
                P.act(ptm[:, mt, 0:n], sps[:, 0:n], AF.Exp, scale=QSC)
            ops_ = nb()
            lps = nb()
            for mt in range(2):
                P.mm(ops_[:, 0:n], mv[:, seq, mt, h * 128:(h + 1) * 128], ptm[:, mt, 0:n], start=(mt == 0), stop=(mt == 1))
            for mt in range(2):
                P.mm(lps[:, 0:n], ones_b, ptm[:, mt, 0:n], start=(mt == 0), stop=(mt == 1))
            P.act(rlm[:, 0:n], lps[:, 0:n], AF.Ln)
            P.act(rlm[:, 0:n], rlm[:, 0:n], AF.Exp, scale=-1.0)
            P.tt(mT[:, h, t0:t0 + n], ops_[:, 0:n], rlm[:, 0:n], ALU.mult)

        mm_A(0)
        for i in range(len(mits)):
            if i + 1 < len(mits):
                mm_A(i + 1)
            mm_B(i)
        dbg("mT", mT[:, 0, :])
        ar.release(s1)

        P.mark("mem")
        mergedT = ar.bf(128, 8, NT)
        wo = ar.bf(128, 8, 1024)
        s2 = ar.mark()
        wgs = [[ar.bf(128, 8, 128) for b in range(3)] for _ in range(2)]
        wbs = [[ar.bf(128, 4, 128) for b in range(3)] for _ in range(2)]
        sg = [ar.f32(128, 512) for b in range(3)]
        macc = ar.f32(128, 512)
        mtmp = ar.f32(128, 512)
        brs = ("w_br_a", "w_br_b", "w_br_m")
        srcs = (aT, bT, mT)
        for oc in range(8):
            wg_ = wgs[oc % 2]
            wb_ = wbs[oc % 2]
            for b in range(3):
                g0 = 4112 + b * 1024 + oc * 128
                P.dma("pool", wg_[b], wv("w_in")[:, :, g0:g0 + 128])
                P.dma("pool", wb_[b], wv(brs[b])[:, :, oc * 128:(oc + 1) * 128])
            if oc == 2:
                P.dma("pool", wo, wv("w_out"))
            for (t0, n) in GROUPS:
                pp = []
                for b in range(3):
                    g0 = 4112 + b * 1024 + oc * 128
                    ps = nb()
                    for c in range(8):
                        P.mm(ps[:, 0:n], wg_[b][:, c, :], xnT[:, c, t0:t0 + n], start=(c == 0), stop=(c == 7))
                    P.act(sg[b][:, 0:n], ps[:, 0:n], AF.Sigmoid, bias=bcol(g0))
                for b in range(3):
                    ps = nb()
                    for c in range(4):
                        P.mm(ps[:, 0:n], wb_[b][:, c, :], srcs[b][:, c, t0:t0 + n], start=(c == 0), stop=(c == 3))
                    pp.append(ps)
                P.tt(macc[:, 0:n], sg[0][:, 0:n], pp[0][:, 0:n], ALU.mult)
                P.tt(mtmp[:, 0:n], sg[1][:, 0:n], pp[1][:, 0:n], ALU.mult)
                P.tt(macc[:, 0:n], macc[:, 0:n], mtmp[:, 0:n], ALU.add)
                P.tt(mtmp[:, 0:n], sg[2][:, 0:n], pp[2][:, 0:n], ALU.mult)
                P.tt(mergedT[:, oc, t0:t0 + n], macc[:, 0:n], mtmp[:, 0:n], ALU.add)
        dbg("mergedT", mergedT[:, 0, :])
        ar.release(s2)

        P.mark("s2_merge")
        oTs = [ar.f32(128, 8, 512), ar.f32_at(aT_off, 128, 8, 512)]
        xss = [ar.f32(128, 8, 512), ar.f32_at(bT_off, 128, 8, 512)]
        rbs = [ar.f32(128, 512), ar.f32(128, 512)]
        sqs = [ar.bf(128, 8, 512), ar.bf_at(mT_off, 128, 8, 512)]
        x1s3w = x1scr.rearrange("(c p) n -> p c n", p=128)

        def ob_A(gi):
            t0, n = GROUPS[gi]
            oT, xs = oTs[gi % 2], xss[gi % 2]
            P.dma("sp", xs[:, :, 0:n], xT3[:, :, t0:t0 + n])
            for oc in range(8):
                ps = nb()
                for c in range(8):
                    P.mm(ps[:, 0:n], wo[:, c, oc * 128:(oc + 1) * 128], mergedT[:, c, t0:t0 + n], start=(c == 0), stop=(c == 7))
                P.act(oT[:, oc, 0:n], ps[:, 0:n], AF.Copy)
                P.act(sqs[gi % 2][:, oc, 0:n], ps[:, 0:n], AF.Square)

        def ob_B(gi):
            t0, n = GROUPS[gi]
            oT, xs, rb, sq = oTs[gi % 2], xss[gi % 2], rbs[gi % 2], sqs[gi % 2]
            ps = nb()
            for c in range(8):
                P.mm(ps[:, 0:n], ones_b, sq[:, c, 0:n], start=(c == 0), stop=(c == 7))
            P.act(rb[:, 0:n], ps[:, 0:n], AF.Ln, bias=eps_col, scale=1.0 / 1024.0)
            P.act(rb[:, 0:n], rb[:, 0:n], AF.Exp, scale=-0.5)
            ps2 = nb()
            for oc in range(8):
                P.stt(oT[:, oc, 0:n], oT[:, oc, 0:n], g_post[:, oc:oc + 1], rb[:, 0:n], ALU.mult, ALU.mult)
                P.tt(xs[:, oc, 0:n], xs[:, oc, 0:n], oT[:, oc, 0:n], ALU.add)
                P.act(sq[:, oc, 0:n], xs[:, oc, 0:n], AF.Square)
                P.mm(ps2[:, 0:n], ones_b, sq[:, oc, 0:n], start=(oc == 0), stop=(oc == 7))
            P.dma("sp", x1s3w[:, :, t0:t0 + n], xs[:, :, 0:n])
            P.act(rb[:, 0:n], ps2[:, 0:n], AF.Ln, bias=eps_col, scale=1.0 / 1024.0)
            P.act(rb[:, 0:n], rb[:, 0:n], AF.Exp, scale=-0.5)
            for c in range(8):
                P.stt(xnT[:, c, t0:t0 + n], xs[:, c, 0:n], g_fpre[:, c:c + 1], rb[:, 0:n], ALU.mult, ALU.mult)

        ob_A(0)
        for gi in range(len(GROUPS)):
            if gi + 1 < len(GROUPS):
                ob_A(gi + 1)
            ob_B(gi)
        dbg("x1nT", xnT[:, 0, :])
        ar.release(m_after_xn)

        P.mark("s2b_out")
        hidT = ar.bf(128, 22, NT)
        wd3 = wv("w_down")
        wd1 = ar.bf(128, 22, 512)
        m_h = ar.mark()
        wus = [ar.bf(128, 8, 256), ar.bf(128, 8, 256)]
        apre_p = ar.f32(128, 2 + TP)
        bpre_p = ar.f32(128, 2 + TP)
        apre_s = ar.f32(128, 2 + TS)
        bpre_s = ar.f32(128, 2 + TS)
        accas = [ar.f32(128, 512) for _ in range(3)]
        accbs = [ar.f32(128, 512) for _ in range(3)]
        gas = [ar.f32(128, 512) for _ in range(3)]
        fit = [0]
        ftail = [None]
        w_up3 = wv("w_up")
        for c in range(22):
            wu = wus[c % 2]
            P.dma("pool", wu[:, :, 0:128], w_up3[:, :, c * 128:(c + 1) * 128])
            P.dma("pool", wu[:, :, 128:256], w_up3[:, :, 2816 + c * 128:2816 + (c + 1) * 128])
            if c == 2:
                P.dma("pool", wd1, wd3[:, :, 512:1024])
            ja, jb = c, 22 + c
            for gi, (t0, n) in enumerate(GROUPS):
                if gi < 4:
                    apre, bpre = apre_p[:, t0:t0 + n + 2], bpre_p[:, t0:t0 + n + 2]
                else:
                    apre, bpre = apre_s, bpre_s
                if gi == 0:
                    P.memset(apre_p[:, 0:2], 0.0)
                    P.memset(bpre_p[:, 0:2], 0.0)
                if gi == 4:
                    P.dma("sp", apre[:, 0:2], I["sfconv"][:, ja, :])
                    P.dma("sp", bpre[:, 0:2], I["sfconv"][:, jb, :])
                acca, accb, ga = accas[fit[0] % 3], accbs[fit[0] % 3], gas[fit[0] % 3]
                fit[0] += 1
                for (pre, off) in ((apre, 0), (bpre, 128)):
                    ps = nb()
                    for k in range(8):
                        P.mm(ps[:, 0:n], wu[:, k, off:off + 128], xnT[:, k, t0:t0 + n], start=(k == 0), stop=(k == 7))
                    P.act(pre[:, 2:2 + n], ps[:, 0:n], AF.Copy)
                    if off == 128:
                        P.act(accb[:, 0:n], ps[:, 0:n], AF.Identity, scale=fconv_w[:, jb, 2:3])
                    else:
                        P.act(acca[:, 0:n], ps[:, 0:n], AF.Identity, scale=fconv_w[:, ja, 2:3])
                P.stt(acca[:, 0:n], apre[:, 0:n], fconv_w[:, ja, 0:1], acca[:, 0:n], ALU.mult, ALU.add)
                P.stt(acca[:, 0:n], apre[:, 1:1 + n], fconv_w[:, ja, 1:2], acca[:, 0:n], ALU.mult, ALU.add)
                P.stt(accb[:, 0:n], bpre[:, 0:n], fconv_w[:, jb, 0:1], accb[:, 0:n], ALU.mult, ALU.add)
                P.stt(accb[:, 0:n], bpre[:, 1:1 + n], fconv_w[:, jb, 1:2], accb[:, 0:n], ALU.mult, ALU.add)
                if ftail[0] is not None:
                    ftail[0]()

                def _tail(ga=ga, acca=acca, accb=accb, n=n, ja=ja, jb=jb, c=c, t0=t0):
                    P.act(ga[:, 0:n], acca[:, 0:n], AF.Gelu_apprx_tanh, bias=fconv_b[:, ja:ja + 1])
                    P.stt(hidT[:, c, t0:t0 + n], accb[:, 0:n], fconv_b[:, jb:jb + 1], ga[:, 0:n], ALU.add, ALU.mult)
                ftail[0] = _tail
                if gi in (3, 4):
                    so = 0 if gi == 3 else 1
                    P.dma("sp", O["ffn_convT"][so][:, ja, :], apre[:, n:n + 2])
                    P.dma("sp", O["ffn_convT"][so][:, jb, :], bpre[:, n:n + 2])
        ftail[0]()
        dbg("hidT", hidT[:, 0, :])
        ar.release(m_h)
        P.mark("ffn_up")
        wd0 = ar.bf_at(xn_off, 128, 22, 512)
        P.dma("pool", wd0, wd3[:, :, 0:512])
        oT = ar.f32(128, 8, 512)
        xs = ar.f32(128, 8, 512)
        rb = ar.f32(128, 512)
        sqd = ar.bf(128, 8, 512)
        x1s3 = x1scr.rearrange("(c p) n -> p c n", p=128)
        yT3 = O["yT"].rearrange("(c p) n -> p c n", p=128)
        for gi, (t0, n) in enumerate(GROUPS):
            P.dma("sp", xs[:, :, 0:n], x1s3[:, :, t0:t0 + n])
            for oc in (4, 5, 6, 7, 0, 1, 2, 3):
                wd = wd0 if oc < 4 else wd1
                o4 = oc % 4
                ps = nb()
                for c in range(22):
                    P.mm(ps[:, 0:n], wd[:, c, o4 * 128:(o4 + 1) * 128], hidT[:, c, t0:t0 + n], start=(c == 0), stop=(c == 21))
                P.act(oT[:, oc, 0:n], ps[:, 0:n], AF.Copy)
                P.act(sqd[:, oc, 0:n], ps[:, 0:n], AF.Square)
            psr = nb()
            for oc in range(8):
                P.mm(psr[:, 0:n], ones_b, sqd[:, oc, 0:n], start=(oc == 0), stop=(oc == 7))
            P.act(rb[:, 0:n], psr[:, 0:n], AF.Ln, bias=eps_col, scale=1.0 / 1024.0)
            P.act(rb[:, 0:n], rb[:, 0:n], AF.Exp, scale=-0.5)
            for oc in range(8):
                P.stt(oT[:, oc, 0:n], oT[:, oc, 0:n], g_fpost[:, oc:oc + 1], rb[:, 0:n], ALU.mult, ALU.mult)
                P.tt(xs[:, oc, 0:n], xs[:, oc, 0:n], oT[:, oc, 0:n], ALU.add)
            P.dma("sp", yT3[:, :, t0:t0 + n], xs[:, :, 0:n])
        P.mark("ffn_down")
        P.emit(Kq={'pool': 3})
    except _Stop:
        pass
    return nc, DBG, P, 0


_CACHE = {}


def _get_nc(debug=()):
    key = tuple(debug)
    if key not in _CACHE:
        _CACHE[key] = build(debug)
    return _CACHE[key]


def _consts():
    ident = np.eye(128, dtype=np.float32)
    s = np.arange(128)
    trimask = (s[None, :] >= s[:, None]).astype(np.float32)
    negmask = np.where(s[:, None] <= s[None, :], 0.0, -30000.0).astype(np.float32)
    bd = np.zeros((128, 128), np.float32)
    bd[:64, :64] = 1.0
    bd[64:, 64:] = 1.0
    sel4 = np.zeros((4, 4, 128), np.float32)
    for h in range(4):
        sel4[h, h, :] = 1.0
    return dict(ident=ident, trimask=trimask, negmask=negmask, bdones=bd, sel4=sel4)


def _fm(v):
    return np.ascontiguousarray(v.reshape(-1, 128).T)


def _prep_shared(inp):
    f = lambda a: np.ascontiguousarray(np.asarray(a, dtype=np.float32))
    b_in = f(inp["b_in"])[0]
    d = dict(
        w_in=f(inp["w_in"])[0],
        b_fm=np.ascontiguousarray(np.stack([b_in[s:s + 128] for s in FM_STARTS], axis=1)),
        bg_fox=np.ascontiguousarray(b_in[1536:1544][:, None]),
        bg_i=np.ascontiguousarray(b_in[3080:3084][:, None]),
        bg_f=np.ascontiguousarray(b_in[3084:3088][:, None]),
        b_row=np.ascontiguousarray(b_in[None, :]),
        g_pre=_fm(f(inp["norm_mix_pre"])[0]),
        gq2=np.ascontiguousarray(np.concatenate([f(inp["fox_q_norm"])[0]] * 2)[:, None]),
        gk2=np.ascontiguousarray(np.concatenate([f(inp["fox_k_norm"])[0]] * 2)[:, None]),
        mconv_w=np.ascontiguousarray(f(inp["mlstm_conv_w"])[0].reshape(4, 8, 128).transpose(2, 1, 0)),
        mconv_b=_fm(f(inp["mlstm_conv_b"])[0]),
        mhn=np.ascontiguousarray(f(inp["mlstm_head_norm"])[0].T),
        g_mem=_fm(f(inp["norm_mem"])[0]),
        w_mem_kv=f(inp["w_mem_kv"])[0],
        w_br_a=f(inp["w_br_a"])[0], w_br_b=f(inp["w_br_b"])[0], w_br_m=f(inp["w_br_m"])[0],
        w_out=f(inp["w_out"])[0],
        g_post=_fm(f(inp["norm_mix_post"])[0]),
        g_fpre=_fm(f(inp["norm_ffn_pre"])[0]),
        w_up=f(inp["w_up"])[0],
        fconv_w=np.ascontiguousarray(f(inp["ffn_conv_w"])[0].reshape(3, 44, 128).transpose(2, 1, 0)),
        fconv_b=_fm(f(inp["ffn_conv_b"])[0]),
        w_down=f(inp["w_down"])[0],
        g_fpost=_fm(f(inp["norm_ffn_post"])[0]),
    )
    d.update(_consts())
    return d


def _prep_core(inp, b):
    f = lambda a: np.asarray(a, dtype=np.float32)
    c = np.ascontiguousarray
    return dict(
        xT=c(np.concatenate([f(inp["x_prompt"])[b].T, f(inp["x_sample"])[b].T], axis=1)),
        ckT=c(f(inp["cache_fox_k"])[0, b].reshape(PAST, 512).T),
        cv=c(f(inp["cache_fox_v"])[0, b].reshape(32, 128, 8, 64).transpose(2, 1, 0, 3)),
        clogfT=c(f(inp["cache_fox_logf"])[0, b].T),
        sC=c(f(inp["state_mlstm_c"])[0, b].transpose(2, 0, 1)),
        sn=c(f(inp["state_mlstm_n"])[0, b].T),
        sm=c(f(inp["state_mlstm_m"])[0, b][:, None]),
        sconv=c(f(inp["state_mlstm_conv"])[0, b].reshape(3, 8, 128).transpose(2, 1, 0)),
        cmkT=c(f(inp["cache_mem_k"])[0, b].reshape(256, 512).T),
        cmv=c(f(inp["cache_mem_v"])[0, b].reshape(256, 512)),
        sfconv=c(f(inp["state_ffn_conv"])[0, b].reshape(2, 44, 128).transpose(2, 1, 0)),
        memT=c(f(inp["mem_prompt"])[b].T),
    )


def _assemble(results):
    B = len(results)
    z = lambda *s: np.zeros(s, np.float32)
    y_p, y_s = z(B, TP, 1024), z(B, TS, 1024)
    fk_p, fv_p, fl_p = z(1, B, TP, 8, 64), z(1, B, TP, 8, 64), z(1, B, TP, 8)
    fk_s, fv_s, fl_s = z(1, B, TS, 8, 64), z(1, B, TS, 8, 64), z(1, B, TS, 8)
    c_p, n_p, m_p = z(1, B, 4, 128, 128), z(1, B, 4, 128), z(1, B, 4)
    c_s, n_s, m_s = z(1, B, 4, 128, 128), z(1, B, 4, 128), z(1, B, 4)
    cv_p, cv_s = z(1, B, 3, 1024), z(1, B, 3, 1024)
    fc_p, fc_s = z(1, B, 2, 5632), z(1, B, 2, 5632)
    mk_p, mv_p = z(1, B, 256, 4, 128), z(1, B, 256, 4, 128)
    for b, r in enumerate(results):
        yT = r["yT"]
        y_p[b] = yT[:, :TP].T
        y_s[b] = yT[:, TP:].T
        kT = r["fox_kT"]
        fk_p[0, b] = kT[:, :TP].T.reshape(TP, 8, 64)
        fk_s[0, b] = kT[:, TP:].T.reshape(TS, 8, 64)
        fv = r["fox_v"]
        fv_p[0, b] = fv[:TP].reshape(TP, 8, 64)
        fv_s[0, b] = fv[TP:].reshape(TS, 8, 64)
        lf = r["fox_logfT"]
        fl_p[0, b] = lf[:, :TP].T
        fl_s[0, b] = lf[:, TP:].T
        for (si, cc, nn, mm, cvv, fcc) in ((0, c_p, n_p, m_p, cv_p, fc_p), (1, c_s, n_s, m_s, cv_s, fc_s)):
            cc[0, b] = r["ml_cT"][si].transpose(1, 2, 0)
            nn[0, b] = r["ml_n"][si].T
            mm[0, b] = r["ml_m"][si][:, 0]
            cvv[0, b] = r["ml_convT"][si].transpose(2, 1, 0).reshape(3, 1024)
            fcc[0, b] = r["ffn_convT"][si].transpose(2, 1, 0).reshape(2, 5632)
        mk_p[0, b] = r["mem_kT"].T.reshape(256, 4, 128)
        mv_p[0, b] = r["mem_v"].reshape(256, 4, 128)
    return (y_p, y_s, fk_p, fv_p, fl_p, c_p, n_p, m_p, cv_p, fc_p, mk_p, mv_p,
            fk_s, fv_s, fl_s, c_s, n_s, m_s, cv_s, fc_s)


def kernel(**inputs):
    nc = _get_nc()[0]
    shared = _prep_shared(inputs)
    in_maps = []
    for b in range(8):
        d = dict(shared)
        d.update(_prep_core(inputs, b))
        in_maps.append(d)
    res = run_bass_kernel_spmd(nc, in_maps, core_ids=list(range(8)))
    return _assemble(res.results)
```
